# Optimizing a Trainium2 kernel written in Bass

```python
import math
import jax
import jax.numpy as jnp
from jax import lax
import numpy as np

D_MODEL = 1024
BATCH = 8
SEQ = 4096
DEPTH = 4

HEAD_DIM = 64
ROPE_THETA = 10000.0
GRID_W = 64
Q_BLOCK = 128
EPS = 1e-6
NEG_INF = -1e30

DA_HEADS = 4
GQ_HEADS = 8
GQ_KV = 2
DIL_GROUPS = ((128, 1), (512, 4), (2048, 16))
DIL_HEADS = 8
WIN_BLOCK = 64
NA_HEADS = 8
NA_ROWS = 8
NA_COLS = 16
NA_QC = 16
NA_KC_BLK = NA_QC + NA_COLS

N_BRANCH = 4
BRANCH_W = 512
D_FF = 2816
CONV_W = 3

A_QK = DA_HEADS * 2 * HEAD_DIM
A_V = DA_HEADS * 2 * HEAD_DIM
B_Q = GQ_HEADS * HEAD_DIM
B_KV = GQ_KV * HEAD_DIM
C_QKV = len(DIL_GROUPS) * DIL_HEADS * HEAD_DIM
D_QKV = NA_HEADS * HEAD_DIM
GATE_W = N_BRANCH * D_MODEL
PART_SIZES = (A_QK, A_QK, A_V, B_Q, B_KV, B_KV, C_QKV, C_QKV, C_QKV, D_QKV, D_QKV, D_QKV, GATE_W)
N_IN = sum(PART_SIZES)

kernel_name = 'hybrid_gated_mixer_encoder'


def rms_norm(x, g):
    xf = x.astype(jnp.float32)
    y = xf * lax.rsqrt(jnp.mean(xf * xf, axis=-1, keepdims=True) + EPS)
    return (y * g.astype(jnp.float32)).astype(x.dtype)


def rope_angles(pos, dim):
    inv = ROPE_THETA ** (-(jnp.arange(0, dim, 2, dtype=jnp.float32) / dim))
    return pos.astype(jnp.float32)[:, None] * inv[None, :]


def rope(x, ang):
    f = ang.shape[-1]
    shape = (1, ang.shape[0]) + (1,) * (x.ndim - 3) + (f,)
    cos = jnp.cos(ang).reshape(shape).astype(x.dtype)
    sin = jnp.sin(ang).reshape(shape).astype(x.dtype)
    x1, x2 = x[..., :f], x[..., f:]
    return jnp.concatenate([x1 * cos - x2 * sin, x2 * cos + x1 * sin], axis=-1)


def axial_rope(x, ang_row, ang_col):
    half = HEAD_DIM // 2
    return jnp.concatenate([rope(x[..., :half], ang_row), rope(x[..., half:], ang_col)], axis=-1)


def split_parts(p):
    out = []
    off = 0
    for n in PART_SIZES:
        out.append(p[..., off:off + n])
        off += n
    return out


def dense_block_sweep(fn, q):
    b, s = q.shape[:2]
    nb = s // Q_BLOCK
    qb = jnp.moveaxis(q.reshape((b, nb, Q_BLOCK) + q.shape[2:]), 1, 0)
    ob = lax.map(fn, qb)
    return jnp.moveaxis(ob, 0, 1).reshape((b, s) + ob.shape[3:])


def diff_attention(q, k, v, lam, subln_g, lambda_init):
    lf = lam.astype(jnp.float32)
    lam_val = jnp.exp(jnp.sum(lf[0] * lf[1])) - jnp.exp(jnp.sum(lf[2] * lf[3])) + lambda_init
    scale = HEAD_DIM ** -0.5

    def block(qb):
        sc = jnp.einsum('bqhcd,bkhcd->bhcqk', qb, k).astype(jnp.float32) * scale
        p = jax.nn.softmax(sc, axis=-1)
        a = p[:, :, 0] - lam_val * p[:, :, 1]
        return jnp.einsum('bhqk,bkhe->bqhe', a.astype(v.dtype), v)

    o = dense_block_sweep(block, q)
    o = rms_norm(o, subln_g) * (1.0 - lambda_init)
    return o.reshape(o.shape[0], o.shape[1], -1)


def gqa_attention(q, k, v):
    b, s, hq, dh = q.shape
    g = k.shape[2]
    q = q.reshape(b, s, g, hq // g, dh)
    scale = dh ** -0.5

    def block(qb):
        sc = jnp.einsum('bqgrd,bkgd->bgrqk', qb, k).astype(jnp.float32) * scale
        p = jax.nn.softmax(sc, axis=-1).astype(v.dtype)
        return jnp.einsum('bgrqk,bkgd->bqgrd', p, v)

    o = dense_block_sweep(block, q)
    return o.reshape(b, s, hq * dh)


def banded_window_attention(q, k, v, radius):
    n, L, h, dh = q.shape
    nb = -(-L // WIN_BLOCK)
    lp = nb * WIN_BLOCK
    span = WIN_BLOCK + 2 * radius
    qb = jnp.pad(q, ((0, 0), (0, lp - L), (0, 0), (0, 0))).reshape(n, nb, WIN_BLOCK, h, dh)
    pad_kv = ((0, 0), (radius, lp - L + radius), (0, 0), (0, 0))
    idx = np.arange(nb)[:, None] * WIN_BLOCK + np.arange(span)[None, :]
    kb = jnp.pad(k, pad_kv)[:, idx]
    vb = jnp.pad(v, pad_kv)[:, idx]
    key_pos = idx - radius
    q_pos = np.arange(nb)[:, None] * WIN_BLOCK + np.arange(WIN_BLOCK)[None, :]
    ok = ((np.abs(key_pos[:, None, :] - q_pos[:, :, None]) <= radius)
          & (key_pos[:, None, :] >= 0) & (key_pos[:, None, :] < L))
    sc = jnp.einsum('nbqhd,nbkhd->nbhqk', qb, kb).astype(jnp.float32) * (dh ** -0.5)
    sc = jnp.where(ok[None, :, None], sc, NEG_INF)
    lse = jax.nn.logsumexp(sc, axis=-1)
    p = jnp.exp(sc - lse[..., None]).astype(v.dtype)
    o = jnp.einsum('nbhqk,nbkhd->nbqhd', p, vb).reshape(n, lp, h, dh)[:, :L]
    lse = jnp.transpose(lse, (0, 1, 3, 2)).reshape(n, lp, h)[:, :L]
    return o, lse


def dilated_group(q, k, v, window, dilation):
    b, s, h, dh = q.shape
    L = s // dilation
    radius = window // (2 * dilation)

    def to_sub(t):
        return t.reshape(b, L, dilation, h, dh).transpose(0, 2, 1, 3, 4).reshape(b * dilation, L, h, dh)

    o, lse = banded_window_attention(to_sub(q), to_sub(k), to_sub(v), radius)
    o = o.reshape(b, dilation, L, h, dh).transpose(0, 2, 1, 3, 4).reshape(b, s, h, dh)
    lse = lse.reshape(b, dilation, L, h).transpose(0, 2, 1, 3).reshape(b, s, h)
    return o, lse


def dilated_attention(q, k, v):
    b, s = q.shape[:2]
    outs, lses = [], []
    for gi, (window, dilation) in enumerate(DIL_GROUPS):
        o, lse = dilated_group(q[:, :, gi], k[:, :, gi], v[:, :, gi], window, dilation)
        outs.append(o)
        lses.append(lse)
    w = jax.nn.softmax(jnp.stack(lses, axis=0), axis=0)
    o = jnp.einsum('gbsh,gbshd->bshd', w, jnp.stack(outs, axis=0).astype(jnp.float32))
    return o.astype(q.dtype).reshape(b, s, -1)


def neighbourhood_attention(q, k, v, rpb):
    b, s, h, dh = q.shape
    rows = s // GRID_W
    kr = min(NA_ROWS, rows)
    ncb = GRID_W // NA_QC
    q_cols = np.arange(ncb)[:, None] * NA_QC + np.arange(NA_QC)[None, :]
    blk_start = np.clip(np.arange(ncb) * NA_QC - NA_COLS // 2, 0, GRID_W - NA_KC_BLK)
    key_cols = blk_start[:, None] + np.arange(NA_KC_BLK)[None, :]
    win_start = np.clip(q_cols - NA_COLS // 2, 0, GRID_W - NA_COLS)
    col_ok = ((key_cols[:, None, :] >= win_start[:, :, None])
              & (key_cols[:, None, :] < win_start[:, :, None] + NA_COLS))
    col_idx = np.clip(key_cols[:, None, :] - q_cols[:, :, None] + NA_COLS - 1, 0, 2 * NA_COLS - 2)
    qg = q.reshape(b, rows, GRID_W, h, dh)
    kg = k.reshape(b, rows, GRID_W, h, dh)[:, :, key_cols]
    vg = v.reshape(b, rows, GRID_W, h, dh)[:, :, key_cols]
    rpb_f = rpb.astype(jnp.float32)
    scale = dh ** -0.5

    def row(i):
        rs = jnp.clip(i - kr // 2, 0, rows - kr)
        kb = lax.dynamic_slice_in_dim(kg, rs, kr, axis=1)
        vb = lax.dynamic_slice_in_dim(vg, rs, kr, axis=1)
        qi = lax.dynamic_index_in_dim(qg, i, axis=1, keepdims=False).reshape(b, ncb, NA_QC, h, dh)
        sc = jnp.einsum('bcqhd,brckhd->bhcqrk', qi, kb).astype(jnp.float32) * scale
        rel_r = rs + jnp.arange(kr) - i + NA_ROWS - 1
        bias = rpb_f[:, rel_r][:, :, col_idx]
        sc = sc + jnp.transpose(bias, (0, 2, 3, 1, 4))[None]
        sc = jnp.where(col_ok[None, None, :, :, None, :], sc, NEG_INF)
        p = jax.nn.softmax(sc.reshape(sc.shape[:4] + (kr * NA_KC_BLK,)), axis=-1).reshape(sc.shape)
        o = jnp.einsum('bhcqrk,brckhd->bcqhd', p.astype(v.dtype), vb)
        return o.reshape(b, GRID_W, h, dh)

    o = lax.map(row, jnp.arange(rows))
    return jnp.moveaxis(o, 0, 1).reshape(b, s, h * dh)


def conv_ffn(h, w_up, conv_w, conv_b, w_down):
    u = h @ w_up
    u = lax.conv_general_dilated(u, conv_w[:, None, :], window_strides=(1,), padding=((CONV_W // 2, CONV_W // 2),),
                                 dimension_numbers=('NWC', 'WIO', 'NWC'), feature_group_count=u.shape[-1]) + conv_b
    a, g = u[..., :D_FF], u[..., D_FF:]
    return (jax.nn.silu(a) * g) @ w_down


def setup_inputs(seed: int = 0) -> dict:
    key = jax.random.key(seed)
    ks = jax.random.split(key, 16)

    def nrm(k, shape, scale):
        return jax.random.normal(k, shape, jnp.float32) * scale

    return {
        'x': nrm(ks[0], (BATCH, SEQ, D_MODEL), 1.0),
        'norm1_g': 1.0 + nrm(ks[1], (DEPTH, D_MODEL), 0.02),
        'w_in': nrm(ks[2], (DEPTH, D_MODEL, N_IN), D_MODEL ** -0.5),
        'qk_g': 1.0 + nrm(ks[3], (DEPTH, N_BRANCH, 2, HEAD_DIM), 0.02),
        'lam': nrm(ks[4], (DEPTH, 4, HEAD_DIM), 0.1),
        'subln_g': 1.0 + nrm(ks[5], (DEPTH, 2 * HEAD_DIM), 0.02),
        'rpb': nrm(ks[6], (DEPTH, NA_HEADS, 2 * NA_ROWS - 1, 2 * NA_COLS - 1), 0.1),
        'w_branch': nrm(ks[7], (DEPTH, N_BRANCH, BRANCH_W, D_MODEL), BRANCH_W ** -0.5),
        'w_out': nrm(ks[8], (DEPTH, D_MODEL, D_MODEL), D_MODEL ** -0.5),
        'norm2_g': 1.0 + nrm(ks[9], (DEPTH, D_MODEL), 0.02),
        'w_up': nrm(ks[10], (DEPTH, D_MODEL, 2 * D_FF), D_MODEL ** -0.5),
        'conv_w': nrm(ks[11], (DEPTH, CONV_W, 2 * D_FF), CONV_W ** -0.5),
        'conv_b': nrm(ks[12], (DEPTH, 2 * D_FF), 0.02),
        'w_down': nrm(ks[13], (DEPTH, D_FF, D_MODEL), D_FF ** -0.5),
    }


def reference(x, norm1_g, w_in, qk_g, lam, subln_g, rpb, w_branch, w_out, norm2_g, w_up, conv_w, conv_b, w_down):
    b, s, _ = x.shape
    pos = jnp.arange(s, dtype=jnp.int32)
    ang_1d = rope_angles(pos, HEAD_DIM)
    ang_row = rope_angles(pos // GRID_W, HEAD_DIM // 2)
    ang_col = rope_angles(pos % GRID_W, HEAD_DIM // 2)
    n_dil = len(DIL_GROUPS)
    for l in range(DEPTH):
        lambda_init = 0.8 - 0.6 * math.exp(-0.3 * l)
        h = rms_norm(x, norm1_g[l])
        aq, ak, av, bq, bk, bv, cq, ck, cv, dq, dk, dv, g = split_parts(h @ w_in[l])
        gq = qk_g[l]
        aq = rope(rms_norm(aq.reshape(b, s, DA_HEADS, 2, HEAD_DIM), gq[0, 0]), ang_1d)
        ak = rope(rms_norm(ak.reshape(b, s, DA_HEADS, 2, HEAD_DIM), gq[0, 1]), ang_1d)
        o_a = diff_attention(aq, ak, av.reshape(b, s, DA_HEADS, 2 * HEAD_DIM), lam[l], subln_g[l], lambda_init)
        bq = axial_rope(rms_norm(bq.reshape(b, s, GQ_HEADS, HEAD_DIM), gq[1, 0]), ang_row, ang_col)
        bk = axial_rope(rms_norm(bk.reshape(b, s, GQ_KV, HEAD_DIM), gq[1, 1]), ang_row, ang_col)
        o_b = gqa_attention(bq, bk, bv.reshape(b, s, GQ_KV, HEAD_DIM))
        cq = rope(rms_norm(cq.reshape(b, s, n_dil, DIL_HEADS, HEAD_DIM), gq[2, 0]), ang_1d)
        ck = rope(rms_norm(ck.reshape(b, s, n_dil, DIL_HEADS, HEAD_DIM), gq[2, 1]), ang_1d)
        o_c = dilated_attention(cq, ck, cv.reshape(b, s, n_dil, DIL_HEADS, HEAD_DIM))
        dq = rms_norm(dq.reshape(b, s, NA_HEADS, HEAD_DIM), gq[3, 0])
        dk = rms_norm(dk.reshape(b, s, NA_HEADS, HEAD_DIM), gq[3, 1])
        o_d = neighbourhood_attention(dq, dk, dv.reshape(b, s, NA_HEADS, HEAD_DIM), rpb[l])
        gate = jax.nn.sigmoid(g.reshape(b, s, N_BRANCH, D_MODEL).astype(jnp.float32)).astype(x.dtype)
        merged = gate[:, :, 0] * (o_a @ w_branch[l, 0])
        for i, o_i in ((1, o_b), (2, o_c), (3, o_d)):
            merged = merged + gate[:, :, i] * (o_i @ w_branch[l, i])
        x = x + merged @ w_out[l]
        x = x + conv_ffn(rms_norm(x, norm2_g[l]), w_up[l], conv_w[l], conv_b[l], w_down[l])
    return x
```

```python
import numpy as np
from contextlib import ExitStack
import concourse.bass as bass
import concourse.mybir as mybir
from concourse.bass_utils import run_bass_kernel_spmd

F32 = mybir.dt.float32
BF16 = mybir.dt.bfloat16
AF = mybir.ActivationFunctionType
ALU = mybir.AluOpType
AX = mybir.AxisListType

EPOCH = 24000
SAME_SYNC = True

SEQ = 4096
DM = 1024
NL = 4
N_IN = 12544
D_FF = 2816
EPS = 1e-6
NT = 8
TC = 512
COL = dict(aq=0, ak=512, av=1024, bq=1536, bk=2048, bv=2176, cq=2304, ck=3840, cv=5376,
           dq=6912, dk=7424, dv=7936, g=8448)
DIL = (1, 4, 16)


class Buf:
    __slots__ = ("name", "w", "r")

    def __init__(self, name=""):
        self.name = name
        self.w = None
        self.r = []


class _Op:
    __slots__ = ("eng", "fn", "deps", "ddeps", "dma", "pub", "seq")

    def __init__(self, eng, fn, deps, ddeps, dma):
        self.eng = eng
        self.fn = fn
        self.deps = deps
        self.ddeps = ddeps
        self.dma = dma
        self.pub = False
        self.seq = None


class Sched:
    ENGS = ("pe", "act", "dve", "pool", "sp")

    def __init__(self, nc, stack):
        self.nc = nc
        self.stack = stack
        self.ops = []
        self.base = 0
        self.cnt = {e: 0 for e in self.ENGS}
        self.sems = {e: [] for e in self.ENGS}
        self.dsem = {}
        self.dcnt = {}
        self.nsem = 0
        self.ninstr = 0

    def _new_sem(self, name):
        self.nsem += 1
        return self.stack.enter_context(self.nc.semaphore(name))

    def _esem(self, e, ep):
        while len(self.sems[e]) <= ep:
            self.sems[e].append(self._new_sem("s_%s_%d" % (e, len(self.sems[e]))))
        return self.sems[e][ep]

    def op(self, eng, fn, reads=(), writes=(), dma=None):
        deps = set()
        ddeps = {}
        base = self.base
        ops = self.ops

        def add(i):
            if i is None or i < base:
                return
            o = ops[i - base]
            if o.dma is not None:
                ddeps[o.dma] = self.dcnt[o.dma]
            else:
                deps.add(i)

        for b in reads:
            add(b.w)
        for b in writes:
            add(b.w)
            for r in b.r:
                add(r)
        idx = base + len(ops)
        if dma is not None:
            if dma not in self.dsem:
                self.dsem[dma] = self._new_sem("d_%d" % len(self.dsem))
                self.dcnt[dma] = 0
            self.dcnt[dma] += 16
        ops.append(_Op(eng, fn, deps, ddeps, dma))
        for b in reads:
            b.r.append(idx)
        for b in writes:
            b.w = idx
            b.r = []
        return idx

    def flush(self):
        ops = self.ops
        base = self.base
        last = {}
        for i, o in enumerate(ops):
            if o.dma is None and o.fn is not None:
                last[o.eng] = i
        for e in self.ENGS:
            deps = set(base + i for ee, i in last.items() if ee != e)
            ops.append(_Op(e, None, deps, dict(self.dcnt), None))
        for o in ops:
            for d in o.deps:
                od = ops[d - base]
                if od.eng != o.eng or o.dma is not None or (SAME_SYNC and o.eng != "pe"):
                    od.pub = True
        for o in ops:
            if o.pub:
                c = self.cnt[o.eng]
                self.cnt[o.eng] = c + 1
                o.seq = (c // EPOCH, c % EPOCH + 1)
        per = {e: [] for e in self.ENGS}
        seen = {e: {} for e in self.ENGS}
        seend = {e: {} for e in self.ENGS}
        for o in ops:
            F = o.eng
            need = {}
            for d in o.deps:
                od = ops[d - base]
                if od.eng == F and o.dma is None and (F == "pe" or not SAME_SYNC):
                    continue
                if od.seq > need.get(od.eng, (-1, -1)):
                    need[od.eng] = od.seq
            waits = []
            for E, sq in need.items():
                if seen[F].get(E, (-1, -1)) >= sq:
                    continue
                seen[F][E] = sq
                waits.append((self._esem(E, sq[0]), sq[1]))
            for k, v in o.ddeps.items():
                if v == 0 or seend[F].get(k, 0) >= v:
                    continue
                seend[F][k] = v
                waits.append((self.dsem[k], v))
            inc = self._esem(F, o.seq[0]) if o.pub else None
            dinc = self.dsem[o.dma] if o.dma is not None else None
            per[F].append((waits, o.fn, inc, dinc))
            self.ninstr += len(waits) + 1

        def run(eng, lst):
            for waits, fn, inc, dinc in lst:
                for s, v in waits:
                    eng.wait_ge(s, v)
                if fn is None:
                    continue
                ins = fn(eng)
                if inc is not None:
                    ins.then_inc(inc, 1)
                if dinc is not None:
                    ins.then_inc(dinc, 16)

        with self.nc.Block() as block:
            @block.tensor
            def _(e):
                run(e, per["pe"])

            @block.scalar
            def _(e):
                run(e, per["act"])

            @block.vector
            def _(e):
                run(e, per["dve"])

            @block.gpsimd
            def _(e):
                run(e, per["pool"])

            @block.sync
            def _(e):
                run(e, per["sp"])
        self.base = base + len(ops)
        self.ops = []


class Tl:
    __slots__ = ("t", "b", "name", "psum")

    def __init__(self, t, name, psum=False):
        self.t = t
        self.b = Buf(name)
        self.name = name
        self.psum = psum

    def __getitem__(self, k):
        return self.t[k]


def _host_consts():
    c = {}
    pos = np.arange(SEQ, dtype=np.float32)
    p = np.arange(128)
    inv32 = (np.float32(10000.0) ** (-(np.arange(0, 64, 2, dtype=np.float32) / np.float32(64)))).astype(np.float32)
    f = p % 32
    half = (p % 64) // 32
    ang = (pos[None, :] * inv32[f][:, None]).astype(np.float32)
    sgn = np.where(half == 0, -1.0, 1.0).astype(np.float32)[:, None]
    tab1 = np.stack([np.cos(ang), np.sin(ang) * sgn], axis=1).astype(np.float32)
    inv16 = (np.float32(10000.0) ** (-(np.arange(0, 32, 2, dtype=np.float32) / np.float32(32)))).astype(np.float32)
    pp = p % 64
    blk = pp // 32
    q = pp % 32
    half2 = q // 16
    f2 = q % 16
    prow = np.floor(pos / 64).astype(np.float32)
    pcol = (pos - prow * 64).astype(np.float32)
    pf = np.where(blk[:, None] == 0, prow[None, :], pcol[None, :]).astype(np.float32)
    ang2 = (pf * inv16[f2][:, None]).astype(np.float32)
    sgn2 = np.where(half2 == 0, -1.0, 1.0).astype(np.float32)[:, None]
    tab2 = np.stack([np.cos(ang2), np.sin(ang2) * sgn2], axis=1).astype(np.float32)
    c["tab1"] = tab1.reshape(128, 2 * SEQ)
    c["tab2"] = tab2.reshape(128, 2 * SEQ)
    mats = np.zeros((128, 5, 128), np.float32)
    for m in range(128):
        part1 = m + 32 if (m % 64) // 32 == 0 else m - 32
        mats[part1, 0, m] = 1.0
        part2 = m + 16 if (m % 32) // 16 == 0 else m - 16
        mats[part2, 1, m] = 1.0
    mats[:, 2, :] = (p[:, None] // 64 == p[None, :] // 64)
    mats[:, 3, :] = 1.0
    mats[:, 4, :] = np.eye(128)
    c["mats"] = mats.reshape(128, 5 * 128)
    i = np.arange(128)[:, None]
    m = np.arange(128)[None, :]
    mask3 = np.stack([(i - m >= 64), (np.abs(i - m) <= 64), (m - i >= 64)], axis=1).astype(np.float32)
    c["mask3"] = mask3.reshape(128, 3 * 128)
    kc = np.arange(128)[:, None] % 64
    qc = np.arange(64)[None, :]
    ws = np.clip(qc - 8, 0, 48)
    c["colmask"] = ((kc >= ws) & (kc < ws + 16)).astype(np.float32)
    return c


P_G1 = 0
P_G2 = 8
P_QKG = 16
P_SUB = 24
P_CW = 25
P_CB = 25 + 132
P_N = 25 + 132 + 44


def _pack_params(inp):
    out = np.zeros((128, NL, P_N), np.float32)
    for l in range(NL):
        out[:, l, P_G1:P_G1 + 8] = inp["norm1_g"][l].reshape(8, 128).T
        out[:, l, P_G2:P_G2 + 8] = inp["norm2_g"][l].reshape(8, 128).T
        qg = inp["qk_g"][l].reshape(8, 64)
        out[:, l, P_QKG:P_QKG + 8] = np.concatenate([qg, qg], axis=1).T
        out[:, l, P_SUB] = inp["subln_g"][l]
        cw = inp["conv_w"][l].reshape(3, 44, 128)
        out[:, l, P_CW:P_CW + 132] = cw.transpose(2, 0, 1).reshape(128, 132)
        out[:, l, P_CB:P_CB + 44] = inp["conv_b"][l].reshape(44, 128).T
    return out.reshape(128, NL * P_N)


def _pack_rpb(rpb):
    kc = np.arange(64)[:, None]
    qc = np.arange(64)[None, :]
    idx = np.clip(kc - qc + 15, 0, 30)
    g = rpb[:, :, :, idx]
    g = np.transpose(g, (0, 3, 1, 2, 4))
    lo = g
    hi = np.concatenate([g[:, :, :, 1:, :], g[:, :, :, 14:15, :]], axis=3)
    return np.ascontiguousarray(np.concatenate([lo, hi], axis=1)).reshape(NL, 128, 8 * 15 * 64)


class Builder:
    def __init__(self, n_layers=NL, debug=False):
        self.n_layers = n_layers
        self.debug = debug
        self.nc = bass.Bass("TRN2", target_bir_lowering=False)
        nc = self.nc
        di = lambda n, s, dt=F32: nc.dram_tensor(n, s, dt, kind="ExternalInput").ap()
        self.xT = di("xT", [DM, SEQ])
        self.w_in = di("w_in", [NL, DM, N_IN])
        self.w_branch = di("w_branch", [NL, 4, 512, DM])
        self.w_out = di("w_out", [NL, DM, DM])
        self.w_up = di("w_up", [NL, DM, 2 * D_FF])
        self.w_down = di("w_down", [NL, D_FF, DM])
        self.params = di("params", [128, NL * P_N])
        self.lam = di("lam", [1, NL * 256])
        self.rpbE = di("rpbE", [NL, 128, 8 * 15 * 64])
        self.c_tab1 = di("tab1", [128, 2 * SEQ])
        self.c_tab2 = di("tab2", [128, 2 * SEQ])
        self.c_mats = di("mats", [128, 5 * 128])
        self.c_mask3 = di("mask3", [128, 3 * 128])
        self.c_colmask = di("colmask", [128, 64])
        self.yT = nc.dram_tensor("yT", [DM, SEQ], F32, kind="ExternalOutput").ap()
        kind = "ExternalOutput" if debug else "Internal"
        ds = lambda n, s, dt: nc.dram_tensor(n, s, dt, kind=kind).ap()
        self.QK = ds("s_qk", [45 * 128, SEQ], BF16)
        self.VA = ds("s_va", [SEQ, 512], BF16)
        self.VB = ds("s_vb", [SEQ, 130], BF16)
        self.VC = ds("s_vc", [3, SEQ, 520], BF16)
        self.VD = ds("s_vd", [SEQ, 520], BF16)
        self.G = ds("s_g", [4096, SEQ], F32)
        self.OT = ds("s_ot", [4, 512, SEQ], BF16)
        self.M = ds("s_m", [D_FF, SEQ], BF16)
        self.XM = ds("s_xm", [DM, SEQ], F32)
        self.XA = ds("s_xa", [DM, SEQ], F32)
        self.XB = ds("s_xb", [DM, SEQ], F32)

    def tile(self, name, shape, dt):
        t = self.ph.enter_context(self.nc.sbuf_tensor(name + "_%d" % self.uid, shape, dt))
        self.uid += 1
        return Tl(t, name)

    def ptile(self, name, shape=(128, 512), dt=F32):
        t = self.ph.enter_context(self.nc.psum_tensor(name + "_%d" % self.uid, [128, 512], F32))
        self.uid += 1
        return Tl(t, name, True)

    def A(self, eng, meth, reads, writes, *a, **k):
        wr = [x.b for x in writes] + [x.b for x in reads if x.psum]
        self.S.op(eng, lambda e: getattr(e, meth)(*a, **k), [x.b for x in reads], wr)

    def load(self, dst_tl, out_ap, in_ap, q="sp"):
        self.S.op(q, lambda e: e.dma_start(out=out_ap, in_=in_ap), [], [dst_tl.b], dma=("L", dst_tl.name))

    def store(self, src_tl, out_ap, in_ap, q="pool"):
        self.S.op(q, lambda e: e.dma_start(out=out_ap, in_=in_ap), [src_tl.b], [], dma=("S", src_tl.name))

    def mm(self, out_tl, out_ap, l_tl, l_ap, r_tl, r_ap, start, stop):
        self.S.op("pe", lambda e: e.matmul(out_ap, l_ap, r_ap, start=start, stop=stop),
                  [l_tl.b, r_tl.b], [out_tl.b])

    def begin(self):
        self.ph = ExitStack()
        self.ph.__enter__()

    def end(self):
        self.S.flush()
        self.ph.__exit__(None, None, None)

    def build(self):
        nc = self.nc
        self.uid = 0
        with ExitStack() as top:
            self.S = Sched(nc, top)
            self.top = top
            self.ph = top
            self.mats = self.tile("mats", [128, 5, 128], F32)
            self.onesb = self.tile("onesb", [128, 128], BF16)
            self.par = self.tile("par", [128, NL, P_N], F32)
            self.nlam = self.tile("nlam", [128, NL], F32)
            self.begin()
            self.lamt = self.tile("lamt", [1, NL * 256], F32)
            self.load(self.mats, self.mats[:].rearrange("p a b -> p (a b)"), self.c_mats)
            self.load(self.par, self.par[:].rearrange("p a b -> p (a b)"), self.params)
            self.load(self.lamt, self.lamt[:], self.lam)
            self.A("dve", "tensor_copy", [self.mats], [self.onesb], out=self.onesb[:], in_=self.mats[:, 3, :])
            self.lambda_setup()
            self.end()
            xin = self.xT
            for l in range(self.n_layers):
                last = (l == self.n_layers - 1)
                xout = self.yT if last else (self.XA if l % 2 == 0 else self.XB)
                self.layer(l, xin, xout)
                xin = xout
        return nc

    def lambda_setup(self):
        import math
        pr = self.tile("lampr", [1, NL * 2, 64], F32)
        sm = self.tile("lamsm", [1, NL * 2], F32)
        lv = self.tile("lamv", [1, NL], F32)
        ps = self.ptile("lamps", (128, NL))
        lt = self.lamt[:].rearrange("p (l a d) -> p l a d", l=NL, a=4)
        for l in range(NL):
            for j in range(2):
                self.A("dve", "tensor_tensor", [self.lamt], [pr], out=pr[:, l * 2 + j, :], in0=lt[:, l, 2 * j, :],
                       in1=lt[:, l, 2 * j + 1, :], op=ALU.mult)
        self.A("dve", "tensor_reduce", [pr], [sm], out=sm[:], in_=pr[:], axis=AX.X, op=ALU.add)
        self.A("act", "activation", [sm], [sm], out=sm[:], in_=sm[:], func=AF.Exp)
        smv = sm[:].rearrange("p (l j) -> p l j", j=2)
        for l in range(NL):
            li = 0.8 - 0.6 * math.exp(-0.3 * l)
            self.A("dve", "scalar_tensor_tensor", [sm], [lv], out=lv[:, l:l + 1], in0=smv[:, l, 1:2], scalar=-li,
                   in1=smv[:, l, 0:1], op0=ALU.add, op1=ALU.subtract)
        self.mm(ps, ps[:, 0:NL], self.mats, self.mats[0:1, 3, :], lv, lv[:], True, True)
        self.A("dve", "tensor_copy", [ps], [self.nlam], out=self.nlam[:], in_=ps[:, 0:NL])

    def rmsnorm_to_hT(self, l, xsrc, hT, goff):
        xs = [self.tile("nx%d" % i, [128, 8, TC], F32) for i in range(2)]
        sq = [self.tile("nsq%d" % i, [128, 8, TC], F32) for i in range(1)]
        rs = [self.tile("nrs%d" % i, [128, TC], F32) for i in range(2)]
        ps = [self.ptile("nps%d" % i) for i in range(2)]
        xv = xsrc.rearrange("(c p) n -> p c n", p=128)
        for t in range(NT):
            x = xs[t % 2]
            s = sq[0]
            r = rs[t % 2]
            p = ps[t % 2]
            self.load(x, x[:], xv[:, :, t * TC:(t + 1) * TC])
            self.A("pool", "tensor_tensor", [x], [s], out=s[:], in0=x[:], in1=x[:], op=ALU.mult)
            for c in range(8):
                self.mm(p, p[:], self.mats, self.mats[:, 3, :], s, s[:, c, :], c == 0, c == 7)
            self.A("act", "activation", [p], [r], out=r[:], in_=p[:], func=AF.Sqrt, scale=1.0 / DM, bias=self.epsc[:, 0:1])
            self.A("dve", "reciprocal", [r], [r], out=r[:], in_=r[:])
            for c in range(8):
                self.A("dve", "scalar_tensor_tensor", [x, r, self.par], [hT], out=hT[:, c, t * TC:(t + 1) * TC],
                       in0=x[:, c, :], scalar=self.par[:, l, goff + c:goff + c + 1], in1=r[:], op0=ALU.mult, op1=ALU.mult)

    def load_weight_bf16(self, dst, dst_ap, src_ap, stg, shape_ap, eng):
        self.load(stg, shape_ap, src_ap)
        self.A(eng, "tensor_copy", [stg], [dst], out=dst_ap, in_=shape_ap)

    def layer(self, l, xin, xout):
        import math
        self.lambda_init = 0.8 - 0.6 * math.exp(-0.3 * l)
        self.begin()
        self.epsc = self.tile("epsc", [128, 1], F32)
        self.A("pool", "memset", [], [self.epsc], self.epsc[:], EPS)
        hT = self.tile("hT", [128, 8, SEQ], BF16)
        sub = ExitStack()
        outer = self.ph
        self.ph = sub
        sub.__enter__()
        self.rmsnorm_to_hT(l, xin, hT, P_G1)
        self.S.flush()
        sub.__exit__(None, None, None)
        self.ph = outer
        self.proj(l, hT)
        self.end()
        self.begin()
        self.epsc = self.tile("epsc", [128, 1], F32)
        self.A("pool", "memset", [], [self.epsc], self.epsc[:], EPS)
        self.attn_a(l)
        self.end()
        self.begin()
        self.attn_b(l)
        self.end()
        self.begin()
        self.attn_c(l)
        self.end()
        self.begin()
        self.attn_d(l)
        self.end()
        self.begin()
        self.merge(l, xin)
        self.end()
        self.begin()
        self.epsc = self.tile("epsc", [128, 1], F32)
        self.A("pool", "memset", [], [self.epsc], self.epsc[:], EPS)
        hT = self.tile("hT2", [128, 8, SEQ], BF16)
        sub = ExitStack()
        outer = self.ph
        self.ph = sub
        sub.__enter__()
        self.rmsnorm_to_hT(l, self.XM, hT, P_G2)
        self.S.flush()
        sub.__exit__(None, None, None)
        self.ph = outer
        self.ffn_up(l, hT)
        self.end()
        self.begin()
        self.ffn_down(l, xout)
        self.end()

    def pipe(self, items, lags):
        n = len(items)
        for j in range(n + max(lags)):
            for si, lg in enumerate(lags):
                i = j - lg
                if 0 <= i < n and len(items[i]) > si and items[i][si] is not None:
                    items[i][si]()

    def proj(self, l, hT):
        wst = self.tile("wst0", [128, 8, 512], F32)
        wbf = [self.tile("wbf%d" % i, [128, 8, 512], BF16) for i in range(2)]
        tab = self.tile("tab", [128, 2, SEQ], F32)
        pacc = [self.ptile("pacc%d" % i) for i in range(3)]
        pss = [self.ptile("pss%d" % i) for i in range(2)]
        prot = [self.ptile("prot%d" % i) for i in range(2)]
        sq = [self.tile("sq%d" % i, [128, TC], F32) for i in range(3)]
        rs = [self.tile("rs%d" % i, [128, TC], F32) for i in range(3)]
        yy = [self.tile("yy%d" % i, [128, TC], F32) for i in range(3)]
        t2 = [self.tile("t2%d" % i, [128, TC], F32) for i in range(3)]
        qo = [self.tile("qo%d" % i, [128, TC], BF16) for i in range(3)]
        rowb = [self.tile("rowb%d" % i, [128, SEQ], BF16) for i in range(2)]
        go = [self.tile("go%d" % i, [128, TC], F32) for i in range(3)]
        vst = [self.tile("vst%d" % i, [128, 4, 520], BF16) for i in range(2)]
        vsta = [self.tile("vsta%d" % i, [128, 4, 512], BF16) for i in range(2)]
        for v in vst:
            self.A("pool", "memset", [], [v], v[:], 1.0)
        wv = self.w_in[l].rearrange("(k p) n -> p k n", p=128)
        cnt = dict(acc=0, q=0, row=0, go=0, vs=0, tmp=0, pp=0)
        G = P_QKG

        def wload(gidx, col0, w):
            wb = wbf[gidx % 2]
            self.load(wst, wst[:, :, 0:w], wv[:, :, col0:col0 + w])
            self.A("pool" if gidx % 2 == 0 else "dve", "tensor_copy", [wst], [wb], out=wb[:, :, 0:w], in_=wst[:, :, 0:w])

        def qk_items(items, wb, off, row, gcol, rope, r):
            mat = 0 if rope == "1d" else 1
            rb = None
            if r > 1:
                rb = rowb[cnt["row"] % 2]
                cnt["row"] += 1
            for t in range(NT):
                pa = pacc[cnt["acc"] % 3]
                cnt["acc"] += 1
                j = cnt["tmp"] % 3
                cnt["tmp"] += 1
                ps_, pr_ = pss[cnt["pp"] % 2], prot[cnt["pp"] % 2]
                cnt["pp"] += 1
                s, rr, y, a2 = sq[j], rs[j], yy[j], t2[j]
                tsl = slice(t * TC, (t + 1) * TC)
                if r > 1:
                    n_ = TC // r
                    dst_tl = rb
                    dst = rb[:].rearrange("p (c i) -> p c i", c=r)[:, :, t * n_:(t + 1) * n_]
                else:
                    dst_tl = qo[cnt["q"] % 3]
                    cnt["q"] += 1
                    dst = dst_tl[:]

                def s0(pa=pa, s=s, tsl=tsl):
                    for k in range(8):
                        self.mm(pa, pa[:], wb, wb[:, k, off:off + 128], hT, hT[:, k, tsl], k == 0, k == 7)
                    self.A("act", "activation", [pa], [s], out=s[:], in_=pa[:], func=AF.Square)

                def s1(pa=pa, s=s, rr=rr, y=y, ps_=ps_, dst_tl=dst_tl, dst=dst):
                    self.mm(ps_, ps_[:], self.mats, self.mats[:, 2, :], s, s[:], True, True)
                    self.A("act", "activation", [ps_], [rr], out=rr[:], in_=ps_[:], func=AF.Sqrt, scale=1.0 / 64, bias=self.epsc[:, 0:1])
                    self.A("dve", "reciprocal", [rr], [rr], out=rr[:], in_=rr[:])
                    if rope is None:
                        self.A("dve", "scalar_tensor_tensor", [pa, rr, self.par], [dst_tl], out=dst, in0=pa[:],
                               scalar=self.par[:, l, gcol:gcol + 1], in1=rr[:], op0=ALU.mult, op1=ALU.mult)
                    else:
                        self.A("dve", "scalar_tensor_tensor", [pa, rr, self.par], [y], out=y[:], in0=pa[:],
                               scalar=self.par[:, l, gcol:gcol + 1], in1=rr[:], op0=ALU.mult, op1=ALU.mult)

                def s2(y=y, a2=a2, pr_=pr_, dst_tl=dst_tl, dst=dst, tsl=tsl, t=t):
                    if rope is not None:
                        self.mm(pr_, pr_[:], self.mats, self.mats[:, mat, :], y, y[:], True, True)
                        self.A("dve", "tensor_tensor", [pr_, tab], [a2], out=a2[:], in0=pr_[:], in1=tab[:, 1, tsl], op=ALU.mult)
                        self.A("pool", "tensor_tensor", [y, tab], [y], out=y[:], in0=y[:], in1=tab[:, 0, tsl], op=ALU.mult)
                        if r > 1:
                            src1 = y[:].rearrange("p (i c) -> p c i", c=r)
                            src2 = a2[:].rearrange("p (i c) -> p c i", c=r)
                        else:
                            src1, src2 = y[:], a2[:]
                        self.A("pool", "tensor_tensor", [y, a2], [dst_tl], out=dst, in0=src1, in1=src2, op=ALU.add)
                    if r == 1:
                        self.store(dst_tl, self.QK[row * 128:(row + 1) * 128, tsl], dst_tl[:])
                    elif t == NT - 1:
                        self.store(rb, self.QK[row * 128:(row + 1) * 128, :], rb[:])

                items.append([s0, s1, s2])

        def v_items(items, wb, off, w, dst, r, nh, hd):
            hv = hT[:].rearrange("p k (i c) -> p k c i", c=r)
            L = SEQ // r
            stride = hd + 1 if hd == 64 else hd
            vs = None
            for tt in range(32):
                c = (tt * 128) // L
                i0 = (tt * 128) % L
                pa = pacc[cnt["acc"] % 3]
                cnt["acc"] += 1
                if tt % 4 == 0:
                    vs = (vst if hd == 64 else vsta)[cnt["vs"] % 2]
                    cnt["vs"] += 1

                def s0(pa=pa, c=c, i0=i0):
                    for k in range(8):
                        self.mm(pa, pa[:, 0:w], hT, hv[:, k, c, i0:i0 + 128], wb, wb[:, k, off:off + w], k == 0, k == 7)

                def s1(pa=pa, vs=vs, tt=tt):
                    o = vs[:, tt % 4, 0:nh * stride].rearrange("p (h d) -> p h d", h=nh)[:, :, 0:hd]
                    i_ = pa[:, 0:w].rearrange("p (h d) -> p h d", h=nh)
                    if tt % 2 == 0:
                        self.A("act", "activation", [pa], [vs], out=o, in_=i_, func=AF.Copy)
                    else:
                        self.A("dve", "tensor_copy", [pa], [vs], out=o, in_=i_)
                    if tt % 4 == 3:
                        t0 = (tt - 3) * 128
                        self.store(vs, dst[t0:t0 + 512, :].rearrange("(a p) n -> p a n", p=128), vs[:, :, 0:nh * stride])

                items.append([s0, s1])

        def gate_items(items, wb, off, grow):
            for t in range(NT):
                pa = pacc[cnt["acc"] % 3]
                cnt["acc"] += 1
                g = go[cnt["go"] % 3]
                cnt["go"] += 1
                tsl = slice(t * TC, (t + 1) * TC)

                def s0(pa=pa, tsl=tsl):
                    for k in range(8):
                        self.mm(pa, pa[:], wb, wb[:, k, off:off + 128], hT, hT[:, k, tsl], k == 0, k == 7)

                def s1(pa=pa, g=g, tsl=tsl):
                    self.A("act", "activation", [pa], [g], out=g[:], in_=pa[:], func=AF.Sigmoid)
                    self.store(g, self.G[grow * 128:(grow + 1) * 128, tsl], g[:])

                items.append([s0, s1])

        jobsB = [
            (COL["bq"], 512, [("qk", i * 128, 8 + i, G + 2, "ax", 1) for i in range(4)]),
            (COL["bk"], 256, [("qk", 0, 12, G + 3, "ax", 1), ("v", 128, 128, self.VB, 1, 2, 64)]),
        ]
        jobs = [
            (COL["aq"], 512, [("qk", i * 128, 0 + i, G + 0, "1d", 1) for i in range(4)]),
            (COL["ak"], 512, [("qk", i * 128, 4 + i, G + 1, "1d", 1) for i in range(4)]),
            (COL["av"], 512, [("v", 0, 512, self.VA, 1, 4, 128)]),
        ]
        for g in range(3):
            jobs.append((COL["cq"] + g * 512, 512, [("qk", i * 128, 13 + g * 4 + i, G + 4, "1d", DIL[g]) for i in range(4)]))
            jobs.append((COL["ck"] + g * 512, 512, [("qk", i * 128, 25 + g * 4 + i, G + 5, "1d", DIL[g]) for i in range(4)]))
            jobs.append((COL["cv"] + g * 512, 512, [("v", 0, 512, self.VC[g], DIL[g], 8, 64)]))
        jobs.append((COL["dq"], 512, [("qk", i * 128, 37 + i, G + 6, None, 1) for i in range(4)]))
        jobs.append((COL["dk"], 512, [("qk", i * 128, 41 + i, G + 7, None, 1) for i in range(4)]))
        jobs.append((COL["dv"], 512, [("v", 0, 512, self.VD, 1, 8, 64)]))
        for gi in range(8):
            jobs.append((COL["g"] + gi * 512, 512, [("gate", i * 128, gi * 4 + i) for i in range(4)]))

        gctr = [0]

        def run_jobs(jl):
            items = []
            gid0 = gctr[0]
            for ji, (col0, w, subs) in enumerate(jl):
                gid = gid0 + ji
                wb = wbf[gid % 2]
                first = len(items)
                for sj in subs:
                    if sj[0] == "qk":
                        qk_items(items, wb, *sj[1:])
                    elif sj[0] == "v":
                        v_items(items, wb, *sj[1:])
                    else:
                        gate_items(items, wb, *sj[1:])
                orig = items[first][0]
                nxt = jl[ji + 1] if ji + 1 < len(jl) else None

                def s0w(orig=orig, nxt=nxt, gid=gid):
                    if nxt is not None:
                        wload(gid + 1, nxt[0], nxt[1])
                    orig()
                items[first][0] = s0w
            wload(gid0, jl[0][0], jl[0][1])
            gctr[0] += len(jl)
            self.pipe(items, [0, 1, 2])

        self.load(tab, tab[:].rearrange("p a n -> p (a n)"), self.c_tab2)
        run_jobs(jobsB)
        self.load(tab, tab[:].rearrange("p a n -> p (a n)"), self.c_tab1)
        run_jobs(jobs)

    def normalize_rows65(self, po, res_tl, res_ap, n, tmp_row, tmp_bc, pbc):
        self.A("dve", "reciprocal", [po], [tmp_row], out=tmp_row[64:65, 0:n], in_=po[64:65, 0:n])
        self.mm(pbc, pbc[0:64, 0:n], self.mats, self.mats[64:65, 3, 0:64], tmp_row, tmp_row[64:65, 0:n], True, True)
        self.A("act", "activation", [pbc], [tmp_bc], out=tmp_bc[0:64, 0:n], in_=pbc[0:64, 0:n], func=AF.Copy)
        self.A("dve", "tensor_tensor", [po, tmp_bc], [res_tl], out=res_ap, in0=po[0:64, 0:n], in1=tmp_bc[0:64, 0:n], op=ALU.mult)

    def attn_a(self, l):
        qa = self.tile("qa", [128, 4, SEQ], BF16)
        ka = self.tile("ka", [128, 4, SEQ], BF16)
        va = self.tile("va", [128, 32, 512], BF16)
        for h in range(4):
            self.load(qa, qa[:, h, :], self.QK[(0 + h) * 128:(1 + h) * 128, :])
            self.load(ka, ka[:, h, :], self.QK[(4 + h) * 128:(5 + h) * 128, :])
        for j in range(4):
            self.load(va, va[:, j * 8:(j + 1) * 8, :], self.VA[j * 1024:(j + 1) * 1024, :].rearrange("(a p) n -> p a n", p=128))
        psc = [self.ptile("psc%d" % i) for i in range(3)]
        po = [self.ptile("po%d" % i) for i in range(2)]
        psm = [self.ptile("psm%d" % i) for i in range(2)]
        pss = self.ptile("pssa")
        pt = [self.tile("pt%d" % i, [128, TC], BF16) for i in range(4)]
        rc = self.tile("rc", [128, TC], F32)
        res = [self.tile("res%d" % i, [128, TC], F32) for i in range(2)]
        dd = self.tile("dd", [128, TC], F32)
        sq = self.tile("sqa", [128, TC], F32)
        rs = self.tile("rsa", [128, TC], F32)
        ob = [self.tile("oba%d" % i, [128, TC], BF16) for i in range(2)]
        items = []
        n = 0
        gi = 0
        for h in range(4):
            for qc in range(NT):
                qs = slice(qc * TC, (qc + 1) * TC)
                for c in range(2):
                    ps_ = slice(c * 64, (c + 1) * 64)
                    po_, psm_ = po[gi % 2], psm[gi % 2]
                    gi += 1
                    for kt in range(32):
                        sc = psc[n % 3]
                        p = pt[n % 4]
                        n += 1

                        def s0(sc=sc, p=p, h=h, qs=qs, ps_=ps_, kt=kt):
                            self.mm(sc, sc[:], ka, ka[ps_, h, kt * 128:(kt + 1) * 128], qa, qa[ps_, h, qs], True, True)
                            self.A("act", "activation", [sc], [p], out=p[:], in_=sc[:], func=AF.Exp, scale=0.125)

                        def s1(p=p, h=h, qc=qc, qs=qs, c=c, kt=kt, po_=po_, psm_=psm_):
                            self.mm(po_, po_[:], va, va[:, kt, h * 128:(h + 1) * 128], p, p[:], kt == 0, kt == 31)
                            self.mm(psm_, psm_[:], self.onesb, self.onesb[:], p, p[:], kt == 0, kt == 31)
                            if kt != 31:
                                return
                            self.A("dve", "reciprocal", [psm_], [rc], out=rc[:], in_=psm_[:])
                            self.A("dve", "tensor_tensor", [po_, rc], [res[c]], out=res[c][:], in0=po_[:], in1=rc[:], op=ALU.mult)
                            if c != 1:
                                return
                            self.A("dve", "scalar_tensor_tensor", [res[0], res[1], self.nlam], [dd], out=dd[:], in0=res[1][:],
                                   scalar=self.nlam[:, l:l + 1], in1=res[0][:], op0=ALU.mult, op1=ALU.add)
                            self.A("pool", "tensor_tensor", [dd], [sq], out=sq[:], in0=dd[:], in1=dd[:], op=ALU.mult)
                            self.mm(pss, pss[:], self.mats, self.mats[:, 3, :], sq, sq[:], True, True)
                            self.A("act", "activation", [pss], [rs], out=rs[:], in_=pss[:], func=AF.Sqrt, scale=1.0 / 128, bias=self.epsc[:, 0:1])
                            self.A("dve", "reciprocal", [rs], [rs], out=rs[:], in_=rs[:])
                            self.A("dve", "scalar_tensor_tensor", [dd, rs, self.par], [dd], out=dd[:], in0=dd[:],
                                   scalar=self.par[:, l, P_SUB:P_SUB + 1], in1=rs[:], op0=ALU.mult, op1=ALU.mult)
                            o = ob[(h * NT + qc) % 2]
                            self.A("act", "activation", [dd], [o], out=o[:], in_=dd[:], func=AF.Copy, scale=float(1.0 - self.lambda_init))
                            self.store(o, self.OT[0, h * 128:(h + 1) * 128, qs], o[:])

                        items.append([s0, s1])
        self.pipe(items, [0, 2])

    def attn_b(self, l):
        qb = self.tile("qb", [128, 4, SEQ], BF16)
        kb = self.tile("kb", [128, SEQ], BF16)
        vb = self.tile("vb", [128, 32, 130], BF16)
        for hq in range(8):
            g, s = hq // 4, hq % 4
            self.load(qb, qb[g * 64:(g + 1) * 64, s, :], self.QK[8 * 128 + hq * 64:8 * 128 + (hq + 1) * 64, :])
        self.load(kb, kb[:], self.QK[12 * 128:13 * 128, :])
        self.load(vb, vb[:], self.VB.rearrange("(a p) n -> p a n", p=128))
        psc = [self.ptile("psc%d" % i) for i in range(4)]
        po = [self.ptile("pob%d" % i) for i in range(2)]
        pbc = self.ptile("pbc")
        pt = [self.tile("pt%d" % i, [128, TC], BF16) for i in range(4)]
        trow = self.tile("trow", [128, TC], F32)
        tbc = self.tile("tbc", [128, TC], F32)
        ob = [self.tile("obb%d" % i, [128, TC], BF16) for i in range(2)]
        items = []
        n = 0
        m = 0
        for hq in range(8):
            g, s = hq // 4, hq % 4
            ps_ = slice(g * 64, (g + 1) * 64)
            for qc in range(NT):
                qs = slice(qc * TC, (qc + 1) * TC)
                o_ps = po[m % 2]
                o = ob[m % 2]
                m += 1
                for kt in range(32):
                    sc = psc[n % 4]
                    p = pt[n % 4]
                    n += 1

                    def s0(sc=sc, p=p, ps_=ps_, s=s, qs=qs, kt=kt):
                        self.mm(sc, sc[:], kb, kb[ps_, kt * 128:(kt + 1) * 128], qb, qb[ps_, s, qs], True, True)
                        self.A("act", "activation", [sc], [p], out=p[:], in_=sc[:], func=AF.Exp, scale=0.125)

                    def s1(p=p, g=g, hq=hq, qs=qs, kt=kt, o_ps=o_ps, o=o):
                        self.mm(o_ps, o_ps[0:65, :], vb, vb[:, kt, g * 65:(g + 1) * 65], p, p[:], kt == 0, kt == 31)
                        if kt != 31:
                            return
                        self.normalize_rows65(o_ps, o, o[0:64, :], TC, trow, tbc, pbc)
                        self.store(o, self.OT[1, hq * 64:(hq + 1) * 64, qs], o[0:64, :])

                    items.append([s0, s1])
        self.pipe(items, [0, 2])

    def attn_c(self, l):
        mask = self.tile("mask3", [128, 3, 128], F32)
        self.load(mask, mask[:].rearrange("p a b -> p (a b)"), self.c_mask3)
        qc_ = self.tile("qc", [128, 3, SEQ], BF16)
        kc_ = self.tile("kc", [128, 3, SEQ], BF16)
        vc_ = self.tile("vc", [128, 3, 32, 130], BF16)
        acc = [self.tile("acc%d" % i, [65, SEQ], F32) for i in range(2)]
        psc = [self.ptile("pscc%d" % i, (128, 384)) for i in range(3)]
        po = [self.ptile("poc%d" % i) for i in range(2)]
        pbc = self.ptile("pbcc")
        ex = [self.tile("ex%d" % i, [128, 3, 128], F32) for i in range(3)]
        pt = [self.tile("ptc%d" % i, [128, 3, 128], BF16) for i in range(4)]
        trow = self.tile("trowc", [128, TC], F32)
        ob = [self.tile("obc%d" % i, [128, TC], BF16) for i in range(2)]
        n = 0
        m = 0
        for jp in range(4):
            for g in range(3):
                self.load(qc_, qc_[:, g, :], self.QK[(13 + g * 4 + jp) * 128:(14 + g * 4 + jp) * 128, :])
                self.load(kc_, kc_[:, g, :], self.QK[(25 + g * 4 + jp) * 128:(26 + g * 4 + jp) * 128, :])
                self.load(vc_, vc_[:, g, :, :], self.VC[g].rearrange("(a p) n -> p a n", p=128)[:, :, jp * 130:(jp + 1) * 130])
            items = []
            for hh in range(2):
                ps_ = slice(hh * 64, (hh + 1) * 64)
                ac = acc[hh]
                hd = jp * 2 + hh
                for g in range(3):
                    r = DIL[g]
                    L = SEQ // r
                    tps = L // 128
                    for qb4 in range(8):
                        o_ps = po[m % 2]
                        m += 1
                        for u in range(4):
                            qb = qb4 * 4 + u
                            seg = qb // tps
                            kts = [k for k in (qb - 1, qb, qb + 1) if k // tps == seg and 0 <= k < 32]
                            j0 = kts[0] - (qb - 1)
                            nk = len(kts)
                            sc = psc[n % 3]
                            e_ = ex[n % 3]
                            p = pt[n % 4]
                            n += 1

                            def s0(sc=sc, e_=e_, p=p, kts=kts, j0=j0, nk=nk, qb=qb, g=g, ps_=ps_):
                                for k in kts:
                                    j = k - (qb - 1)
                                    self.mm(sc, sc[:, j * 128:(j + 1) * 128], kc_, kc_[ps_, g, k * 128:(k + 1) * 128],
                                            qc_, qc_[ps_, g, qb * 128:(qb + 1) * 128], True, True)
                                scv = sc[:, 0:384].rearrange("p (a b) -> p a b", a=3)
                                self.A("act", "activation", [sc], [e_], out=e_[:, j0:j0 + nk, :], in_=scv[:, j0:j0 + nk, :], func=AF.Exp, scale=0.125)
                                self.A("pool", "tensor_tensor", [e_, mask], [p], out=p[:, j0:j0 + nk, :], in0=e_[:, j0:j0 + nk, :],
                                       in1=mask[:, j0:j0 + nk, :], op=ALU.mult)

                            def s1(p=p, kts=kts, nk=nk, qb=qb, g=g, hh=hh, u=u, o_ps=o_ps, qb4=qb4, r=r, L=L, ac=ac, hd=hd):
                                for ki, k in enumerate(kts):
                                    j = k - (qb - 1)
                                    self.mm(o_ps, o_ps[0:65, u * 128:(u + 1) * 128], vc_, vc_[:, g, k, hh * 65:(hh + 1) * 65],
                                            p, p[:, j, :], ki == 0, ki == nk - 1)
                                if u != 3:
                                    return
                                pos0 = qb4 * 512
                                av = ac[:].rearrange("p (i c) -> p c i", c=r)
                                if L >= 512:
                                    c0, i0 = pos0 // L, pos0 % L
                                    dst = av[0:65, c0:c0 + 1, i0:i0 + 512]
                                    src = o_ps[0:65, :].rearrange("p (c i) -> p c i", c=1)
                                else:
                                    ncl = 512 // L
                                    c0 = pos0 // L
                                    dst = av[0:65, c0:c0 + ncl, :]
                                    src = o_ps[0:65, :].rearrange("p (c i) -> p c i", c=ncl)
                                if g == 0:
                                    self.A("act", "activation", [o_ps], [ac], out=dst, in_=src, func=AF.Copy)
                                else:
                                    self.A("dve", "tensor_tensor", [o_ps, ac], [ac], out=dst, in0=dst, in1=src, op=ALU.add)
                                if g == 2 and qb4 == 7:
                                    for t in range(NT):
                                        o = ob[(hd * NT + t) % 2]
                                        ts_ = slice(t * TC, (t + 1) * TC)
                                        self.A("dve", "reciprocal", [ac], [trow], out=trow[64:65, :], in_=ac[64:65, ts_])
                                        self.mm(pbc, pbc[0:64, :], self.mats, self.mats[64:65, 3, 0:64], trow, trow[64:65, :], True, True)
                                        self.A("dve", "tensor_tensor", [pbc, ac], [o], out=o[0:64, :], in0=pbc[0:64, :], in1=ac[0:64, ts_], op=ALU.mult)
                                        self.store(o, self.OT[2, hd * 64:(hd + 1) * 64, ts_], o[0:64, :])

                            items.append([s0, s1])
            self.pipe(items, [0, 2])

    def attn_d(self, l):
        qd = self.tile("qd", [128, 4, SEQ], BF16)
        kd = self.tile("kd", [128, 4, SEQ], BF16)
        ve = self.tile("ve", [128, 32, 520], BF16)
        vo = self.tile("vo", [128, 31, 520], BF16)
        E = self.tile("E", [128, 8, 15, 64], F32)
        cm = self.tile("cm", [128, 64], F32)
        self.load(cm, cm[:], self.c_colmask)
        for i in range(4):
            self.load(qd, qd[:, i, :], self.QK[(37 + i) * 128:(38 + i) * 128, :])
            self.load(kd, kd[:, i, :], self.QK[(41 + i) * 128:(42 + i) * 128, :])
        for j in range(4):
            self.load(ve, ve[:, j * 8:(j + 1) * 8, :], self.VD[j * 1024:(j + 1) * 1024, :].rearrange("(a p) n -> p a n", p=128))
        for j in range(4):
            na = 8 if j < 3 else 7
            self.load(vo, vo[:, j * 8:j * 8 + na, :], self.VD[64 + j * 1024:64 + j * 1024 + na * 128, :].rearrange("(a p) n -> p a n", p=128))
        for h in range(8):
            self.load(E, E[:, h, :, :].rearrange("p a b -> p (a b)"), self.rpbE[l][:, h * 960:(h + 1) * 960])
        for h in range(8):
            self.A("act", "activation", [E], [E], out=E[:, h, :, :], in_=E[:, h, :, :], func=AF.Exp)
            self.A("pool", "tensor_tensor", [E, cm], [E], out=E[:, h, :, :], in0=E[:, h, :, :],
                   in1=cm[:].rearrange("p (a b) -> p a b", a=1).to_broadcast([128, 15, 64]), op=ALU.mult)
        psc = [self.ptile("pscd%d" % i, (128, 256)) for i in range(3)]
        po = [self.ptile("pod%d" % i) for i in range(2)]
        pbc = self.ptile("pbcd")
        ex = [self.tile("exd%d" % i, [128, 4, 64], F32) for i in range(3)]
        pt = [self.tile("ptd%d" % i, [128, 4, 64], BF16) for i in range(4)]
        trow = self.tile("trowd", [128, TC], F32)
        tbc = self.tile("tbcd", [128, TC], F32)
        ob = [self.tile("obd%d" % i, [128, TC], BF16) for i in range(2)]
        items = []
        n = 0
        m = 0
        for h in range(8):
            ch, hh = h // 2, h % 2
            ps_ = slice(hh * 64, (hh + 1) * 64)
            for r8 in range(8):
                o_ps = po[m % 2]
                o = ob[m % 2]
                m += 1
                for u in range(8):
                    r = r8 * 8 + u
                    rs_ = min(max(r - 4, 0), 56)
                    base = rs_ - r + 7
                    sc = psc[n % 3]
                    e_ = ex[n % 3]
                    p = pt[n % 4]
                    n += 1

                    def s0(sc=sc, e_=e_, p=p, r=r, rs_=rs_, base=base, h=h, ch=ch, ps_=ps_, n=n):
                        for i in range(4):
                            k0 = (rs_ + 2 * i) * 64
                            self.mm(sc, sc[:, i * 64:(i + 1) * 64], kd, kd[ps_, ch, k0:k0 + 128], qd, qd[ps_, ch, r * 64:(r + 1) * 64], True, True)
                        self.A("act", "activation", [sc], [e_], out=e_[:], in_=sc[:, 0:256].rearrange("p (a b) -> p a b", a=4), func=AF.Exp, scale=0.125)
                        ev = E[:, h, base:base + 7:2, :]
                        self.A("pool" if n % 2 == 0 else "dve", "tensor_tensor", [e_, E], [p], out=p[:], in0=e_[:], in1=ev, op=ALU.mult)

                    def s1(p=p, rs_=rs_, h=h, u=u, o_ps=o_ps, o=o, r8=r8):
                        for i in range(4):
                            row0 = rs_ + 2 * i
                            if row0 % 2 == 0:
                                vt, vi = ve, row0 // 2
                            else:
                                vt, vi = vo, (row0 - 1) // 2
                            self.mm(o_ps, o_ps[0:65, u * 64:(u + 1) * 64], vt, vt[:, vi, h * 65:(h + 1) * 65], p, p[:, i, :], i == 0, i == 3)
                        if u != 7:
                            return
                        self.normalize_rows65(o_ps, o, o[0:64, :], TC, trow, tbc, pbc)
                        self.store(o, self.OT[3, h * 64:(h + 1) * 64, r8 * TC:(r8 + 1) * TC], o[0:64, :])

                    items.append([s0, s1])
        self.pipe(items, [0, 2])

    def merge(self, l, xin):
        wb = self.tile("wbr", [128, 16, DM], BF16)
        wo = self.tile("wo", [128, 8, DM], BF16)
        stg = [self.tile("mstg%d" % i, [128, 4, DM], F32) for i in range(2)]
        n = 0
        for i in range(4):
            s = stg[n % 2]
            n += 1
            self.load(s, s[:], self.w_branch[l, i].rearrange("(k p) n -> p k n", p=128))
            self.A("pool" if n % 2 == 0 else "dve", "tensor_copy", [s], [wb], out=wb[:, i * 4:(i + 1) * 4, :], in_=s[:])
        for j in range(2):
            s = stg[n % 2]
            n += 1
            self.load(s, s[:], self.w_out[l][j * 512:(j + 1) * 512, :].rearrange("(k p) n -> p k n", p=128))
            self.A("pool" if n % 2 == 0 else "dve", "tensor_copy", [s], [wo], out=wo[:, j * 4:(j + 1) * 4, :], in_=s[:])
        ot = [self.tile("mot%d" % i, [128, 16, TC], BF16) for i in range(2)]
        xs = [self.tile("mx%d" % i, [128, 8, TC], F32) for i in range(2)]
        gt = [self.tile("mg%d" % i, [128, TC], F32) for i in range(4)]
        mt = [self.tile("mm%d" % i, [128, 8, TC], BF16) for i in range(2)]
        acc = [self.tile("macc%d" % i, [128, TC], F32) for i in range(2)]
        tmp = [self.tile("mtmp%d" % i, [128, TC], F32) for i in range(2)]
        py = [self.ptile("py%d" % i) for i in range(3)]
        px = [self.ptile("px%d" % i) for i in range(2)]
        xo = [self.tile("mxo%d" % i, [128, TC], F32) for i in range(3)]
        xv = xin.rearrange("(c p) n -> p c n", p=128)
        otv = self.OT.rearrange("b (k p) n -> p (b k) n", p=128)
        ng = 0
        ny = 0
        nx = 0
        for t in range(NT):
            ts_ = slice(t * TC, (t + 1) * TC)
            o = ot[t % 2]
            x = xs[t % 2]
            mtt = mt[t % 2]
            for i in range(4):
                self.load(o, o[:, i * 4:(i + 1) * 4, :], otv[:, i * 4:(i + 1) * 4, ts_])
            self.load(x, x[:], xv[:, :, ts_])
            for f in range(8):
                a = acc[f % 2]
                for i in range(4):
                    g = gt[ng % 4]
                    ng += 1
                    self.load(g, g[:], self.G[(i * 8 + f) * 128:(i * 8 + f + 1) * 128, ts_])
                    p = py[ny % 3]
                    ny += 1
                    for k in range(4):
                        self.mm(p, p[:], wb, wb[:, i * 4 + k, f * 128:(f + 1) * 128], o, o[:, i * 4 + k, :], k == 0, k == 3)
                    if i == 0:
                        self.A("dve", "tensor_tensor", [p, g], [a], out=a[:], in0=p[:], in1=g[:], op=ALU.mult)
                    else:
                        tm = tmp[i % 2]
                        self.A("dve", "tensor_tensor", [p, g], [tm], out=tm[:], in0=p[:], in1=g[:], op=ALU.mult)
                        if i < 3:
                            self.A("pool", "tensor_tensor", [a, tm], [a], out=a[:], in0=a[:], in1=tm[:], op=ALU.add)
                        else:
                            self.A("pool", "tensor_tensor", [a, tm], [mtt], out=mtt[:, f, :], in0=a[:], in1=tm[:], op=ALU.add)
            for fo in range(8):
                p = px[nx % 2]
                xo_ = xo[nx % 3]
                nx += 1
                for k in range(8):
                    self.mm(p, p[:], wo, wo[:, k, fo * 128:(fo + 1) * 128], mtt, mtt[:, k, :], k == 0, k == 7)
                self.A("dve", "tensor_tensor", [p, x], [xo_], out=xo_[:], in0=p[:], in1=x[:, fo, :], op=ALU.add)
                self.store(xo_, self.XM[fo * 128:(fo + 1) * 128, ts_], xo_[:])

    def ffn_up(self, l, hT):
        wst = [self.tile("fst%d" % i, [128, 8, 256], F32) for i in range(2)]
        wbf = [self.tile("fbf%d" % i, [128, 8, 256], BF16) for i in range(4)]
        ua = self.tile("ua", [128, SEQ + 2], F32)
        ug = self.tile("ug", [128, SEQ + 2], F32)
        ca = self.tile("ca", [128, SEQ], F32)
        cg = self.tile("cg", [128, SEQ], F32)
        mrow = [self.tile("mrow%d" % i, [128, SEQ], BF16) for i in range(2)]
        pacc = [self.ptile("fpa%d" % i) for i in range(4)]
        for u in (ua, ug):
            self.A("pool", "memset", [], [u], u[:, 0:1], 0.0)
            self.A("pool", "memset", [], [u], u[:, SEQ + 1:SEQ + 2], 0.0)
        wv = self.w_up[l].rearrange("(k p) n -> p k n", p=128)
        ngrp = 0
        nacc = 0
        for j in range(11):
            w = 256
            wbs = []
            for part in range(2):
                st = wst[ngrp % 2]
                wb = wbf[ngrp % 4]
                ngrp += 1
                c0 = part * D_FF + j * 256
                self.load(st, st[:, :, 0:w], wv[:, :, c0:c0 + w])
                self.A("pool" if part == 0 else "dve", "tensor_copy", [st], [wb], out=wb[:, :, 0:w], in_=st[:, :, 0:w])
                wbs.append(wb)
            for ii in range(w // 128):
                i = j * 2 + ii
                for part, (u, cdst) in enumerate(((ua, ca), (ug, cg))):
                    wb = wbs[part]
                    for t in range(NT):
                        p = pacc[nacc % 4]
                        nacc += 1
                        for k in range(8):
                            self.mm(p, p[:], wb, wb[:, k, ii * 128:(ii + 1) * 128], hT, hT[:, k, t * TC:(t + 1) * TC], k == 0, k == 7)
                        if t % 2 == 0:
                            self.A("act", "activation", [p], [u], out=u[:, 1 + t * TC:1 + (t + 1) * TC], in_=p[:], func=AF.Copy)
                        else:
                            self.A("dve", "tensor_copy", [p], [u], out=u[:, 1 + t * TC:1 + (t + 1) * TC], in_=p[:])
                    ch = part * 22 + i
                    w0 = self.par[:, l, P_CW + 0 * 44 + ch:P_CW + 0 * 44 + ch + 1]
                    w1 = self.par[:, l, P_CW + 1 * 44 + ch:P_CW + 1 * 44 + ch + 1]
                    w2 = self.par[:, l, P_CW + 2 * 44 + ch:P_CW + 2 * 44 + ch + 1]
                    bb = self.par[:, l, P_CB + ch:P_CB + ch + 1]
                    eng = "dve"
                    for hf in range(2):
                        hs = slice(hf * 2048, (hf + 1) * 2048)
                        self.A(eng, "tensor_scalar", [u, self.par], [cdst], out=cdst[:, hs], in0=u[:, hf * 2048:hf * 2048 + 2048],
                               scalar1=w0, scalar2=bb, op0=ALU.mult, op1=ALU.add)
                        self.A(eng, "scalar_tensor_tensor", [u, cdst, self.par], [cdst], out=cdst[:, hs], in0=u[:, 1 + hf * 2048:1 + hf * 2048 + 2048],
                               scalar=w1, in1=cdst[:, hs], op0=ALU.mult, op1=ALU.add)
                        self.A(eng, "scalar_tensor_tensor", [u, cdst, self.par], [cdst], out=cdst[:, hs], in0=u[:, 2 + hf * 2048:2 + hf * 2048 + 2048],
                               scalar=w2, in1=cdst[:, hs], op0=ALU.mult, op1=ALU.add)
                mr = mrow[i % 2]
                for hf in range(2):
                    hs = slice(hf * 2048, (hf + 1) * 2048)
                    self.A("act", "activation", [ca], [ca], out=ca[:, hs], in_=ca[:, hs], func=AF.Silu)
                    self.A("pool" if hf == 0 else "dve", "tensor_tensor", [ca, cg], [mr], out=mr[:, hs], in0=ca[:, hs], in1=cg[:, hs], op=ALU.mult)
                self.store(mr, self.M[i * 128:(i + 1) * 128, :], mr[:])

    def ffn_down(self, l, xout):
        wd = self.tile("wd", [128, 22, DM], BF16)
        stg = [self.tile("dstg%d" % i, [128, 2, DM], F32) for i in range(2)]
        wv = self.w_down[l].rearrange("(k p) n -> p k n", p=128)
        for j in range(11):
            s = stg[j % 2]
            self.load(s, s[:], wv[:, j * 2:(j + 1) * 2, :])
            self.A("pool" if j % 2 == 0 else "dve", "tensor_copy", [s], [wd], out=wd[:, j * 2:(j + 1) * 2, :], in_=s[:])
        mt = [self.tile("dm%d" % i, [128, 22, TC], BF16) for i in range(2)]
        xs = [self.tile("dx%d" % i, [128, 8, TC], F32) for i in range(2)]
        xo = [self.tile("dxo%d" % i, [128, TC], F32) for i in range(3)]
        px = [self.ptile("dpx%d" % i) for i in range(3)]
        mv = self.M.rearrange("(k p) n -> p k n", p=128)
        xv = self.XM.rearrange("(c p) n -> p c n", p=128)
        nx = 0
        for t in range(NT):
            ts_ = slice(t * TC, (t + 1) * TC)
            m = mt[t % 2]
            x = xs[t % 2]
            self.load(m, m[:, 0:11, :], mv[:, 0:11, ts_])
            self.load(m, m[:, 11:22, :], mv[:, 11:22, ts_])
            self.load(x, x[:], xv[:, :, ts_])
            for fo in range(8):
                p = px[nx % 3]
                xo_ = xo[nx % 3]
                nx += 1
                for k in range(22):
                    self.mm(p, p[:], wd, wd[:, k, fo * 128:(fo + 1) * 128], m, m[:, k, :], k == 0, k == 21)
                self.A("dve", "tensor_tensor", [p, x], [xo_], out=xo_[:], in0=p[:], in1=x[:, fo, :], op=ALU.add)
                self.store(xo_, xout[fo * 128:(fo + 1) * 128, ts_], xo_[:])


_CACHE = {}


def _get_nc(n_layers=NL, debug=False):
    key = (n_layers, debug)
    if key not in _CACHE:
        b = Builder(n_layers, debug)
        _CACHE[key] = (b.build(), b)
    return _CACHE[key]


def make_in_maps(inp):
    consts = _host_consts()
    params = _pack_params(inp)
    lam = np.ascontiguousarray(np.asarray(inp["lam"], np.float32).reshape(1, NL * 256))
    rpbE = _pack_rpb(np.asarray(inp["rpb"], np.float32))
    shared = dict(
        w_in=np.ascontiguousarray(inp["w_in"], np.float32),
        w_branch=np.ascontiguousarray(inp["w_branch"], np.float32),
        w_out=np.ascontiguousarray(inp["w_out"], np.float32),
        w_up=np.ascontiguousarray(inp["w_up"], np.float32),
        w_down=np.ascontiguousarray(inp["w_down"], np.float32),
        params=params, lam=lam, rpbE=rpbE,
        tab1=consts["tab1"], tab2=consts["tab2"], mats=consts["mats"], mask3=consts["mask3"],
        colmask=consts["colmask"],
    )
    x = np.asarray(inp["x"], np.float32)
    maps = []
    for b in range(8):
        d = dict(shared)
        d["xT"] = np.ascontiguousarray(x[b].T)
        maps.append(d)
    return maps


def kernel(**inputs):
    inp = {k: np.asarray(v) for k, v in inputs.items()}
    nc, _ = _get_nc()
    maps = make_in_maps(inp)
    res = run_bass_kernel_spmd(nc, maps, core_ids=list(range(8)))
    out = np.stack([np.ascontiguousarray(res.results[b]["yT"].T) for b in range(8)], axis=0)
    return out.astype(np.float32)
```

```python
import numpy as np
from contextlib import ExitStack
import concourse.bass as bass
import concourse.mybir as mybir
from concourse.bass_utils import run_bass_kernel_spmd

F32 = mybir.dt.float32
BF16 = mybir.dt.bfloat16
AF = mybir.ActivationFunctionType
ALU = mybir.AluOpType
AX = mybir.AxisListType

EPOCH = 24000
SAME_SYNC = True

SEQ = 4096
DM = 1024
NL = 4
N_IN = 12544
D_FF = 2816
EPS = 1e-6
NT = 8
TC = 512
COL = dict(aq=0, ak=512, av=1024, bq=1536, bk=2048, bv=2176, cq=2304, ck=3840, cv=5376,
           dq=6912, dk=7424, dv=7936, g=8448)
DIL = (1, 4, 16)


class Buf:
    __slots__ = ("name", "w", "r")

    def __init__(self, name=""):
        self.name = name
        self.w = None
        self.r = []


class _Op:
    __slots__ = ("eng", "fn", "deps", "ddeps", "dma", "pub", "seq")

    def __init__(self, eng, fn, deps, ddeps, dma):
        self.eng = eng
        self.fn = fn
        self.deps = deps
        self.ddeps = ddeps
        self.dma = dma
        self.pub = False
        self.seq = None


class Sched:
    ENGS = ("pe", "act", "dve", "pool", "sp")

    def __init__(self, nc, stack):
        self.nc = nc
        self.stack = stack
        self.ops = []
        self.base = 0
        self.cnt = {e: 0 for e in self.ENGS}
        self.sems = {e: [] for e in self.ENGS}
        self.dsem = {}
        self.dcnt = {}
        self.nsem = 0
        self.ninstr = 0

    def _new_sem(self, name):
        self.nsem += 1
        return self.stack.enter_context(self.nc.semaphore(name))

    def _esem(self, e, ep):
        while len(self.sems[e]) <= ep:
            self.sems[e].append(self._new_sem("s_%s_%d" % (e, len(self.sems[e]))))
        return self.sems[e][ep]

    def op(self, eng, fn, reads=(), writes=(), dma=None):
        deps = set()
        ddeps = {}
        base = self.base
        ops = self.ops

        def add(i):
            if i is None or i < base:
                return
            o = ops[i - base]
            if o.dma is not None:
                ddeps[o.dma] = self.dcnt[o.dma]
            else:
                deps.add(i)

        for b in reads:
            add(b.w)
        for b in writes:
            add(b.w)
            for r in b.r:
                add(r)
        idx = base + len(ops)
        if dma is not None:
            if dma not in self.dsem:
                self.dsem[dma] = self._new_sem("d_%d" % len(self.dsem))
                self.dcnt[dma] = 0
            self.dcnt[dma] += 16
        ops.append(_Op(eng, fn, deps, ddeps, dma))
        for b in reads:
            b.r.append(idx)
        for b in writes:
            b.w = idx
            b.r = []
        return idx

    def flush(self):
        ops = self.ops
        base = self.base
        last = {}
        for i, o in enumerate(ops):
            if o.dma is None and o.fn is not None:
                last[o.eng] = i
        for e in self.ENGS:
            deps = set(base + i for ee, i in last.items() if ee != e)
            ops.append(_Op(e, None, deps, dict(self.dcnt), None))
        for o in ops:
            for d in o.deps:
                od = ops[d - base]
                if od.eng != o.eng or o.dma is not None or (SAME_SYNC and o.eng != "pe"):
                    od.pub = True
        for o in ops:
            if o.pub:
                c = self.cnt[o.eng]
                self.cnt[o.eng] = c + 1
                o.seq = (c // EPOCH, c % EPOCH + 1)
        per = {e: [] for e in self.ENGS}
        seen = {e: {} for e in self.ENGS}
        seend = {e: {} for e in self.ENGS}
        for o in ops:
            F = o.eng
            need = {}
            for d in o.deps:
                od = ops[d - base]
                if od.eng == F and o.dma is None and (F == "pe" or not SAME_SYNC):
                    continue
                if od.seq > need.get(od.eng, (-1, -1)):
                    need[od.eng] = od.seq
            waits = []
            for E, sq in need.items():
                if seen[F].get(E, (-1, -1)) >= sq:
                    continue
                seen[F][E] = sq
                waits.append((self._esem(E, sq[0]), sq[1]))
            for k, v in o.ddeps.items():
                if v == 0 or seend[F].get(k, 0) >= v:
                    continue
                seend[F][k] = v
                waits.append((self.dsem[k], v))
            inc = self._esem(F, o.seq[0]) if o.pub else None
            dinc = self.dsem[o.dma] if o.dma is not None else None
            per[F].append((waits, o.fn, inc, dinc))
            self.ninstr += len(waits) + 1

        def run(eng, lst):
            for waits, fn, inc, dinc in lst:
                for s, v in waits:
                    eng.wait_ge(s, v)
                if fn is None:
                    continue
                ins = fn(eng)
                if inc is not None:
                    ins.then_inc(inc, 1)
                if dinc is not None:
                    ins.then_inc(dinc, 16)

        with self.nc.Block() as block:
            @block.tensor
            def _(e):
                run(e, per["pe"])

            @block.scalar
            def _(e):
                run(e, per["act"])

            @block.vector
            def _(e):
                run(e, per["dve"])

            @block.gpsimd
            def _(e):
                run(e, per["pool"])

            @block.sync
            def _(e):
                run(e, per["sp"])
        self.base = base + len(ops)
        self.ops = []


class Tl:
    __slots__ = ("t", "b", "name", "psum")

    def __init__(self, t, name, psum=False):
        self.t = t
        self.b = Buf(name)
        self.name = name
        self.psum = psum

    def __getitem__(self, k):
        return self.t[k]


def _host_consts():
    c = {}
    pos = np.arange(SEQ, dtype=np.float32)
    p = np.arange(128)
    inv32 = (np.float32(10000.0) ** (-(np.arange(0, 64, 2, dtype=np.float32) / np.float32(64)))).astype(np.float32)
    f = p % 32
    half = (p % 64) // 32
    ang = (pos[None, :] * inv32[f][:, None]).astype(np.float32)
    sgn = np.where(half == 0, -1.0, 1.0).astype(np.float32)[:, None]
    tab1 = np.stack([np.cos(ang), np.sin(ang) * sgn], axis=1).astype(np.float32)
    inv16 = (np.float32(10000.0) ** (-(np.arange(0, 32, 2, dtype=np.float32) / np.float32(32)))).astype(np.float32)
    pp = p % 64
    blk = pp // 32
    q = pp % 32
    half2 = q // 16
    f2 = q % 16
    prow = np.floor(pos / 64).astype(np.float32)
    pcol = (pos - prow * 64).astype(np.float32)
    pf = np.where(blk[:, None] == 0, prow[None, :], pcol[None, :]).astype(np.float32)
    ang2 = (pf * inv16[f2][:, None]).astype(np.float32)
    sgn2 = np.where(half2 == 0, -1.0, 1.0).astype(np.float32)[:, None]
    tab2 = np.stack([np.cos(ang2), np.sin(ang2) * sgn2], axis=1).astype(np.float32)
    c["tab1"] = tab1.reshape(128, 2 * SEQ)
    c["tab2"] = tab2.reshape(128, 2 * SEQ)
    mats = np.zeros((128, 5, 128), np.float32)
    for m in range(128):
        part1 = m + 32 if (m % 64) // 32 == 0 else m - 32
        mats[part1, 0, m] = 1.0
        part2 = m + 16 if (m % 32) // 16 == 0 else m - 16
        mats[part2, 1, m] = 1.0
    mats[:, 2, :] = (p[:, None] // 64 == p[None, :] // 64)
    mats[:, 3, :] = 1.0
    mats[:, 4, :] = np.eye(128)
    c["mats"] = mats.reshape(128, 5 * 128)
    i = np.arange(128)[:, None]
    m = np.arange(128)[None, :]
    mask3 = np.stack([(i - m >= 64), (np.abs(i - m) <= 64), (m - i >= 64)], axis=1).astype(np.float32)
    c["mask3"] = mask3.reshape(128, 3 * 128)
    kc = np.arange(128)[:, None] % 64
    qc = np.arange(64)[None, :]
    ws = np.clip(qc - 8, 0, 48)
    c["colmask"] = ((kc >= ws) & (kc < ws + 16)).astype(np.float32)
    return c


P_G1 = 0
P_G2 = 8
P_QKG = 16
P_SUB = 24
P_CW = 25
P_CB = 25 + 132
P_N = 25 + 132 + 44


def _pack_params(inp):
    out = np.zeros((128, NL, P_N), np.float32)
    for l in range(NL):
        out[:, l, P_G1:P_G1 + 8] = inp["norm1_g"][l].reshape(8, 128).T
        out[:, l, P_G2:P_G2 + 8] = inp["norm2_g"][l].reshape(8, 128).T
        qg = inp["qk_g"][l].reshape(8, 64)
        out[:, l, P_QKG:P_QKG + 8] = np.concatenate([qg, qg], axis=1).T
        out[:, l, P_SUB] = inp["subln_g"][l]
        cw = inp["conv_w"][l].reshape(3, 44, 128)
        out[:, l, P_CW:P_CW + 132] = cw.transpose(2, 0, 1).reshape(128, 132)
        out[:, l, P_CB:P_CB + 44] = inp["conv_b"][l].reshape(44, 128).T
    return out.reshape(128, NL * P_N)


def _pack_rpb(rpb):
    kc = np.arange(64)[:, None]
    qc = np.arange(64)[None, :]
    idx = np.clip(kc - qc + 15, 0, 30)
    g = rpb[:, :, :, idx]
    g = np.transpose(g, (0, 3, 1, 2, 4))
    lo = g
    hi = np.concatenate([g[:, :, :, 1:, :], g[:, :, :, 14:15, :]], axis=3)
    return np.ascontiguousarray(np.concatenate([lo, hi], axis=1)).reshape(NL, 128, 8 * 15 * 64)


class Builder:
    def __init__(self, n_layers=NL, debug=False):
        self.n_layers = n_layers
        self.debug = debug
        self.nc = bass.Bass("TRN2", target_bir_lowering=False)
        nc = self.nc
        di = lambda n, s, dt=F32: nc.dram_tensor(n, s, dt, kind="ExternalInput").ap()
        self.xT = di("xT", [DM, SEQ])
        self.w_in = di("w_in", [NL, DM, N_IN])
        self.w_branch = di("w_branch", [NL, 4, 512, DM])
        self.w_out = di("w_out", [NL, DM, DM])
        self.w_up = di("w_up", [NL, DM, 2 * D_FF])
        self.w_down = di("w_down", [NL, D_FF, DM])
        self.params = di("params", [128, NL * P_N])
        self.lam = di("lam", [1, NL * 256])
        self.rpbE = di("rpbE", [NL, 128, 8 * 15 * 64])
        self.c_tab1 = di("tab1", [128, 2 * SEQ])
        self.c_tab2 = di("tab2", [128, 2 * SEQ])
        self.c_mats = di("mats", [128, 5 * 128])
        self.c_mask3 = di("mask3", [128, 3 * 128])
        self.c_colmask = di("colmask", [128, 64])
        self.yT = nc.dram_tensor("yT", [DM, SEQ], F32, kind="ExternalOutput").ap()
        kind = "ExternalOutput" if debug else "Internal"
        ds = lambda n, s, dt: nc.dram_tensor(n, s, dt, kind=kind).ap()
        self.QK = ds("s_qk", [45 * 128, SEQ], BF16)
        self.VA = ds("s_va", [SEQ, 512], BF16)
        self.VB = ds("s_vb", [SEQ, 130], BF16)
        self.VC = ds("s_vc", [3, SEQ, 520], BF16)
        self.VD = ds("s_vd", [SEQ, 520], BF16)
        self.G = ds("s_g", [4096, SEQ], F32)
        self.OT = ds("s_ot", [4, 512, SEQ], BF16)
        self.M = ds("s_m", [D_FF, SEQ], BF16)
        self.XM = ds("s_xm", [DM, SEQ], F32)
        self.XA = ds("s_xa", [DM, SEQ], F32)
        self.XB = ds("s_xb", [DM, SEQ], F32)

    def tile(self, name, shape, dt):
        t = self.ph.enter_context(self.nc.sbuf_tensor(name + "_%d" % self.uid, shape, dt))
        self.uid += 1
        return Tl(t, name)

    def ptile(self, name, shape=(128, 512), dt=F32):
        t = self.ph.enter_context(self.nc.psum_tensor(name + "_%d" % self.uid, [128, 512], F32))
        self.uid += 1
        return Tl(t, name, True)

    def A(self, eng, meth, reads, writes, *a, **k):
        wr = [x.b for x in writes] + [x.b for x in reads if x.psum]
        self.S.op(eng, lambda e: getattr(e, meth)(*a, **k), [x.b for x in reads], wr)

    def load(self, dst_tl, out_ap, in_ap, q="sp"):
        self.S.op(q, lambda e: e.dma_start(out=out_ap, in_=in_ap), [], [dst_tl.b], dma=("L", dst_tl.name))

    def store(self, src_tl, out_ap, in_ap, q="pool"):
        self.S.op(q, lambda e: e.dma_start(out=out_ap, in_=in_ap), [src_tl.b], [], dma=("S", src_tl.name))

    def mm(self, out_tl, out_ap, l_tl, l_ap, r_tl, r_ap, start, stop):
        self.S.op("pe", lambda e: e.matmul(out_ap, l_ap, r_ap, start=start, stop=stop),
                  [l_tl.b, r_tl.b], [out_tl.b])

    def begin(self):
        self.ph = ExitStack()
        self.ph.__enter__()

    def end(self):
        self.S.flush()
        self.ph.__exit__(None, None, None)

    def build(self):
        nc = self.nc
        self.uid = 0
        with ExitStack() as top:
            self.S = Sched(nc, top)
            self.top = top
            self.ph = top
            self.mats = self.tile("mats", [128, 5, 128], F32)
            self.onesb = self.tile("onesb", [128, 128], BF16)
            self.par = self.tile("par", [128, NL, P_N], F32)
            self.nlam = self.tile("nlam", [128, NL], F32)
            self.begin()
            self.lamt = self.tile("lamt", [1, NL * 256], F32)
            self.load(self.mats, self.mats[:].rearrange("p a b -> p (a b)"), self.c_mats)
            self.load(self.par, self.par[:].rearrange("p a b -> p (a b)"), self.params)
            self.load(self.lamt, self.lamt[:], self.lam)
            self.A("dve", "tensor_copy", [self.mats], [self.onesb], out=self.onesb[:], in_=self.mats[:, 3, :])
            self.lambda_setup()
            self.end()
            xin = self.xT
            for l in range(self.n_layers):
                last = (l == self.n_layers - 1)
                xout = self.yT if last else (self.XA if l % 2 == 0 else self.XB)
                self.layer(l, xin, xout)
                xin = xout
        return nc

    def lambda_setup(self):
        import math
        pr = self.tile("lampr", [1, NL * 2, 64], F32)
        sm = self.tile("lamsm", [1, NL * 2], F32)
        lv = self.tile("lamv", [1, NL], F32)
        ps = self.ptile("lamps", (128, NL))
        lt = self.lamt[:].rearrange("p (l a d) -> p l a d", l=NL, a=4)
        for l in range(NL):
            for j in range(2):
                self.A("dve", "tensor_tensor", [self.lamt], [pr], out=pr[:, l * 2 + j, :], in0=lt[:, l, 2 * j, :],
                       in1=lt[:, l, 2 * j + 1, :], op=ALU.mult)
        self.A("dve", "tensor_reduce", [pr], [sm], out=sm[:], in_=pr[:], axis=AX.X, op=ALU.add)
        self.A("act", "activation", [sm], [sm], out=sm[:], in_=sm[:], func=AF.Exp)
        smv = sm[:].rearrange("p (l j) -> p l j", j=2)
        for l in range(NL):
            li = 0.8 - 0.6 * math.exp(-0.3 * l)
            self.A("dve", "scalar_tensor_tensor", [sm], [lv], out=lv[:, l:l + 1], in0=smv[:, l, 1:2], scalar=-li,
                   in1=smv[:, l, 0:1], op0=ALU.add, op1=ALU.subtract)
        self.mm(ps, ps[:, 0:NL], self.mats, self.mats[0:1, 3, :], lv, lv[:], True, True)
        self.A("dve", "tensor_copy", [ps], [self.nlam], out=self.nlam[:], in_=ps[:, 0:NL])

    def rmsnorm_to_hT(self, l, xsrc, hT, goff):
        xs = [self.tile("nx%d" % i, [128, 8, TC], F32) for i in range(2)]
        sq = [self.tile("nsq%d" % i, [128, 8, TC], F32) for i in range(1)]
        rs = [self.tile("nrs%d" % i, [128, TC], F32) for i in range(2)]
        ps = [self.ptile("nps%d" % i) for i in range(2)]
        xv = xsrc.rearrange("(c p) n -> p c n", p=128)
        for t in range(NT):
            x = xs[t % 2]
            s = sq[0]
            r = rs[t % 2]
            p = ps[t % 2]
            self.load(x, x[:], xv[:, :, t * TC:(t + 1) * TC])
            self.A("pool", "tensor_tensor", [x], [s], out=s[:], in0=x[:], in1=x[:], op=ALU.mult)
            for c in range(8):
                self.mm(p, p[:], self.mats, self.mats[:, 3, :], s, s[:, c, :], c == 0, c == 7)
            self.A("act", "activation", [p], [r], out=r[:], in_=p[:], func=AF.Sqrt, scale=1.0 / DM, bias=self.epsc[:, 0:1])
            self.A("dve", "reciprocal", [r], [r], out=r[:], in_=r[:])
            for c in range(8):
                self.A("dve", "scalar_tensor_tensor", [x, r, self.par], [hT], out=hT[:, c, t * TC:(t + 1) * TC],
                       in0=x[:, c, :], scalar=self.par[:, l, goff + c:goff + c + 1], in1=r[:], op0=ALU.mult, op1=ALU.mult)

    def load_weight_bf16(self, dst, dst_ap, src_ap, stg, shape_ap, eng):
        self.load(stg, shape_ap, src_ap)
        self.A(eng, "tensor_copy", [stg], [dst], out=dst_ap, in_=shape_ap)

    def layer(self, l, xin, xout):
        import math
        self.lambda_init = 0.8 - 0.6 * math.exp(-0.3 * l)
        self.begin()
        self.epsc = self.tile("epsc", [128, 1], F32)
        self.A("pool", "memset", [], [self.epsc], self.epsc[:], EPS)
        hT = self.tile("hT", [128, 8, SEQ], BF16)
        sub = ExitStack()
        outer = self.ph
        self.ph = sub
        sub.__enter__()
        self.rmsnorm_to_hT(l, xin, hT, P_G1)
        self.S.flush()
        sub.__exit__(None, None, None)
        self.ph = outer
        self.proj(l, hT)
        self.end()
        self.begin()
        self.epsc = self.tile("epsc", [128, 1], F32)
        self.A("pool", "memset", [], [self.epsc], self.epsc[:], EPS)
        self.attn_a(l)
        self.end()
        self.begin()
        self.attn_b(l)
        self.end()
        self.begin()
        self.attn_c(l)
        self.end()
        self.begin()
        self.attn_d(l)
        self.end()
        self.begin()
        self.merge(l, xin)
        self.end()
        self.begin()
        self.epsc = self.tile("epsc", [128, 1], F32)
        self.A("pool", "memset", [], [self.epsc], self.epsc[:], EPS)
        hT = self.tile("hT2", [128, 8, SEQ], BF16)
        sub = ExitStack()
        outer = self.ph
        self.ph = sub
        sub.__enter__()
        self.rmsnorm_to_hT(l, self.XM, hT, P_G2)
        self.S.flush()
        sub.__exit__(None, None, None)
        self.ph = outer
        self.ffn_up(l, hT)
        self.end()
        self.begin()
        self.ffn_down(l, xout)
        self.end()

    def pipe(self, items, lags):
        n = len(items)
        for j in range(n + max(lags)):
            for si, lg in enumerate(lags):
                i = j - lg
                if 0 <= i < n and len(items[i]) > si and items[i][si] is not None:
                    items[i][si]()

    def proj(self, l, hT):
        wst = self.tile("wst0", [128, 8, 512], F32)
        wbf = [self.tile("wbf%d" % i, [128, 8, 512], BF16) for i in range(2)]
        tab = self.tile("tab", [128, 2, SEQ], F32)
        pacc = [self.ptile("pacc%d" % i) for i in range(3)]
        pss = [self.ptile("pss%d" % i) for i in range(2)]
        prot = [self.ptile("prot%d" % i) for i in range(2)]
        sq = [self.tile("sq%d" % i, [128, TC], F32) for i in range(3)]
        rs = [self.tile("rs%d" % i, [128, TC], F32) for i in range(3)]
        yy = [self.tile("yy%d" % i, [128, TC], F32) for i in range(3)]
        t2 = [self.tile("t2%d" % i, [128, TC], F32) for i in range(3)]
        qo = [self.tile("qo%d" % i, [128, TC], BF16) for i in range(3)]
        rowb = [self.tile("rowb%d" % i, [128, SEQ], BF16) for i in range(2)]
        go = [self.tile("go%d" % i, [128, TC], F32) for i in range(3)]
        vst = [self.tile("vst%d" % i, [128, 4, 520], BF16) for i in range(2)]
        vsta = [self.tile("vsta%d" % i, [128, 4, 512], BF16) for i in range(2)]
        for v in vst:
            self.A("pool", "memset", [], [v], v[:], 1.0)
        wv = self.w_in[l].rearrange("(k p) n -> p k n", p=128)
        cnt = dict(acc=0, q=0, row=0, go=0, vs=0, tmp=0, pp=0)
        G = P_QKG

        def wload(gidx, col0, w):
            wb = wbf[gidx % 2]
            self.load(wst, wst[:, :, 0:w], wv[:, :, col0:col0 + w])
            self.A("pool" if gidx % 2 == 0 else "dve", "tensor_copy", [wst], [wb], out=wb[:, :, 0:w], in_=wst[:, :, 0:w])

        def qk_items(items, wb, off, row, gcol, rope, r):
            mat = 0 if rope == "1d" else 1
            rb = None
            if r > 1:
                rb = rowb[cnt["row"] % 2]
                cnt["row"] += 1
            for t in range(NT):
                pa = pacc[cnt["acc"] % 3]
                cnt["acc"] += 1
                j = cnt["tmp"] % 3
                cnt["tmp"] += 1
                ps_, pr_ = pss[cnt["pp"] % 2], prot[cnt["pp"] % 2]
                cnt["pp"] += 1
                s, rr, y, a2 = sq[j], rs[j], yy[j], t2[j]
                tsl = slice(t * TC, (t + 1) * TC)
                if r > 1:
                    n_ = TC // r
                    dst_tl = rb
                    dst = rb[:].rearrange("p (c i) -> p c i", c=r)[:, :, t * n_:(t + 1) * n_]
                else:
                    dst_tl = qo[cnt["q"] % 3]
                    cnt["q"] += 1
                    dst = dst_tl[:]

                def s0(pa=pa, s=s, tsl=tsl):
                    for k in range(8):
                        self.mm(pa, pa[:], wb, wb[:, k, off:off + 128], hT, hT[:, k, tsl], k == 0, k == 7)
                    self.A("act", "activation", [pa], [s], out=s[:], in_=pa[:], func=AF.Square)

                def s1(pa=pa, s=s, rr=rr, y=y, ps_=ps_, dst_tl=dst_tl, dst=dst):
                    self.mm(ps_, ps_[:], self.mats, self.mats[:, 2, :], s, s[:], True, True)
                    self.A("act", "activation", [ps_], [rr], out=rr[:], in_=ps_[:], func=AF.Sqrt, scale=1.0 / 64, bias=self.epsc[:, 0:1])
                    self.A("dve", "reciprocal", [rr], [rr], out=rr[:], in_=rr[:])
                    if rope is None:
                        self.A("dve", "scalar_tensor_tensor", [pa, rr, self.par], [dst_tl], out=dst, in0=pa[:],
                               scalar=self.par[:, l, gcol:gcol + 1], in1=rr[:], op0=ALU.mult, op1=ALU.mult)
                    else:
                        self.A("dve", "scalar_tensor_tensor", [pa, rr, self.par], [y], out=y[:], in0=pa[:],
                               scalar=self.par[:, l, gcol:gcol + 1], in1=rr[:], op0=ALU.mult, op1=ALU.mult)

                def s2(y=y, a2=a2, pr_=pr_, dst_tl=dst_tl, dst=dst, tsl=tsl, t=t):
                    if rope is not None:
                        self.mm(pr_, pr_[:], self.mats, self.mats[:, mat, :], y, y[:], True, True)
                        self.A("dve", "tensor_tensor", [pr_, tab], [a2], out=a2[:], in0=pr_[:], in1=tab[:, 1, tsl], op=ALU.mult)
                        self.A("pool", "tensor_tensor", [y, tab], [y], out=y[:], in0=y[:], in1=tab[:, 0, tsl], op=ALU.mult)
                        if r > 1:
                            src1 = y[:].rearrange("p (i c) -> p c i", c=r)
                            src2 = a2[:].rearrange("p (i c) -> p c i", c=r)
                        else:
                            src1, src2 = y[:], a2[:]
                        self.A("pool", "tensor_tensor", [y, a2], [dst_tl], out=dst, in0=src1, in1=src2, op=ALU.add)
                    if r == 1:
                        self.store(dst_tl, self.QK[row * 128:(row + 1) * 128, tsl], dst_tl[:])
                    elif t == NT - 1:
                        self.store(rb, self.QK[row * 128:(row + 1) * 128, :], rb[:])

                items.append([s0, s1, s2])

        def v_items(items, wb, off, w, dst, r, nh, hd):
            hv = hT[:].rearrange("p k (i c) -> p k c i", c=r)
            L = SEQ // r
            stride = hd + 1 if hd == 64 else hd
            vs = None
            for tt in range(32):
                c = (tt * 128) // L
                i0 = (tt * 128) % L
                pa = pacc[cnt["acc"] % 3]
                cnt["acc"] += 1
                if tt % 4 == 0:
                    vs = (vst if hd == 64 else vsta)[cnt["vs"] % 2]
                    cnt["vs"] += 1

                def s0(pa=pa, c=c, i0=i0):
                    for k in range(8):
                        self.mm(pa, pa[:, 0:w], hT, hv[:, k, c, i0:i0 + 128], wb, wb[:, k, off:off + w], k == 0, k == 7)

                def s1(pa=pa, vs=vs, tt=tt):
                    o = vs[:, tt % 4, 0:nh * stride].rearrange("p (h d) -> p h d", h=nh)[:, :, 0:hd]
                    i_ = pa[:, 0:w].rearrange("p (h d) -> p h d", h=nh)
                    if tt % 2 == 0:
                        self.A("act", "activation", [pa], [vs], out=o, in_=i_, func=AF.Copy)
                    else:
                        self.A("dve", "tensor_copy", [pa], [vs], out=o, in_=i_)
                    if tt % 4 == 3:
                        t0 = (tt - 3) * 128
                        self.store(vs, dst[t0:t0 + 512, :].rearrange("(a p) n -> p a n", p=128), vs[:, :, 0:nh * stride])

                items.append([s0, s1])

        def gate_items(items, wb, off, grow):
            for t in range(NT):
                pa = pacc[cnt["acc"] % 3]
                cnt["acc"] += 1
                g = go[cnt["go"] % 3]
                cnt["go"] += 1
                tsl = slice(t * TC, (t + 1) * TC)

                def s0(pa=pa, tsl=tsl):
                    for k in range(8):
                        self.mm(pa, pa[:], wb, wb[:, k, off:off + 128], hT, hT[:, k, tsl], k == 0, k == 7)

                def s1(pa=pa, g=g, tsl=tsl):
                    self.A("act", "activation", [pa], [g], out=g[:], in_=pa[:], func=AF.Sigmoid)
                    self.store(g, self.G[grow * 128:(grow + 1) * 128, tsl], g[:])

                items.append([s0, s1])

        jobsB = [
            (COL["bq"], 512, [("qk", i * 128, 8 + i, G + 2, "ax", 1) for i in range(4)]),
            (COL["bk"], 256, [("qk", 0, 12, G + 3, "ax", 1), ("v", 128, 128, self.VB, 1, 2, 64)]),
        ]
        jobs = [
            (COL["aq"], 512, [("qk", i * 128, 0 + i, G + 0, "1d", 1) for i in range(4)]),
            (COL["ak"], 512, [("qk", i * 128, 4 + i, G + 1, "1d", 1) for i in range(4)]),
            (COL["av"], 512, [("v", 0, 512, self.VA, 1, 4, 128)]),
        ]
        for g in range(3):
            jobs.append((COL["cq"] + g * 512, 512, [("qk", i * 128, 13 + g * 4 + i, G + 4, "1d", DIL[g]) for i in range(4)]))
            jobs.append((COL["ck"] + g * 512, 512, [("qk", i * 128, 25 + g * 4 + i, G + 5, "1d", DIL[g]) for i in range(4)]))
            jobs.append((COL["cv"] + g * 512, 512, [("v", 0, 512, self.VC[g], DIL[g], 8, 64)]))
        jobs.append((COL["dq"], 512, [("qk", i * 128, 37 + i, G + 6, None, 1) for i in range(4)]))
        jobs.append((COL["dk"], 512, [("qk", i * 128, 41 + i, G + 7, None, 1) for i in range(4)]))
        jobs.append((COL["dv"], 512, [("v", 0, 512, self.VD, 1, 8, 64)]))
        for gi in range(8):
            jobs.append((COL["g"] + gi * 512, 512, [("gate", i * 128, gi * 4 + i) for i in range(4)]))

        gctr = [0]

        def run_jobs(jl):
            items = []
            gid0 = gctr[0]
            for ji, (col0, w, subs) in enumerate(jl):
                gid = gid0 + ji
                wb = wbf[gid % 2]
                first = len(items)
                for sj in subs:
                    if sj[0] == "qk":
                        qk_items(items, wb, *sj[1:])
                    elif sj[0] == "v":
                        v_items(items, wb, *sj[1:])
                    else:
                        gate_items(items, wb, *sj[1:])
                orig = items[first][0]
                nxt = jl[ji + 1] if ji + 1 < len(jl) else None

                def s0w(orig=orig, nxt=nxt, gid=gid):
                    if nxt is not None:
                        wload(gid + 1, nxt[0], nxt[1])
                    orig()
                items[first][0] = s0w
            wload(gid0, jl[0][0], jl[0][1])
            gctr[0] += len(jl)
            self.pipe(items, [0, 1, 2])

        self.load(tab, tab[:].rearrange("p a n -> p (a n)"), self.c_tab2)
        run_jobs(jobsB)
        self.load(tab, tab[:].rearrange("p a n -> p (a n)"), self.c_tab1)
        run_jobs(jobs)

    def normalize_rows65(self, po, res_tl, res_ap, n, tmp_row, tmp_bc, pbc):
        self.A("dve", "reciprocal", [po], [tmp_row], out=tmp_row[64:65, 0:n], in_=po[64:65, 0:n])
        self.mm(pbc, pbc[0:64, 0:n], self.mats, self.mats[64:65, 3, 0:64], tmp_row, tmp_row[64:65, 0:n], True, True)
        self.A("act", "activation", [pbc], [tmp_bc], out=tmp_bc[0:64, 0:n], in_=pbc[0:64, 0:n], func=AF.Copy)
        self.A("dve", "tensor_tensor", [po, tmp_bc], [res_tl], out=res_ap, in0=po[0:64, 0:n], in1=tmp_bc[0:64, 0:n], op=ALU.mult)

    def attn_a(self, l):
        qa = self.tile("qa", [128, 4, SEQ], BF16)
        ka = self.tile("ka", [128, 4, SEQ], BF16)
        va = self.tile("va", [128, 32, 512], BF16)
        for h in range(4):
            self.load(qa, qa[:, h, :], self.QK[(0 + h) * 128:(1 + h) * 128, :])
            self.load(ka, ka[:, h, :], self.QK[(4 + h) * 128:(5 + h) * 128, :])
        for j in range(4):
            self.load(va, va[:, j * 8:(j + 1) * 8, :], self.VA[j * 1024:(j + 1) * 1024, :].rearrange("(a p) n -> p a n", p=128))
        psc = [self.ptile("psc%d" % i) for i in range(3)]
        po = [[self.ptile("po%d%d" % (c, i)) for i in range(2)] for c in range(2)]
        pfin = self.ptile("pfin")
        pt = [self.tile("pt%d" % i, [128, TC], BF16) for i in range(6)]
        sacc = [[self.tile("sacc%d%d" % (c, i), [128, TC], F32) for i in range(2)] for c in range(3)]
        sbf = self.tile("sbf", [128, TC], BF16)
        rc = self.tile("rc", [128, TC], F32)
        res = [self.tile("res%d" % i, [128, TC], F32) for i in range(2)]
        dd = self.tile("dd", [128, TC], F32)
        sq = self.tile("sqa", [128, TC], F32)
        rs = self.tile("rsa", [128, TC], F32)
        ob = [self.tile("oba%d" % i, [128, TC], BF16) for i in range(2)]
        items = []
        n = 0
        gi = 0
        for h in range(4):
            for qc in range(NT):
                qs = slice(qc * TC, (qc + 1) * TC)
                buf = gi % 2
                gi += 1
                for kt in range(32):
                    for c in range(2):
                        ps_ = slice(c * 64, (c + 1) * 64)
                        sc = psc[n % 3]
                        p = pt[n % 6]
                        n += 1
                        po_ = po[c][buf]

                        def s0(sc=sc, p=p, h=h, qs=qs, ps_=ps_, kt=kt):
                            self.mm(sc, sc[:], ka, ka[ps_, h, kt * 128:(kt + 1) * 128], qa, qa[ps_, h, qs], True, True)
                            self.A("act", "activation", [sc], [p], out=p[:], in_=sc[:], func=AF.Exp, scale=0.125)

                        def s1(p=p, h=h, c=c, kt=kt, po_=po_, buf=buf):
                            self.mm(po_, po_[:], va, va[:, kt, h * 128:(h + 1) * 128], p, p[:], kt == 0, kt == 31)
                            if c == 0:
                                a, eng, first = sacc[0][buf], "dve", kt == 0
                            elif kt % 2 == 0:
                                a, eng, first = sacc[1][buf], "dve", kt == 0
                            else:
                                a, eng, first = sacc[2][buf], "pool", kt == 1
                            if first:
                                self.A(eng, "tensor_copy", [p], [a], out=a[:], in_=p[:])
                            else:
                                self.A(eng, "tensor_tensor", [a, p], [a], out=a[:], in0=a[:], in1=p[:], op=ALU.add)

                        def s2(h=h, qc=qc, qs=qs, c=c, kt=kt, po_=po_, buf=buf):
                            if kt != 31:
                                return
                            parts = [sacc[0][buf]] if c == 0 else [sacc[1][buf], sacc[2][buf]]
                            for pi, a in enumerate(parts):
                                self.A("pool", "tensor_copy", [a], [sbf], out=sbf[:], in_=a[:])
                                self.mm(pfin, pfin[:], self.onesb, self.onesb[:], sbf, sbf[:], pi == 0, pi == len(parts) - 1)
                            self.A("dve", "reciprocal", [pfin], [rc], out=rc[:], in_=pfin[:])
                            self.A("dve", "tensor_tensor", [po_, rc], [res[c]], out=res[c][:], in0=po_[:], in1=rc[:], op=ALU.mult)
                            if c != 1:
                                return
                            self.A("dve", "scalar_tensor_tensor", [res[0], res[1], self.nlam], [dd], out=dd[:], in0=res[1][:],
                                   scalar=self.nlam[:, l:l + 1], in1=res[0][:], op0=ALU.mult, op1=ALU.add)
                            self.A("pool", "tensor_tensor", [dd], [sq], out=sq[:], in0=dd[:], in1=dd[:], op=ALU.mult)
                            self.mm(pfin, pfin[:], self.mats, self.mats[:, 3, :], sq, sq[:], True, True)
                            self.A("act", "activation", [pfin], [rs], out=rs[:], in_=pfin[:], func=AF.Sqrt, scale=1.0 / 128, bias=self.epsc[:, 0:1])
                            self.A("dve", "reciprocal", [rs], [rs], out=rs[:], in_=rs[:])
                            self.A("dve", "scalar_tensor_tensor", [dd, rs, self.par], [dd], out=dd[:], in0=dd[:],
                                   scalar=self.par[:, l, P_SUB:P_SUB + 1], in1=rs[:], op0=ALU.mult, op1=ALU.mult)
                            o = ob[(h * NT + qc) % 2]
                            self.A("act", "activation", [dd], [o], out=o[:], in_=dd[:], func=AF.Copy, scale=float(1.0 - self.lambda_init))
                            self.store(o, self.OT[0, h * 128:(h + 1) * 128, qs], o[:])

                        items.append([s0, s1, s2])
        self.pipe(items, [0, 3, 12])

    def attn_b(self, l):
        qb = self.tile("qb", [128, 4, SEQ], BF16)
        kb = self.tile("kb", [128, SEQ], BF16)
        vb = self.tile("vb", [128, 32, 130], BF16)
        for hq in range(8):
            g, s = hq // 4, hq % 4
            self.load(qb, qb[g * 64:(g + 1) * 64, s, :], self.QK[8 * 128 + hq * 64:8 * 128 + (hq + 1) * 64, :])
        self.load(kb, kb[:], self.QK[12 * 128:13 * 128, :])
        self.load(vb, vb[:], self.VB.rearrange("(a p) n -> p a n", p=128))
        psc = [self.ptile("psc%d" % i) for i in range(3)]
        po = [[self.ptile("pob%d%d" % (g, i)) for i in range(2)] for g in range(2)]
        pbc = self.ptile("pbc")
        pt = [self.tile("pt%d" % i, [128, TC], BF16) for i in range(6)]
        trow = self.tile("trow", [128, TC], F32)
        tbc = self.tile("tbc", [128, TC], F32)
        ob = [self.tile("obb%d" % i, [128, TC], BF16) for i in range(2)]
        items = []
        n = 0
        m = 0
        gi = 0
        for s in range(4):
            for qc in range(NT):
                qs = slice(qc * TC, (qc + 1) * TC)
                buf = gi % 2
                gi += 1
                for kt in range(32):
                    for g in range(2):
                        hq = g * 4 + s
                        ps_ = slice(g * 64, (g + 1) * 64)
                        o_ps = po[g][buf]
                        sc = psc[n % 3]
                        p = pt[n % 6]
                        n += 1

                        def s0(sc=sc, p=p, ps_=ps_, s=s, qs=qs, kt=kt):
                            self.mm(sc, sc[:], kb, kb[ps_, kt * 128:(kt + 1) * 128], qb, qb[ps_, s, qs], True, True)
                            self.A("act", "activation", [sc], [p], out=p[:], in_=sc[:], func=AF.Exp, scale=0.125)

                        def s1(p=p, g=g, kt=kt, o_ps=o_ps):
                            self.mm(o_ps, o_ps[0:65, :], vb, vb[:, kt, g * 65:(g + 1) * 65], p, p[:], kt == 0, kt == 31)

                        def s2(g=g, hq=hq, qs=qs, kt=kt, o_ps=o_ps):
                            if kt != 31:
                                return
                            o = ob[hq % 2]
                            self.normalize_rows65(o_ps, o, o[0:64, :], TC, trow, tbc, pbc)
                            self.store(o, self.OT[1, hq * 64:(hq + 1) * 64, qs], o[0:64, :])

                        items.append([s0, s1, s2])
        self.pipe(items, [0, 3, 12])

    def attn_c(self, l):
        mask = self.tile("mask3", [128, 3, 128], F32)
        self.load(mask, mask[:].rearrange("p a b -> p (a b)"), self.c_mask3)
        qc_ = self.tile("qc", [128, 3, SEQ], BF16)
        kc_ = self.tile("kc", [128, 3, SEQ], BF16)
        vc_ = self.tile("vc", [128, 3, 32, 130], BF16)
        acc = [self.tile("acc%d" % i, [65, SEQ], F32) for i in range(2)]
        psc = [self.ptile("pscc%d" % i, (128, 384)) for i in range(3)]
        po = [self.ptile("poc%d" % i) for i in range(2)]
        pbc = self.ptile("pbcc")
        ex = [self.tile("ex%d" % i, [128, 3, 128], F32) for i in range(3)]
        pt = [self.tile("ptc%d" % i, [128, 3, 128], BF16) for i in range(4)]
        trow = self.tile("trowc", [128, TC], F32)
        ob = [self.tile("obc%d" % i, [128, TC], BF16) for i in range(2)]
        n = 0
        m = 0
        for jp in range(4):
            for g in range(3):
                self.load(qc_, qc_[:, g, :], self.QK[(13 + g * 4 + jp) * 128:(14 + g * 4 + jp) * 128, :])
                self.load(kc_, kc_[:, g, :], self.QK[(25 + g * 4 + jp) * 128:(26 + g * 4 + jp) * 128, :])
                self.load(vc_, vc_[:, g, :, :], self.VC[g].rearrange("(a p) n -> p a n", p=128)[:, :, jp * 130:(jp + 1) * 130])
            items = []
            for hh in range(2):
                ps_ = slice(hh * 64, (hh + 1) * 64)
                ac = acc[hh]
                hd = jp * 2 + hh
                for g in range(3):
                    r = DIL[g]
                    L = SEQ // r
                    tps = L // 128
                    for qb4 in range(8):
                        o_ps = po[m % 2]
                        m += 1
                        for u in range(4):
                            qb = qb4 * 4 + u
                            seg = qb // tps
                            kts = [k for k in (qb - 1, qb, qb + 1) if k // tps == seg and 0 <= k < 32]
                            j0 = kts[0] - (qb - 1)
                            nk = len(kts)
                            sc = psc[n % 3]
                            e_ = ex[n % 3]
                            p = pt[n % 4]
                            n += 1

                            def s0(sc=sc, e_=e_, p=p, kts=kts, j0=j0, nk=nk, qb=qb, g=g, ps_=ps_):
                                for k in kts:
                                    j = k - (qb - 1)
                                    self.mm(sc, sc[:, j * 128:(j + 1) * 128], kc_, kc_[ps_, g, k * 128:(k + 1) * 128],
                                            qc_, qc_[ps_, g, qb * 128:(qb + 1) * 128], True, True)
                                scv = sc[:, 0:384].rearrange("p (a b) -> p a b", a=3)
                                self.A("act", "activation", [sc], [e_], out=e_[:, j0:j0 + nk, :], in_=scv[:, j0:j0 + nk, :], func=AF.Exp, scale=0.125)
                                self.A("pool", "tensor_tensor", [e_, mask], [p], out=p[:, j0:j0 + nk, :], in0=e_[:, j0:j0 + nk, :],
                                       in1=mask[:, j0:j0 + nk, :], op=ALU.mult)

                            def s1(p=p, kts=kts, nk=nk, qb=qb, g=g, hh=hh, u=u, o_ps=o_ps, qb4=qb4, r=r, L=L, ac=ac, hd=hd):
                                for ki, k in enumerate(kts):
                                    j = k - (qb - 1)
                                    self.mm(o_ps, o_ps[0:65, u * 128:(u + 1) * 128], vc_, vc_[:, g, k, hh * 65:(hh + 1) * 65],
                                            p, p[:, j, :], ki == 0, ki == nk - 1)
                                if u != 3:
                                    return
                                pos0 = qb4 * 512
                                av = ac[:].rearrange("p (i c) -> p c i", c=r)
                                if L >= 512:
                                    c0, i0 = pos0 // L, pos0 % L
                                    dst = av[0:65, c0:c0 + 1, i0:i0 + 512]
                                    src = o_ps[0:65, :].rearrange("p (c i) -> p c i", c=1)
                                else:
                                    ncl = 512 // L
                                    c0 = pos0 // L
                                    dst = av[0:65, c0:c0 + ncl, :]
                                    src = o_ps[0:65, :].rearrange("p (c i) -> p c i", c=ncl)
                                if g == 0:
                                    self.A("act", "activation", [o_ps], [ac], out=dst, in_=src, func=AF.Copy)
                                else:
                                    self.A("dve", "tensor_tensor", [o_ps, ac], [ac], out=dst, in0=dst, in1=src, op=ALU.add)
                                if g == 2 and qb4 == 7:
                                    for t in range(NT):
                                        o = ob[(hd * NT + t) % 2]
                                        ts_ = slice(t * TC, (t + 1) * TC)
                                        self.A("dve", "reciprocal", [ac], [trow], out=trow[64:65, :], in_=ac[64:65, ts_])
                                        self.mm(pbc, pbc[0:64, :], self.mats, self.mats[64:65, 3, 0:64], trow, trow[64:65, :], True, True)
                                        self.A("dve", "tensor_tensor", [pbc, ac], [o], out=o[0:64, :], in0=pbc[0:64, :], in1=ac[0:64, ts_], op=ALU.mult)
                                        self.store(o, self.OT[2, hd * 64:(hd + 1) * 64, ts_], o[0:64, :])

                            items.append([s0, s1])
            self.pipe(items, [0, 2])

    def attn_d(self, l):
        qd = self.tile("qd", [128, 4, SEQ], BF16)
        kd = self.tile("kd", [128, 4, SEQ], BF16)
        ve = self.tile("ve", [128, 32, 520], BF16)
        vo = self.tile("vo", [128, 31, 520], BF16)
        E = self.tile("E", [128, 8, 15, 64], F32)
        cm = self.tile("cm", [128, 64], F32)
        self.load(cm, cm[:], self.c_colmask)
        for i in range(4):
            self.load(qd, qd[:, i, :], self.QK[(37 + i) * 128:(38 + i) * 128, :])
            self.load(kd, kd[:, i, :], self.QK[(41 + i) * 128:(42 + i) * 128, :])
        for j in range(4):
            self.load(ve, ve[:, j * 8:(j + 1) * 8, :], self.VD[j * 1024:(j + 1) * 1024, :].rearrange("(a p) n -> p a n", p=128))
        for j in range(4):
            na = 8 if j < 3 else 7
            self.load(vo, vo[:, j * 8:j * 8 + na, :], self.VD[64 + j * 1024:64 + j * 1024 + na * 128, :].rearrange("(a p) n -> p a n", p=128))
        for h in range(8):
            self.load(E, E[:, h, :, :].rearrange("p a b -> p (a b)"), self.rpbE[l][:, h * 960:(h + 1) * 960])
        for h in range(8):
            self.A("act", "activation", [E], [E], out=E[:, h, :, :], in_=E[:, h, :, :], func=AF.Exp)
            self.A("pool", "tensor_tensor", [E, cm], [E], out=E[:, h, :, :], in0=E[:, h, :, :],
                   in1=cm[:].rearrange("p (a b) -> p a b", a=1).to_broadcast([128, 15, 64]), op=ALU.mult)
        psc = [self.ptile("pscd%d" % i, (128, 256)) for i in range(3)]
        po = [self.ptile("pod%d" % i) for i in range(2)]
        pbc = self.ptile("pbcd")
        ex = [self.tile("exd%d" % i, [128, 4, 64], F32) for i in range(3)]
        pt = [self.tile("ptd%d" % i, [128, 4, 64], BF16) for i in range(4)]
        trow = self.tile("trowd", [128, TC], F32)
        tbc = self.tile("tbcd", [128, TC], F32)
        ob = [self.tile("obd%d" % i, [128, TC], BF16) for i in range(2)]
        items = []
        n = 0
        m = 0
        for h in range(8):
            ch, hh = h // 2, h % 2
            ps_ = slice(hh * 64, (hh + 1) * 64)
            for r8 in range(8):
                o_ps = po[m % 2]
                o = ob[m % 2]
                m += 1
                for u in range(8):
                    r = r8 * 8 + u
                    rs_ = min(max(r - 4, 0), 56)
                    base = rs_ - r + 7
                    sc = psc[n % 3]
                    e_ = ex[n % 3]
                    p = pt[n % 4]
                    n += 1

                    def s0(sc=sc, e_=e_, p=p, r=r, rs_=rs_, base=base, h=h, ch=ch, ps_=ps_, n=n):
                        for i in range(4):
                            k0 = (rs_ + 2 * i) * 64
                            self.mm(sc, sc[:, i * 64:(i + 1) * 64], kd, kd[ps_, ch, k0:k0 + 128], qd, qd[ps_, ch, r * 64:(r + 1) * 64], True, True)
                        self.A("act", "activation", [sc], [e_], out=e_[:], in_=sc[:, 0:256].rearrange("p (a b) -> p a b", a=4), func=AF.Exp, scale=0.125)
                        ev = E[:, h, base:base + 7:2, :]
                        self.A("pool" if n % 2 == 0 else "dve", "tensor_tensor", [e_, E], [p], out=p[:], in0=e_[:], in1=ev, op=ALU.mult)

                    def s1(p=p, rs_=rs_, h=h, u=u, o_ps=o_ps, o=o, r8=r8):
                        for i in range(4):
                            row0 = rs_ + 2 * i
                            if row0 % 2 == 0:
                                vt, vi = ve, row0 // 2
                            else:
                                vt, vi = vo, (row0 - 1) // 2
                            self.mm(o_ps, o_ps[0:65, u * 64:(u + 1) * 64], vt, vt[:, vi, h * 65:(h + 1) * 65], p, p[:, i, :], i == 0, i == 3)
                        if u != 7:
                            return
                        self.normalize_rows65(o_ps, o, o[0:64, :], TC, trow, tbc, pbc)
                        self.store(o, self.OT[3, h * 64:(h + 1) * 64, r8 * TC:(r8 + 1) * TC], o[0:64, :])

                    items.append([s0, s1])
        self.pipe(items, [0, 2])

    def merge(self, l, xin):
        wb = self.tile("wbr", [128, 16, DM], BF16)
        wo = self.tile("wo", [128, 8, DM], BF16)
        stg = [self.tile("mstg%d" % i, [128, 4, DM], F32) for i in range(2)]
        n = 0
        for i in range(4):
            s = stg[n % 2]
            n += 1
            self.load(s, s[:], self.w_branch[l, i].rearrange("(k p) n -> p k n", p=128))
            self.A("pool" if n % 2 == 0 else "dve", "tensor_copy", [s], [wb], out=wb[:, i * 4:(i + 1) * 4, :], in_=s[:])
        for j in range(2):
            s = stg[n % 2]
            n += 1
            self.load(s, s[:], self.w_out[l][j * 512:(j + 1) * 512, :].rearrange("(k p) n -> p k n", p=128))
            self.A("pool" if n % 2 == 0 else "dve", "tensor_copy", [s], [wo], out=wo[:, j * 4:(j + 1) * 4, :], in_=s[:])
        ot = [self.tile("mot%d" % i, [128, 16, TC], BF16) for i in range(2)]
        xs = [self.tile("mx%d" % i, [128, 8, TC], F32) for i in range(2)]
        gt = [self.tile("mg%d" % i, [128, TC], F32) for i in range(4)]
        mt = [self.tile("mm%d" % i, [128, 8, TC], BF16) for i in range(2)]
        acc = [self.tile("macc%d" % i, [128, TC], F32) for i in range(2)]
        tmp = [self.tile("mtmp%d" % i, [128, TC], F32) for i in range(2)]
        py = [self.ptile("py%d" % i) for i in range(3)]
        px = [self.ptile("px%d" % i) for i in range(2)]
        xo = [self.tile("mxo%d" % i, [128, TC], F32) for i in range(3)]
        xv = xin.rearrange("(c p) n -> p c n", p=128)
        otv = self.OT.rearrange("b (k p) n -> p (b k) n", p=128)
        ng = 0
        ny = 0
        nx = 0
        for t in range(NT):
            ts_ = slice(t * TC, (t + 1) * TC)
            o = ot[t % 2]
            x = xs[t % 2]
            mtt = mt[t % 2]
            for i in range(4):
                self.load(o, o[:, i * 4:(i + 1) * 4, :], otv[:, i * 4:(i + 1) * 4, ts_])
            self.load(x, x[:], xv[:, :, ts_])
            for f in range(8):
                a = acc[f % 2]
                for i in range(4):
                    g = gt[ng % 4]
                    ng += 1
                    self.load(g, g[:], self.G[(i * 8 + f) * 128:(i * 8 + f + 1) * 128, ts_])
                    p = py[ny % 3]
                    ny += 1
                    for k in range(4):
                        self.mm(p, p[:], wb, wb[:, i * 4 + k, f * 128:(f + 1) * 128], o, o[:, i * 4 + k, :], k == 0, k == 3)
                    if i == 0:
                        self.A("dve", "tensor_tensor", [p, g], [a], out=a[:], in0=p[:], in1=g[:], op=ALU.mult)
                    else:
                        tm = tmp[i % 2]
                        self.A("dve", "tensor_tensor", [p, g], [tm], out=tm[:], in0=p[:], in1=g[:], op=ALU.mult)
                        if i < 3:
                            self.A("pool", "tensor_tensor", [a, tm], [a], out=a[:], in0=a[:], in1=tm[:], op=ALU.add)
                        else:
                            self.A("pool", "tensor_tensor", [a, tm], [mtt], out=mtt[:, f, :], in0=a[:], in1=tm[:], op=ALU.add)
            for fo in range(8):
                p = px[nx % 2]
                xo_ = xo[nx % 3]
                nx += 1
                for k in range(8):
                    self.mm(p, p[:], wo, wo[:, k, fo * 128:(fo + 1) * 128], mtt, mtt[:, k, :], k == 0, k == 7)
                self.A("dve", "tensor_tensor", [p, x], [xo_], out=xo_[:], in0=p[:], in1=x[:, fo, :], op=ALU.add)
                self.store(xo_, self.XM[fo * 128:(fo + 1) * 128, ts_], xo_[:])

    def ffn_up(self, l, hT):
        wst = [self.tile("fst%d" % i, [128, 8, 256], F32) for i in range(2)]
        wbf = [self.tile("fbf%d" % i, [128, 8, 256], BF16) for i in range(4)]
        ua = self.tile("ua", [128, SEQ + 2], F32)
        ug = self.tile("ug", [128, SEQ + 2], F32)
        ca = self.tile("ca", [128, SEQ], F32)
        cg = self.tile("cg", [128, SEQ], F32)
        mrow = [self.tile("mrow%d" % i, [128, SEQ], BF16) for i in range(2)]
        pacc = [self.ptile("fpa%d" % i) for i in range(4)]
        for u in (ua, ug):
            self.A("pool", "memset", [], [u], u[:, 0:1], 0.0)
            self.A("pool", "memset", [], [u], u[:, SEQ + 1:SEQ + 2], 0.0)
        wv = self.w_up[l].rearrange("(k p) n -> p k n", p=128)
        ngrp = 0
        nacc = 0
        for j in range(11):
            w = 256
            wbs = []
            for part in range(2):
                st = wst[ngrp % 2]
                wb = wbf[ngrp % 4]
                ngrp += 1
                c0 = part * D_FF + j * 256
                self.load(st, st[:, :, 0:w], wv[:, :, c0:c0 + w])
                self.A("pool" if part == 0 else "dve", "tensor_copy", [st], [wb], out=wb[:, :, 0:w], in_=st[:, :, 0:w])
                wbs.append(wb)
            for ii in range(w // 128):
                i = j * 2 + ii
                for part, (u, cdst) in enumerate(((ua, ca), (ug, cg))):
                    wb = wbs[part]
                    for t in range(NT):
                        p = pacc[nacc % 4]
                        nacc += 1
                        for k in range(8):
                            self.mm(p, p[:], wb, wb[:, k, ii * 128:(ii + 1) * 128], hT, hT[:, k, t * TC:(t + 1) * TC], k == 0, k == 7)
                        if t % 2 == 0:
                            self.A("act", "activation", [p], [u], out=u[:, 1 + t * TC:1 + (t + 1) * TC], in_=p[:], func=AF.Copy)
                        else:
                            self.A("dve", "tensor_copy", [p], [u], out=u[:, 1 + t * TC:1 + (t + 1) * TC], in_=p[:])
                    ch = part * 22 + i
                    w0 = self.par[:, l, P_CW + 0 * 44 + ch:P_CW + 0 * 44 + ch + 1]
                    w1 = self.par[:, l, P_CW + 1 * 44 + ch:P_CW + 1 * 44 + ch + 1]
                    w2 = self.par[:, l, P_CW + 2 * 44 + ch:P_CW + 2 * 44 + ch + 1]
                    bb = self.par[:, l, P_CB + ch:P_CB + ch + 1]
                    eng = "dve"
                    for hf in range(2):
                        hs = slice(hf * 2048, (hf + 1) * 2048)
                        self.A(eng, "tensor_scalar", [u, self.par], [cdst], out=cdst[:, hs], in0=u[:, hf * 2048:hf * 2048 + 2048],
                               scalar1=w0, scalar2=bb, op0=ALU.mult, op1=ALU.add)
                        self.A(eng, "scalar_tensor_tensor", [u, cdst, self.par], [cdst], out=cdst[:, hs], in0=u[:, 1 + hf * 2048:1 + hf * 2048 + 2048],
                               scalar=w1, in1=cdst[:, hs], op0=ALU.mult, op1=ALU.add)
                        self.A(eng, "scalar_tensor_tensor", [u, cdst, self.par], [cdst], out=cdst[:, hs], in0=u[:, 2 + hf * 2048:2 + hf * 2048 + 2048],
                               scalar=w2, in1=cdst[:, hs], op0=ALU.mult, op1=ALU.add)
                mr = mrow[i % 2]
                for hf in range(2):
                    hs = slice(hf * 2048, (hf + 1) * 2048)
                    self.A("act", "activation", [ca], [ca], out=ca[:, hs], in_=ca[:, hs], func=AF.Silu)
                    self.A("pool" if hf == 0 else "dve", "tensor_tensor", [ca, cg], [mr], out=mr[:, hs], in0=ca[:, hs], in1=cg[:, hs], op=ALU.mult)
                self.store(mr, self.M[i * 128:(i + 1) * 128, :], mr[:])

    def ffn_down(self, l, xout):
        wd = self.tile("wd", [128, 22, DM], BF16)
        stg = [self.tile("dstg%d" % i, [128, 2, DM], F32) for i in range(2)]
        wv = self.w_down[l].rearrange("(k p) n -> p k n", p=128)
        for j in range(11):
            s = stg[j % 2]
            self.load(s, s[:], wv[:, j * 2:(j + 1) * 2, :])
            self.A("pool" if j % 2 == 0 else "dve", "tensor_copy", [s], [wd], out=wd[:, j * 2:(j + 1) * 2, :], in_=s[:])
        mt = [self.tile("dm%d" % i, [128, 22, TC], BF16) for i in range(2)]
        xs = [self.tile("dx%d" % i, [128, 8, TC], F32) for i in range(2)]
        xo = [self.tile("dxo%d" % i, [128, TC], F32) for i in range(3)]
        px = [self.ptile("dpx%d" % i) for i in range(3)]
        mv = self.M.rearrange("(k p) n -> p k n", p=128)
        xv = self.XM.rearrange("(c p) n -> p c n", p=128)
        nx = 0
        for t in range(NT):
            ts_ = slice(t * TC, (t + 1) * TC)
            m = mt[t % 2]
            x = xs[t % 2]
            self.load(m, m[:, 0:11, :], mv[:, 0:11, ts_])
            self.load(m, m[:, 11:22, :], mv[:, 11:22, ts_])
            self.load(x, x[:], xv[:, :, ts_])
            for fo in range(8):
                p = px[nx % 3]
                xo_ = xo[nx % 3]
                nx += 1
                for k in range(22):
                    self.mm(p, p[:], wd, wd[:, k, fo * 128:(fo + 1) * 128], m, m[:, k, :], k == 0, k == 21)
                self.A("dve", "tensor_tensor", [p, x], [xo_], out=xo_[:], in0=p[:], in1=x[:, fo, :], op=ALU.add)
                self.store(xo_, xout[fo * 128:(fo + 1) * 128, ts_], xo_[:])


_CACHE = {}


def _get_nc(n_layers=NL, debug=False):
    key = (n_layers, debug)
    if key not in _CACHE:
        b = Builder(n_layers, debug)
        _CACHE[key] = (b.build(), b)
    return _CACHE[key]


def make_in_maps(inp):
    consts = _host_consts()
    params = _pack_params(inp)
    lam = np.ascontiguousarray(np.asarray(inp["lam"], np.float32).reshape(1, NL * 256))
    rpbE = _pack_rpb(np.asarray(inp["rpb"], np.float32))
    shared = dict(
        w_in=np.ascontiguousarray(inp["w_in"], np.float32),
        w_branch=np.ascontiguousarray(inp["w_branch"], np.float32),
        w_out=np.ascontiguousarray(inp["w_out"], np.float32),
        w_up=np.ascontiguousarray(inp["w_up"], np.float32),
        w_down=np.ascontiguousarray(inp["w_down"], np.float32),
        params=params, lam=lam, rpbE=rpbE,
        tab1=consts["tab1"], tab2=consts["tab2"], mats=consts["mats"], mask3=consts["mask3"],
        colmask=consts["colmask"],
    )
    x = np.asarray(inp["x"], np.float32)
    maps = []
    for b in range(8):
        d = dict(shared)
        d["xT"] = np.ascontiguousarray(x[b].T)
        maps.append(d)
    return maps


def kernel(**inputs):
    inp = {k: np.asarray(v) for k, v in inputs.items()}
    nc, _ = _get_nc()
    maps = make_in_maps(inp)
    res = run_bass_kernel_spmd(nc, maps, core_ids=list(range(8)))
    out = np.stack([np.ascontiguousarray(res.results[b]["yT"].T) for b in range(8)], axis=0)
    return out.astype(np.float32)
```

```python
import numpy as np
from contextlib import ExitStack
import concourse.bass as bass
import concourse.mybir as mybir
from concourse.bass_utils import run_bass_kernel_spmd

F32 = mybir.dt.float32
BF16 = mybir.dt.bfloat16
AF = mybir.ActivationFunctionType
ALU = mybir.AluOpType
AX = mybir.AxisListType

EPOCH = 24000
SAME_SYNC = True

SEQ = 4096
DM = 1024
NL = 4
N_IN = 12544
D_FF = 2816
EPS = 1e-6
NT = 8
TC = 512
COL = dict(aq=0, ak=512, av=1024, bq=1536, bk=2048, bv=2176, cq=2304, ck=3840, cv=5376,
           dq=6912, dk=7424, dv=7936, g=8448)
DIL = (1, 4, 16)


class Buf:
    __slots__ = ("name", "w", "r")

    def __init__(self, name=""):
        self.name = name
        self.w = None
        self.r = []


class _Op:
    __slots__ = ("eng", "fn", "deps", "ldeps", "ddeps", "dma", "pub", "seq")

    def __init__(self, eng, fn, deps, ddeps, dma, ldeps=()):
        self.eng = eng
        self.fn = fn
        self.deps = deps
        self.ldeps = ldeps
        self.ddeps = ddeps
        self.dma = dma
        self.pub = False
        self.seq = None


class Sched:
    ENGS = ("pe", "act", "dve", "pool", "sp")

    def __init__(self, nc, stack):
        self.nc = nc
        self.stack = stack
        self.ops = []
        self.base = 0
        self.cnt = {e: 0 for e in self.ENGS}
        self.sems = {e: [] for e in self.ENGS}
        self.dsem = {}
        self.dcnt = {}
        self.nsem = 0
        self.ninstr = 0

    def _new_sem(self, name):
        self.nsem += 1
        return self.stack.enter_context(self.nc.semaphore(name))

    def _esem(self, e, ep):
        while len(self.sems[e]) <= ep:
            self.sems[e].append(self._new_sem("s_%s_%d" % (e, len(self.sems[e]))))
        return self.sems[e][ep]

    def op(self, eng, fn, reads=(), writes=(), dma=None):
        deps = set()
        ldeps = set()
        ddeps = {}
        base = self.base
        ops = self.ops

        def add(i, dst):
            if i is None or i < base:
                return
            o = ops[i - base]
            if o.dma is not None:
                ddeps[o.dma] = self.dcnt[o.dma]
            else:
                dst.add(i)

        for b in reads:
            add(b.w, deps)
        for b in writes:
            add(b.w, ldeps)
            for r in b.r:
                add(r, ldeps)
        ldeps -= deps
        idx = base + len(ops)
        if dma is not None:
            if dma not in self.dsem:
                self.dsem[dma] = self._new_sem("d_%d" % len(self.dsem))
                self.dcnt[dma] = 0
            self.dcnt[dma] += 16
        ops.append(_Op(eng, fn, deps, ddeps, dma, ldeps))
        for b in reads:
            b.r.append(idx)
        for b in writes:
            b.w = idx
            b.r = []
        return idx

    def flush(self):
        ops = self.ops
        base = self.base
        last = {}
        for i, o in enumerate(ops):
            if o.dma is None and o.fn is not None:
                last[o.eng] = i
        for e in self.ENGS:
            deps = set(base + i for ee, i in last.items() if ee != e)
            ops.append(_Op(e, None, deps, dict(self.dcnt), None))
        import bisect

        def cross(od, o):
            return od.eng != o.eng or o.dma is not None or (SAME_SYNC and o.eng != "pe")

        for o in ops:
            for d in o.deps:
                od = ops[d - base]
                if cross(od, o):
                    od.pub = True
        publ = {e: [] for e in self.ENGS}
        for i, o in enumerate(ops):
            if o.pub:
                publ[o.eng].append(i)
        for i, o in enumerate(ops):
            for d in o.ldeps:
                od = ops[d - base]
                if not cross(od, o):
                    continue
                if od.pub:
                    o.deps.add(d)
                    continue
                lst = publ[od.eng]
                k = bisect.bisect_left(lst, d - base)
                if k < len(lst) and lst[k] < i:
                    o.deps.add(base + lst[k])
                else:
                    od.pub = True
                    bisect.insort(lst, d - base)
                    o.deps.add(d)
        for o in ops:
            if o.pub:
                c = self.cnt[o.eng]
                self.cnt[o.eng] = c + 1
                o.seq = (c // EPOCH, c % EPOCH + 1)
        per = {e: [] for e in self.ENGS}
        seen = {e: {} for e in self.ENGS}
        seend = {e: {} for e in self.ENGS}
        for o in ops:
            F = o.eng
            need = {}
            for d in o.deps:
                od = ops[d - base]
                if od.eng == F and o.dma is None and (F == "pe" or not SAME_SYNC):
                    continue
                if od.seq > need.get(od.eng, (-1, -1)):
                    need[od.eng] = od.seq
            waits = []
            for E, sq in need.items():
                if seen[F].get(E, (-1, -1)) >= sq:
                    continue
                seen[F][E] = sq
                waits.append((self._esem(E, sq[0]), sq[1]))
            for k, v in o.ddeps.items():
                if v == 0 or seend[F].get(k, 0) >= v:
                    continue
                seend[F][k] = v
                waits.append((self.dsem[k], v))
            inc = self._esem(F, o.seq[0]) if o.pub else None
            dinc = self.dsem[o.dma] if o.dma is not None else None
            per[F].append((waits, o.fn, inc, dinc))
            self.ninstr += len(waits) + 1

        def run(eng, lst):
            for waits, fn, inc, dinc in lst:
                for s, v in waits:
                    eng.wait_ge(s, v)
                if fn is None:
                    continue
                ins = fn(eng)
                if inc is not None:
                    ins.then_inc(inc, 1)
                if dinc is not None:
                    ins.then_inc(dinc, 16)

        with self.nc.Block() as block:
            @block.tensor
            def _(e):
                run(e, per["pe"])

            @block.scalar
            def _(e):
                run(e, per["act"])

            @block.vector
            def _(e):
                run(e, per["dve"])

            @block.gpsimd
            def _(e):
                run(e, per["pool"])

            @block.sync
            def _(e):
                run(e, per["sp"])
        self.base = base + len(ops)
        self.ops = []


class Tl:
    __slots__ = ("t", "b", "name", "psum")

    def __init__(self, t, name, psum=False):
        self.t = t
        self.b = Buf(name)
        self.name = name
        self.psum = psum

    def __getitem__(self, k):
        return self.t[k]


def _host_consts():
    c = {}
    pos = np.arange(SEQ, dtype=np.float32)
    p = np.arange(128)
    inv32 = (np.float32(10000.0) ** (-(np.arange(0, 64, 2, dtype=np.float32) / np.float32(64)))).astype(np.float32)
    f = p % 32
    half = (p % 64) // 32
    ang = (pos[None, :] * inv32[f][:, None]).astype(np.float32)
    sgn = np.where(half == 0, -1.0, 1.0).astype(np.float32)[:, None]
    tab1 = np.stack([np.cos(ang), np.sin(ang) * sgn], axis=1).astype(np.float32)
    inv16 = (np.float32(10000.0) ** (-(np.arange(0, 32, 2, dtype=np.float32) / np.float32(32)))).astype(np.float32)
    pp = p % 64
    blk = pp // 32
    q = pp % 32
    half2 = q // 16
    f2 = q % 16
    prow = np.floor(pos / 64).astype(np.float32)
    pcol = (pos - prow * 64).astype(np.float32)
    pf = np.where(blk[:, None] == 0, prow[None, :], pcol[None, :]).astype(np.float32)
    ang2 = (pf * inv16[f2][:, None]).astype(np.float32)
    sgn2 = np.where(half2 == 0, -1.0, 1.0).astype(np.float32)[:, None]
    tab2 = np.stack([np.cos(ang2), np.sin(ang2) * sgn2], axis=1).astype(np.float32)
    c["tab1"] = tab1.reshape(128, 2 * SEQ)
    c["tab2"] = tab2.reshape(128, 2 * SEQ)
    mats = np.zeros((128, 5, 128), np.float32)
    for m in range(128):
        part1 = m + 32 if (m % 64) // 32 == 0 else m - 32
        mats[part1, 0, m] = 1.0
        part2 = m + 16 if (m % 32) // 16 == 0 else m - 16
        mats[part2, 1, m] = 1.0
    mats[:, 2, :] = (p[:, None] // 64 == p[None, :] // 64)
    mats[:, 3, :] = 1.0
    mats[:, 4, :] = np.eye(128)
    c["mats"] = mats.reshape(128, 5 * 128)
    i = np.arange(128)[:, None]
    m = np.arange(128)[None, :]
    mask3 = np.stack([(i - m >= 64), (np.abs(i - m) <= 64), (m - i >= 64)], axis=1).astype(np.float32)
    c["mask3"] = mask3.reshape(128, 3 * 128)
    kc = np.arange(128)[:, None] % 64
    qc = np.arange(64)[None, :]
    ws = np.clip(qc - 8, 0, 48)
    c["colmask"] = ((kc >= ws) & (kc < ws + 16)).astype(np.float32)
    return c


P_G1 = 0
P_G2 = 8
P_QKG = 16
P_SUB = 24
P_CW = 25
P_CB = 25 + 132
P_N = 25 + 132 + 44


def _pack_params(inp):
    out = np.zeros((128, NL, P_N), np.float32)
    for l in range(NL):
        out[:, l, P_G1:P_G1 + 8] = inp["norm1_g"][l].reshape(8, 128).T
        out[:, l, P_G2:P_G2 + 8] = inp["norm2_g"][l].reshape(8, 128).T
        qg = inp["qk_g"][l].reshape(8, 64)
        out[:, l, P_QKG:P_QKG + 8] = np.concatenate([qg, qg], axis=1).T
        out[:, l, P_SUB] = inp["subln_g"][l]
        cw = inp["conv_w"][l].reshape(3, 44, 128)
        out[:, l, P_CW:P_CW + 132] = cw.transpose(2, 0, 1).reshape(128, 132)
        out[:, l, P_CB:P_CB + 44] = inp["conv_b"][l].reshape(44, 128).T
    return out.reshape(128, NL * P_N)


def _pack_rpb(rpb):
    kc = np.arange(64)[:, None]
    qc = np.arange(64)[None, :]
    idx = np.clip(kc - qc + 15, 0, 30)
    g = rpb[:, :, :, idx]
    g = np.transpose(g, (0, 3, 1, 2, 4))
    lo = g
    hi = np.concatenate([g[:, :, :, 1:, :], g[:, :, :, 14:15, :]], axis=3)
    return np.ascontiguousarray(np.concatenate([lo, hi], axis=1)).reshape(NL, 128, 8 * 15 * 64)


class Builder:
    def __init__(self, n_layers=NL, debug=False):
        self.n_layers = n_layers
        self.debug = debug
        self.nc = bass.Bass("TRN2", target_bir_lowering=False)
        nc = self.nc
        di = lambda n, s, dt=F32: nc.dram_tensor(n, s, dt, kind="ExternalInput").ap()
        self.xT = di("xT", [DM, SEQ])
        self.w_in = di("w_in", [NL, DM, N_IN])
        self.w_branch = di("w_branch", [NL, 4, 512, DM])
        self.w_out = di("w_out", [NL, DM, DM])
        self.w_up = di("w_up", [NL, DM, 2 * D_FF])
        self.w_down = di("w_down", [NL, D_FF, DM])
        self.params = di("params", [128, NL * P_N])
        self.lam = di("lam", [1, NL * 256])
        self.rpbE = di("rpbE", [NL, 128, 8 * 15 * 64])
        self.c_tab1 = di("tab1", [128, 2 * SEQ])
        self.c_tab2 = di("tab2", [128, 2 * SEQ])
        self.c_mats = di("mats", [128, 5 * 128])
        self.c_mask3 = di("mask3", [128, 3 * 128])
        self.c_colmask = di("colmask", [128, 64])
        self.yT = nc.dram_tensor("yT", [DM, SEQ], F32, kind="ExternalOutput").ap()
        kind = "ExternalOutput" if debug else "Internal"
        ds = lambda n, s, dt: nc.dram_tensor(n, s, dt, kind=kind).ap()
        self.QK = ds("s_qk", [45 * 128, SEQ], BF16)
        self.VA = ds("s_va", [SEQ, 512], BF16)
        self.VB = ds("s_vb", [SEQ, 130], BF16)
        self.VC = ds("s_vc", [3, SEQ, 520], BF16)
        self.VD = ds("s_vd", [SEQ, 520], BF16)
        self.G = ds("s_g", [4096, SEQ], F32)
        self.OT = ds("s_ot", [4, 512, SEQ], BF16)
        self.M = ds("s_m", [D_FF, SEQ], BF16)
        self.XM = ds("s_xm", [DM, SEQ], F32)
        self.XA = ds("s_xa", [DM, SEQ], F32)
        self.XB = ds("s_xb", [DM, SEQ], F32)

    def tile(self, name, shape, dt):
        t = self.ph.enter_context(self.nc.sbuf_tensor(name + "_%d" % self.uid, shape, dt))
        self.uid += 1
        return Tl(t, name)

    def ptile(self, name, shape=(128, 512), dt=F32):
        t = self.ph.enter_context(self.nc.psum_tensor(name + "_%d" % self.uid, [128, 512], F32))
        self.uid += 1
        return Tl(t, name, True)

    def A(self, eng, meth, reads, writes, *a, **k):
        wr = [x.b for x in writes] + [x.b for x in reads if x.psum]
        self.S.op(eng, lambda e: getattr(e, meth)(*a, **k), [x.b for x in reads], wr)

    def load(self, dst_tl, out_ap, in_ap, q="sp"):
        self.S.op(q, lambda e: e.dma_start(out=out_ap, in_=in_ap), [], [dst_tl.b], dma=("L", dst_tl.name))

    def store(self, src_tl, out_ap, in_ap, q="pool"):
        self.S.op(q, lambda e: e.dma_start(out=out_ap, in_=in_ap), [src_tl.b], [], dma=("S", src_tl.name))

    def mm(self, out_tl, out_ap, l_tl, l_ap, r_tl, r_ap, start, stop):
        self.S.op("pe", lambda e: e.matmul(out_ap, l_ap, r_ap, start=start, stop=stop),
                  [l_tl.b, r_tl.b], [out_tl.b])

    def begin(self):
        self.ph = ExitStack()
        self.ph.__enter__()

    def end(self):
        self.S.flush()
        self.ph.__exit__(None, None, None)

    def build(self):
        nc = self.nc
        self.uid = 0
        with ExitStack() as top:
            self.S = Sched(nc, top)
            self.top = top
            self.ph = top
            self.mats = self.tile("mats", [128, 5, 128], F32)
            self.onesb = self.tile("onesb", [128, 128], BF16)
            self.par = self.tile("par", [128, NL, P_N], F32)
            self.nlam = self.tile("nlam", [128, NL], F32)
            self.begin()
            self.lamt = self.tile("lamt", [1, NL * 256], F32)
            self.load(self.mats, self.mats[:].rearrange("p a b -> p (a b)"), self.c_mats)
            self.load(self.par, self.par[:].rearrange("p a b -> p (a b)"), self.params)
            self.load(self.lamt, self.lamt[:], self.lam)
            self.A("dve", "tensor_copy", [self.mats], [self.onesb], out=self.onesb[:], in_=self.mats[:, 3, :])
            self.lambda_setup()
            self.end()
            xin = self.xT
            for l in range(self.n_layers):
                last = (l == self.n_layers - 1)
                xout = self.yT if last else (self.XA if l % 2 == 0 else self.XB)
                self.layer(l, xin, xout)
                xin = xout
        return nc

    def lambda_setup(self):
        import math
        pr = self.tile("lampr", [1, NL * 2, 64], F32)
        sm = self.tile("lamsm", [1, NL * 2], F32)
        lv = self.tile("lamv", [1, NL], F32)
        ps = self.ptile("lamps", (128, NL))
        lt = self.lamt[:].rearrange("p (l a d) -> p l a d", l=NL, a=4)
        for l in range(NL):
            for j in range(2):
                self.A("dve", "tensor_tensor", [self.lamt], [pr], out=pr[:, l * 2 + j, :], in0=lt[:, l, 2 * j, :],
                       in1=lt[:, l, 2 * j + 1, :], op=ALU.mult)
        self.A("dve", "tensor_reduce", [pr], [sm], out=sm[:], in_=pr[:], axis=AX.X, op=ALU.add)
        self.A("act", "activation", [sm], [sm], out=sm[:], in_=sm[:], func=AF.Exp)
        smv = sm[:].rearrange("p (l j) -> p l j", j=2)
        for l in range(NL):
            li = 0.8 - 0.6 * math.exp(-0.3 * l)
            self.A("dve", "scalar_tensor_tensor", [sm], [lv], out=lv[:, l:l + 1], in0=smv[:, l, 1:2], scalar=-li,
                   in1=smv[:, l, 0:1], op0=ALU.add, op1=ALU.subtract)
        self.mm(ps, ps[:, 0:NL], self.mats, self.mats[0:1, 3, :], lv, lv[:], True, True)
        self.A("dve", "tensor_copy", [ps], [self.nlam], out=self.nlam[:], in_=ps[:, 0:NL])

    def rmsnorm_to_hT(self, l, xsrc, hT, goff):
        xs = [self.tile("nx%d" % i, [128, 8, TC], F32) for i in range(2)]
        sq = [self.tile("nsq%d" % i, [128, 8, TC], F32) for i in range(1)]
        rs = [self.tile("nrs%d" % i, [128, TC], F32) for i in range(2)]
        ps = [self.ptile("nps%d" % i) for i in range(2)]
        xv = xsrc.rearrange("(c p) n -> p c n", p=128)
        for t in range(NT):
            x = xs[t % 2]
            s = sq[0]
            r = rs[t % 2]
            p = ps[t % 2]
            self.load(x, x[:], xv[:, :, t * TC:(t + 1) * TC])
            self.A("pool", "tensor_tensor", [x], [s], out=s[:], in0=x[:], in1=x[:], op=ALU.mult)
            for c in range(8):
                self.mm(p, p[:], self.mats, self.mats[:, 3, :], s, s[:, c, :], c == 0, c == 7)
            self.A("act", "activation", [p], [r], out=r[:], in_=p[:], func=AF.Ln, scale=1.0 / DM, bias=self.epsc[:, 0:1])
            self.A("act", "activation", [r], [r], out=r[:], in_=r[:], func=AF.Exp, scale=-0.5)
            for c in range(8):
                self.A("dve", "scalar_tensor_tensor", [x, r, self.par], [hT], out=hT[:, c, t * TC:(t + 1) * TC],
                       in0=x[:, c, :], scalar=self.par[:, l, goff + c:goff + c + 1], in1=r[:], op0=ALU.mult, op1=ALU.mult)

    def load_weight_bf16(self, dst, dst_ap, src_ap, stg, shape_ap, eng):
        self.load(stg, shape_ap, src_ap)
        self.A(eng, "tensor_copy", [stg], [dst], out=dst_ap, in_=shape_ap)

    def layer(self, l, xin, xout):
        import math
        self.lambda_init = 0.8 - 0.6 * math.exp(-0.3 * l)
        self.begin()
        self.epsc = self.tile("epsc", [128, 1], F32)
        self.A("pool", "memset", [], [self.epsc], self.epsc[:], EPS)
        hT = self.tile("hT", [128, 8, SEQ], BF16)
        sub = ExitStack()
        outer = self.ph
        self.ph = sub
        sub.__enter__()
        self.rmsnorm_to_hT(l, xin, hT, P_G1)
        self.S.flush()
        sub.__exit__(None, None, None)
        self.ph = outer
        self.proj(l, hT)
        self.end()
        self.begin()
        self.epsc = self.tile("epsc", [128, 1], F32)
        self.A("pool", "memset", [], [self.epsc], self.epsc[:], EPS)
        self.attn_a(l)
        self.end()
        self.begin()
        self.attn_b(l)
        self.end()
        self.begin()
        self.attn_c(l)
        self.end()
        self.begin()
        self.attn_d(l)
        self.end()
        self.begin()
        self.merge(l, xin)
        self.end()
        self.begin()
        self.epsc = self.tile("epsc", [128, 1], F32)
        self.A("pool", "memset", [], [self.epsc], self.epsc[:], EPS)
        hT = self.tile("hT2", [128, 8, SEQ], BF16)
        sub = ExitStack()
        outer = self.ph
        self.ph = sub
        sub.__enter__()
        self.rmsnorm_to_hT(l, self.XM, hT, P_G2)
        self.S.flush()
        sub.__exit__(None, None, None)
        self.ph = outer
        self.ffn_up(l, hT)
        self.end()
        self.begin()
        self.ffn_down(l, xout)
        self.end()

    def pipe(self, items, lags):
        n = len(items)
        for j in range(n + max(lags)):
            for si, lg in enumerate(lags):
                i = j - lg
                if 0 <= i < n and len(items[i]) > si and items[i][si] is not None:
                    items[i][si]()

    def proj(self, l, hT):
        wst = self.tile("wst0", [128, 8, 512], F32)
        wbf = [self.tile("wbf%d" % i, [128, 8, 512], BF16) for i in range(2)]
        tab = self.tile("tab", [128, 2, SEQ], F32)
        pacc = [self.ptile("pacc%d" % i) for i in range(3)]
        pss = [self.ptile("pss%d" % i) for i in range(2)]
        prot = [self.ptile("prot%d" % i) for i in range(2)]
        sq = [self.tile("sq%d" % i, [128, TC], F32) for i in range(3)]
        rs = [self.tile("rs%d" % i, [128, TC], F32) for i in range(3)]
        yy = [self.tile("yy%d" % i, [128, TC], F32) for i in range(3)]
        t2 = [self.tile("t2%d" % i, [128, TC], F32) for i in range(3)]
        qo = [self.tile("qo%d" % i, [128, TC], BF16) for i in range(3)]
        rowb = [self.tile("rowb%d" % i, [128, SEQ], BF16) for i in range(2)]
        go = [self.tile("go%d" % i, [128, TC], F32) for i in range(3)]
        vst = [self.tile("vst%d" % i, [128, 4, 520], BF16) for i in range(2)]
        vsta = [self.tile("vsta%d" % i, [128, 4, 512], BF16) for i in range(2)]
        for v in vst:
            self.A("pool", "memset", [], [v], v[:], 1.0)
        wv = self.w_in[l].rearrange("(k p) n -> p k n", p=128)
        cnt = dict(acc=0, q=0, row=0, go=0, vs=0, tmp=0, pp=0)
        G = P_QKG

        def wload(gidx, col0, w):
            wb = wbf[gidx % 2]
            self.load(wst, wst[:, :, 0:w], wv[:, :, col0:col0 + w])
            self.A("pool" if gidx % 2 == 0 else "dve", "tensor_copy", [wst], [wb], out=wb[:, :, 0:w], in_=wst[:, :, 0:w])

        def qk_items(items, wb, off, row, gcol, rope, r):
            mat = 0 if rope == "1d" else 1
            rb = None
            if r > 1:
                rb = rowb[cnt["row"] % 2]
                cnt["row"] += 1
            for t in range(NT):
                pa = pacc[cnt["acc"] % 3]
                cnt["acc"] += 1
                j = cnt["tmp"] % 3
                cnt["tmp"] += 1
                ps_, pr_ = pss[cnt["pp"] % 2], prot[cnt["pp"] % 2]
                cnt["pp"] += 1
                s, rr, y, a2 = sq[j], rs[j], yy[j], t2[j]
                tsl = slice(t * TC, (t + 1) * TC)
                if r > 1:
                    n_ = TC // r
                    dst_tl = rb
                    dst = rb[:].rearrange("p (c i) -> p c i", c=r)[:, :, t * n_:(t + 1) * n_]
                else:
                    dst_tl = qo[cnt["q"] % 3]
                    cnt["q"] += 1
                    dst = dst_tl[:]

                def s0(pa=pa, s=s, tsl=tsl):
                    for k in range(8):
                        self.mm(pa, pa[:], wb, wb[:, k, off:off + 128], hT, hT[:, k, tsl], k == 0, k == 7)
                    self.A("act", "activation", [pa], [s], out=s[:], in_=pa[:], func=AF.Square)

                def s1(pa=pa, s=s, rr=rr, y=y, ps_=ps_, dst_tl=dst_tl, dst=dst):
                    self.mm(ps_, ps_[:], self.mats, self.mats[:, 2, :], s, s[:], True, True)
                    self.A("act", "activation", [ps_], [rr], out=rr[:], in_=ps_[:], func=AF.Ln, scale=1.0 / 64, bias=self.epsc[:, 0:1])
                    self.A("act", "activation", [rr], [rr], out=rr[:], in_=rr[:], func=AF.Exp, scale=-0.5)
                    if rope is None:
                        self.A("dve", "scalar_tensor_tensor", [pa, rr, self.par], [dst_tl], out=dst, in0=pa[:],
                               scalar=self.par[:, l, gcol:gcol + 1], in1=rr[:], op0=ALU.mult, op1=ALU.mult)
                    else:
                        self.A("dve", "scalar_tensor_tensor", [pa, rr, self.par], [y], out=y[:], in0=pa[:],
                               scalar=self.par[:, l, gcol:gcol + 1], in1=rr[:], op0=ALU.mult, op1=ALU.mult)

                def s2(y=y, a2=a2, pr_=pr_, dst_tl=dst_tl, dst=dst, tsl=tsl, t=t):
                    if rope is not None:
                        self.mm(pr_, pr_[:], self.mats, self.mats[:, mat, :], y, y[:], True, True)
                        self.A("dve", "tensor_tensor", [pr_, tab], [a2], out=a2[:], in0=pr_[:], in1=tab[:, 1, tsl], op=ALU.mult)
                        self.A("pool", "tensor_tensor", [y, tab], [y], out=y[:], in0=y[:], in1=tab[:, 0, tsl], op=ALU.mult)
                        if r > 1:
                            src1 = y[:].rearrange("p (i c) -> p c i", c=r)
                            src2 = a2[:].rearrange("p (i c) -> p c i", c=r)
                        else:
                            src1, src2 = y[:], a2[:]
                        self.A("pool", "tensor_tensor", [y, a2], [dst_tl], out=dst, in0=src1, in1=src2, op=ALU.add)
                    if r == 1:
                        self.store(dst_tl, self.QK[row * 128:(row + 1) * 128, tsl], dst_tl[:])
                    elif t == NT - 1:
                        self.store(rb, self.QK[row * 128:(row + 1) * 128, :], rb[:])

                items.append([s0, s1, s2])

        def v_items(items, wb, off, w, dst, r, nh, hd):
            hv = hT[:].rearrange("p k (i c) -> p k c i", c=r)
            L = SEQ // r
            stride = hd + 1 if hd == 64 else hd
            vs = None
            for tt in range(32):
                c = (tt * 128) // L
                i0 = (tt * 128) % L
                pa = pacc[cnt["acc"] % 3]
                cnt["acc"] += 1
                if tt % 4 == 0:
                    vs = (vst if hd == 64 else vsta)[cnt["vs"] % 2]
                    cnt["vs"] += 1

                def s0(pa=pa, c=c, i0=i0):
                    for k in range(8):
                        self.mm(pa, pa[:, 0:w], hT, hv[:, k, c, i0:i0 + 128], wb, wb[:, k, off:off + w], k == 0, k == 7)

                def s1(pa=pa, vs=vs, tt=tt):
                    o = vs[:, tt % 4, 0:nh * stride].rearrange("p (h d) -> p h d", h=nh)[:, :, 0:hd]
                    i_ = pa[:, 0:w].rearrange("p (h d) -> p h d", h=nh)
                    if tt % 2 == 0:
                        self.A("act", "activation", [pa], [vs], out=o, in_=i_, func=AF.Copy)
                    else:
                        self.A("dve", "tensor_copy", [pa], [vs], out=o, in_=i_)
                    if tt % 4 == 3:
                        t0 = (tt - 3) * 128
                        self.store(vs, dst[t0:t0 + 512, :].rearrange("(a p) n -> p a n", p=128), vs[:, :, 0:nh * stride])

                items.append([s0, s1])

        def gate_items(items, wb, off, grow):
            for t in range(NT):
                pa = pacc[cnt["acc"] % 3]
                cnt["acc"] += 1
                g = go[cnt["go"] % 3]
                cnt["go"] += 1
                tsl = slice(t * TC, (t + 1) * TC)

                def s0(pa=pa, tsl=tsl):
                    for k in range(8):
                        self.mm(pa, pa[:], wb, wb[:, k, off:off + 128], hT, hT[:, k, tsl], k == 0, k == 7)

                def s1(pa=pa, g=g, tsl=tsl):
                    self.A("act", "activation", [pa], [g], out=g[:], in_=pa[:], func=AF.Sigmoid)
                    self.store(g, self.G[grow * 128:(grow + 1) * 128, tsl], g[:])

                items.append([s0, s1])

        jobsB = [
            (COL["bq"], 512, [("qk", i * 128, 8 + i, G + 2, "ax", 1) for i in range(4)]),
            (COL["bk"], 256, [("qk", 0, 12, G + 3, "ax", 1), ("v", 128, 128, self.VB, 1, 2, 64)]),
        ]
        jobs = [
            (COL["aq"], 512, [("qk", i * 128, 0 + i, G + 0, "1d", 1) for i in range(4)]),
            (COL["ak"], 512, [("qk", i * 128, 4 + i, G + 1, "1d", 1) for i in range(4)]),
            (COL["av"], 512, [("v", 0, 512, self.VA, 1, 4, 128)]),
        ]
        for g in range(3):
            jobs.append((COL["cq"] + g * 512, 512, [("qk", i * 128, 13 + g * 4 + i, G + 4, "1d", DIL[g]) for i in range(4)]))
            jobs.append((COL["ck"] + g * 512, 512, [("qk", i * 128, 25 + g * 4 + i, G + 5, "1d", DIL[g]) for i in range(4)]))
            jobs.append((COL["cv"] + g * 512, 512, [("v", 0, 512, self.VC[g], DIL[g], 8, 64)]))
        jobs.append((COL["dq"], 512, [("qk", i * 128, 37 + i, G + 6, None, 1) for i in range(4)]))
        jobs.append((COL["dk"], 512, [("qk", i * 128, 41 + i, G + 7, None, 1) for i in range(4)]))
        jobs.append((COL["dv"], 512, [("v", 0, 512, self.VD, 1, 8, 64)]))
        for gi in range(8):
            jobs.append((COL["g"] + gi * 512, 512, [("gate", i * 128, gi * 4 + i) for i in range(4)]))

        gctr = [0]

        def run_jobs(jl):
            items = []
            gid0 = gctr[0]
            for ji, (col0, w, subs) in enumerate(jl):
                gid = gid0 + ji
                wb = wbf[gid % 2]
                first = len(items)
                for sj in subs:
                    if sj[0] == "qk":
                        qk_items(items, wb, *sj[1:])
                    elif sj[0] == "v":
                        v_items(items, wb, *sj[1:])
                    else:
                        gate_items(items, wb, *sj[1:])
                orig = items[first][0]
                nxt = jl[ji + 1] if ji + 1 < len(jl) else None

                def s0w(orig=orig, nxt=nxt, gid=gid):
                    if nxt is not None:
                        wload(gid + 1, nxt[0], nxt[1])
                    orig()
                items[first][0] = s0w
            wload(gid0, jl[0][0], jl[0][1])
            gctr[0] += len(jl)
            self.pipe(items, [0, 1, 2])

        self.load(tab, tab[:].rearrange("p a n -> p (a n)"), self.c_tab2)
        run_jobs(jobsB)
        self.load(tab, tab[:].rearrange("p a n -> p (a n)"), self.c_tab1)
        run_jobs(jobs)

    def normalize_rows65(self, po, res_tl, res_ap, n, tmp_row, tmp_bc, pbc):
        self.A("dve", "reciprocal", [po], [tmp_row], out=tmp_row[64:65, 0:n], in_=po[64:65, 0:n])
        self.mm(pbc, pbc[0:64, 0:n], self.mats, self.mats[64:65, 3, 0:64], tmp_row, tmp_row[64:65, 0:n], True, True)
        self.A("act", "activation", [pbc], [tmp_bc], out=tmp_bc[0:64, 0:n], in_=pbc[0:64, 0:n], func=AF.Copy)
        self.A("dve", "tensor_tensor", [po, tmp_bc], [res_tl], out=res_ap, in0=po[0:64, 0:n], in1=tmp_bc[0:64, 0:n], op=ALU.mult)

    def attn_a(self, l):
        qa = self.tile("qa", [128, 4, SEQ], BF16)
        ka = self.tile("ka", [128, 4, SEQ], BF16)
        va = self.tile("va", [128, 32, 512], BF16)
        for h in range(4):
            self.load(qa, qa[:, h, :], self.QK[(0 + h) * 128:(1 + h) * 128, :])
            self.load(ka, ka[:, h, :], self.QK[(4 + h) * 128:(5 + h) * 128, :])
        for j in range(4):
            self.load(va, va[:, j * 8:(j + 1) * 8, :], self.VA[j * 1024:(j + 1) * 1024, :].rearrange("(a p) n -> p a n", p=128))
        psc = [self.ptile("psc%d" % i) for i in range(3)]
        po = [[self.ptile("po%d%d" % (c, i)) for i in range(2)] for c in range(2)]
        pfin = self.ptile("pfin")
        pt = [self.tile("pt%d" % i, [128, TC], BF16) for i in range(6)]
        sacc = [[self.tile("sacc%d%d" % (c, i), [128, TC], F32) for i in range(2)] for c in range(3)]
        sbf = self.tile("sbf", [128, TC], BF16)
        rc = self.tile("rc", [128, TC], F32)
        res = [self.tile("res%d" % i, [128, TC], F32) for i in range(2)]
        dd = self.tile("dd", [128, TC], F32)
        sq = self.tile("sqa", [128, TC], F32)
        rs = self.tile("rsa", [128, TC], F32)
        ob = [self.tile("oba%d" % i, [128, TC], BF16) for i in range(2)]
        items = []
        n = 0
        gi = 0
        for h in range(4):
            for qc in range(NT):
                qs = slice(qc * TC, (qc + 1) * TC)
                buf = gi % 2
                gi += 1
                for kt in range(32):
                    for c in range(2):
                        ps_ = slice(c * 64, (c + 1) * 64)
                        sc = psc[n % 3]
                        p = pt[n % 6]
                        n += 1
                        po_ = po[c][buf]

                        def s0(sc=sc, p=p, h=h, qs=qs, ps_=ps_, kt=kt):
                            self.mm(sc, sc[:], ka, ka[ps_, h, kt * 128:(kt + 1) * 128], qa, qa[ps_, h, qs], True, True)
                            self.A("act", "activation", [sc], [p], out=p[:], in_=sc[:], func=AF.Exp, scale=0.125)

                        def s1(p=p, h=h, c=c, kt=kt, po_=po_, buf=buf):
                            self.mm(po_, po_[:], va, va[:, kt, h * 128:(h + 1) * 128], p, p[:], kt == 0, kt == 31)
                            if c == 0:
                                a, eng, first = sacc[0][buf], "dve", kt == 0
                            elif kt % 2 == 0:
                                a, eng, first = sacc[1][buf], "dve", kt == 0
                            else:
                                a, eng, first = sacc[2][buf], "pool", kt == 1
                            if first:
                                self.A(eng, "tensor_copy", [p], [a], out=a[:], in_=p[:])
                            else:
                                self.A(eng, "tensor_tensor", [a, p], [a], out=a[:], in0=a[:], in1=p[:], op=ALU.add)

                        def s2(h=h, qc=qc, qs=qs, c=c, kt=kt, po_=po_, buf=buf):
                            if kt != 31:
                                return
                            parts = [sacc[0][buf]] if c == 0 else [sacc[1][buf], sacc[2][buf]]
                            for pi, a in enumerate(parts):
                                self.A("pool", "tensor_copy", [a], [sbf], out=sbf[:], in_=a[:])
                                self.mm(pfin, pfin[:], self.onesb, self.onesb[:], sbf, sbf[:], pi == 0, pi == len(parts) - 1)
                            self.A("dve", "reciprocal", [pfin], [rc], out=rc[:], in_=pfin[:])
                            self.A("dve", "tensor_tensor", [po_, rc], [res[c]], out=res[c][:], in0=po_[:], in1=rc[:], op=ALU.mult)
                            if c != 1:
                                return
                            self.A("dve", "scalar_tensor_tensor", [res[0], res[1], self.nlam], [dd], out=dd[:], in0=res[1][:],
                                   scalar=self.nlam[:, l:l + 1], in1=res[0][:], op0=ALU.mult, op1=ALU.add)
                            self.A("pool", "tensor_tensor", [dd], [sq], out=sq[:], in0=dd[:], in1=dd[:], op=ALU.mult)
                            self.mm(pfin, pfin[:], self.mats, self.mats[:, 3, :], sq, sq[:], True, True)
                            self.A("act", "activation", [pfin], [rs], out=rs[:], in_=pfin[:], func=AF.Sqrt, scale=1.0 / 128, bias=self.epsc[:, 0:1])
                            self.A("dve", "reciprocal", [rs], [rs], out=rs[:], in_=rs[:])
                            self.A("dve", "scalar_tensor_tensor", [dd, rs, self.par], [dd], out=dd[:], in0=dd[:],
                                   scalar=self.par[:, l, P_SUB:P_SUB + 1], in1=rs[:], op0=ALU.mult, op1=ALU.mult)
                            o = ob[(h * NT + qc) % 2]
                            self.A("act", "activation", [dd], [o], out=o[:], in_=dd[:], func=AF.Copy, scale=float(1.0 - self.lambda_init))
                            self.store(o, self.OT[0, h * 128:(h + 1) * 128, qs], o[:])

                        items.append([s0, s1, s2])
        self.pipe(items, [0, 3, 12])

    def attn_b(self, l):
        qb = self.tile("qb", [128, 4, SEQ], BF16)
        kb = self.tile("kb", [128, SEQ], BF16)
        vb = self.tile("vb", [128, 32, 130], BF16)
        for hq in range(8):
            g, s = hq // 4, hq % 4
            self.load(qb, qb[g * 64:(g + 1) * 64, s, :], self.QK[8 * 128 + hq * 64:8 * 128 + (hq + 1) * 64, :])
        self.load(kb, kb[:], self.QK[12 * 128:13 * 128, :])
        self.load(vb, vb[:], self.VB.rearrange("(a p) n -> p a n", p=128))
        psc = [self.ptile("psc%d" % i) for i in range(3)]
        po = [[self.ptile("pob%d%d" % (g, i)) for i in range(2)] for g in range(2)]
        pbc = self.ptile("pbc")
        pt = [self.tile("pt%d" % i, [128, TC], BF16) for i in range(6)]
        trow = self.tile("trow", [128, TC], F32)
        tbc = self.tile("tbc", [128, TC], F32)
        ob = [self.tile("obb%d" % i, [128, TC], BF16) for i in range(2)]
        items = []
        n = 0
        m = 0
        gi = 0
        for s in range(4):
            for qc in range(NT):
                qs = slice(qc * TC, (qc + 1) * TC)
                buf = gi % 2
                gi += 1
                for kt in range(32):
                    for g in range(2):
                        hq = g * 4 + s
                        ps_ = slice(g * 64, (g + 1) * 64)
                        o_ps = po[g][buf]
                        sc = psc[n % 3]
                        p = pt[n % 6]
                        n += 1

                        def s0(sc=sc, p=p, ps_=ps_, s=s, qs=qs, kt=kt):
                            self.mm(sc, sc[:], kb, kb[ps_, kt * 128:(kt + 1) * 128], qb, qb[ps_, s, qs], True, True)
                            self.A("act", "activation", [sc], [p], out=p[:], in_=sc[:], func=AF.Exp, scale=0.125)

                        def s1(p=p, g=g, kt=kt, o_ps=o_ps):
                            self.mm(o_ps, o_ps[0:65, :], vb, vb[:, kt, g * 65:(g + 1) * 65], p, p[:], kt == 0, kt == 31)

                        def s2(g=g, hq=hq, qs=qs, kt=kt, o_ps=o_ps):
                            if kt != 31:
                                return
                            o = ob[hq % 2]
                            self.normalize_rows65(o_ps, o, o[0:64, :], TC, trow, tbc, pbc)
                            self.store(o, self.OT[1, hq * 64:(hq + 1) * 64, qs], o[0:64, :])

                        items.append([s0, s1, s2])
        self.pipe(items, [0, 3, 12])

    def attn_c(self, l):
        mask = self.tile("mask3", [128, 3, 128], F32)
        self.load(mask, mask[:].rearrange("p a b -> p (a b)"), self.c_mask3)
        qc_ = self.tile("qc", [128, 3, SEQ], BF16)
        kc_ = self.tile("kc", [128, 3, SEQ], BF16)
        vc_ = self.tile("vc", [128, 3, 32, 130], BF16)
        acc = [self.tile("acc%d" % i, [65, SEQ], F32) for i in range(2)]
        psc = [self.ptile("pscc%d" % i, (128, 384)) for i in range(3)]
        po = [self.ptile("poc%d" % i) for i in range(2)]
        pbc = self.ptile("pbcc")
        ex = [self.tile("ex%d" % i, [128, 3, 128], F32) for i in range(3)]
        pt = [self.tile("ptc%d" % i, [128, 3, 128], BF16) for i in range(4)]
        trow = self.tile("trowc", [128, TC], F32)
        ob = [self.tile("obc%d" % i, [128, TC], BF16) for i in range(2)]
        n = 0
        m = 0
        for jp in range(4):
            for g in range(3):
                self.load(qc_, qc_[:, g, :], self.QK[(13 + g * 4 + jp) * 128:(14 + g * 4 + jp) * 128, :])
                self.load(kc_, kc_[:, g, :], self.QK[(25 + g * 4 + jp) * 128:(26 + g * 4 + jp) * 128, :])
                self.load(vc_, vc_[:, g, :, :], self.VC[g].rearrange("(a p) n -> p a n", p=128)[:, :, jp * 130:(jp + 1) * 130])
            items = []
            for hh in range(2):
                ps_ = slice(hh * 64, (hh + 1) * 64)
                ac = acc[hh]
                hd = jp * 2 + hh
                for g in range(3):
                    r = DIL[g]
                    L = SEQ // r
                    tps = L // 128
                    for qb4 in range(8):
                        o_ps = po[m % 2]
                        m += 1
                        for u in range(4):
                            qb = qb4 * 4 + u
                            seg = qb // tps
                            kts = [k for k in (qb - 1, qb, qb + 1) if k // tps == seg and 0 <= k < 32]
                            j0 = kts[0] - (qb - 1)
                            nk = len(kts)
                            sc = psc[n % 3]
                            e_ = ex[n % 3]
                            p = pt[n % 4]
                            n += 1

                            def s0(sc=sc, e_=e_, p=p, kts=kts, j0=j0, nk=nk, qb=qb, g=g, ps_=ps_):
                                for k in kts:
                                    j = k - (qb - 1)
                                    self.mm(sc, sc[:, j * 128:(j + 1) * 128], kc_, kc_[ps_, g, k * 128:(k + 1) * 128],
                                            qc_, qc_[ps_, g, qb * 128:(qb + 1) * 128], True, True)
                                scv = sc[:, 0:384].rearrange("p (a b) -> p a b", a=3)
                                self.A("act", "activation", [sc], [e_], out=e_[:, j0:j0 + nk, :], in_=scv[:, j0:j0 + nk, :], func=AF.Exp, scale=0.125)
                                self.A("pool", "tensor_tensor", [e_, mask], [p], out=p[:, j0:j0 + nk, :], in0=e_[:, j0:j0 + nk, :],
                                       in1=mask[:, j0:j0 + nk, :], op=ALU.mult)

                            def s1(p=p, kts=kts, nk=nk, qb=qb, g=g, hh=hh, u=u, o_ps=o_ps, qb4=qb4, r=r, L=L, ac=ac, hd=hd):
                                for ki, k in enumerate(kts):
                                    j = k - (qb - 1)
                                    self.mm(o_ps, o_ps[0:65, u * 128:(u + 1) * 128], vc_, vc_[:, g, k, hh * 65:(hh + 1) * 65],
                                            p, p[:, j, :], ki == 0, ki == nk - 1)
                                if u != 3:
                                    return
                                pos0 = qb4 * 512
                                av = ac[:].rearrange("p (i c) -> p c i", c=r)
                                if L >= 512:
                                    c0, i0 = pos0 // L, pos0 % L
                                    dst = av[0:65, c0:c0 + 1, i0:i0 + 512]
                                    src = o_ps[0:65, :].rearrange("p (c i) -> p c i", c=1)
                                else:
                                    ncl = 512 // L
                                    c0 = pos0 // L
                                    dst = av[0:65, c0:c0 + ncl, :]
                                    src = o_ps[0:65, :].rearrange("p (c i) -> p c i", c=ncl)
                                if g == 0:
                                    self.A("act", "activation", [o_ps], [ac], out=dst, in_=src, func=AF.Copy)
                                else:
                                    self.A("dve", "tensor_tensor", [o_ps, ac], [ac], out=dst, in0=dst, in1=src, op=ALU.add)
                                if g == 2 and qb4 == 7:
                                    for t in range(NT):
                                        o = ob[(hd * NT + t) % 2]
                                        ts_ = slice(t * TC, (t + 1) * TC)
                                        self.A("dve", "reciprocal", [ac], [trow], out=trow[64:65, :], in_=ac[64:65, ts_])
                                        self.mm(pbc, pbc[0:64, :], self.mats, self.mats[64:65, 3, 0:64], trow, trow[64:65, :], True, True)
                                        self.A("dve", "tensor_tensor", [pbc, ac], [o], out=o[0:64, :], in0=pbc[0:64, :], in1=ac[0:64, ts_], op=ALU.mult)
                                        self.store(o, self.OT[2, hd * 64:(hd + 1) * 64, ts_], o[0:64, :])

                            items.append([s0, s1])
            self.pipe(items, [0, 2])

    def attn_d(self, l):
        qd = self.tile("qd", [128, 4, SEQ], BF16)
        kd = self.tile("kd", [128, 4, SEQ], BF16)
        ve = self.tile("ve", [128, 32, 520], BF16)
        vo = self.tile("vo", [128, 31, 520], BF16)
        E = self.tile("E", [128, 8, 15, 64], F32)
        cm = self.tile("cm", [128, 64], F32)
        self.load(cm, cm[:], self.c_colmask)
        for i in range(4):
            self.load(qd, qd[:, i, :], self.QK[(37 + i) * 128:(38 + i) * 128, :])
            self.load(kd, kd[:, i, :], self.QK[(41 + i) * 128:(42 + i) * 128, :])
        for j in range(4):
            self.load(ve, ve[:, j * 8:(j + 1) * 8, :], self.VD[j * 1024:(j + 1) * 1024, :].rearrange("(a p) n -> p a n", p=128))
        for j in range(4):
            na = 8 if j < 3 else 7
            self.load(vo, vo[:, j * 8:j * 8 + na, :], self.VD[64 + j * 1024:64 + j * 1024 + na * 128, :].rearrange("(a p) n -> p a n", p=128))
        for h in range(8):
            self.load(E, E[:, h, :, :].rearrange("p a b -> p (a b)"), self.rpbE[l][:, h * 960:(h + 1) * 960])
        for h in range(8):
            self.A("act", "activation", [E], [E], out=E[:, h, :, :], in_=E[:, h, :, :], func=AF.Exp)
            self.A("pool", "tensor_tensor", [E, cm], [E], out=E[:, h, :, :], in0=E[:, h, :, :],
                   in1=cm[:].rearrange("p (a b) -> p a b", a=1).to_broadcast([128, 15, 64]), op=ALU.mult)
        psc = [self.ptile("pscd%d" % i, (128, 256)) for i in range(3)]
        po = [self.ptile("pod%d" % i) for i in range(2)]
        pbc = self.ptile("pbcd")
        ex = [self.tile("exd%d" % i, [128, 4, 64], F32) for i in range(3)]
        pt = [self.tile("ptd%d" % i, [128, 4, 64], BF16) for i in range(4)]
        trow = self.tile("trowd", [128, TC], F32)
        tbc = self.tile("tbcd", [128, TC], F32)
        ob = [self.tile("obd%d" % i, [128, TC], BF16) for i in range(2)]
        items = []
        n = 0
        m = 0
        for h in range(8):
            ch, hh = h // 2, h % 2
            ps_ = slice(hh * 64, (hh + 1) * 64)
            for r8 in range(8):
                o_ps = po[m % 2]
                o = ob[m % 2]
                m += 1
                for u in range(8):
                    r = r8 * 8 + u
                    rs_ = min(max(r - 4, 0), 56)
                    base = rs_ - r + 7
                    sc = psc[n % 3]
                    e_ = ex[n % 3]
                    p = pt[n % 4]
                    n += 1

                    def s0(sc=sc, e_=e_, p=p, r=r, rs_=rs_, base=base, h=h, ch=ch, ps_=ps_, n=n):
                        for i in range(4):
                            k0 = (rs_ + 2 * i) * 64
                            self.mm(sc, sc[:, i * 64:(i + 1) * 64], kd, kd[ps_, ch, k0:k0 + 128], qd, qd[ps_, ch, r * 64:(r + 1) * 64], True, True)
                        self.A("act", "activation", [sc], [e_], out=e_[:], in_=sc[:, 0:256].rearrange("p (a b) -> p a b", a=4), func=AF.Exp, scale=0.125)
                        ev = E[:, h, base:base + 7:2, :]
                        self.A("pool" if n % 2 == 0 else "dve", "tensor_tensor", [e_, E], [p], out=p[:], in0=e_[:], in1=ev, op=ALU.mult)

                    def s1(p=p, rs_=rs_, h=h, u=u, o_ps=o_ps, o=o, r8=r8):
                        for i in range(4):
                            row0 = rs_ + 2 * i
                            if row0 % 2 == 0:
                                vt, vi = ve, row0 // 2
                            else:
                                vt, vi = vo, (row0 - 1) // 2
                            self.mm(o_ps, o_ps[0:65, u * 64:(u + 1) * 64], vt, vt[:, vi, h * 65:(h + 1) * 65], p, p[:, i, :], i == 0, i == 3)
                        if u != 7:
                            return
                        self.normalize_rows65(o_ps, o, o[0:64, :], TC, trow, tbc, pbc)
                        self.store(o, self.OT[3, h * 64:(h + 1) * 64, r8 * TC:(r8 + 1) * TC], o[0:64, :])

                    items.append([s0, s1])
        self.pipe(items, [0, 2])

    def merge(self, l, xin):
        wb = self.tile("wbr", [128, 16, DM], BF16)
        wo = self.tile("wo", [128, 8, DM], BF16)
        stg = [self.tile("mstg%d" % i, [128, 4, DM], F32) for i in range(2)]
        n = 0
        for i in range(4):
            s = stg[n % 2]
            n += 1
            self.load(s, s[:], self.w_branch[l, i].rearrange("(k p) n -> p k n", p=128))
            self.A("pool" if n % 2 == 0 else "dve", "tensor_copy", [s], [wb], out=wb[:, i * 4:(i + 1) * 4, :], in_=s[:])
        for j in range(2):
            s = stg[n % 2]
            n += 1
            self.load(s, s[:], self.w_out[l][j * 512:(j + 1) * 512, :].rearrange("(k p) n -> p k n", p=128))
            self.A("pool" if n % 2 == 0 else "dve", "tensor_copy", [s], [wo], out=wo[:, j * 4:(j + 1) * 4, :], in_=s[:])
        ot = [self.tile("mot%d" % i, [128, 16, TC], BF16) for i in range(2)]
        xs = [self.tile("mx%d" % i, [128, 8, TC], F32) for i in range(2)]
        gt = [self.tile("mg%d" % i, [128, TC], F32) for i in range(4)]
        mt = [self.tile("mm%d" % i, [128, 8, TC], BF16) for i in range(2)]
        acc = [self.tile("macc%d" % i, [128, TC], F32) for i in range(2)]
        tmp = [self.tile("mtmp%d" % i, [128, TC], F32) for i in range(2)]
        py = [self.ptile("py%d" % i) for i in range(3)]
        px = [self.ptile("px%d" % i) for i in range(2)]
        xo = [self.tile("mxo%d" % i, [128, TC], F32) for i in range(3)]
        xv = xin.rearrange("(c p) n -> p c n", p=128)
        otv = self.OT.rearrange("b (k p) n -> p (b k) n", p=128)
        ng = 0
        ny = 0
        nx = 0
        for t in range(NT):
            ts_ = slice(t * TC, (t + 1) * TC)
            o = ot[t % 2]
            x = xs[t % 2]
            mtt = mt[t % 2]
            for i in range(4):
                self.load(o, o[:, i * 4:(i + 1) * 4, :], otv[:, i * 4:(i + 1) * 4, ts_])
            self.load(x, x[:], xv[:, :, ts_])
            for f in range(8):
                a = acc[f % 2]
                for i in range(4):
                    g = gt[ng % 4]
                    ng += 1
                    self.load(g, g[:], self.G[(i * 8 + f) * 128:(i * 8 + f + 1) * 128, ts_])
                    p = py[ny % 3]
                    ny += 1
                    for k in range(4):
                        self.mm(p, p[:], wb, wb[:, i * 4 + k, f * 128:(f + 1) * 128], o, o[:, i * 4 + k, :], k == 0, k == 3)
                    if i == 0:
                        self.A("dve", "tensor_tensor", [p, g], [a], out=a[:], in0=p[:], in1=g[:], op=ALU.mult)
                    else:
                        tm = tmp[i % 2]
                        self.A("dve", "tensor_tensor", [p, g], [tm], out=tm[:], in0=p[:], in1=g[:], op=ALU.mult)
                        if i < 3:
                            self.A("pool", "tensor_tensor", [a, tm], [a], out=a[:], in0=a[:], in1=tm[:], op=ALU.add)
                        else:
                            self.A("pool", "tensor_tensor", [a, tm], [mtt], out=mtt[:, f, :], in0=a[:], in1=tm[:], op=ALU.add)
            for fo in range(8):
                p = px[nx % 2]
                xo_ = xo[nx % 3]
                nx += 1
                for k in range(8):
                    self.mm(p, p[:], wo, wo[:, k, fo * 128:(fo + 1) * 128], mtt, mtt[:, k, :], k == 0, k == 7)
                self.A("dve", "tensor_tensor", [p, x], [xo_], out=xo_[:], in0=p[:], in1=x[:, fo, :], op=ALU.add)
                self.store(xo_, self.XM[fo * 128:(fo + 1) * 128, ts_], xo_[:])

    def ffn_up(self, l, hT):
        wst = [self.tile("fst%d" % i, [128, 8, 256], F32) for i in range(2)]
        wbf = [self.tile("fbf%d" % i, [128, 8, 256], BF16) for i in range(4)]
        ua = self.tile("ua", [128, SEQ + 2], F32)
        ug = self.tile("ug", [128, SEQ + 2], F32)
        ca = self.tile("ca", [128, SEQ], F32)
        cg = self.tile("cg", [128, SEQ], F32)
        mrow = [self.tile("mrow%d" % i, [128, SEQ], BF16) for i in range(2)]
        pacc = [self.ptile("fpa%d" % i) for i in range(4)]
        for u in (ua, ug):
            self.A("pool", "memset", [], [u], u[:, 0:1], 0.0)
            self.A("pool", "memset", [], [u], u[:, SEQ + 1:SEQ + 2], 0.0)
        wv = self.w_up[l].rearrange("(k p) n -> p k n", p=128)
        ngrp = 0
        nacc = 0
        for j in range(11):
            w = 256
            wbs = []
            for part in range(2):
                st = wst[ngrp % 2]
                wb = wbf[ngrp % 4]
                ngrp += 1
                c0 = part * D_FF + j * 256
                self.load(st, st[:, :, 0:w], wv[:, :, c0:c0 + w])
                self.A("pool" if part == 0 else "dve", "tensor_copy", [st], [wb], out=wb[:, :, 0:w], in_=st[:, :, 0:w])
                wbs.append(wb)
            for ii in range(w // 128):
                i = j * 2 + ii
                for part, (u, cdst) in enumerate(((ua, ca), (ug, cg))):
                    wb = wbs[part]
                    for t in range(NT):
                        p = pacc[nacc % 4]
                        nacc += 1
                        for k in range(8):
                            self.mm(p, p[:], wb, wb[:, k, ii * 128:(ii + 1) * 128], hT, hT[:, k, t * TC:(t + 1) * TC], k == 0, k == 7)
                        if t % 2 == 0:
                            self.A("act", "activation", [p], [u], out=u[:, 1 + t * TC:1 + (t + 1) * TC], in_=p[:], func=AF.Copy)
                        else:
                            self.A("dve", "tensor_copy", [p], [u], out=u[:, 1 + t * TC:1 + (t + 1) * TC], in_=p[:])
                    ch = part * 22 + i
                    w0 = self.par[:, l, P_CW + 0 * 44 + ch:P_CW + 0 * 44 + ch + 1]
                    w1 = self.par[:, l, P_CW + 1 * 44 + ch:P_CW + 1 * 44 + ch + 1]
                    w2 = self.par[:, l, P_CW + 2 * 44 + ch:P_CW + 2 * 44 + ch + 1]
                    bb = self.par[:, l, P_CB + ch:P_CB + ch + 1]
                    eng = "dve"
                    for hf in range(2):
                        hs = slice(hf * 2048, (hf + 1) * 2048)
                        self.A(eng, "tensor_scalar", [u, self.par], [cdst], out=cdst[:, hs], in0=u[:, hf * 2048:hf * 2048 + 2048],
                               scalar1=w0, scalar2=bb, op0=ALU.mult, op1=ALU.add)
                        self.A(eng, "scalar_tensor_tensor", [u, cdst, self.par], [cdst], out=cdst[:, hs], in0=u[:, 1 + hf * 2048:1 + hf * 2048 + 2048],
                               scalar=w1, in1=cdst[:, hs], op0=ALU.mult, op1=ALU.add)
                        self.A(eng, "scalar_tensor_tensor", [u, cdst, self.par], [cdst], out=cdst[:, hs], in0=u[:, 2 + hf * 2048:2 + hf * 2048 + 2048],
                               scalar=w2, in1=cdst[:, hs], op0=ALU.mult, op1=ALU.add)
                mr = mrow[i % 2]
                for hf in range(2):
                    hs = slice(hf * 2048, (hf + 1) * 2048)
                    self.A("act", "activation", [ca], [ca], out=ca[:, hs], in_=ca[:, hs], func=AF.Silu)
                    self.A("pool" if hf == 0 else "dve", "tensor_tensor", [ca, cg], [mr], out=mr[:, hs], in0=ca[:, hs], in1=cg[:, hs], op=ALU.mult)
                self.store(mr, self.M[i * 128:(i + 1) * 128, :], mr[:])

    def ffn_down(self, l, xout):
        wd = self.tile("wd", [128, 22, DM], BF16)
        stg = [self.tile("dstg%d" % i, [128, 2, DM], F32) for i in range(2)]
        wv = self.w_down[l].rearrange("(k p) n -> p k n", p=128)
        for j in range(11):
            s = stg[j % 2]
            self.load(s, s[:], wv[:, j * 2:(j + 1) * 2, :])
            self.A("pool" if j % 2 == 0 else "dve", "tensor_copy", [s], [wd], out=wd[:, j * 2:(j + 1) * 2, :], in_=s[:])
        mt = [self.tile("dm%d" % i, [128, 22, TC], BF16) for i in range(2)]
        xs = [self.tile("dx%d" % i, [128, 8, TC], F32) for i in range(2)]
        xo = [self.tile("dxo%d" % i, [128, TC], F32) for i in range(3)]
        px = [self.ptile("dpx%d" % i) for i in range(3)]
        mv = self.M.rearrange("(k p) n -> p k n", p=128)
        xv = self.XM.rearrange("(c p) n -> p c n", p=128)
        nx = 0
        for t in range(NT):
            ts_ = slice(t * TC, (t + 1) * TC)
            m = mt[t % 2]
            x = xs[t % 2]
            self.load(m, m[:, 0:11, :], mv[:, 0:11, ts_])
            self.load(m, m[:, 11:22, :], mv[:, 11:22, ts_])
            self.load(x, x[:], xv[:, :, ts_])
            for fo in range(8):
                p = px[nx % 3]
                xo_ = xo[nx % 3]
                nx += 1
                for k in range(22):
                    self.mm(p, p[:], wd, wd[:, k, fo * 128:(fo + 1) * 128], m, m[:, k, :], k == 0, k == 21)
                self.A("dve", "tensor_tensor", [p, x], [xo_], out=xo_[:], in0=p[:], in1=x[:, fo, :], op=ALU.add)
                self.store(xo_, xout[fo * 128:(fo + 1) * 128, ts_], xo_[:])


_CACHE = {}


def _get_nc(n_layers=NL, debug=False):
    key = (n_layers, debug)
    if key not in _CACHE:
        b = Builder(n_layers, debug)
        _CACHE[key] = (b.build(), b)
    return _CACHE[key]


def make_in_maps(inp):
    consts = _host_consts()
    params = _pack_params(inp)
    lam = np.ascontiguousarray(np.asarray(inp["lam"], np.float32).reshape(1, NL * 256))
    rpbE = _pack_rpb(np.asarray(inp["rpb"], np.float32))
    shared = dict(
        w_in=np.ascontiguousarray(inp["w_in"], np.float32),
        w_branch=np.ascontiguousarray(inp["w_branch"], np.float32),
        w_out=np.ascontiguousarray(inp["w_out"], np.float32),
        w_up=np.ascontiguousarray(inp["w_up"], np.float32),
        w_down=np.ascontiguousarray(inp["w_down"], np.float32),
        params=params, lam=lam, rpbE=rpbE,
        tab1=consts["tab1"], tab2=consts["tab2"], mats=consts["mats"], mask3=consts["mask3"],
        colmask=consts["colmask"],
    )
    x = np.asarray(inp["x"], np.float32)
    maps = []
    for b in range(8):
        d = dict(shared)
        d["xT"] = np.ascontiguousarray(x[b].T)
        maps.append(d)
    return maps


def kernel(**inputs):
    inp = {k: np.asarray(v) for k, v in inputs.items()}
    nc, _ = _get_nc()
    maps = make_in_maps(inp)
    res = run_bass_kernel_spmd(nc, maps, core_ids=list(range(8)))
    out = np.stack([np.ascontiguousarray(res.results[b]["yT"].T) for b in range(8)], axis=0)
    return out.astype(np.float32)
```

```python
import numpy as np
from contextlib import ExitStack
import concourse.bass as bass
import concourse.mybir as mybir
from concourse.bass_utils import run_bass_kernel_spmd

F32 = mybir.dt.float32
BF16 = mybir.dt.bfloat16
AF = mybir.ActivationFunctionType
ALU = mybir.AluOpType
AX = mybir.AxisListType

EPOCH = 24000
SAME_SYNC = True

SEQ = 4096
DM = 1024
NL = 4
N_IN = 12544
D_FF = 2816
EPS = 1e-6
NT = 8
TC = 512
COL = dict(aq=0, ak=512, av=1024, bq=1536, bk=2048, bv=2176, cq=2304, ck=3840, cv=5376,
           dq=6912, dk=7424, dv=7936, g=8448)
DIL = (1, 4, 16)


class Buf:
    __slots__ = ("name", "w", "r")

    def __init__(self, name=""):
        self.name = name
        self.w = None
        self.r = []


class _Op:
    __slots__ = ("eng", "fn", "deps", "ldeps", "ddeps", "dma", "pub", "seq")

    def __init__(self, eng, fn, deps, ddeps, dma, ldeps=()):
        self.eng = eng
        self.fn = fn
        self.deps = deps
        self.ldeps = ldeps
        self.ddeps = ddeps
        self.dma = dma
        self.pub = False
        self.seq = None


class Sched:
    ENGS = ("pe", "act", "dve", "pool", "sp")

    def __init__(self, nc, stack):
        self.nc = nc
        self.stack = stack
        self.ops = []
        self.base = 0
        self.cnt = {e: 0 for e in self.ENGS}
        self.sems = {e: [] for e in self.ENGS}
        self.dsem = {}
        self.dcnt = {}
        self.nsem = 0
        self.ninstr = 0

    def _new_sem(self, name):
        self.nsem += 1
        return self.stack.enter_context(self.nc.semaphore(name))

    def _esem(self, e, ep):
        while len(self.sems[e]) <= ep:
            self.sems[e].append(self._new_sem("s_%s_%d" % (e, len(self.sems[e]))))
        return self.sems[e][ep]

    def op(self, eng, fn, reads=(), writes=(), dma=None):
        deps = set()
        ldeps = set()
        ddeps = {}
        base = self.base
        ops = self.ops

        def add(i, dst):
            if i is None or i < base:
                return
            o = ops[i - base]
            if o.dma is not None:
                ddeps[o.dma] = self.dcnt[o.dma]
            else:
                dst.add(i)

        for b in reads:
            add(b.w, deps)
        for b in writes:
            add(b.w, ldeps)
            for r in b.r:
                add(r, ldeps)
        ldeps -= deps
        idx = base + len(ops)
        if dma is not None:
            if dma not in self.dsem:
                self.dsem[dma] = self._new_sem("d_%d" % len(self.dsem))
                self.dcnt[dma] = 0
            self.dcnt[dma] += 16
        ops.append(_Op(eng, fn, deps, ddeps, dma, ldeps))
        for b in reads:
            b.r.append(idx)
        for b in writes:
            b.w = idx
            b.r = []
        return idx

    def flush(self):
        ops = self.ops
        base = self.base
        last = {}
        for i, o in enumerate(ops):
            if o.dma is None and o.fn is not None:
                last[o.eng] = i
        for e in self.ENGS:
            deps = set(base + i for ee, i in last.items() if ee != e)
            ops.append(_Op(e, None, deps, dict(self.dcnt), None))
        import bisect

        def cross(od, o):
            return od.eng != o.eng or o.dma is not None or (SAME_SYNC and o.eng != "pe")

        for o in ops:
            for d in o.deps:
                od = ops[d - base]
                if cross(od, o):
                    od.pub = True
        publ = {e: [] for e in self.ENGS}
        for i, o in enumerate(ops):
            if o.pub:
                publ[o.eng].append(i)
        for i, o in enumerate(ops):
            for d in o.ldeps:
                od = ops[d - base]
                if not cross(od, o):
                    continue
                if od.pub:
                    o.deps.add(d)
                    continue
                lst = publ[od.eng]
                k = bisect.bisect_left(lst, d - base)
                if k < len(lst) and lst[k] < i:
                    o.deps.add(base + lst[k])
                else:
                    od.pub = True
                    bisect.insort(lst, d - base)
                    o.deps.add(d)
        for o in ops:
            if o.pub:
                c = self.cnt[o.eng]
                self.cnt[o.eng] = c + 1
                o.seq = (c // EPOCH, c % EPOCH + 1)
        per = {e: [] for e in self.ENGS}
        seen = {e: {} for e in self.ENGS}
        seend = {e: {} for e in self.ENGS}
        for o in ops:
            F = o.eng
            need = {}
            for d in o.deps:
                od = ops[d - base]
                if od.eng == F and o.dma is None and (F == "pe" or not SAME_SYNC):
                    continue
                if od.seq > need.get(od.eng, (-1, -1)):
                    need[od.eng] = od.seq
            waits = []
            for E, sq in need.items():
                if seen[F].get(E, (-1, -1)) >= sq:
                    continue
                seen[F][E] = sq
                waits.append((self._esem(E, sq[0]), sq[1]))
            for k, v in o.ddeps.items():
                if v == 0 or seend[F].get(k, 0) >= v:
                    continue
                seend[F][k] = v
                waits.append((self.dsem[k], v))
            inc = self._esem(F, o.seq[0]) if o.pub else None
            dinc = self.dsem[o.dma] if o.dma is not None else None
            per[F].append((waits, o.fn, inc, dinc))
            self.ninstr += len(waits) + 1

        def run(eng, lst):
            for waits, fn, inc, dinc in lst:
                emb = None
                if fn is not None and dinc is None and waits:
                    emb = waits[-1]
                    waits = waits[:-1]
                for s, v in waits:
                    eng.wait_ge(s, v)
                if fn is None:
                    continue
                ins = fn(eng)
                if emb is not None:
                    ins._wait_ge(emb[0], emb[1])
                if inc is not None:
                    ins.then_inc(inc, 1)
                if dinc is not None:
                    ins.then_inc(dinc, 16)

        with self.nc.Block() as block:
            @block.tensor
            def _(e):
                run(e, per["pe"])

            @block.scalar
            def _(e):
                run(e, per["act"])

            @block.vector
            def _(e):
                run(e, per["dve"])

            @block.gpsimd
            def _(e):
                run(e, per["pool"])

            @block.sync
            def _(e):
                run(e, per["sp"])
        self.base = base + len(ops)
        self.ops = []


class Tl:
    __slots__ = ("t", "b", "name", "psum")

    def __init__(self, t, name, psum=False):
        self.t = t
        self.b = Buf(name)
        self.name = name
        self.psum = psum

    def __getitem__(self, k):
        return self.t[k]


def _host_consts():
    c = {}
    pos = np.arange(SEQ, dtype=np.float32)
    p = np.arange(128)
    inv32 = (np.float32(10000.0) ** (-(np.arange(0, 64, 2, dtype=np.float32) / np.float32(64)))).astype(np.float32)
    f = p % 32
    half = (p % 64) // 32
    ang = (pos[None, :] * inv32[f][:, None]).astype(np.float32)
    sgn = np.where(half == 0, -1.0, 1.0).astype(np.float32)[:, None]
    tab1 = np.stack([np.cos(ang), np.sin(ang) * sgn], axis=1).astype(np.float32)
    inv16 = (np.float32(10000.0) ** (-(np.arange(0, 32, 2, dtype=np.float32) / np.float32(32)))).astype(np.float32)
    pp = p % 64
    blk = pp // 32
    q = pp % 32
    half2 = q // 16
    f2 = q % 16
    prow = np.floor(pos / 64).astype(np.float32)
    pcol = (pos - prow * 64).astype(np.float32)
    pf = np.where(blk[:, None] == 0, prow[None, :], pcol[None, :]).astype(np.float32)
    ang2 = (pf * inv16[f2][:, None]).astype(np.float32)
    sgn2 = np.where(half2 == 0, -1.0, 1.0).astype(np.float32)[:, None]
    tab2 = np.stack([np.cos(ang2), np.sin(ang2) * sgn2], axis=1).astype(np.float32)
    c["tab1"] = tab1.reshape(128, 2 * SEQ)
    c["tab2"] = tab2.reshape(128, 2 * SEQ)
    mats = np.zeros((128, 5, 128), np.float32)
    for m in range(128):
        part1 = m + 32 if (m % 64) // 32 == 0 else m - 32
        mats[part1, 0, m] = 1.0
        part2 = m + 16 if (m % 32) // 16 == 0 else m - 16
        mats[part2, 1, m] = 1.0
    mats[:, 2, :] = (p[:, None] // 64 == p[None, :] // 64)
    mats[:, 3, :] = 1.0
    mats[:, 4, :] = np.eye(128)
    c["mats"] = mats.reshape(128, 5 * 128)
    i = np.arange(128)[:, None]
    m = np.arange(128)[None, :]
    mask3 = np.stack([(i - m >= 64), (np.abs(i - m) <= 64), (m - i >= 64)], axis=1).astype(np.float32)
    c["mask3"] = mask3.reshape(128, 3 * 128)
    kc = np.arange(128)[:, None] % 64
    qc = np.arange(64)[None, :]
    ws = np.clip(qc - 8, 0, 48)
    c["colmask"] = ((kc >= ws) & (kc < ws + 16)).astype(np.float32)
    return c


P_G1 = 0
P_G2 = 8
P_QKG = 16
P_SUB = 24
P_CW = 25
P_CB = 25 + 132
P_N = 25 + 132 + 44


def _pack_params(inp):
    out = np.zeros((128, NL, P_N), np.float32)
    for l in range(NL):
        out[:, l, P_G1:P_G1 + 8] = inp["norm1_g"][l].reshape(8, 128).T
        out[:, l, P_G2:P_G2 + 8] = inp["norm2_g"][l].reshape(8, 128).T
        qg = inp["qk_g"][l].reshape(8, 64)
        out[:, l, P_QKG:P_QKG + 8] = np.concatenate([qg, qg], axis=1).T
        out[:, l, P_SUB] = inp["subln_g"][l]
        cw = inp["conv_w"][l].reshape(3, 44, 128)
        out[:, l, P_CW:P_CW + 132] = cw.transpose(2, 0, 1).reshape(128, 132)
        out[:, l, P_CB:P_CB + 44] = inp["conv_b"][l].reshape(44, 128).T
    return out.reshape(128, NL * P_N)


def _pack_rpb(rpb):
    kc = np.arange(64)[:, None]
    qc = np.arange(64)[None, :]
    idx = np.clip(kc - qc + 15, 0, 30)
    g = rpb[:, :, :, idx]
    g = np.transpose(g, (0, 3, 1, 2, 4))
    lo = g
    hi = np.concatenate([g[:, :, :, 1:, :], g[:, :, :, 14:15, :]], axis=3)
    return np.ascontiguousarray(np.concatenate([lo, hi], axis=1)).reshape(NL, 128, 8 * 15 * 64)


class Builder:
    def __init__(self, n_layers=NL, debug=False):
        self.n_layers = n_layers
        self.debug = debug
        self.nc = bass.Bass("TRN2", target_bir_lowering=False)
        nc = self.nc
        di = lambda n, s, dt=F32: nc.dram_tensor(n, s, dt, kind="ExternalInput").ap()
        self.xT = di("xT", [DM, SEQ])
        self.w_in = di("w_in", [NL, DM, N_IN])
        self.w_branch = di("w_branch", [NL, 4, 512, DM])
        self.w_out = di("w_out", [NL, DM, DM])
        self.w_up = di("w_up", [NL, DM, 2 * D_FF])
        self.w_down = di("w_down", [NL, D_FF, DM])
        self.params = di("params", [128, NL * P_N])
        self.lam = di("lam", [1, NL * 256])
        self.rpbE = di("rpbE", [NL, 128, 8 * 15 * 64])
        self.c_tab1 = di("tab1", [128, 2 * SEQ])
        self.c_tab2 = di("tab2", [128, 2 * SEQ])
        self.c_mats = di("mats", [128, 5 * 128])
        self.c_mask3 = di("mask3", [128, 3 * 128])
        self.c_colmask = di("colmask", [128, 64])
        self.yT = nc.dram_tensor("yT", [DM, SEQ], F32, kind="ExternalOutput").ap()
        kind = "ExternalOutput" if debug else "Internal"
        ds = lambda n, s, dt: nc.dram_tensor(n, s, dt, kind=kind).ap()
        self.QK = ds("s_qk", [45 * 128, SEQ], BF16)
        self.VA = ds("s_va", [SEQ, 512], BF16)
        self.VB = ds("s_vb", [SEQ, 130], BF16)
        self.VC = ds("s_vc", [3, SEQ, 520], BF16)
        self.VD = ds("s_vd", [SEQ, 520], BF16)
        self.G = ds("s_g", [4096, SEQ], F32)
        self.OT = ds("s_ot", [4, 512, SEQ], BF16)
        self.M = ds("s_m", [D_FF, SEQ], BF16)
        self.XM = ds("s_xm", [DM, SEQ], F32)
        self.XA = ds("s_xa", [DM, SEQ], F32)
        self.XB = ds("s_xb", [DM, SEQ], F32)

    def tile(self, name, shape, dt):
        t = self.ph.enter_context(self.nc.sbuf_tensor(name + "_%d" % self.uid, shape, dt))
        self.uid += 1
        return Tl(t, name)

    def ptile(self, name, shape=(128, 512), dt=F32):
        t = self.ph.enter_context(self.nc.psum_tensor(name + "_%d" % self.uid, [128, 512], F32))
        self.uid += 1
        return Tl(t, name, True)

    def A(self, eng, meth, reads, writes, *a, **k):
        wr = [x.b for x in writes] + [x.b for x in reads if x.psum]
        self.S.op(eng, lambda e: getattr(e, meth)(*a, **k), [x.b for x in reads], wr)

    def load(self, dst_tl, out_ap, in_ap, q="sp"):
        self.S.op(q, lambda e: e.dma_start(out=out_ap, in_=in_ap), [], [dst_tl.b], dma=("L", dst_tl.name))

    def store(self, src_tl, out_ap, in_ap, q="pool"):
        self.S.op(q, lambda e: e.dma_start(out=out_ap, in_=in_ap), [src_tl.b], [], dma=("S", src_tl.name))

    def mm(self, out_tl, out_ap, l_tl, l_ap, r_tl, r_ap, start, stop):
        self.S.op("pe", lambda e: e.matmul(out_ap, l_ap, r_ap, start=start, stop=stop),
                  [l_tl.b, r_tl.b], [out_tl.b])

    def begin(self):
        self.ph = ExitStack()
        self.ph.__enter__()

    def end(self):
        self.S.flush()
        self.ph.__exit__(None, None, None)

    def build(self):
        nc = self.nc
        self.uid = 0
        with ExitStack() as top:
            self.S = Sched(nc, top)
            self.top = top
            self.ph = top
            self.mats = self.tile("mats", [128, 5, 128], F32)
            self.onesb = self.tile("onesb", [128, 128], BF16)
            self.par = self.tile("par", [128, NL, P_N], F32)
            self.nlam = self.tile("nlam", [128, NL], F32)
            self.begin()
            self.lamt = self.tile("lamt", [1, NL * 256], F32)
            self.load(self.mats, self.mats[:].rearrange("p a b -> p (a b)"), self.c_mats)
            self.load(self.par, self.par[:].rearrange("p a b -> p (a b)"), self.params)
            self.load(self.lamt, self.lamt[:], self.lam)
            self.A("dve", "tensor_copy", [self.mats], [self.onesb], out=self.onesb[:], in_=self.mats[:, 3, :])
            self.lambda_setup()
            self.end()
            xin = self.xT
            for l in range(self.n_layers):
                last = (l == self.n_layers - 1)
                xout = self.yT if last else (self.XA if l % 2 == 0 else self.XB)
                self.layer(l, xin, xout)
                xin = xout
        return nc

    def lambda_setup(self):
        import math
        pr = self.tile("lampr", [1, NL * 2, 64], F32)
        sm = self.tile("lamsm", [1, NL * 2], F32)
        lv = self.tile("lamv", [1, NL], F32)
        ps = self.ptile("lamps", (128, NL))
        lt = self.lamt[:].rearrange("p (l a d) -> p l a d", l=NL, a=4)
        for l in range(NL):
            for j in range(2):
                self.A("dve", "tensor_tensor", [self.lamt], [pr], out=pr[:, l * 2 + j, :], in0=lt[:, l, 2 * j, :],
                       in1=lt[:, l, 2 * j + 1, :], op=ALU.mult)
        self.A("dve", "tensor_reduce", [pr], [sm], out=sm[:], in_=pr[:], axis=AX.X, op=ALU.add)
        self.A("act", "activation", [sm], [sm], out=sm[:], in_=sm[:], func=AF.Exp)
        smv = sm[:].rearrange("p (l j) -> p l j", j=2)
        for l in range(NL):
            li = 0.8 - 0.6 * math.exp(-0.3 * l)
            self.A("dve", "scalar_tensor_tensor", [sm], [lv], out=lv[:, l:l + 1], in0=smv[:, l, 1:2], scalar=-li,
                   in1=smv[:, l, 0:1], op0=ALU.add, op1=ALU.subtract)
        self.mm(ps, ps[:, 0:NL], self.mats, self.mats[0:1, 3, :], lv, lv[:], True, True)
        self.A("dve", "tensor_copy", [ps], [self.nlam], out=self.nlam[:], in_=ps[:, 0:NL])

    def rmsnorm_to_hT(self, l, xsrc, hT, goff):
        xs = [self.tile("nx%d" % i, [128, 8, TC], F32) for i in range(2)]
        sq = [self.tile("nsq%d" % i, [128, 8, TC], F32) for i in range(1)]
        rs = [self.tile("nrs%d" % i, [128, TC], F32) for i in range(2)]
        ps = [self.ptile("nps%d" % i) for i in range(2)]
        xv = xsrc.rearrange("(c p) n -> p c n", p=128)
        for t in range(NT):
            x = xs[t % 2]
            s = sq[0]
            r = rs[t % 2]
            p = ps[t % 2]
            self.load(x, x[:], xv[:, :, t * TC:(t + 1) * TC])
            self.A("pool", "tensor_tensor", [x], [s], out=s[:], in0=x[:], in1=x[:], op=ALU.mult)
            for c in range(8):
                self.mm(p, p[:], self.mats, self.mats[:, 3, :], s, s[:, c, :], c == 0, c == 7)
            self.A("act", "activation", [p], [r], out=r[:], in_=p[:], func=AF.Ln, scale=1.0 / DM, bias=self.epsc[:, 0:1])
            self.A("act", "activation", [r], [r], out=r[:], in_=r[:], func=AF.Exp, scale=-0.5)
            for c in range(8):
                self.A("dve", "scalar_tensor_tensor", [x, r, self.par], [hT], out=hT[:, c, t * TC:(t + 1) * TC],
                       in0=x[:, c, :], scalar=self.par[:, l, goff + c:goff + c + 1], in1=r[:], op0=ALU.mult, op1=ALU.mult)

    def load_weight_bf16(self, dst, dst_ap, src_ap, stg, shape_ap, eng):
        self.load(stg, shape_ap, src_ap)
        self.A(eng, "tensor_copy", [stg], [dst], out=dst_ap, in_=shape_ap)

    def layer(self, l, xin, xout):
        import math
        self.lambda_init = 0.8 - 0.6 * math.exp(-0.3 * l)
        self.begin()
        self.epsc = self.tile("epsc", [128, 1], F32)
        self.A("pool", "memset", [], [self.epsc], self.epsc[:], EPS)
        hT = self.tile("hT", [128, 8, SEQ], BF16)
        sub = ExitStack()
        outer = self.ph
        self.ph = sub
        sub.__enter__()
        self.rmsnorm_to_hT(l, xin, hT, P_G1)
        self.S.flush()
        sub.__exit__(None, None, None)
        self.ph = outer
        self.proj(l, hT)
        self.end()
        self.begin()
        self.epsc = self.tile("epsc", [128, 1], F32)
        self.A("pool", "memset", [], [self.epsc], self.epsc[:], EPS)
        self.attn_a(l)
        self.end()
        self.begin()
        self.attn_b(l)
        self.end()
        self.begin()
        self.attn_c(l)
        self.end()
        self.begin()
        self.attn_d(l)
        self.end()
        self.begin()
        self.merge(l, xin)
        self.end()
        self.begin()
        self.epsc = self.tile("epsc", [128, 1], F32)
        self.A("pool", "memset", [], [self.epsc], self.epsc[:], EPS)
        hT = self.tile("hT2", [128, 8, SEQ], BF16)
        sub = ExitStack()
        outer = self.ph
        self.ph = sub
        sub.__enter__()
        self.rmsnorm_to_hT(l, self.XM, hT, P_G2)
        self.S.flush()
        sub.__exit__(None, None, None)
        self.ph = outer
        self.ffn_up(l, hT)
        self.end()
        self.begin()
        self.ffn_down(l, xout)
        self.end()

    def pipe(self, items, lags):
        n = len(items)
        for j in range(n + max(lags)):
            for si, lg in enumerate(lags):
                i = j - lg
                if 0 <= i < n and len(items[i]) > si and items[i][si] is not None:
                    items[i][si]()

    def proj(self, l, hT):
        wst = self.tile("wst0", [128, 8, 512], F32)
        wbf = [self.tile("wbf%d" % i, [128, 8, 512], BF16) for i in range(2)]
        tab = self.tile("tab", [128, 2, SEQ], F32)
        pacc = [self.ptile("pacc%d" % i) for i in range(3)]
        pss = [self.ptile("pss%d" % i) for i in range(2)]
        prot = [self.ptile("prot%d" % i) for i in range(2)]
        sq = [self.tile("sq%d" % i, [128, TC], F32) for i in range(3)]
        rs = [self.tile("rs%d" % i, [128, TC], F32) for i in range(3)]
        yy = [self.tile("yy%d" % i, [128, TC], F32) for i in range(3)]
        t2 = [self.tile("t2%d" % i, [128, TC], F32) for i in range(3)]
        qo = [self.tile("qo%d" % i, [128, TC], BF16) for i in range(3)]
        rowb = [self.tile("rowb%d" % i, [128, SEQ], BF16) for i in range(2)]
        go = [self.tile("go%d" % i, [128, TC], F32) for i in range(3)]
        vst = [self.tile("vst%d" % i, [128, 4, 520], BF16) for i in range(2)]
        vsta = [self.tile("vsta%d" % i, [128, 4, 512], BF16) for i in range(2)]
        for v in vst:
            self.A("pool", "memset", [], [v], v[:], 1.0)
        wv = self.w_in[l].rearrange("(k p) n -> p k n", p=128)
        cnt = dict(acc=0, q=0, row=0, go=0, vs=0, tmp=0, pp=0)
        G = P_QKG

        def wload(gidx, col0, w):
            wb = wbf[gidx % 2]
            self.load(wst, wst[:, :, 0:w], wv[:, :, col0:col0 + w])
            self.A("pool" if gidx % 2 == 0 else "dve", "tensor_copy", [wst], [wb], out=wb[:, :, 0:w], in_=wst[:, :, 0:w])

        def qk_items(items, wb, off, row, gcol, rope, r):
            mat = 0 if rope == "1d" else 1
            rb = None
            if r > 1:
                rb = rowb[cnt["row"] % 2]
                cnt["row"] += 1
            for t in range(NT):
                pa = pacc[cnt["acc"] % 3]
                cnt["acc"] += 1
                j = cnt["tmp"] % 3
                cnt["tmp"] += 1
                ps_, pr_ = pss[cnt["pp"] % 2], prot[cnt["pp"] % 2]
                cnt["pp"] += 1
                s, rr, y, a2 = sq[j], rs[j], yy[j], t2[j]
                tsl = slice(t * TC, (t + 1) * TC)
                if r > 1:
                    n_ = TC // r
                    dst_tl = rb
                    dst = rb[:].rearrange("p (c i) -> p c i", c=r)[:, :, t * n_:(t + 1) * n_]
                else:
                    dst_tl = qo[cnt["q"] % 3]
                    cnt["q"] += 1
                    dst = dst_tl[:]

                def s0(pa=pa, s=s, tsl=tsl):
                    for k in range(8):
                        self.mm(pa, pa[:], wb, wb[:, k, off:off + 128], hT, hT[:, k, tsl], k == 0, k == 7)
                    self.A("act", "activation", [pa], [s], out=s[:], in_=pa[:], func=AF.Square)

                def s1(pa=pa, s=s, rr=rr, y=y, ps_=ps_, dst_tl=dst_tl, dst=dst):
                    self.mm(ps_, ps_[:], self.mats, self.mats[:, 2, :], s, s[:], True, True)
                    self.A("act", "activation", [ps_], [rr], out=rr[:], in_=ps_[:], func=AF.Ln, scale=1.0 / 64, bias=self.epsc[:, 0:1])
                    self.A("act", "activation", [rr], [rr], out=rr[:], in_=rr[:], func=AF.Exp, scale=-0.5)
                    if rope is None:
                        self.A("dve", "scalar_tensor_tensor", [pa, rr, self.par], [dst_tl], out=dst, in0=pa[:],
                               scalar=self.par[:, l, gcol:gcol + 1], in1=rr[:], op0=ALU.mult, op1=ALU.mult)
                    else:
                        self.A("dve", "scalar_tensor_tensor", [pa, rr, self.par], [y], out=y[:], in0=pa[:],
                               scalar=self.par[:, l, gcol:gcol + 1], in1=rr[:], op0=ALU.mult, op1=ALU.mult)

                def s2(y=y, a2=a2, pr_=pr_, dst_tl=dst_tl, dst=dst, tsl=tsl, t=t):
                    if rope is not None:
                        self.mm(pr_, pr_[:], self.mats, self.mats[:, mat, :], y, y[:], True, True)
                        self.A("dve", "tensor_tensor", [pr_, tab], [a2], out=a2[:], in0=pr_[:], in1=tab[:, 1, tsl], op=ALU.mult)
                        self.A("pool", "tensor_tensor", [y, tab], [y], out=y[:], in0=y[:], in1=tab[:, 0, tsl], op=ALU.mult)
                        if r > 1:
                            src1 = y[:].rearrange("p (i c) -> p c i", c=r)
                            src2 = a2[:].rearrange("p (i c) -> p c i", c=r)
                        else:
                            src1, src2 = y[:], a2[:]
                        self.A("pool", "tensor_tensor", [y, a2], [dst_tl], out=dst, in0=src1, in1=src2, op=ALU.add)
                    if r == 1:
                        self.store(dst_tl, self.QK[row * 128:(row + 1) * 128, tsl], dst_tl[:])
                    elif t == NT - 1:
                        self.store(rb, self.QK[row * 128:(row + 1) * 128, :], rb[:])

                items.append([s0, s1, s2])

        def v_items(items, wb, off, w, dst, r, nh, hd):
            hv = hT[:].rearrange("p k (i c) -> p k c i", c=r)
            L = SEQ // r
            stride = hd + 1 if hd == 64 else hd
            vs = None
            for tt in range(32):
                c = (tt * 128) // L
                i0 = (tt * 128) % L
                pa = pacc[cnt["acc"] % 3]
                cnt["acc"] += 1
                if tt % 4 == 0:
                    vs = (vst if hd == 64 else vsta)[cnt["vs"] % 2]
                    cnt["vs"] += 1

                def s0(pa=pa, c=c, i0=i0):
                    for k in range(8):
                        self.mm(pa, pa[:, 0:w], hT, hv[:, k, c, i0:i0 + 128], wb, wb[:, k, off:off + w], k == 0, k == 7)

                def s1(pa=pa, vs=vs, tt=tt):
                    o = vs[:, tt % 4, 0:nh * stride].rearrange("p (h d) -> p h d", h=nh)[:, :, 0:hd]
                    i_ = pa[:, 0:w].rearrange("p (h d) -> p h d", h=nh)
                    if tt % 2 == 0:
                        self.A("act", "activation", [pa], [vs], out=o, in_=i_, func=AF.Copy)
                    else:
                        self.A("dve", "tensor_copy", [pa], [vs], out=o, in_=i_)
                    if tt % 4 == 3:
                        t0 = (tt - 3) * 128
                        self.store(vs, dst[t0:t0 + 512, :].rearrange("(a p) n -> p a n", p=128), vs[:, :, 0:nh * stride])

                items.append([s0, s1])

        def gate_items(items, wb, off, grow):
            for t in range(NT):
                pa = pacc[cnt["acc"] % 3]
                cnt["acc"] += 1
                g = go[cnt["go"] % 3]
                cnt["go"] += 1
                tsl = slice(t * TC, (t + 1) * TC)

                def s0(pa=pa, tsl=tsl):
                    for k in range(8):
                        self.mm(pa, pa[:], wb, wb[:, k, off:off + 128], hT, hT[:, k, tsl], k == 0, k == 7)

                def s1(pa=pa, g=g, tsl=tsl):
                    self.A("act", "activation", [pa], [g], out=g[:], in_=pa[:], func=AF.Sigmoid)
                    self.store(g, self.G[grow * 128:(grow + 1) * 128, tsl], g[:])

                items.append([s0, s1])

        jobsB = [
            (COL["bq"], 512, [("qk", i * 128, 8 + i, G + 2, "ax", 1) for i in range(4)]),
            (COL["bk"], 256, [("qk", 0, 12, G + 3, "ax", 1), ("v", 128, 128, self.VB, 1, 2, 64)]),
        ]
        jobs = [
            (COL["aq"], 512, [("qk", i * 128, 0 + i, G + 0, "1d", 1) for i in range(4)]),
            (COL["ak"], 512, [("qk", i * 128, 4 + i, G + 1, "1d", 1) for i in range(4)]),
            (COL["av"], 512, [("v", 0, 512, self.VA, 1, 4, 128)]),
        ]
        for g in range(3):
            jobs.append((COL["cq"] + g * 512, 512, [("qk", i * 128, 13 + g * 4 + i, G + 4, "1d", DIL[g]) for i in range(4)]))
            jobs.append((COL["ck"] + g * 512, 512, [("qk", i * 128, 25 + g * 4 + i, G + 5, "1d", DIL[g]) for i in range(4)]))
            jobs.append((COL["cv"] + g * 512, 512, [("v", 0, 512, self.VC[g], DIL[g], 8, 64)]))
        jobs.append((COL["dq"], 512, [("qk", i * 128, 37 + i, G + 6, None, 1) for i in range(4)]))
        jobs.append((COL["dk"], 512, [("qk", i * 128, 41 + i, G + 7, None, 1) for i in range(4)]))
        jobs.append((COL["dv"], 512, [("v", 0, 512, self.VD, 1, 8, 64)]))
        for gi in range(8):
            jobs.append((COL["g"] + gi * 512, 512, [("gate", i * 128, gi * 4 + i) for i in range(4)]))

        gctr = [0]

        def run_jobs(jl):
            items = []
            gid0 = gctr[0]
            for ji, (col0, w, subs) in enumerate(jl):
                gid = gid0 + ji
                wb = wbf[gid % 2]
                first = len(items)
                for sj in subs:
                    if sj[0] == "qk":
                        qk_items(items, wb, *sj[1:])
                    elif sj[0] == "v":
                        v_items(items, wb, *sj[1:])
                    else:
                        gate_items(items, wb, *sj[1:])
                orig = items[first][0]
                nxt = jl[ji + 1] if ji + 1 < len(jl) else None

                def s0w(orig=orig, nxt=nxt, gid=gid):
                    if nxt is not None:
                        wload(gid + 1, nxt[0], nxt[1])
                    orig()
                items[first][0] = s0w
            wload(gid0, jl[0][0], jl[0][1])
            gctr[0] += len(jl)
            self.pipe(items, [0, 1, 2])

        self.load(tab, tab[:].rearrange("p a n -> p (a n)"), self.c_tab2)
        run_jobs(jobsB)
        self.load(tab, tab[:].rearrange("p a n -> p (a n)"), self.c_tab1)
        run_jobs(jobs)

    def normalize_rows65(self, po, res_tl, res_ap, n, tmp_row, tmp_bc, pbc):
        self.A("dve", "reciprocal", [po], [tmp_row], out=tmp_row[64:65, 0:n], in_=po[64:65, 0:n])
        self.mm(pbc, pbc[0:64, 0:n], self.mats, self.mats[64:65, 3, 0:64], tmp_row, tmp_row[64:65, 0:n], True, True)
        self.A("act", "activation", [pbc], [tmp_bc], out=tmp_bc[0:64, 0:n], in_=pbc[0:64, 0:n], func=AF.Copy)
        self.A("dve", "tensor_tensor", [po, tmp_bc], [res_tl], out=res_ap, in0=po[0:64, 0:n], in1=tmp_bc[0:64, 0:n], op=ALU.mult)

    def attn_a(self, l):
        qa = self.tile("qa", [128, 4, SEQ], BF16)
        ka = self.tile("ka", [128, 4, SEQ], BF16)
        va = self.tile("va", [128, 32, 512], BF16)
        for h in range(4):
            self.load(qa, qa[:, h, :], self.QK[(0 + h) * 128:(1 + h) * 128, :])
            self.load(ka, ka[:, h, :], self.QK[(4 + h) * 128:(5 + h) * 128, :])
        for j in range(4):
            self.load(va, va[:, j * 8:(j + 1) * 8, :], self.VA[j * 1024:(j + 1) * 1024, :].rearrange("(a p) n -> p a n", p=128))
        psc = [self.ptile("psc%d" % i) for i in range(3)]
        po = [[self.ptile("po%d%d" % (c, i)) for i in range(2)] for c in range(2)]
        pfin = self.ptile("pfin")
        pt = [self.tile("pt%d" % i, [128, TC], BF16) for i in range(6)]
        sacc = [[self.tile("sacc%d%d" % (c, i), [128, TC], F32) for i in range(2)] for c in range(3)]
        sbf = self.tile("sbf", [128, TC], BF16)
        rc = self.tile("rc", [128, TC], F32)
        res = [self.tile("res%d" % i, [128, TC], F32) for i in range(2)]
        dd = self.tile("dd", [128, TC], F32)
        sq = self.tile("sqa", [128, TC], F32)
        rs = self.tile("rsa", [128, TC], F32)
        ob = [self.tile("oba%d" % i, [128, TC], BF16) for i in range(2)]
        items = []
        n = 0
        gi = 0
        for h in range(4):
            for qc in range(NT):
                qs = slice(qc * TC, (qc + 1) * TC)
                buf = gi % 2
                gi += 1
                for kt in range(32):
                    for c in range(2):
                        ps_ = slice(c * 64, (c + 1) * 64)
                        sc = psc[n % 3]
                        p = pt[n % 6]
                        n += 1
                        po_ = po[c][buf]

                        def s0(sc=sc, p=p, h=h, qs=qs, ps_=ps_, kt=kt):
                            self.mm(sc, sc[:], ka, ka[ps_, h, kt * 128:(kt + 1) * 128], qa, qa[ps_, h, qs], True, True)
                            self.A("act", "activation", [sc], [p], out=p[:], in_=sc[:], func=AF.Exp, scale=0.125)

                        def s1(p=p, h=h, c=c, kt=kt, po_=po_, buf=buf):
                            self.mm(po_, po_[:], va, va[:, kt, h * 128:(h + 1) * 128], p, p[:], kt == 0, kt == 31)
                            if c == 0:
                                a, eng, first = sacc[0][buf], "dve", kt == 0
                            elif kt % 2 == 0:
                                a, eng, first = sacc[1][buf], "dve", kt == 0
                            else:
                                a, eng, first = sacc[2][buf], "pool", kt == 1
                            if first:
                                self.A(eng, "tensor_copy", [p], [a], out=a[:], in_=p[:])
                            else:
                                self.A(eng, "tensor_tensor", [a, p], [a], out=a[:], in0=a[:], in1=p[:], op=ALU.add)

                        def s2(h=h, qc=qc, qs=qs, c=c, kt=kt, po_=po_, buf=buf):
                            if kt != 31:
                                return
                            parts = [sacc[0][buf]] if c == 0 else [sacc[1][buf], sacc[2][buf]]
                            for pi, a in enumerate(parts):
                                self.A("pool", "tensor_copy", [a], [sbf], out=sbf[:], in_=a[:])
                                self.mm(pfin, pfin[:], self.onesb, self.onesb[:], sbf, sbf[:], pi == 0, pi == len(parts) - 1)
                            self.A("dve", "reciprocal", [pfin], [rc], out=rc[:], in_=pfin[:])
                            self.A("dve", "tensor_tensor", [po_, rc], [res[c]], out=res[c][:], in0=po_[:], in1=rc[:], op=ALU.mult)
                            if c != 1:
                                return
                            self.A("dve", "scalar_tensor_tensor", [res[0], res[1], self.nlam], [dd], out=dd[:], in0=res[1][:],
                                   scalar=self.nlam[:, l:l + 1], in1=res[0][:], op0=ALU.mult, op1=ALU.add)
                            self.A("pool", "tensor_tensor", [dd], [sq], out=sq[:], in0=dd[:], in1=dd[:], op=ALU.mult)
                            self.mm(pfin, pfin[:], self.mats, self.mats[:, 3, :], sq, sq[:], True, True)
                            self.A("act", "activation", [pfin], [rs], out=rs[:], in_=pfin[:], func=AF.Sqrt, scale=1.0 / 128, bias=self.epsc[:, 0:1])
                            self.A("dve", "reciprocal", [rs], [rs], out=rs[:], in_=rs[:])
                            self.A("dve", "scalar_tensor_tensor", [dd, rs, self.par], [dd], out=dd[:], in0=dd[:],
                                   scalar=self.par[:, l, P_SUB:P_SUB + 1], in1=rs[:], op0=ALU.mult, op1=ALU.mult)
                            o = ob[(h * NT + qc) % 2]
                            self.A("act", "activation", [dd], [o], out=o[:], in_=dd[:], func=AF.Copy, scale=float(1.0 - self.lambda_init))
                            self.store(o, self.OT[0, h * 128:(h + 1) * 128, qs], o[:])

                        items.append([s0, s1, s2])
        self.pipe(items, [0, 3, 12])

    def attn_b(self, l):
        qb = self.tile("qb", [128, 4, SEQ], BF16)
        kb = self.tile("kb", [128, SEQ], BF16)
        vb = self.tile("vb", [128, 32, 130], BF16)
        for hq in range(8):
            g, s = hq // 4, hq % 4
            self.load(qb, qb[g * 64:(g + 1) * 64, s, :], self.QK[8 * 128 + hq * 64:8 * 128 + (hq + 1) * 64, :])
        self.load(kb, kb[:], self.QK[12 * 128:13 * 128, :])
        self.load(vb, vb[:], self.VB.rearrange("(a p) n -> p a n", p=128))
        psc = [self.ptile("psc%d" % i) for i in range(3)]
        po = [[self.ptile("pob%d%d" % (g, i)) for i in range(2)] for g in range(2)]
        pbc = self.ptile("pbc")
        pt = [self.tile("pt%d" % i, [128, TC], BF16) for i in range(6)]
        trow = self.tile("trow", [128, TC], F32)
        tbc = self.tile("tbc", [128, TC], F32)
        ob = [self.tile("obb%d" % i, [128, TC], BF16) for i in range(2)]
        items = []
        n = 0
        m = 0
        gi = 0
        for s in range(4):
            for qc in range(NT):
                qs = slice(qc * TC, (qc + 1) * TC)
                buf = gi % 2
                gi += 1
                for kt in range(32):
                    for g in range(2):
                        hq = g * 4 + s
                        ps_ = slice(g * 64, (g + 1) * 64)
                        o_ps = po[g][buf]
                        sc = psc[n % 3]
                        p = pt[n % 6]
                        n += 1

                        def s0(sc=sc, p=p, ps_=ps_, s=s, qs=qs, kt=kt):
                            self.mm(sc, sc[:], kb, kb[ps_, kt * 128:(kt + 1) * 128], qb, qb[ps_, s, qs], True, True)
                            self.A("act", "activation", [sc], [p], out=p[:], in_=sc[:], func=AF.Exp, scale=0.125)

                        def s1(p=p, g=g, kt=kt, o_ps=o_ps):
                            self.mm(o_ps, o_ps[0:65, :], vb, vb[:, kt, g * 65:(g + 1) * 65], p, p[:], kt == 0, kt == 31)

                        def s2(g=g, hq=hq, qs=qs, kt=kt, o_ps=o_ps):
                            if kt != 31:
                                return
                            o = ob[hq % 2]
                            self.normalize_rows65(o_ps, o, o[0:64, :], TC, trow, tbc, pbc)
                            self.store(o, self.OT[1, hq * 64:(hq + 1) * 64, qs], o[0:64, :])

                        items.append([s0, s1, s2])
        self.pipe(items, [0, 3, 12])

    def attn_c(self, l):
        mask = self.tile("mask3", [128, 3, 128], F32)
        self.load(mask, mask[:].rearrange("p a b -> p (a b)"), self.c_mask3)
        qc_ = self.tile("qc", [128, 3, SEQ], BF16)
        kc_ = self.tile("kc", [128, 3, SEQ], BF16)
        vc_ = self.tile("vc", [128, 3, 32, 130], BF16)
        acc = [self.tile("acc%d" % i, [65, SEQ], F32) for i in range(2)]
        psc = [self.ptile("pscc%d" % i, (128, 384)) for i in range(3)]
        po = [self.ptile("poc%d" % i) for i in range(2)]
        pbc = self.ptile("pbcc")
        ex = [self.tile("ex%d" % i, [128, 3, 128], F32) for i in range(3)]
        pt = [self.tile("ptc%d" % i, [128, 3, 128], BF16) for i in range(4)]
        trow = self.tile("trowc", [128, TC], F32)
        ob = [self.tile("obc%d" % i, [128, TC], BF16) for i in range(2)]
        n = 0
        m = 0
        for jp in range(4):
            for g in range(3):
                self.load(qc_, qc_[:, g, :], self.QK[(13 + g * 4 + jp) * 128:(14 + g * 4 + jp) * 128, :])
                self.load(kc_, kc_[:, g, :], self.QK[(25 + g * 4 + jp) * 128:(26 + g * 4 + jp) * 128, :])
                self.load(vc_, vc_[:, g, :, :], self.VC[g].rearrange("(a p) n -> p a n", p=128)[:, :, jp * 130:(jp + 1) * 130])
            items = []
            for hh in range(2):
                ps_ = slice(hh * 64, (hh + 1) * 64)
                ac = acc[hh]
                hd = jp * 2 + hh
                for g in range(3):
                    r = DIL[g]
                    L = SEQ // r
                    tps = L // 128
                    for qb4 in range(8):
                        o_ps = po[m % 2]
                        m += 1
                        for u in range(4):
                            qb = qb4 * 4 + u
                            seg = qb // tps
                            kts = [k for k in (qb - 1, qb, qb + 1) if k // tps == seg and 0 <= k < 32]
                            j0 = kts[0] - (qb - 1)
                            nk = len(kts)
                            sc = psc[n % 3]
                            e_ = ex[n % 3]
                            p = pt[n % 4]
                            n += 1

                            def s0(sc=sc, e_=e_, p=p, kts=kts, j0=j0, nk=nk, qb=qb, g=g, ps_=ps_):
                                for k in kts:
                                    j = k - (qb - 1)
                                    self.mm(sc, sc[:, j * 128:(j + 1) * 128], kc_, kc_[ps_, g, k * 128:(k + 1) * 128],
                                            qc_, qc_[ps_, g, qb * 128:(qb + 1) * 128], True, True)
                                scv = sc[:, 0:384].rearrange("p (a b) -> p a b", a=3)
                                self.A("act", "activation", [sc], [e_], out=e_[:, j0:j0 + nk, :], in_=scv[:, j0:j0 + nk, :], func=AF.Exp, scale=0.125)
                                self.A("pool", "tensor_tensor", [e_, mask], [p], out=p[:, j0:j0 + nk, :], in0=e_[:, j0:j0 + nk, :],
                                       in1=mask[:, j0:j0 + nk, :], op=ALU.mult)

                            def s1(p=p, kts=kts, nk=nk, qb=qb, g=g, hh=hh, u=u, o_ps=o_ps, qb4=qb4, r=r, L=L, ac=ac, hd=hd):
                                for ki, k in enumerate(kts):
                                    j = k - (qb - 1)
                                    self.mm(o_ps, o_ps[0:65, u * 128:(u + 1) * 128], vc_, vc_[:, g, k, hh * 65:(hh + 1) * 65],
                                            p, p[:, j, :], ki == 0, ki == nk - 1)
                                if u != 3:
                                    return
                                pos0 = qb4 * 512
                                av = ac[:].rearrange("p (i c) -> p c i", c=r)
                                if L >= 512:
                                    c0, i0 = pos0 // L, pos0 % L
                                    dst = av[0:65, c0:c0 + 1, i0:i0 + 512]
                                    src = o_ps[0:65, :].rearrange("p (c i) -> p c i", c=1)
                                else:
                                    ncl = 512 // L
                                    c0 = pos0 // L
                                    dst = av[0:65, c0:c0 + ncl, :]
                                    src = o_ps[0:65, :].rearrange("p (c i) -> p c i", c=ncl)
                                if g == 0:
                                    self.A("act", "activation", [o_ps], [ac], out=dst, in_=src, func=AF.Copy)
                                else:
                                    self.A("dve", "tensor_tensor", [o_ps, ac], [ac], out=dst, in0=dst, in1=src, op=ALU.add)
                                if g == 2 and qb4 == 7:
                                    for t in range(NT):
                                        o = ob[(hd * NT + t) % 2]
                                        ts_ = slice(t * TC, (t + 1) * TC)
                                        self.A("dve", "reciprocal", [ac], [trow], out=trow[64:65, :], in_=ac[64:65, ts_])
                                        self.mm(pbc, pbc[0:64, :], self.mats, self.mats[64:65, 3, 0:64], trow, trow[64:65, :], True, True)
                                        self.A("dve", "tensor_tensor", [pbc, ac], [o], out=o[0:64, :], in0=pbc[0:64, :], in1=ac[0:64, ts_], op=ALU.mult)
                                        self.store(o, self.OT[2, hd * 64:(hd + 1) * 64, ts_], o[0:64, :])

                            items.append([s0, s1])
            self.pipe(items, [0, 2])

    def attn_d(self, l):
        qd = self.tile("qd", [128, 4, SEQ], BF16)
        kd = self.tile("kd", [128, 4, SEQ], BF16)
        ve = self.tile("ve", [128, 32, 520], BF16)
        vo = self.tile("vo", [128, 31, 520], BF16)
        E = self.tile("E", [128, 8, 15, 64], F32)
        cm = self.tile("cm", [128, 64], F32)
        self.load(cm, cm[:], self.c_colmask)
        for i in range(4):
            self.load(qd, qd[:, i, :], self.QK[(37 + i) * 128:(38 + i) * 128, :])
            self.load(kd, kd[:, i, :], self.QK[(41 + i) * 128:(42 + i) * 128, :])
        for j in range(4):
            self.load(ve, ve[:, j * 8:(j + 1) * 8, :], self.VD[j * 1024:(j + 1) * 1024, :].rearrange("(a p) n -> p a n", p=128))
        for j in range(4):
            na = 8 if j < 3 else 7
            self.load(vo, vo[:, j * 8:j * 8 + na, :], self.VD[64 + j * 1024:64 + j * 1024 + na * 128, :].rearrange("(a p) n -> p a n", p=128))
        for h in range(8):
            self.load(E, E[:, h, :, :].rearrange("p a b -> p (a b)"), self.rpbE[l][:, h * 960:(h + 1) * 960])
        for h in range(8):
            self.A("act", "activation", [E], [E], out=E[:, h, :, :], in_=E[:, h, :, :], func=AF.Exp)
            self.A("pool", "tensor_tensor", [E, cm], [E], out=E[:, h, :, :], in0=E[:, h, :, :],
                   in1=cm[:].rearrange("p (a b) -> p a b", a=1).to_broadcast([128, 15, 64]), op=ALU.mult)
        psc = [self.ptile("pscd%d" % i, (128, 256)) for i in range(3)]
        po = [self.ptile("pod%d" % i) for i in range(2)]
        pbc = self.ptile("pbcd")
        ex = [self.tile("exd%d" % i, [128, 4, 64], F32) for i in range(3)]
        pt = [self.tile("ptd%d" % i, [128, 4, 64], BF16) for i in range(4)]
        trow = self.tile("trowd", [128, TC], F32)
        tbc = self.tile("tbcd", [128, TC], F32)
        ob = [self.tile("obd%d" % i, [128, TC], BF16) for i in range(2)]
        items = []
        n = 0
        m = 0
        for h in range(8):
            ch, hh = h // 2, h % 2
            ps_ = slice(hh * 64, (hh + 1) * 64)
            for r8 in range(8):
                o_ps = po[m % 2]
                o = ob[m % 2]
                m += 1
                for u in range(8):
                    r = r8 * 8 + u
                    rs_ = min(max(r - 4, 0), 56)
                    base = rs_ - r + 7
                    sc = psc[n % 3]
                    e_ = ex[n % 3]
                    p = pt[n % 4]
                    n += 1

                    def s0(sc=sc, e_=e_, p=p, r=r, rs_=rs_, base=base, h=h, ch=ch, ps_=ps_, n=n):
                        for i in range(4):
                            k0 = (rs_ + 2 * i) * 64
                            self.mm(sc, sc[:, i * 64:(i + 1) * 64], kd, kd[ps_, ch, k0:k0 + 128], qd, qd[ps_, ch, r * 64:(r + 1) * 64], True, True)
                        self.A("act", "activation", [sc], [e_], out=e_[:], in_=sc[:, 0:256].rearrange("p (a b) -> p a b", a=4), func=AF.Exp, scale=0.125)
                        ev = E[:, h, base:base + 7:2, :]
                        self.A("pool" if n % 2 == 0 else "dve", "tensor_tensor", [e_, E], [p], out=p[:], in0=e_[:], in1=ev, op=ALU.mult)

                    def s1(p=p, rs_=rs_, h=h, u=u, o_ps=o_ps, o=o, r8=r8):
                        for i in range(4):
                            row0 = rs_ + 2 * i
                            if row0 % 2 == 0:
                                vt, vi = ve, row0 // 2
                            else:
                                vt, vi = vo, (row0 - 1) // 2
                            self.mm(o_ps, o_ps[0:65, u * 64:(u + 1) * 64], vt, vt[:, vi, h * 65:(h + 1) * 65], p, p[:, i, :], i == 0, i == 3)
                        if u != 7:
                            return
                        self.normalize_rows65(o_ps, o, o[0:64, :], TC, trow, tbc, pbc)
                        self.store(o, self.OT[3, h * 64:(h + 1) * 64, r8 * TC:(r8 + 1) * TC], o[0:64, :])

                    items.append([s0, s1])
        self.pipe(items, [0, 2])

    def merge(self, l, xin):
        wb = self.tile("wbr", [128, 16, DM], BF16)
        wo = self.tile("wo", [128, 8, DM], BF16)
        stg = [self.tile("mstg%d" % i, [128, 4, DM], F32) for i in range(2)]
        n = 0
        for i in range(4):
            s = stg[n % 2]
            n += 1
            self.load(s, s[:], self.w_branch[l, i].rearrange("(k p) n -> p k n", p=128))
            self.A("pool" if n % 2 == 0 else "dve", "tensor_copy", [s], [wb], out=wb[:, i * 4:(i + 1) * 4, :], in_=s[:])
        for j in range(2):
            s = stg[n % 2]
            n += 1
            self.load(s, s[:], self.w_out[l][j * 512:(j + 1) * 512, :].rearrange("(k p) n -> p k n", p=128))
            self.A("pool" if n % 2 == 0 else "dve", "tensor_copy", [s], [wo], out=wo[:, j * 4:(j + 1) * 4, :], in_=s[:])
        ot = [self.tile("mot%d" % i, [128, 16, TC], BF16) for i in range(2)]
        xs = [self.tile("mx%d" % i, [128, 8, TC], F32) for i in range(2)]
        gt = [self.tile("mg%d" % i, [128, TC], F32) for i in range(4)]
        mt = [self.tile("mm%d" % i, [128, 8, TC], BF16) for i in range(2)]
        acc = [self.tile("macc%d" % i, [128, TC], F32) for i in range(2)]
        tmp = [self.tile("mtmp%d" % i, [128, TC], F32) for i in range(2)]
        py = [self.ptile("py%d" % i) for i in range(3)]
        px = [self.ptile("px%d" % i) for i in range(2)]
        xo = [self.tile("mxo%d" % i, [128, TC], F32) for i in range(3)]
        xv = xin.rearrange("(c p) n -> p c n", p=128)
        otv = self.OT.rearrange("b (k p) n -> p (b k) n", p=128)
        ng = 0
        ny = 0
        nx = 0
        for t in range(NT):
            ts_ = slice(t * TC, (t + 1) * TC)
            o = ot[t % 2]
            x = xs[t % 2]
            mtt = mt[t % 2]
            for i in range(4):
                self.load(o, o[:, i * 4:(i + 1) * 4, :], otv[:, i * 4:(i + 1) * 4, ts_])
            self.load(x, x[:], xv[:, :, ts_])
            for f in range(8):
                a = acc[f % 2]
                for i in range(4):
                    g = gt[ng % 4]
                    ng += 1
                    self.load(g, g[:], self.G[(i * 8 + f) * 128:(i * 8 + f + 1) * 128, ts_])
                    p = py[ny % 3]
                    ny += 1
                    for k in range(4):
                        self.mm(p, p[:], wb, wb[:, i * 4 + k, f * 128:(f + 1) * 128], o, o[:, i * 4 + k, :], k == 0, k == 3)
                    if i == 0:
                        self.A("dve", "tensor_tensor", [p, g], [a], out=a[:], in0=p[:], in1=g[:], op=ALU.mult)
                    else:
                        tm = tmp[i % 2]
                        self.A("dve", "tensor_tensor", [p, g], [tm], out=tm[:], in0=p[:], in1=g[:], op=ALU.mult)
                        if i < 3:
                            self.A("pool", "tensor_tensor", [a, tm], [a], out=a[:], in0=a[:], in1=tm[:], op=ALU.add)
                        else:
                            self.A("pool", "tensor_tensor", [a, tm], [mtt], out=mtt[:, f, :], in0=a[:], in1=tm[:], op=ALU.add)
            for fo in range(8):
                p = px[nx % 2]
                xo_ = xo[nx % 3]
                nx += 1
                for k in range(8):
                    self.mm(p, p[:], wo, wo[:, k, fo * 128:(fo + 1) * 128], mtt, mtt[:, k, :], k == 0, k == 7)
                self.A("dve", "tensor_tensor", [p, x], [xo_], out=xo_[:], in0=p[:], in1=x[:, fo, :], op=ALU.add)
                self.store(xo_, self.XM[fo * 128:(fo + 1) * 128, ts_], xo_[:])

    def ffn_up(self, l, hT):
        wst = [self.tile("fst%d" % i, [128, 8, 256], F32) for i in range(2)]
        wbf = [self.tile("fbf%d" % i, [128, 8, 256], BF16) for i in range(4)]
        ua = self.tile("ua", [128, SEQ + 2], F32)
        ug = self.tile("ug", [128, SEQ + 2], F32)
        ca = self.tile("ca", [128, SEQ], F32)
        cg = self.tile("cg", [128, SEQ], F32)
        mrow = [self.tile("mrow%d" % i, [128, SEQ], BF16) for i in range(2)]
        pacc = [self.ptile("fpa%d" % i) for i in range(4)]
        for u in (ua, ug):
            self.A("pool", "memset", [], [u], u[:, 0:1], 0.0)
            self.A("pool", "memset", [], [u], u[:, SEQ + 1:SEQ + 2], 0.0)
        wv = self.w_up[l].rearrange("(k p) n -> p k n", p=128)
        ngrp = 0
        nacc = 0
        for j in range(11):
            w = 256
            wbs = []
            for part in range(2):
                st = wst[ngrp % 2]
                wb = wbf[ngrp % 4]
                ngrp += 1
                c0 = part * D_FF + j * 256
                self.load(st, st[:, :, 0:w], wv[:, :, c0:c0 + w])
                self.A("pool" if part == 0 else "dve", "tensor_copy", [st], [wb], out=wb[:, :, 0:w], in_=st[:, :, 0:w])
                wbs.append(wb)
            for ii in range(w // 128):
                i = j * 2 + ii
                for part, (u, cdst) in enumerate(((ua, ca), (ug, cg))):
                    wb = wbs[part]
                    for t in range(NT):
                        p = pacc[nacc % 4]
                        nacc += 1
                        for k in range(8):
                            self.mm(p, p[:], wb, wb[:, k, ii * 128:(ii + 1) * 128], hT, hT[:, k, t * TC:(t + 1) * TC], k == 0, k == 7)
                        if t % 2 == 0:
                            self.A("act", "activation", [p], [u], out=u[:, 1 + t * TC:1 + (t + 1) * TC], in_=p[:], func=AF.Copy)
                        else:
                            self.A("dve", "tensor_copy", [p], [u], out=u[:, 1 + t * TC:1 + (t + 1) * TC], in_=p[:])
                    ch = part * 22 + i
                    w0 = self.par[:, l, P_CW + 0 * 44 + ch:P_CW + 0 * 44 + ch + 1]
                    w1 = self.par[:, l, P_CW + 1 * 44 + ch:P_CW + 1 * 44 + ch + 1]
                    w2 = self.par[:, l, P_CW + 2 * 44 + ch:P_CW + 2 * 44 + ch + 1]
                    bb = self.par[:, l, P_CB + ch:P_CB + ch + 1]
                    eng = "dve"
                    for hf in range(2):
                        hs = slice(hf * 2048, (hf + 1) * 2048)
                        self.A(eng, "tensor_scalar", [u, self.par], [cdst], out=cdst[:, hs], in0=u[:, hf * 2048:hf * 2048 + 2048],
                               scalar1=w0, scalar2=bb, op0=ALU.mult, op1=ALU.add)
                        self.A(eng, "scalar_tensor_tensor", [u, cdst, self.par], [cdst], out=cdst[:, hs], in0=u[:, 1 + hf * 2048:1 + hf * 2048 + 2048],
                               scalar=w1, in1=cdst[:, hs], op0=ALU.mult, op1=ALU.add)
                        self.A(eng, "scalar_tensor_tensor", [u, cdst, self.par], [cdst], out=cdst[:, hs], in0=u[:, 2 + hf * 2048:2 + hf * 2048 + 2048],
                               scalar=w2, in1=cdst[:, hs], op0=ALU.mult, op1=ALU.add)
                mr = mrow[i % 2]
                for hf in range(2):
                    hs = slice(hf * 2048, (hf + 1) * 2048)
                    self.A("act", "activation", [ca], [ca], out=ca[:, hs], in_=ca[:, hs], func=AF.Silu)
                    self.A("pool" if hf == 0 else "dve", "tensor_tensor", [ca, cg], [mr], out=mr[:, hs], in0=ca[:, hs], in1=cg[:, hs], op=ALU.mult)
                self.store(mr, self.M[i * 128:(i + 1) * 128, :], mr[:])

    def ffn_down(self, l, xout):
        wd = self.tile("wd", [128, 22, DM], BF16)
        stg = [self.tile("dstg%d" % i, [128, 2, DM], F32) for i in range(2)]
        wv = self.w_down[l].rearrange("(k p) n -> p k n", p=128)
        for j in range(11):
            s = stg[j % 2]
            self.load(s, s[:], wv[:, j * 2:(j + 1) * 2, :])
            self.A("pool" if j % 2 == 0 else "dve", "tensor_copy", [s], [wd], out=wd[:, j * 2:(j + 1) * 2, :], in_=s[:])
        mt = [self.tile("dm%d" % i, [128, 22, TC], BF16) for i in range(2)]
        xs = [self.tile("dx%d" % i, [128, 8, TC], F32) for i in range(2)]
        xo = [self.tile("dxo%d" % i, [128, TC], F32) for i in range(3)]
        px = [self.ptile("dpx%d" % i) for i in range(3)]
        mv = self.M.rearrange("(k p) n -> p k n", p=128)
        xv = self.XM.rearrange("(c p) n -> p c n", p=128)
        nx = 0
        for t in range(NT):
            ts_ = slice(t * TC, (t + 1) * TC)
            m = mt[t % 2]
            x = xs[t % 2]
            self.load(m, m[:, 0:11, :], mv[:, 0:11, ts_])
            self.load(m, m[:, 11:22, :], mv[:, 11:22, ts_])
            self.load(x, x[:], xv[:, :, ts_])
            for fo in range(8):
                p = px[nx % 3]
                xo_ = xo[nx % 3]
                nx += 1
                for k in range(22):
                    self.mm(p, p[:], wd, wd[:, k, fo * 128:(fo + 1) * 128], m, m[:, k, :], k == 0, k == 21)
                self.A("dve", "tensor_tensor", [p, x], [xo_], out=xo_[:], in0=p[:], in1=x[:, fo, :], op=ALU.add)
                self.store(xo_, xout[fo * 128:(fo + 1) * 128, ts_], xo_[:])


_CACHE = {}


def _get_nc(n_layers=NL, debug=False):
    key = (n_layers, debug)
    if key not in _CACHE:
        b = Builder(n_layers, debug)
        _CACHE[key] = (b.build(), b)
    return _CACHE[key]


def make_in_maps(inp):
    consts = _host_consts()
    params = _pack_params(inp)
    lam = np.ascontiguousarray(np.asarray(inp["lam"], np.float32).reshape(1, NL * 256))
    rpbE = _pack_rpb(np.asarray(inp["rpb"], np.float32))
    shared = dict(
        w_in=np.ascontiguousarray(inp["w_in"], np.float32),
        w_branch=np.ascontiguousarray(inp["w_branch"], np.float32),
        w_out=np.ascontiguousarray(inp["w_out"], np.float32),
        w_up=np.ascontiguousarray(inp["w_up"], np.float32),
        w_down=np.ascontiguousarray(inp["w_down"], np.float32),
        params=params, lam=lam, rpbE=rpbE,
        tab1=consts["tab1"], tab2=consts["tab2"], mats=consts["mats"], mask3=consts["mask3"],
        colmask=consts["colmask"],
    )
    x = np.asarray(inp["x"], np.float32)
    maps = []
    for b in range(8):
        d = dict(shared)
        d["xT"] = np.ascontiguousarray(x[b].T)
        maps.append(d)
    return maps


def kernel(**inputs):
    inp = {k: np.asarray(v) for k, v in inputs.items()}
    nc, _ = _get_nc()
    maps = make_in_maps(inp)
    res = run_bass_kernel_spmd(nc, maps, core_ids=list(range(8)))
    out = np.stack([np.ascontiguousarray(res.results[b]["yT"].T) for b in range(8)], axis=0)
    return out.astype(np.float32)
```

```python
import numpy as np
from contextlib import ExitStack
import concourse.bass as bass
import concourse.mybir as mybir
from concourse.bass_utils import run_bass_kernel_spmd

F32 = mybir.dt.float32
BF16 = mybir.dt.bfloat16
AF = mybir.ActivationFunctionType
ALU = mybir.AluOpType
AX = mybir.AxisListType

EPOCH = 24000
SAME_SYNC = True

SEQ = 4096
DM = 1024
NL = 4
N_IN = 12544
D_FF = 2816
EPS = 1e-6
NT = 8
TC = 512
COL = dict(aq=0, ak=512, av=1024, bq=1536, bk=2048, bv=2176, cq=2304, ck=3840, cv=5376,
           dq=6912, dk=7424, dv=7936, g=8448)
DIL = (1, 4, 16)


class Buf:
    __slots__ = ("name", "w", "r")

    def __init__(self, name=""):
        self.name = name
        self.w = None
        self.r = []


class _Op:
    __slots__ = ("eng", "fn", "deps", "ldeps", "ddeps", "dma", "pub", "seq")

    def __init__(self, eng, fn, deps, ddeps, dma, ldeps=()):
        self.eng = eng
        self.fn = fn
        self.deps = deps
        self.ldeps = ldeps
        self.ddeps = ddeps
        self.dma = dma
        self.pub = False
        self.seq = None


class Sched:
    ENGS = ("pe", "act", "dve", "pool", "sp")

    def __init__(self, nc, stack):
        self.nc = nc
        self.stack = stack
        self.ops = []
        self.base = 0
        self.cnt = {e: 0 for e in self.ENGS}
        self.sems = {e: [] for e in self.ENGS}
        self.dsem = {}
        self.dcnt = {}
        self.nsem = 0
        self.ninstr = 0

    def _new_sem(self, name):
        self.nsem += 1
        return self.stack.enter_context(self.nc.semaphore(name))

    def _esem(self, e, ep):
        while len(self.sems[e]) <= ep:
            self.sems[e].append(self._new_sem("s_%s_%d" % (e, len(self.sems[e]))))
        return self.sems[e][ep]

    def op(self, eng, fn, reads=(), writes=(), dma=None):
        deps = set()
        ldeps = set()
        ddeps = {}
        base = self.base
        ops = self.ops

        def add(i, dst):
            if i is None or i < base:
                return
            o = ops[i - base]
            if o.dma is not None:
                ddeps[o.dma] = self.dcnt[o.dma]
            else:
                dst.add(i)

        for b in reads:
            add(b.w, deps)
        for b in writes:
            add(b.w, ldeps)
            for r in b.r:
                add(r, ldeps)
        ldeps -= deps
        idx = base + len(ops)
        if dma is not None:
            if dma not in self.dsem:
                self.dsem[dma] = self._new_sem("d_%d" % len(self.dsem))
                self.dcnt[dma] = 0
            self.dcnt[dma] += 16
        ops.append(_Op(eng, fn, deps, ddeps, dma, ldeps))
        for b in reads:
            b.r.append(idx)
        for b in writes:
            b.w = idx
            b.r = []
        return idx

    def flush(self):
        ops = self.ops
        base = self.base
        last = {}
        for i, o in enumerate(ops):
            if o.dma is None and o.fn is not None:
                last[o.eng] = i
        for e in self.ENGS:
            deps = set(base + i for ee, i in last.items() if ee != e)
            ops.append(_Op(e, None, deps, dict(self.dcnt), None))
        import bisect

        def cross(od, o):
            return od.eng != o.eng or o.dma is not None or (SAME_SYNC and o.eng != "pe")

        for o in ops:
            for d in o.deps:
                od = ops[d - base]
                if cross(od, o):
                    od.pub = True
        publ = {e: [] for e in self.ENGS}
        for i, o in enumerate(ops):
            if o.pub:
                publ[o.eng].append(i)
        for i, o in enumerate(ops):
            for d in o.ldeps:
                od = ops[d - base]
                if not cross(od, o):
                    continue
                if od.pub:
                    o.deps.add(d)
                    continue
                lst = publ[od.eng]
                k = bisect.bisect_left(lst, d - base)
                if k < len(lst) and lst[k] < i:
                    o.deps.add(base + lst[k])
                else:
                    od.pub = True
                    bisect.insort(lst, d - base)
                    o.deps.add(d)
        for o in ops:
            if o.pub:
                c = self.cnt[o.eng]
                self.cnt[o.eng] = c + 1
                o.seq = (c // EPOCH, c % EPOCH + 1)
        per = {e: [] for e in self.ENGS}
        seen = {e: {} for e in self.ENGS}
        seend = {e: {} for e in self.ENGS}
        for o in ops:
            F = o.eng
            need = {}
            for d in o.deps:
                od = ops[d - base]
                if od.eng == F and o.dma is None and (F == "pe" or not SAME_SYNC):
                    continue
                if od.seq > need.get(od.eng, (-1, -1)):
                    need[od.eng] = od.seq
            waits = []
            for E, sq in need.items():
                if seen[F].get(E, (-1, -1)) >= sq:
                    continue
                seen[F][E] = sq
                waits.append((self._esem(E, sq[0]), sq[1]))
            for k, v in o.ddeps.items():
                if v == 0 or seend[F].get(k, 0) >= v:
                    continue
                seend[F][k] = v
                waits.append((self.dsem[k], v))
            inc = self._esem(F, o.seq[0]) if o.pub else None
            dinc = self.dsem[o.dma] if o.dma is not None else None
            per[F].append((waits, o.fn, inc, dinc))
            self.ninstr += len(waits) + 1

        def run(eng, lst):
            for waits, fn, inc, dinc in lst:
                emb = None
                if fn is not None and dinc is None and waits:
                    emb = waits[-1]
                    waits = waits[:-1]
                for s, v in waits:
                    eng.wait_ge(s, v)
                if fn is None:
                    continue
                ins = fn(eng)
                if emb is not None:
                    ins._wait_ge(emb[0], emb[1])
                if inc is not None:
                    ins.then_inc(inc, 1)
                if dinc is not None:
                    ins.then_inc(dinc, 16)

        with self.nc.Block() as block:
            @block.tensor
            def _(e):
                run(e, per["pe"])

            @block.scalar
            def _(e):
                run(e, per["act"])

            @block.vector
            def _(e):
                run(e, per["dve"])

            @block.gpsimd
            def _(e):
                run(e, per["pool"])

            @block.sync
            def _(e):
                run(e, per["sp"])
        self.base = base + len(ops)
        self.ops = []


class Tl:
    __slots__ = ("t", "b", "name", "psum")

    def __init__(self, t, name, psum=False):
        self.t = t
        self.b = Buf(name)
        self.name = name
        self.psum = psum

    def __getitem__(self, k):
        return self.t[k]


def _host_consts():
    c = {}
    pos = np.arange(SEQ, dtype=np.float32)
    p = np.arange(128)
    inv32 = (np.float32(10000.0) ** (-(np.arange(0, 64, 2, dtype=np.float32) / np.float32(64)))).astype(np.float32)
    f = p % 32
    half = (p % 64) // 32
    ang = (pos[None, :] * inv32[f][:, None]).astype(np.float32)
    sgn = np.where(half == 0, -1.0, 1.0).astype(np.float32)[:, None]
    tab1 = np.stack([np.cos(ang), np.sin(ang) * sgn], axis=1).astype(np.float32)
    inv16 = (np.float32(10000.0) ** (-(np.arange(0, 32, 2, dtype=np.float32) / np.float32(32)))).astype(np.float32)
    pp = p % 64
    blk = pp // 32
    q = pp % 32
    half2 = q // 16
    f2 = q % 16
    prow = np.floor(pos / 64).astype(np.float32)
    pcol = (pos - prow * 64).astype(np.float32)
    pf = np.where(blk[:, None] == 0, prow[None, :], pcol[None, :]).astype(np.float32)
    ang2 = (pf * inv16[f2][:, None]).astype(np.float32)
    sgn2 = np.where(half2 == 0, -1.0, 1.0).astype(np.float32)[:, None]
    tab2 = np.stack([np.cos(ang2), np.sin(ang2) * sgn2], axis=1).astype(np.float32)
    c["tab1"] = tab1.reshape(128, 2 * SEQ)
    c["tab2"] = tab2.reshape(128, 2 * SEQ)
    mats = np.zeros((128, 5, 128), np.float32)
    for m in range(128):
        part1 = m + 32 if (m % 64) // 32 == 0 else m - 32
        mats[part1, 0, m] = 1.0
        part2 = m + 16 if (m % 32) // 16 == 0 else m - 16
        mats[part2, 1, m] = 1.0
    mats[:, 2, :] = (p[:, None] // 64 == p[None, :] // 64)
    mats[:, 3, :] = 1.0
    mats[:, 4, :] = np.eye(128)
    c["mats"] = mats.reshape(128, 5 * 128)
    i = np.arange(128)[:, None]
    m = np.arange(128)[None, :]
    mask3 = np.stack([(i - m >= 64), (np.abs(i - m) <= 64), (m - i >= 64)], axis=1).astype(np.float32)
    c["mask3"] = mask3.reshape(128, 3 * 128)
    kc = np.arange(128)[:, None] % 64
    qc = np.arange(64)[None, :]
    ws = np.clip(qc - 8, 0, 48)
    c["colmask"] = ((kc >= ws) & (kc < ws + 16)).astype(np.float32)
    return c


P_G1 = 0
P_G2 = 8
P_QKG = 16
P_SUB = 24
P_CW = 25
P_CB = 25 + 132
P_N = 25 + 132 + 44


def _pack_params(inp):
    out = np.zeros((128, NL, P_N), np.float32)
    for l in range(NL):
        out[:, l, P_G1:P_G1 + 8] = inp["norm1_g"][l].reshape(8, 128).T
        out[:, l, P_G2:P_G2 + 8] = inp["norm2_g"][l].reshape(8, 128).T
        qg = inp["qk_g"][l].reshape(8, 64)
        out[:, l, P_QKG:P_QKG + 8] = np.concatenate([qg, qg], axis=1).T
        out[:, l, P_SUB] = inp["subln_g"][l]
        cw = inp["conv_w"][l].reshape(3, 44, 128)
        out[:, l, P_CW:P_CW + 132] = cw.transpose(2, 0, 1).reshape(128, 132)
        out[:, l, P_CB:P_CB + 44] = inp["conv_b"][l].reshape(44, 128).T
    return out.reshape(128, NL * P_N)


def _pack_rpb(rpb):
    kc = np.arange(64)[:, None]
    qc = np.arange(64)[None, :]
    idx = np.clip(kc - qc + 15, 0, 30)
    g = rpb[:, :, :, idx]
    g = np.transpose(g, (0, 3, 1, 2, 4))
    lo = g
    hi = np.concatenate([g[:, :, :, 1:, :], g[:, :, :, 14:15, :]], axis=3)
    return np.ascontiguousarray(np.concatenate([lo, hi], axis=1)).reshape(NL, 128, 8 * 15 * 64)


class Builder:
    def __init__(self, n_layers=NL, debug=False):
        self.n_layers = n_layers
        self.debug = debug
        self.nc = bass.Bass("TRN2", target_bir_lowering=False)
        nc = self.nc
        di = lambda n, s, dt=F32: nc.dram_tensor(n, s, dt, kind="ExternalInput").ap()
        self.xT = di("xT", [DM, SEQ])
        self.w_in = di("w_in", [NL, DM, N_IN])
        self.w_branch = di("w_branch", [NL, 4, 512, DM])
        self.w_out = di("w_out", [NL, DM, DM])
        self.w_up = di("w_up", [NL, DM, 2 * D_FF])
        self.w_down = di("w_down", [NL, D_FF, DM])
        self.params = di("params", [128, NL * P_N])
        self.lam = di("lam", [1, NL * 256])
        self.rpbE = di("rpbE", [NL, 128, 8 * 15 * 64])
        self.c_tab1 = di("tab1", [128, 2 * SEQ])
        self.c_tab2 = di("tab2", [128, 2 * SEQ])
        self.c_mats = di("mats", [128, 5 * 128])
        self.c_mask3 = di("mask3", [128, 3 * 128])
        self.c_colmask = di("colmask", [128, 64])
        self.yT = nc.dram_tensor("yT", [DM, SEQ], F32, kind="ExternalOutput").ap()
        kind = "ExternalOutput" if debug else "Internal"
        ds = lambda n, s, dt: nc.dram_tensor(n, s, dt, kind=kind).ap()
        self.QK = ds("s_qk", [45 * 128, SEQ], BF16)
        self.VA = ds("s_va", [SEQ, 512], BF16)
        self.VB = ds("s_vb", [SEQ, 130], BF16)
        self.VC = ds("s_vc", [3, SEQ, 520], BF16)
        self.VD = ds("s_vd", [SEQ, 520], BF16)
        self.G = ds("s_g", [4096, SEQ], F32)
        self.OT = ds("s_ot", [4, 512, SEQ], BF16)
        self.M = ds("s_m", [D_FF, SEQ], BF16)
        self.XM = ds("s_xm", [DM, SEQ], F32)
        self.XA = ds("s_xa", [DM, SEQ], F32)
        self.XB = ds("s_xb", [DM, SEQ], F32)

    def tile(self, name, shape, dt):
        t = self.ph.enter_context(self.nc.sbuf_tensor(name + "_%d" % self.uid, shape, dt))
        self.uid += 1
        return Tl(t, name)

    def ptile(self, name, shape=(128, 512), dt=F32):
        t = self.ph.enter_context(self.nc.psum_tensor(name + "_%d" % self.uid, [128, 512], F32))
        self.uid += 1
        return Tl(t, name, True)

    def A(self, eng, meth, reads, writes, *a, **k):
        wr = [x.b for x in writes] + [x.b for x in reads if x.psum]
        self.S.op(eng, lambda e: getattr(e, meth)(*a, **k), [x.b for x in reads], wr)

    def load(self, dst_tl, out_ap, in_ap, q="sp"):
        self.S.op(q, lambda e: e.dma_start(out=out_ap, in_=in_ap), [], [dst_tl.b], dma=("L", dst_tl.name))

    def store(self, src_tl, out_ap, in_ap, q="pool"):
        self.S.op(q, lambda e: e.dma_start(out=out_ap, in_=in_ap), [src_tl.b], [], dma=("S", src_tl.name))

    def mm(self, out_tl, out_ap, l_tl, l_ap, r_tl, r_ap, start, stop):
        self.S.op("pe", lambda e: e.matmul(out_ap, l_ap, r_ap, start=start, stop=stop),
                  [l_tl.b, r_tl.b], [out_tl.b])

    def begin(self):
        self.ph = ExitStack()
        self.ph.__enter__()

    def end(self):
        self.S.flush()
        self.ph.__exit__(None, None, None)

    def build(self):
        nc = self.nc
        self.uid = 0
        with ExitStack() as top:
            self.S = Sched(nc, top)
            self.top = top
            self.ph = top
            self.mats = self.tile("mats", [128, 5, 128], F32)
            self.onesb = self.tile("onesb", [128, 128], BF16)
            self.par = self.tile("par", [128, NL, P_N], F32)
            self.nlam = self.tile("nlam", [128, NL], F32)
            self.begin()
            self.lamt = self.tile("lamt", [1, NL * 256], F32)
            self.load(self.mats, self.mats[:].rearrange("p a b -> p (a b)"), self.c_mats)
            self.load(self.par, self.par[:].rearrange("p a b -> p (a b)"), self.params)
            self.load(self.lamt, self.lamt[:], self.lam)
            self.A("dve", "tensor_copy", [self.mats], [self.onesb], out=self.onesb[:], in_=self.mats[:, 3, :])
            self.lambda_setup()
            self.end()
            xin = self.xT
            for l in range(self.n_layers):
                last = (l == self.n_layers - 1)
                xout = self.yT if last else (self.XA if l % 2 == 0 else self.XB)
                self.layer(l, xin, xout)
                xin = xout
        return nc

    def lambda_setup(self):
        import math
        pr = self.tile("lampr", [1, NL * 2, 64], F32)
        sm = self.tile("lamsm", [1, NL * 2], F32)
        lv = self.tile("lamv", [1, NL], F32)
        ps = self.ptile("lamps", (128, NL))
        lt = self.lamt[:].rearrange("p (l a d) -> p l a d", l=NL, a=4)
        for l in range(NL):
            for j in range(2):
                self.A("dve", "tensor_tensor", [self.lamt], [pr], out=pr[:, l * 2 + j, :], in0=lt[:, l, 2 * j, :],
                       in1=lt[:, l, 2 * j + 1, :], op=ALU.mult)
        self.A("dve", "tensor_reduce", [pr], [sm], out=sm[:], in_=pr[:], axis=AX.X, op=ALU.add)
        self.A("act", "activation", [sm], [sm], out=sm[:], in_=sm[:], func=AF.Exp)
        smv = sm[:].rearrange("p (l j) -> p l j", j=2)
        for l in range(NL):
            li = 0.8 - 0.6 * math.exp(-0.3 * l)
            self.A("dve", "scalar_tensor_tensor", [sm], [lv], out=lv[:, l:l + 1], in0=smv[:, l, 1:2], scalar=-li,
                   in1=smv[:, l, 0:1], op0=ALU.add, op1=ALU.subtract)
        self.mm(ps, ps[:, 0:NL], self.mats, self.mats[0:1, 3, :], lv, lv[:], True, True)
        self.A("dve", "tensor_copy", [ps], [self.nlam], out=self.nlam[:], in_=ps[:, 0:NL])

    def rmsnorm_to_hT(self, l, xsrc, hT, goff):
        xs = [self.tile("nx%d" % i, [128, 8, TC], F32) for i in range(2)]
        sq = [self.tile("nsq%d" % i, [128, 8, TC], F32) for i in range(1)]
        rs = [self.tile("nrs%d" % i, [128, TC], F32) for i in range(2)]
        ps = [self.ptile("nps%d" % i) for i in range(2)]
        xv = xsrc.rearrange("(c p) n -> p c n", p=128)
        for t in range(NT):
            x = xs[t % 2]
            s = sq[0]
            r = rs[t % 2]
            p = ps[t % 2]
            self.load(x, x[:], xv[:, :, t * TC:(t + 1) * TC])
            self.A("pool", "tensor_tensor", [x], [s], out=s[:], in0=x[:], in1=x[:], op=ALU.mult)
            for c in range(8):
                self.mm(p, p[:], self.mats, self.mats[:, 3, :], s, s[:, c, :], c == 0, c == 7)
            self.A("act", "activation", [p], [r], out=r[:], in_=p[:], func=AF.Ln, scale=1.0 / DM, bias=self.epsc[:, 0:1])
            self.A("act", "activation", [r], [r], out=r[:], in_=r[:], func=AF.Exp, scale=-0.5)
            for c in range(8):
                self.A("dve", "scalar_tensor_tensor", [x, r, self.par], [hT], out=hT[:, c, t * TC:(t + 1) * TC],
                       in0=x[:, c, :], scalar=self.par[:, l, goff + c:goff + c + 1], in1=r[:], op0=ALU.mult, op1=ALU.mult)

    def load_weight_bf16(self, dst, dst_ap, src_ap, stg, shape_ap, eng):
        self.load(stg, shape_ap, src_ap)
        self.A(eng, "tensor_copy", [stg], [dst], out=dst_ap, in_=shape_ap)

    def layer(self, l, xin, xout):
        import math
        self.lambda_init = 0.8 - 0.6 * math.exp(-0.3 * l)
        self.begin()
        self.epsc = self.tile("epsc", [128, 1], F32)
        self.A("pool", "memset", [], [self.epsc], self.epsc[:], EPS)
        hT = self.tile("hT", [128, 8, SEQ], BF16)
        sub = ExitStack()
        outer = self.ph
        self.ph = sub
        sub.__enter__()
        self.rmsnorm_to_hT(l, xin, hT, P_G1)
        self.S.flush()
        sub.__exit__(None, None, None)
        self.ph = outer
        self.proj(l, hT)
        self.end()
        self.begin()
        self.epsc = self.tile("epsc", [128, 1], F32)
        self.A("pool", "memset", [], [self.epsc], self.epsc[:], EPS)
        self.attn_a(l)
        self.end()
        self.begin()
        self.attn_b(l)
        self.end()
        self.begin()
        self.attn_c(l)
        self.end()
        self.begin()
        self.attn_d(l)
        self.end()
        self.begin()
        self.merge(l, xin)
        self.end()
        self.begin()
        self.epsc = self.tile("epsc", [128, 1], F32)
        self.A("pool", "memset", [], [self.epsc], self.epsc[:], EPS)
        hT = self.tile("hT2", [128, 8, SEQ], BF16)
        sub = ExitStack()
        outer = self.ph
        self.ph = sub
        sub.__enter__()
        self.rmsnorm_to_hT(l, self.XM, hT, P_G2)
        self.S.flush()
        sub.__exit__(None, None, None)
        self.ph = outer
        self.ffn_up(l, hT)
        self.end()
        self.begin()
        self.ffn_down(l, xout)
        self.end()

    def pipe(self, items, lags):
        n = len(items)
        for j in range(n + max(lags)):
            for si, lg in enumerate(lags):
                i = j - lg
                if 0 <= i < n and len(items[i]) > si and items[i][si] is not None:
                    items[i][si]()

    def proj(self, l, hT):
        wst = self.tile("wst0", [128, 8, 512], F32)
        wbf = [self.tile("wbf%d" % i, [128, 8, 512], BF16) for i in range(2)]
        tab = self.tile("tab", [128, 2, SEQ], F32)
        pacc = [self.ptile("pacc%d" % i) for i in range(3)]
        pss = [self.ptile("pss%d" % i) for i in range(2)]
        prot = [self.ptile("prot%d" % i) for i in range(2)]
        sq = [self.tile("sq%d" % i, [128, TC], F32) for i in range(3)]
        rs = [self.tile("rs%d" % i, [128, TC], F32) for i in range(3)]
        yy = [self.tile("yy%d" % i, [128, TC], F32) for i in range(3)]
        t2 = [self.tile("t2%d" % i, [128, TC], F32) for i in range(3)]
        qo = [self.tile("qo%d" % i, [128, TC], BF16) for i in range(3)]
        rowb = [self.tile("rowb%d" % i, [128, SEQ], BF16) for i in range(2)]
        go = [self.tile("go%d" % i, [128, TC], F32) for i in range(3)]
        vst = [self.tile("vst%d" % i, [128, 4, 520], BF16) for i in range(2)]
        vsta = [self.tile("vsta%d" % i, [128, 4, 512], BF16) for i in range(2)]
        for v in vst:
            self.A("pool", "memset", [], [v], v[:], 1.0)
        wv = self.w_in[l].rearrange("(k p) n -> p k n", p=128)
        cnt = dict(acc=0, q=0, row=0, go=0, vs=0, tmp=0, pp=0)
        G = P_QKG

        def wload(gidx, col0, w):
            wb = wbf[gidx % 2]
            self.load(wst, wst[:, :, 0:w], wv[:, :, col0:col0 + w])
            self.A("pool" if gidx % 2 == 0 else "dve", "tensor_copy", [wst], [wb], out=wb[:, :, 0:w], in_=wst[:, :, 0:w])

        def qk_items(items, wb, off, row, gcol, rope, r):
            mat = 0 if rope == "1d" else 1
            rb = None
            if r > 1:
                rb = rowb[cnt["row"] % 2]
                cnt["row"] += 1
            for t in range(NT):
                pa = pacc[cnt["acc"] % 3]
                cnt["acc"] += 1
                j = cnt["tmp"] % 3
                cnt["tmp"] += 1
                ps_, pr_ = pss[cnt["pp"] % 2], prot[cnt["pp"] % 2]
                cnt["pp"] += 1
                s, rr, y, a2 = sq[j], rs[j], yy[j], t2[j]
                tsl = slice(t * TC, (t + 1) * TC)
                if r > 1:
                    n_ = TC // r
                    dst_tl = rb
                    dst = rb[:].rearrange("p (c i) -> p c i", c=r)[:, :, t * n_:(t + 1) * n_]
                else:
                    dst_tl = qo[cnt["q"] % 3]
                    cnt["q"] += 1
                    dst = dst_tl[:]

                def s0(pa=pa, s=s, tsl=tsl):
                    for k in range(8):
                        self.mm(pa, pa[:], wb, wb[:, k, off:off + 128], hT, hT[:, k, tsl], k == 0, k == 7)
                    self.A("act", "activation", [pa], [s], out=s[:], in_=pa[:], func=AF.Square)

                def s1(pa=pa, s=s, rr=rr, y=y, ps_=ps_, dst_tl=dst_tl, dst=dst):
                    self.mm(ps_, ps_[:], self.mats, self.mats[:, 2, :], s, s[:], True, True)
                    self.A("act", "activation", [ps_], [rr], out=rr[:], in_=ps_[:], func=AF.Ln, scale=1.0 / 64, bias=self.epsc[:, 0:1])
                    self.A("act", "activation", [rr], [rr], out=rr[:], in_=rr[:], func=AF.Exp, scale=-0.5)
                    if rope is None:
                        self.A("dve", "scalar_tensor_tensor", [pa, rr, self.par], [dst_tl], out=dst, in0=pa[:],
                               scalar=self.par[:, l, gcol:gcol + 1], in1=rr[:], op0=ALU.mult, op1=ALU.mult)
                    else:
                        self.A("dve", "scalar_tensor_tensor", [pa, rr, self.par], [y], out=y[:], in0=pa[:],
                               scalar=self.par[:, l, gcol:gcol + 1], in1=rr[:], op0=ALU.mult, op1=ALU.mult)

                def s2(y=y, a2=a2, pr_=pr_, dst_tl=dst_tl, dst=dst, tsl=tsl, t=t):
                    if rope is not None:
                        self.mm(pr_, pr_[:], self.mats, self.mats[:, mat, :], y, y[:], True, True)
                        self.A("dve", "tensor_tensor", [pr_, tab], [a2], out=a2[:], in0=pr_[:], in1=tab[:, 1, tsl], op=ALU.mult)
                        self.A("pool", "tensor_tensor", [y, tab], [y], out=y[:], in0=y[:], in1=tab[:, 0, tsl], op=ALU.mult)
                        if r > 1:
                            src1 = y[:].rearrange("p (i c) -> p c i", c=r)
                            src2 = a2[:].rearrange("p (i c) -> p c i", c=r)
                        else:
                            src1, src2 = y[:], a2[:]
                        self.A("pool", "tensor_tensor", [y, a2], [dst_tl], out=dst, in0=src1, in1=src2, op=ALU.add)
                    if r == 1:
                        self.store(dst_tl, self.QK[row * 128:(row + 1) * 128, tsl], dst_tl[:])
                    elif t == NT - 1:
                        self.store(rb, self.QK[row * 128:(row + 1) * 128, :], rb[:])

                items.append([s0, s1, s2])

        def v_items(items, wb, off, w, dst, r, nh, hd):
            hv = hT[:].rearrange("p k (i c) -> p k c i", c=r)
            L = SEQ // r
            stride = hd + 1 if hd == 64 else hd
            vs = None
            for tt in range(32):
                c = (tt * 128) // L
                i0 = (tt * 128) % L
                pa = pacc[cnt["acc"] % 3]
                cnt["acc"] += 1
                if tt % 4 == 0:
                    vs = (vst if hd == 64 else vsta)[cnt["vs"] % 2]
                    cnt["vs"] += 1

                def s0(pa=pa, c=c, i0=i0):
                    for k in range(8):
                        self.mm(pa, pa[:, 0:w], hT, hv[:, k, c, i0:i0 + 128], wb, wb[:, k, off:off + w], k == 0, k == 7)

                def s1(pa=pa, vs=vs, tt=tt):
                    o = vs[:, tt % 4, 0:nh * stride].rearrange("p (h d) -> p h d", h=nh)[:, :, 0:hd]
                    i_ = pa[:, 0:w].rearrange("p (h d) -> p h d", h=nh)
                    if tt % 2 == 0:
                        self.A("act", "activation", [pa], [vs], out=o, in_=i_, func=AF.Copy)
                    else:
                        self.A("dve", "tensor_copy", [pa], [vs], out=o, in_=i_)
                    if tt % 4 == 3:
                        t0 = (tt - 3) * 128
                        self.store(vs, dst[t0:t0 + 512, :].rearrange("(a p) n -> p a n", p=128), vs[:, :, 0:nh * stride])

                items.append([s0, s1])

        def gate_items(items, wb, off, grow):
            for t in range(NT):
                pa = pacc[cnt["acc"] % 3]
                cnt["acc"] += 1
                g = go[cnt["go"] % 3]
                cnt["go"] += 1
                tsl = slice(t * TC, (t + 1) * TC)

                def s0(pa=pa, tsl=tsl):
                    for k in range(8):
                        self.mm(pa, pa[:], wb, wb[:, k, off:off + 128], hT, hT[:, k, tsl], k == 0, k == 7)

                def s1(pa=pa, g=g, tsl=tsl):
                    self.A("act", "activation", [pa], [g], out=g[:], in_=pa[:], func=AF.Sigmoid)
                    self.store(g, self.G[grow * 128:(grow + 1) * 128, tsl], g[:])

                items.append([s0, s1])

        jobsB = [
            (COL["bq"], 512, [("qk", i * 128, 8 + i, G + 2, "ax", 1) for i in range(4)]),
            (COL["bk"], 256, [("qk", 0, 12, G + 3, "ax", 1), ("v", 128, 128, self.VB, 1, 2, 64)]),
        ]
        jobs = [
            (COL["aq"], 512, [("qk", i * 128, 0 + i, G + 0, "1d", 1) for i in range(4)]),
            (COL["ak"], 512, [("qk", i * 128, 4 + i, G + 1, "1d", 1) for i in range(4)]),
            (COL["av"], 512, [("v", 0, 512, self.VA, 1, 4, 128)]),
        ]
        for g in range(3):
            jobs.append((COL["cq"] + g * 512, 512, [("qk", i * 128, 13 + g * 4 + i, G + 4, "1d", DIL[g]) for i in range(4)]))
            jobs.append((COL["ck"] + g * 512, 512, [("qk", i * 128, 25 + g * 4 + i, G + 5, "1d", DIL[g]) for i in range(4)]))
            jobs.append((COL["cv"] + g * 512, 512, [("v", 0, 512, self.VC[g], DIL[g], 8, 64)]))
        jobs.append((COL["dq"], 512, [("qk", i * 128, 37 + i, G + 6, None, 1) for i in range(4)]))
        jobs.append((COL["dk"], 512, [("qk", i * 128, 41 + i, G + 7, None, 1) for i in range(4)]))
        jobs.append((COL["dv"], 512, [("v", 0, 512, self.VD, 1, 8, 64)]))
        for gi in range(8):
            jobs.append((COL["g"] + gi * 512, 512, [("gate", i * 128, gi * 4 + i) for i in range(4)]))

        gctr = [0]

        def run_jobs(jl):
            items = []
            gid0 = gctr[0]
            for ji, (col0, w, subs) in enumerate(jl):
                gid = gid0 + ji
                wb = wbf[gid % 2]
                first = len(items)
                for sj in subs:
                    if sj[0] == "qk":
                        qk_items(items, wb, *sj[1:])
                    elif sj[0] == "v":
                        v_items(items, wb, *sj[1:])
                    else:
                        gate_items(items, wb, *sj[1:])
                orig = items[first][0]
                nxt = jl[ji + 1] if ji + 1 < len(jl) else None

                def s0w(orig=orig, nxt=nxt, gid=gid):
                    if nxt is not None:
                        wload(gid + 1, nxt[0], nxt[1])
                    orig()
                items[first][0] = s0w
            wload(gid0, jl[0][0], jl[0][1])
            gctr[0] += len(jl)
            self.pipe(items, [0, 1, 2])

        self.load(tab, tab[:].rearrange("p a n -> p (a n)"), self.c_tab2)
        run_jobs(jobsB)
        self.load(tab, tab[:].rearrange("p a n -> p (a n)"), self.c_tab1)
        run_jobs(jobs)

    def normalize_rows65(self, po, res_tl, res_ap, n, tmp_row, tmp_bc, pbc):
        self.A("dve", "reciprocal", [po], [tmp_row], out=tmp_row[64:65, 0:n], in_=po[64:65, 0:n])
        self.mm(pbc, pbc[0:64, 0:n], self.mats, self.mats[64:65, 3, 0:64], tmp_row, tmp_row[64:65, 0:n], True, True)
        self.A("act", "activation", [pbc], [tmp_bc], out=tmp_bc[0:64, 0:n], in_=pbc[0:64, 0:n], func=AF.Copy)
        self.A("dve", "tensor_tensor", [po, tmp_bc], [res_tl], out=res_ap, in0=po[0:64, 0:n], in1=tmp_bc[0:64, 0:n], op=ALU.mult)

    def attn_a(self, l):
        qz = [self.tile("qz%d" % c, [128, 4, SEQ], BF16) for c in range(2)]
        ka = self.tile("ka", [128, 4, SEQ], BF16)
        va = self.tile("va", [128, 32, 512], BF16)
        for c in range(2):
            oc = 1 - c
            self.A("pool", "memset", [], [qz[c]], qz[c][oc * 64:(oc + 1) * 64, :, :], 0.0)
        for h in range(4):
            for c in range(2):
                self.load(qz[c], qz[c][c * 64:(c + 1) * 64, h, :], self.QK[h * 128 + c * 64:h * 128 + (c + 1) * 64, :])
            self.load(ka, ka[:, h, :], self.QK[(4 + h) * 128:(5 + h) * 128, :])
        for j in range(4):
            self.load(va, va[:, j * 8:(j + 1) * 8, :], self.VA[j * 1024:(j + 1) * 1024, :].rearrange("(a p) n -> p a n", p=128))
        psc = [self.ptile("psc%d" % i) for i in range(3)]
        po = [[self.ptile("po%d%d" % (c, i)) for i in range(2)] for c in range(2)]
        pfin = self.ptile("pfin")
        pt = [self.tile("pt%d" % i, [128, TC], BF16) for i in range(6)]
        sacc = [[self.tile("sacc%d%d" % (c, i), [128, TC], F32) for i in range(2)] for c in range(3)]
        sbf = self.tile("sbf", [128, TC], BF16)
        rc = self.tile("rc", [128, TC], F32)
        res = [self.tile("res%d" % i, [128, TC], F32) for i in range(2)]
        dd = self.tile("dd", [128, TC], F32)
        sq = self.tile("sqa", [128, TC], F32)
        rs = self.tile("rsa", [128, TC], F32)
        ob = [self.tile("oba%d" % i, [128, TC], BF16) for i in range(2)]
        items = []
        n = 0
        gi = 0
        for h in range(4):
            for qc in range(NT):
                qs = slice(qc * TC, (qc + 1) * TC)
                buf = gi % 2
                gi += 1
                for kt in range(32):
                    for c in range(2):
                        ps_ = slice(c * 64, (c + 1) * 64)
                        sc = psc[n % 3]
                        p = pt[n % 6]
                        n += 1
                        po_ = po[c][buf]

                        def s0(sc=sc, p=p, h=h, qs=qs, c=c, kt=kt):
                            self.mm(sc, sc[:], ka, ka[:, h, kt * 128:(kt + 1) * 128], qz[c], qz[c][:, h, qs], True, True)
                            self.A("act", "activation", [sc], [p], out=p[:], in_=sc[:], func=AF.Exp, scale=0.125)

                        def s1(p=p, h=h, c=c, kt=kt, po_=po_, buf=buf):
                            self.mm(po_, po_[:], va, va[:, kt, h * 128:(h + 1) * 128], p, p[:], kt == 0, kt == 31)
                            if c == 0:
                                a, eng, first = sacc[0][buf], "dve", kt == 0
                            elif kt % 2 == 0:
                                a, eng, first = sacc[1][buf], "dve", kt == 0
                            else:
                                a, eng, first = sacc[2][buf], "pool", kt == 1
                            if first:
                                self.A(eng, "tensor_copy", [p], [a], out=a[:], in_=p[:])
                            else:
                                self.A(eng, "tensor_tensor", [a, p], [a], out=a[:], in0=a[:], in1=p[:], op=ALU.add)

                        def s2(h=h, qc=qc, qs=qs, c=c, kt=kt, po_=po_, buf=buf):
                            if kt != 31:
                                return
                            parts = [sacc[0][buf]] if c == 0 else [sacc[1][buf], sacc[2][buf]]
                            for pi, a in enumerate(parts):
                                self.A("pool", "tensor_copy", [a], [sbf], out=sbf[:], in_=a[:])
                                self.mm(pfin, pfin[:], self.onesb, self.onesb[:], sbf, sbf[:], pi == 0, pi == len(parts) - 1)
                            self.A("dve", "reciprocal", [pfin], [rc], out=rc[:], in_=pfin[:])
                            self.A("dve", "tensor_tensor", [po_, rc], [res[c]], out=res[c][:], in0=po_[:], in1=rc[:], op=ALU.mult)
                            if c != 1:
                                return
                            self.A("dve", "scalar_tensor_tensor", [res[0], res[1], self.nlam], [dd], out=dd[:], in0=res[1][:],
                                   scalar=self.nlam[:, l:l + 1], in1=res[0][:], op0=ALU.mult, op1=ALU.add)
                            self.A("pool", "tensor_tensor", [dd], [sq], out=sq[:], in0=dd[:], in1=dd[:], op=ALU.mult)
                            self.mm(pfin, pfin[:], self.mats, self.mats[:, 3, :], sq, sq[:], True, True)
                            self.A("act", "activation", [pfin], [rs], out=rs[:], in_=pfin[:], func=AF.Sqrt, scale=1.0 / 128, bias=self.epsc[:, 0:1])
                            self.A("dve", "reciprocal", [rs], [rs], out=rs[:], in_=rs[:])
                            self.A("dve", "scalar_tensor_tensor", [dd, rs, self.par], [dd], out=dd[:], in0=dd[:],
                                   scalar=self.par[:, l, P_SUB:P_SUB + 1], in1=rs[:], op0=ALU.mult, op1=ALU.mult)
                            o = ob[(h * NT + qc) % 2]
                            self.A("act", "activation", [dd], [o], out=o[:], in_=dd[:], func=AF.Copy, scale=float(1.0 - self.lambda_init))
                            self.store(o, self.OT[0, h * 128:(h + 1) * 128, qs], o[:])

                        items.append([s0, s1, s2])
        self.pipe(items, [0, 3, 12])

    def attn_b(self, l):
        qb = self.tile("qb", [128, 8, SEQ], BF16)
        kb = self.tile("kb", [128, SEQ], BF16)
        vb = self.tile("vb", [128, 32, 130], BF16)
        for hq in range(8):
            og = 1 - hq // 4
            self.A("pool", "memset", [], [qb], qb[og * 64:(og + 1) * 64, hq, :], 0.0)
        for hq in range(8):
            g, s = hq // 4, hq % 4
            self.load(qb, qb[g * 64:(g + 1) * 64, hq, :], self.QK[8 * 128 + hq * 64:8 * 128 + (hq + 1) * 64, :])
        self.load(kb, kb[:], self.QK[12 * 128:13 * 128, :])
        self.load(vb, vb[:], self.VB.rearrange("(a p) n -> p a n", p=128))
        psc = [self.ptile("psc%d" % i) for i in range(3)]
        po = [[self.ptile("pob%d%d" % (g, i)) for i in range(2)] for g in range(2)]
        pbc = self.ptile("pbc")
        pt = [self.tile("pt%d" % i, [128, TC], BF16) for i in range(6)]
        trow = self.tile("trow", [128, TC], F32)
        tbc = self.tile("tbc", [128, TC], F32)
        ob = [self.tile("obb%d" % i, [128, TC], BF16) for i in range(2)]
        items = []
        n = 0
        m = 0
        gi = 0
        for s in range(4):
            for qc in range(NT):
                qs = slice(qc * TC, (qc + 1) * TC)
                buf = gi % 2
                gi += 1
                for kt in range(32):
                    for g in range(2):
                        hq = g * 4 + s
                        ps_ = slice(g * 64, (g + 1) * 64)
                        o_ps = po[g][buf]
                        sc = psc[n % 3]
                        p = pt[n % 6]
                        n += 1

                        def s0(sc=sc, p=p, hq=hq, qs=qs, kt=kt):
                            self.mm(sc, sc[:], kb, kb[:, kt * 128:(kt + 1) * 128], qb, qb[:, hq, qs], True, True)
                            self.A("act", "activation", [sc], [p], out=p[:], in_=sc[:], func=AF.Exp, scale=0.125)

                        def s1(p=p, g=g, kt=kt, o_ps=o_ps):
                            self.mm(o_ps, o_ps[0:65, :], vb, vb[:, kt, g * 65:(g + 1) * 65], p, p[:], kt == 0, kt == 31)

                        def s2(g=g, hq=hq, qs=qs, kt=kt, o_ps=o_ps):
                            if kt != 31:
                                return
                            o = ob[hq % 2]
                            self.normalize_rows65(o_ps, o, o[0:64, :], TC, trow, tbc, pbc)
                            self.store(o, self.OT[1, hq * 64:(hq + 1) * 64, qs], o[0:64, :])

                        items.append([s0, s1, s2])
        self.pipe(items, [0, 3, 12])

    def attn_c(self, l):
        mask = self.tile("mask3", [128, 3, 128], F32)
        self.load(mask, mask[:].rearrange("p a b -> p (a b)"), self.c_mask3)
        qc_ = self.tile("qc", [128, 3, SEQ], BF16)
        kc_ = self.tile("kc", [128, 3, SEQ], BF16)
        vc_ = self.tile("vc", [128, 3, 32, 130], BF16)
        acc = [self.tile("acc%d" % i, [65, SEQ], F32) for i in range(2)]
        psc = [self.ptile("pscc%d" % i, (128, 384)) for i in range(3)]
        po = [self.ptile("poc%d" % i) for i in range(2)]
        pbc = self.ptile("pbcc")
        ex = [self.tile("ex%d" % i, [128, 3, 128], F32) for i in range(3)]
        pt = [self.tile("ptc%d" % i, [128, 3, 128], BF16) for i in range(4)]
        trow = self.tile("trowc", [128, TC], F32)
        ob = [self.tile("obc%d" % i, [128, TC], BF16) for i in range(2)]
        n = 0
        m = 0
        for jp in range(4):
            for g in range(3):
                self.load(qc_, qc_[:, g, :], self.QK[(13 + g * 4 + jp) * 128:(14 + g * 4 + jp) * 128, :])
                self.load(kc_, kc_[:, g, :], self.QK[(25 + g * 4 + jp) * 128:(26 + g * 4 + jp) * 128, :])
                self.load(vc_, vc_[:, g, :, :], self.VC[g].rearrange("(a p) n -> p a n", p=128)[:, :, jp * 130:(jp + 1) * 130])
            items = []
            for hh in range(2):
                ps_ = slice(hh * 64, (hh + 1) * 64)
                ac = acc[hh]
                hd = jp * 2 + hh
                for g in range(3):
                    r = DIL[g]
                    L = SEQ // r
                    tps = L // 128
                    for qb4 in range(8):
                        o_ps = po[m % 2]
                        m += 1
                        for u in range(4):
                            qb = qb4 * 4 + u
                            seg = qb // tps
                            kts = [k for k in (qb - 1, qb, qb + 1) if k // tps == seg and 0 <= k < 32]
                            j0 = kts[0] - (qb - 1)
                            nk = len(kts)
                            sc = psc[n % 3]
                            e_ = ex[n % 3]
                            p = pt[n % 4]
                            n += 1

                            def s0(sc=sc, e_=e_, p=p, kts=kts, j0=j0, nk=nk, qb=qb, g=g, ps_=ps_):
                                for k in kts:
                                    j = k - (qb - 1)
                                    self.mm(sc, sc[:, j * 128:(j + 1) * 128], kc_, kc_[ps_, g, k * 128:(k + 1) * 128],
                                            qc_, qc_[ps_, g, qb * 128:(qb + 1) * 128], True, True)
                                scv = sc[:, 0:384].rearrange("p (a b) -> p a b", a=3)
                                self.A("act", "activation", [sc], [e_], out=e_[:, j0:j0 + nk, :], in_=scv[:, j0:j0 + nk, :], func=AF.Exp, scale=0.125)
                                self.A("pool", "tensor_tensor", [e_, mask], [p], out=p[:, j0:j0 + nk, :], in0=e_[:, j0:j0 + nk, :],
                                       in1=mask[:, j0:j0 + nk, :], op=ALU.mult)

                            def s1(p=p, kts=kts, nk=nk, qb=qb, g=g, hh=hh, u=u, o_ps=o_ps, qb4=qb4, r=r, L=L, ac=ac, hd=hd):
                                for ki, k in enumerate(kts):
                                    j = k - (qb - 1)
                                    self.mm(o_ps, o_ps[0:65, u * 128:(u + 1) * 128], vc_, vc_[:, g, k, hh * 65:(hh + 1) * 65],
                                            p, p[:, j, :], ki == 0, ki == nk - 1)
                                if u != 3:
                                    return
                                pos0 = qb4 * 512
                                av = ac[:].rearrange("p (i c) -> p c i", c=r)
                                if L >= 512:
                                    c0, i0 = pos0 // L, pos0 % L
                                    dst = av[0:65, c0:c0 + 1, i0:i0 + 512]
                                    src = o_ps[0:65, :].rearrange("p (c i) -> p c i", c=1)
                                else:
                                    ncl = 512 // L
                                    c0 = pos0 // L
                                    dst = av[0:65, c0:c0 + ncl, :]
                                    src = o_ps[0:65, :].rearrange("p (c i) -> p c i", c=ncl)
                                if g == 0:
                                    self.A("act", "activation", [o_ps], [ac], out=dst, in_=src, func=AF.Copy)
                                else:
                                    self.A("dve", "tensor_tensor", [o_ps, ac], [ac], out=dst, in0=dst, in1=src, op=ALU.add)
                                if g == 2 and qb4 == 7:
                                    for t in range(NT):
                                        o = ob[(hd * NT + t) % 2]
                                        ts_ = slice(t * TC, (t + 1) * TC)
                                        self.A("dve", "reciprocal", [ac], [trow], out=trow[64:65, :], in_=ac[64:65, ts_])
                                        self.mm(pbc, pbc[0:64, :], self.mats, self.mats[64:65, 3, 0:64], trow, trow[64:65, :], True, True)
                                        self.A("dve", "tensor_tensor", [pbc, ac], [o], out=o[0:64, :], in0=pbc[0:64, :], in1=ac[0:64, ts_], op=ALU.mult)
                                        self.store(o, self.OT[2, hd * 64:(hd + 1) * 64, ts_], o[0:64, :])

                            items.append([s0, s1])
            self.pipe(items, [0, 2])

    def attn_d(self, l):
        qd = self.tile("qd", [128, 4, SEQ], BF16)
        kd = self.tile("kd", [128, 4, SEQ], BF16)
        ve = self.tile("ve", [128, 32, 520], BF16)
        vo = self.tile("vo", [128, 31, 520], BF16)
        E = self.tile("E", [128, 8, 15, 64], F32)
        cm = self.tile("cm", [128, 64], F32)
        self.load(cm, cm[:], self.c_colmask)
        for i in range(4):
            self.load(qd, qd[:, i, :], self.QK[(37 + i) * 128:(38 + i) * 128, :])
            self.load(kd, kd[:, i, :], self.QK[(41 + i) * 128:(42 + i) * 128, :])
        for j in range(4):
            self.load(ve, ve[:, j * 8:(j + 1) * 8, :], self.VD[j * 1024:(j + 1) * 1024, :].rearrange("(a p) n -> p a n", p=128))
        for j in range(4):
            na = 8 if j < 3 else 7
            self.load(vo, vo[:, j * 8:j * 8 + na, :], self.VD[64 + j * 1024:64 + j * 1024 + na * 128, :].rearrange("(a p) n -> p a n", p=128))
        for h in range(8):
            self.load(E, E[:, h, :, :].rearrange("p a b -> p (a b)"), self.rpbE[l][:, h * 960:(h + 1) * 960])
        for h in range(8):
            self.A("act", "activation", [E], [E], out=E[:, h, :, :], in_=E[:, h, :, :], func=AF.Exp)
            self.A("pool", "tensor_tensor", [E, cm], [E], out=E[:, h, :, :], in0=E[:, h, :, :],
                   in1=cm[:].rearrange("p (a b) -> p a b", a=1).to_broadcast([128, 15, 64]), op=ALU.mult)
        psc = [self.ptile("pscd%d" % i, (128, 256)) for i in range(3)]
        po = [self.ptile("pod%d" % i) for i in range(2)]
        pbc = self.ptile("pbcd")
        ex = [self.tile("exd%d" % i, [128, 4, 64], F32) for i in range(3)]
        pt = [self.tile("ptd%d" % i, [128, 4, 64], BF16) for i in range(4)]
        trow = self.tile("trowd", [128, TC], F32)
        tbc = self.tile("tbcd", [128, TC], F32)
        ob = [self.tile("obd%d" % i, [128, TC], BF16) for i in range(2)]
        items = []
        n = 0
        m = 0
        for h in range(8):
            ch, hh = h // 2, h % 2
            ps_ = slice(hh * 64, (hh + 1) * 64)
            for r8 in range(8):
                o_ps = po[m % 2]
                o = ob[m % 2]
                m += 1
                for u in range(8):
                    r = r8 * 8 + u
                    rs_ = min(max(r - 4, 0), 56)
                    base = rs_ - r + 7
                    sc = psc[n % 3]
                    e_ = ex[n % 3]
                    p = pt[n % 4]
                    n += 1

                    def s0(sc=sc, e_=e_, p=p, r=r, rs_=rs_, base=base, h=h, ch=ch, ps_=ps_, n=n):
                        for i in range(4):
                            k0 = (rs_ + 2 * i) * 64
                            self.mm(sc, sc[:, i * 64:(i + 1) * 64], kd, kd[ps_, ch, k0:k0 + 128], qd, qd[ps_, ch, r * 64:(r + 1) * 64], True, True)
                        self.A("act", "activation", [sc], [e_], out=e_[:], in_=sc[:, 0:256].rearrange("p (a b) -> p a b", a=4), func=AF.Exp, scale=0.125)
                        ev = E[:, h, base:base + 7:2, :]
                        self.A("pool" if n % 2 == 0 else "dve", "tensor_tensor", [e_, E], [p], out=p[:], in0=e_[:], in1=ev, op=ALU.mult)

                    def s1(p=p, rs_=rs_, h=h, u=u, o_ps=o_ps, o=o, r8=r8):
                        for i in range(4):
                            row0 = rs_ + 2 * i
                            if row0 % 2 == 0:
                                vt, vi = ve, row0 // 2
                            else:
                                vt, vi = vo, (row0 - 1) // 2
                            self.mm(o_ps, o_ps[0:65, u * 64:(u + 1) * 64], vt, vt[:, vi, h * 65:(h + 1) * 65], p, p[:, i, :], i == 0, i == 3)
                        if u != 7:
                            return
                        self.normalize_rows65(o_ps, o, o[0:64, :], TC, trow, tbc, pbc)
                        self.store(o, self.OT[3, h * 64:(h + 1) * 64, r8 * TC:(r8 + 1) * TC], o[0:64, :])

                    items.append([s0, s1])
        self.pipe(items, [0, 2])

    def merge(self, l, xin):
        wb = self.tile("wbr", [128, 16, DM], BF16)
        wo = self.tile("wo", [128, 8, DM], BF16)
        stg = [self.tile("mstg%d" % i, [128, 4, DM], F32) for i in range(2)]
        n = 0
        for i in range(4):
            s = stg[n % 2]
            n += 1
            self.load(s, s[:], self.w_branch[l, i].rearrange("(k p) n -> p k n", p=128))
            self.A("pool" if n % 2 == 0 else "dve", "tensor_copy", [s], [wb], out=wb[:, i * 4:(i + 1) * 4, :], in_=s[:])
        for j in range(2):
            s = stg[n % 2]
            n += 1
            self.load(s, s[:], self.w_out[l][j * 512:(j + 1) * 512, :].rearrange("(k p) n -> p k n", p=128))
            self.A("pool" if n % 2 == 0 else "dve", "tensor_copy", [s], [wo], out=wo[:, j * 4:(j + 1) * 4, :], in_=s[:])
        ot = [self.tile("mot%d" % i, [128, 16, TC], BF16) for i in range(2)]
        xs = [self.tile("mx%d" % i, [128, 8, TC], F32) for i in range(2)]
        gt = [self.tile("mg%d" % i, [128, TC], F32) for i in range(4)]
        mt = [self.tile("mm%d" % i, [128, 8, TC], BF16) for i in range(2)]
        acc = [self.tile("macc%d" % i, [128, TC], F32) for i in range(2)]
        tmp = [self.tile("mtmp%d" % i, [128, TC], F32) for i in range(2)]
        py = [self.ptile("py%d" % i) for i in range(3)]
        px = [self.ptile("px%d" % i) for i in range(2)]
        xo = [self.tile("mxo%d" % i, [128, TC], F32) for i in range(3)]
        xv = xin.rearrange("(c p) n -> p c n", p=128)
        otv = self.OT.rearrange("b (k p) n -> p (b k) n", p=128)
        ng = 0
        ny = 0
        nx = 0
        for t in range(NT):
            ts_ = slice(t * TC, (t + 1) * TC)
            o = ot[t % 2]
            x = xs[t % 2]
            mtt = mt[t % 2]
            for i in range(4):
                self.load(o, o[:, i * 4:(i + 1) * 4, :], otv[:, i * 4:(i + 1) * 4, ts_])
            self.load(x, x[:], xv[:, :, ts_])
            for f in range(8):
                a = acc[f % 2]
                for i in range(4):
                    g = gt[ng % 4]
                    ng += 1
                    self.load(g, g[:], self.G[(i * 8 + f) * 128:(i * 8 + f + 1) * 128, ts_])
                    p = py[ny % 3]
                    ny += 1
                    for k in range(4):
                        self.mm(p, p[:], wb, wb[:, i * 4 + k, f * 128:(f + 1) * 128], o, o[:, i * 4 + k, :], k == 0, k == 3)
                    if i == 0:
                        self.A("dve", "tensor_tensor", [p, g], [a], out=a[:], in0=p[:], in1=g[:], op=ALU.mult)
                    else:
                        tm = tmp[i % 2]
                        self.A("dve", "tensor_tensor", [p, g], [tm], out=tm[:], in0=p[:], in1=g[:], op=ALU.mult)
                        if i < 3:
                            self.A("pool", "tensor_tensor", [a, tm], [a], out=a[:], in0=a[:], in1=tm[:], op=ALU.add)
                        else:
                            self.A("pool", "tensor_tensor", [a, tm], [mtt], out=mtt[:, f, :], in0=a[:], in1=tm[:], op=ALU.add)
            for fo in range(8):
                p = px[nx % 2]
                xo_ = xo[nx % 3]
                nx += 1
                for k in range(8):
                    self.mm(p, p[:], wo, wo[:, k, fo * 128:(fo + 1) * 128], mtt, mtt[:, k, :], k == 0, k == 7)
                self.A("dve", "tensor_tensor", [p, x], [xo_], out=xo_[:], in0=p[:], in1=x[:, fo, :], op=ALU.add)
                self.store(xo_, self.XM[fo * 128:(fo + 1) * 128, ts_], xo_[:])

    def ffn_up(self, l, hT):
        wst = [self.tile("fst%d" % i, [128, 8, 256], F32) for i in range(2)]
        wbf = [self.tile("fbf%d" % i, [128, 8, 256], BF16) for i in range(4)]
        ua = self.tile("ua", [128, SEQ + 2], F32)
        ug = self.tile("ug", [128, SEQ + 2], F32)
        ca = self.tile("ca", [128, SEQ], F32)
        cg = self.tile("cg", [128, SEQ], F32)
        mrow = [self.tile("mrow%d" % i, [128, SEQ], BF16) for i in range(2)]
        pacc = [self.ptile("fpa%d" % i) for i in range(4)]
        for u in (ua, ug):
            self.A("pool", "memset", [], [u], u[:, 0:1], 0.0)
            self.A("pool", "memset", [], [u], u[:, SEQ + 1:SEQ + 2], 0.0)
        wv = self.w_up[l].rearrange("(k p) n -> p k n", p=128)
        ngrp = 0
        nacc = 0
        for j in range(11):
            w = 256
            wbs = []
            for part in range(2):
                st = wst[ngrp % 2]
                wb = wbf[ngrp % 4]
                ngrp += 1
                c0 = part * D_FF + j * 256
                self.load(st, st[:, :, 0:w], wv[:, :, c0:c0 + w])
                self.A("pool" if part == 0 else "dve", "tensor_copy", [st], [wb], out=wb[:, :, 0:w], in_=st[:, :, 0:w])
                wbs.append(wb)
            for ii in range(w // 128):
                i = j * 2 + ii
                for part, (u, cdst) in enumerate(((ua, ca), (ug, cg))):
                    wb = wbs[part]
                    for t in range(NT):
                        p = pacc[nacc % 4]
                        nacc += 1
                        for k in range(8):
                            self.mm(p, p[:], wb, wb[:, k, ii * 128:(ii + 1) * 128], hT, hT[:, k, t * TC:(t + 1) * TC], k == 0, k == 7)
                        if t % 2 == 0:
                            self.A("act", "activation", [p], [u], out=u[:, 1 + t * TC:1 + (t + 1) * TC], in_=p[:], func=AF.Copy)
                        else:
                            self.A("dve", "tensor_copy", [p], [u], out=u[:, 1 + t * TC:1 + (t + 1) * TC], in_=p[:])
                    ch = part * 22 + i
                    w0 = self.par[:, l, P_CW + 0 * 44 + ch:P_CW + 0 * 44 + ch + 1]
                    w1 = self.par[:, l, P_CW + 1 * 44 + ch:P_CW + 1 * 44 + ch + 1]
                    w2 = self.par[:, l, P_CW + 2 * 44 + ch:P_CW + 2 * 44 + ch + 1]
                    bb = self.par[:, l, P_CB + ch:P_CB + ch + 1]
                    eng = "dve"
                    for hf in range(2):
                        hs = slice(hf * 2048, (hf + 1) * 2048)
                        self.A(eng, "tensor_scalar", [u, self.par], [cdst], out=cdst[:, hs], in0=u[:, hf * 2048:hf * 2048 + 2048],
                               scalar1=w0, scalar2=bb, op0=ALU.mult, op1=ALU.add)
                        self.A(eng, "scalar_tensor_tensor", [u, cdst, self.par], [cdst], out=cdst[:, hs], in0=u[:, 1 + hf * 2048:1 + hf * 2048 + 2048],
                               scalar=w1, in1=cdst[:, hs], op0=ALU.mult, op1=ALU.add)
                        self.A(eng, "scalar_tensor_tensor", [u, cdst, self.par], [cdst], out=cdst[:, hs], in0=u[:, 2 + hf * 2048:2 + hf * 2048 + 2048],
                               scalar=w2, in1=cdst[:, hs], op0=ALU.mult, op1=ALU.add)
                mr = mrow[i % 2]
                for hf in range(2):
                    hs = slice(hf * 2048, (hf + 1) * 2048)
                    self.A("act", "activation", [ca], [ca], out=ca[:, hs], in_=ca[:, hs], func=AF.Silu)
                    self.A("pool" if hf == 0 else "dve", "tensor_tensor", [ca, cg], [mr], out=mr[:, hs], in0=ca[:, hs], in1=cg[:, hs], op=ALU.mult)
                self.store(mr, self.M[i * 128:(i + 1) * 128, :], mr[:])

    def ffn_down(self, l, xout):
        wd = self.tile("wd", [128, 22, DM], BF16)
        stg = [self.tile("dstg%d" % i, [128, 2, DM], F32) for i in range(2)]
        wv = self.w_down[l].rearrange("(k p) n -> p k n", p=128)
        for j in range(11):
            s = stg[j % 2]
            self.load(s, s[:], wv[:, j * 2:(j + 1) * 2, :])
            self.A("pool" if j % 2 == 0 else "dve", "tensor_copy", [s], [wd], out=wd[:, j * 2:(j + 1) * 2, :], in_=s[:])
        mt = [self.tile("dm%d" % i, [128, 22, TC], BF16) for i in range(2)]
        xs = [self.tile("dx%d" % i, [128, 8, TC], F32) for i in range(2)]
        xo = [self.tile("dxo%d" % i, [128, TC], F32) for i in range(3)]
        px = [self.ptile("dpx%d" % i) for i in range(3)]
        mv = self.M.rearrange("(k p) n -> p k n", p=128)
        xv = self.XM.rearrange("(c p) n -> p c n", p=128)
        nx = 0
        for t in range(NT):
            ts_ = slice(t * TC, (t + 1) * TC)
            m = mt[t % 2]
            x = xs[t % 2]
            self.load(m, m[:, 0:11, :], mv[:, 0:11, ts_])
            self.load(m, m[:, 11:22, :], mv[:, 11:22, ts_])
            self.load(x, x[:], xv[:, :, ts_])
            for fo in range(8):
                p = px[nx % 3]
                xo_ = xo[nx % 3]
                nx += 1
                for k in range(22):
                    self.mm(p, p[:], wd, wd[:, k, fo * 128:(fo + 1) * 128], m, m[:, k, :], k == 0, k == 21)
                self.A("dve", "tensor_tensor", [p, x], [xo_], out=xo_[:], in0=p[:], in1=x[:, fo, :], op=ALU.add)
                self.store(xo_, xout[fo * 128:(fo + 1) * 128, ts_], xo_[:])


_CACHE = {}


def _get_nc(n_layers=NL, debug=False):
    key = (n_layers, debug)
    if key not in _CACHE:
        b = Builder(n_layers, debug)
        _CACHE[key] = (b.build(), b)
    return _CACHE[key]


def make_in_maps(inp):
    consts = _host_consts()
    params = _pack_params(inp)
    lam = np.ascontiguousarray(np.asarray(inp["lam"], np.float32).reshape(1, NL * 256))
    rpbE = _pack_rpb(np.asarray(inp["rpb"], np.float32))
    shared = dict(
        w_in=np.ascontiguousarray(inp["w_in"], np.float32),
        w_branch=np.ascontiguousarray(inp["w_branch"], np.float32),
        w_out=np.ascontiguousarray(inp["w_out"], np.float32),
        w_up=np.ascontiguousarray(inp["w_up"], np.float32),
        w_down=np.ascontiguousarray(inp["w_down"], np.float32),
        params=params, lam=lam, rpbE=rpbE,
        tab1=consts["tab1"], tab2=consts["tab2"], mats=consts["mats"], mask3=consts["mask3"],
        colmask=consts["colmask"],
    )
    x = np.asarray(inp["x"], np.float32)
    maps = []
    for b in range(8):
        d = dict(shared)
        d["xT"] = np.ascontiguousarray(x[b].T)
        maps.append(d)
    return maps


def kernel(**inputs):
    inp = {k: np.asarray(v) for k, v in inputs.items()}
    nc, _ = _get_nc()
    maps = make_in_maps(inp)
    res = run_bass_kernel_spmd(nc, maps, core_ids=list(range(8)))
    out = np.stack([np.ascontiguousarray(res.results[b]["yT"].T) for b in range(8)], axis=0)
    return out.astype(np.float32)
```

```python
import numpy as np
from contextlib import ExitStack
import concourse.bass as bass
import concourse.mybir as mybir
from concourse.bass_utils import run_bass_kernel_spmd

F32 = mybir.dt.float32
BF16 = mybir.dt.bfloat16
AF = mybir.ActivationFunctionType
ALU = mybir.AluOpType
AX = mybir.AxisListType

EPOCH = 24000
SAME_SYNC = True

SEQ = 4096
DM = 1024
NL = 4
N_IN = 12544
D_FF = 2816
EPS = 1e-6
NT = 8
TC = 512
COL = dict(aq=0, ak=512, av=1024, bq=1536, bk=2048, bv=2176, cq=2304, ck=3840, cv=5376,
           dq=6912, dk=7424, dv=7936, g=8448)
DIL = (1, 4, 16)


class Buf:
    __slots__ = ("name", "w", "r")

    def __init__(self, name=""):
        self.name = name
        self.w = None
        self.r = []


class _Op:
    __slots__ = ("eng", "fn", "deps", "ldeps", "ddeps", "dma", "pub", "seq")

    def __init__(self, eng, fn, deps, ddeps, dma, ldeps=()):
        self.eng = eng
        self.fn = fn
        self.deps = deps
        self.ldeps = ldeps
        self.ddeps = ddeps
        self.dma = dma
        self.pub = False
        self.seq = None


class Sched:
    ENGS = ("pe", "act", "dve", "pool", "sp")

    def __init__(self, nc, stack):
        self.nc = nc
        self.stack = stack
        self.ops = []
        self.base = 0
        self.cnt = {e: 0 for e in self.ENGS}
        self.sems = {e: [] for e in self.ENGS}
        self.dsem = {}
        self.dcnt = {}
        self.nsem = 0
        self.ninstr = 0

    def _new_sem(self, name):
        self.nsem += 1
        return self.stack.enter_context(self.nc.semaphore(name))

    def _esem(self, e, ep):
        while len(self.sems[e]) <= ep:
            self.sems[e].append(self._new_sem("s_%s_%d" % (e, len(self.sems[e]))))
        return self.sems[e][ep]

    def op(self, eng, fn, reads=(), writes=(), dma=None):
        deps = set()
        ldeps = set()
        ddeps = {}
        base = self.base
        ops = self.ops

        def add(i, dst):
            if i is None or i < base:
                return
            o = ops[i - base]
            if o.dma is not None:
                ddeps[o.dma] = self.dcnt[o.dma]
            else:
                dst.add(i)

        for b in reads:
            add(b.w, deps)
        for b in writes:
            add(b.w, ldeps)
            for r in b.r:
                add(r, ldeps)
        ldeps -= deps
        idx = base + len(ops)
        if dma is not None:
            if dma not in self.dsem:
                self.dsem[dma] = self._new_sem("d_%d" % len(self.dsem))
                self.dcnt[dma] = 0
            self.dcnt[dma] += 16
        ops.append(_Op(eng, fn, deps, ddeps, dma, ldeps))
        for b in reads:
            b.r.append(idx)
        for b in writes:
            b.w = idx
            b.r = []
        return idx

    def flush(self):
        ops = self.ops
        base = self.base
        last = {}
        for i, o in enumerate(ops):
            if o.dma is None and o.fn is not None:
                last[o.eng] = i
        for e in self.ENGS:
            deps = set(base + i for ee, i in last.items() if ee != e)
            ops.append(_Op(e, None, deps, dict(self.dcnt), None))
        import bisect

        def cross(od, o):
            return od.eng != o.eng or o.dma is not None or (SAME_SYNC and o.eng != "pe")

        for o in ops:
            for d in o.deps:
                od = ops[d - base]
                if cross(od, o):
                    od.pub = True
        publ = {e: [] for e in self.ENGS}
        for i, o in enumerate(ops):
            if o.pub:
                publ[o.eng].append(i)
        for i, o in enumerate(ops):
            for d in o.ldeps:
                od = ops[d - base]
                if not cross(od, o):
                    continue
                if od.pub:
                    o.deps.add(d)
                    continue
                lst = publ[od.eng]
                k = bisect.bisect_left(lst, d - base)
                if k < len(lst) and lst[k] < i:
                    o.deps.add(base + lst[k])
                else:
                    od.pub = True
                    bisect.insort(lst, d - base)
                    o.deps.add(d)
        for o in ops:
            if o.pub:
                c = self.cnt[o.eng]
                self.cnt[o.eng] = c + 1
                o.seq = (c // EPOCH, c % EPOCH + 1)
        per = {e: [] for e in self.ENGS}
        seen = {e: {} for e in self.ENGS}
        seend = {e: {} for e in self.ENGS}
        for o in ops:
            F = o.eng
            need = {}
            for d in o.deps:
                od = ops[d - base]
                if od.eng == F and o.dma is None and (F == "pe" or not SAME_SYNC):
                    continue
                if od.seq > need.get(od.eng, (-1, -1)):
                    need[od.eng] = od.seq
            waits = []
            for E, sq in need.items():
                if seen[F].get(E, (-1, -1)) >= sq:
                    continue
                seen[F][E] = sq
                waits.append((self._esem(E, sq[0]), sq[1]))
            for k, v in o.ddeps.items():
                if v == 0 or seend[F].get(k, 0) >= v:
                    continue
                seend[F][k] = v
                waits.append((self.dsem[k], v))
            inc = self._esem(F, o.seq[0]) if o.pub else None
            dinc = self.dsem[o.dma] if o.dma is not None else None
            per[F].append((waits, o.fn, inc, dinc))
            self.ninstr += len(waits) + 1

        def run(eng, lst):
            for waits, fn, inc, dinc in lst:
                emb = None
                if fn is not None and dinc is None and waits:
                    emb = waits[-1]
                    waits = waits[:-1]
                for s, v in waits:
                    eng.wait_ge(s, v)
                if fn is None:
                    continue
                ins = fn(eng)
                if emb is not None:
                    ins._wait_ge(emb[0], emb[1])
                if inc is not None:
                    ins.then_inc(inc, 1)
                if dinc is not None:
                    ins.then_inc(dinc, 16)

        with self.nc.Block() as block:
            @block.tensor
            def _(e):
                run(e, per["pe"])

            @block.scalar
            def _(e):
                run(e, per["act"])

            @block.vector
            def _(e):
                run(e, per["dve"])

            @block.gpsimd
            def _(e):
                run(e, per["pool"])

            @block.sync
            def _(e):
                run(e, per["sp"])
        self.base = base + len(ops)
        self.ops = []


class Tl:
    __slots__ = ("t", "b", "name", "psum")

    def __init__(self, t, name, psum=False):
        self.t = t
        self.b = Buf(name)
        self.name = name
        self.psum = psum

    def __getitem__(self, k):
        return self.t[k]


def _host_consts():
    c = {}
    pos = np.arange(SEQ, dtype=np.float32)
    p = np.arange(128)
    inv32 = (np.float32(10000.0) ** (-(np.arange(0, 64, 2, dtype=np.float32) / np.float32(64)))).astype(np.float32)
    f = p % 32
    half = (p % 64) // 32
    ang = (pos[None, :] * inv32[f][:, None]).astype(np.float32)
    sgn = np.where(half == 0, -1.0, 1.0).astype(np.float32)[:, None]
    tab1 = np.stack([np.cos(ang), np.sin(ang) * sgn], axis=1).astype(np.float32)
    inv16 = (np.float32(10000.0) ** (-(np.arange(0, 32, 2, dtype=np.float32) / np.float32(32)))).astype(np.float32)
    pp = p % 64
    blk = pp // 32
    q = pp % 32
    half2 = q // 16
    f2 = q % 16
    prow = np.floor(pos / 64).astype(np.float32)
    pcol = (pos - prow * 64).astype(np.float32)
    pf = np.where(blk[:, None] == 0, prow[None, :], pcol[None, :]).astype(np.float32)
    ang2 = (pf * inv16[f2][:, None]).astype(np.float32)
    sgn2 = np.where(half2 == 0, -1.0, 1.0).astype(np.float32)[:, None]
    tab2 = np.stack([np.cos(ang2), np.sin(ang2) * sgn2], axis=1).astype(np.float32)
    c["tab1"] = tab1.reshape(128, 2 * SEQ)
    c["tab2"] = tab2.reshape(128, 2 * SEQ)
    mats = np.zeros((128, 5, 128), np.float32)
    for m in range(128):
        part1 = m + 32 if (m % 64) // 32 == 0 else m - 32
        mats[part1, 0, m] = 1.0
        part2 = m + 16 if (m % 32) // 16 == 0 else m - 16
        mats[part2, 1, m] = 1.0
    mats[:, 2, :] = (p[:, None] // 64 == p[None, :] // 64)
    mats[:, 3, :] = 1.0
    mats[:, 4, :] = np.eye(128)
    c["mats"] = mats.reshape(128, 5 * 128)
    i = np.arange(128)[:, None]
    m = np.arange(128)[None, :]
    mask3 = np.stack([(i - m >= 64), (np.abs(i - m) <= 64), (m - i >= 64)], axis=1).astype(np.float32)
    c["mask3"] = mask3.reshape(128, 3 * 128)
    kc = np.arange(128)[:, None] % 64
    qc = np.arange(64)[None, :]
    ws = np.clip(qc - 8, 0, 48)
    c["colmask"] = ((kc >= ws) & (kc < ws + 16)).astype(np.float32)
    return c


P_G1 = 0
P_G2 = 8
P_QKG = 16
P_SUB = 24
P_CW = 25
P_CB = 25 + 132
P_N = 25 + 132 + 44


def _pack_params(inp):
    out = np.zeros((128, NL, P_N), np.float32)
    for l in range(NL):
        out[:, l, P_G1:P_G1 + 8] = inp["norm1_g"][l].reshape(8, 128).T
        out[:, l, P_G2:P_G2 + 8] = inp["norm2_g"][l].reshape(8, 128).T
        qg = inp["qk_g"][l].reshape(8, 64)
        out[:, l, P_QKG:P_QKG + 8] = np.concatenate([qg, qg], axis=1).T
        out[:, l, P_SUB] = inp["subln_g"][l]
        cw = inp["conv_w"][l].reshape(3, 44, 128)
        out[:, l, P_CW:P_CW + 132] = cw.transpose(2, 0, 1).reshape(128, 132)
        out[:, l, P_CB:P_CB + 44] = inp["conv_b"][l].reshape(44, 128).T
    return out.reshape(128, NL * P_N)


def _pack_rpb(rpb):
    kc = np.arange(64)[:, None]
    qc = np.arange(64)[None, :]
    idx = np.clip(kc - qc + 15, 0, 30)
    g = rpb[:, :, :, idx]
    g = np.transpose(g, (0, 3, 1, 2, 4))
    lo = g
    hi = np.concatenate([g[:, :, :, 1:, :], g[:, :, :, 14:15, :]], axis=3)
    return np.ascontiguousarray(np.concatenate([lo, hi], axis=1)).reshape(NL, 128, 8 * 15 * 64)


class Builder:
    def __init__(self, n_layers=NL, debug=False):
        self.n_layers = n_layers
        self.debug = debug
        self.nc = bass.Bass("TRN2", target_bir_lowering=False)
        nc = self.nc
        di = lambda n, s, dt=F32: nc.dram_tensor(n, s, dt, kind="ExternalInput").ap()
        self.xT = di("xT", [DM, SEQ])
        self.w_in = di("w_in", [NL, DM, N_IN])
        self.w_branch = di("w_branch", [NL, 4, 512, DM])
        self.w_out = di("w_out", [NL, DM, DM])
        self.w_up = di("w_up", [NL, DM, 2 * D_FF])
        self.w_down = di("w_down", [NL, D_FF, DM])
        self.params = di("params", [128, NL * P_N])
        self.lam = di("lam", [1, NL * 256])
        self.rpbE = di("rpbE", [NL, 128, 8 * 15 * 64])
        self.c_tab1 = di("tab1", [128, 2 * SEQ])
        self.c_tab2 = di("tab2", [128, 2 * SEQ])
        self.c_mats = di("mats", [128, 5 * 128])
        self.c_mask3 = di("mask3", [128, 3 * 128])
        self.c_colmask = di("colmask", [128, 64])
        self.yT = nc.dram_tensor("yT", [DM, SEQ], F32, kind="ExternalOutput").ap()
        kind = "ExternalOutput" if debug else "Internal"
        ds = lambda n, s, dt: nc.dram_tensor(n, s, dt, kind=kind).ap()
        self.QK = ds("s_qk", [45 * 128, SEQ], BF16)
        self.VA = ds("s_va", [SEQ, 512], BF16)
        self.VB = ds("s_vb", [SEQ, 130], BF16)
        self.VC = ds("s_vc", [3, SEQ, 520], BF16)
        self.VD = ds("s_vd", [SEQ, 520], BF16)
        self.G = ds("s_g", [4096, SEQ], F32)
        self.OT = ds("s_ot", [4, 512, SEQ], BF16)
        self.M = ds("s_m", [D_FF, SEQ], BF16)
        self.XM = ds("s_xm", [DM, SEQ], F32)
        self.XA = ds("s_xa", [DM, SEQ], F32)
        self.XB = ds("s_xb", [DM, SEQ], F32)

    def tile(self, name, shape, dt):
        t = self.ph.enter_context(self.nc.sbuf_tensor(name + "_%d" % self.uid, shape, dt))
        self.uid += 1
        return Tl(t, name)

    def ptile(self, name, shape=(128, 512), dt=F32):
        t = self.ph.enter_context(self.nc.psum_tensor(name + "_%d" % self.uid, [128, 512], F32))
        self.uid += 1
        return Tl(t, name, True)

    def A(self, eng, meth, reads, writes, *a, **k):
        wr = [x.b for x in writes] + [x.b for x in reads if x.psum]
        self.S.op(eng, lambda e: getattr(e, meth)(*a, **k), [x.b for x in reads], wr)

    def load(self, dst_tl, out_ap, in_ap, q="sp"):
        self.S.op(q, lambda e: e.dma_start(out=out_ap, in_=in_ap), [], [dst_tl.b], dma=("L", dst_tl.name))

    def store(self, src_tl, out_ap, in_ap, q="pool"):
        self.S.op(q, lambda e: e.dma_start(out=out_ap, in_=in_ap), [src_tl.b], [], dma=("S", src_tl.name))

    def mm(self, out_tl, out_ap, l_tl, l_ap, r_tl, r_ap, start, stop):
        self.S.op("pe", lambda e: e.matmul(out_ap, l_ap, r_ap, start=start, stop=stop),
                  [l_tl.b, r_tl.b], [out_tl.b])

    def begin(self):
        self.ph = ExitStack()
        self.ph.__enter__()

    def end(self):
        self.S.flush()
        self.ph.__exit__(None, None, None)

    def build(self):
        nc = self.nc
        self.uid = 0
        with ExitStack() as top:
            self.S = Sched(nc, top)
            self.top = top
            self.ph = top
            self.mats = self.tile("mats", [128, 5, 128], F32)
            self.onesb = self.tile("onesb", [128, 128], BF16)
            self.blkb = self.tile("blkb", [128, 128], BF16)
            self.par = self.tile("par", [128, NL, P_N], F32)
            self.nlam = self.tile("nlam", [128, NL], F32)
            self.begin()
            self.lamt = self.tile("lamt", [1, NL * 256], F32)
            self.load(self.mats, self.mats[:].rearrange("p a b -> p (a b)"), self.c_mats)
            self.load(self.par, self.par[:].rearrange("p a b -> p (a b)"), self.params)
            self.load(self.lamt, self.lamt[:], self.lam)
            self.A("dve", "tensor_copy", [self.mats], [self.onesb], out=self.onesb[:], in_=self.mats[:, 3, :])
            self.A("dve", "tensor_copy", [self.mats], [self.blkb], out=self.blkb[:], in_=self.mats[:, 2, :])
            self.lambda_setup()
            self.end()
            xin = self.xT
            for l in range(self.n_layers):
                last = (l == self.n_layers - 1)
                xout = self.yT if last else (self.XA if l % 2 == 0 else self.XB)
                self.layer(l, xin, xout)
                xin = xout
        return nc

    def lambda_setup(self):
        import math
        pr = self.tile("lampr", [1, NL * 2, 64], F32)
        sm = self.tile("lamsm", [1, NL * 2], F32)
        lv = self.tile("lamv", [1, NL], F32)
        ps = self.ptile("lamps", (128, NL))
        lt = self.lamt[:].rearrange("p (l a d) -> p l a d", l=NL, a=4)
        for l in range(NL):
            for j in range(2):
                self.A("dve", "tensor_tensor", [self.lamt], [pr], out=pr[:, l * 2 + j, :], in0=lt[:, l, 2 * j, :],
                       in1=lt[:, l, 2 * j + 1, :], op=ALU.mult)
        self.A("dve", "tensor_reduce", [pr], [sm], out=sm[:], in_=pr[:], axis=AX.X, op=ALU.add)
        self.A("act", "activation", [sm], [sm], out=sm[:], in_=sm[:], func=AF.Exp)
        smv = sm[:].rearrange("p (l j) -> p l j", j=2)
        for l in range(NL):
            li = 0.8 - 0.6 * math.exp(-0.3 * l)
            self.A("dve", "scalar_tensor_tensor", [sm], [lv], out=lv[:, l:l + 1], in0=smv[:, l, 1:2], scalar=-li,
                   in1=smv[:, l, 0:1], op0=ALU.add, op1=ALU.subtract)
        self.mm(ps, ps[:, 0:NL], self.mats, self.mats[0:1, 3, :], lv, lv[:], True, True)
        self.A("dve", "tensor_copy", [ps], [self.nlam], out=self.nlam[:], in_=ps[:, 0:NL])

    def rmsnorm_to_hT(self, l, xsrc, hT, goff):
        xs = [self.tile("nx%d" % i, [128, 8, TC], F32) for i in range(2)]
        sq = [self.tile("nsq%d" % i, [128, 8, TC], F32) for i in range(1)]
        rs = [self.tile("nrs%d" % i, [128, TC], F32) for i in range(2)]
        ps = [self.ptile("nps%d" % i) for i in range(2)]
        xv = xsrc.rearrange("(c p) n -> p c n", p=128)
        for t in range(NT):
            x = xs[t % 2]
            s = sq[0]
            r = rs[t % 2]
            p = ps[t % 2]
            self.load(x, x[:], xv[:, :, t * TC:(t + 1) * TC])
            self.A("pool", "tensor_tensor", [x], [s], out=s[:], in0=x[:], in1=x[:], op=ALU.mult)
            for c in range(8):
                self.mm(p, p[:], self.mats, self.mats[:, 3, :], s, s[:, c, :], c == 0, c == 7)
            self.A("act", "activation", [p], [r], out=r[:], in_=p[:], func=AF.Ln, scale=1.0 / DM, bias=self.epsc[:, 0:1])
            self.A("act", "activation", [r], [r], out=r[:], in_=r[:], func=AF.Exp, scale=-0.5)
            for c in range(8):
                self.A("dve", "scalar_tensor_tensor", [x, r, self.par], [hT], out=hT[:, c, t * TC:(t + 1) * TC],
                       in0=x[:, c, :], scalar=self.par[:, l, goff + c:goff + c + 1], in1=r[:], op0=ALU.mult, op1=ALU.mult)

    def load_weight_bf16(self, dst, dst_ap, src_ap, stg, shape_ap, eng):
        self.load(stg, shape_ap, src_ap)
        self.A(eng, "tensor_copy", [stg], [dst], out=dst_ap, in_=shape_ap)

    def layer(self, l, xin, xout):
        import math
        self.lambda_init = 0.8 - 0.6 * math.exp(-0.3 * l)
        self.begin()
        self.epsc = self.tile("epsc", [128, 1], F32)
        self.A("pool", "memset", [], [self.epsc], self.epsc[:], EPS)
        hT = self.tile("hT", [128, 8, SEQ], BF16)
        sub = ExitStack()
        outer = self.ph
        self.ph = sub
        sub.__enter__()
        self.rmsnorm_to_hT(l, xin, hT, P_G1)
        self.S.flush()
        sub.__exit__(None, None, None)
        self.ph = outer
        self.proj(l, hT)
        self.end()
        self.begin()
        self.epsc = self.tile("epsc", [128, 1], F32)
        self.A("pool", "memset", [], [self.epsc], self.epsc[:], EPS)
        self.attn_a(l)
        self.end()
        self.begin()
        self.attn_b(l)
        self.end()
        self.begin()
        self.attn_c(l)
        self.end()
        self.begin()
        self.attn_d(l)
        self.end()
        self.begin()
        self.merge(l, xin)
        self.end()
        self.begin()
        self.epsc = self.tile("epsc", [128, 1], F32)
        self.A("pool", "memset", [], [self.epsc], self.epsc[:], EPS)
        hT = self.tile("hT2", [128, 8, SEQ], BF16)
        sub = ExitStack()
        outer = self.ph
        self.ph = sub
        sub.__enter__()
        self.rmsnorm_to_hT(l, self.XM, hT, P_G2)
        self.S.flush()
        sub.__exit__(None, None, None)
        self.ph = outer
        self.ffn_up(l, hT)
        self.end()
        self.begin()
        self.ffn_down(l, xout)
        self.end()

    def pipe(self, items, lags):
        n = len(items)
        for j in range(n + max(lags)):
            for si, lg in enumerate(lags):
                i = j - lg
                if 0 <= i < n and len(items[i]) > si and items[i][si] is not None:
                    items[i][si]()

    def proj(self, l, hT):
        wst = self.tile("wst0", [128, 8, 512], F32)
        wbf = [self.tile("wbf%d" % i, [128, 8, 512], BF16) for i in range(2)]
        tab = self.tile("tab", [128, 2, SEQ], F32)
        pacc = [self.ptile("pacc%d" % i) for i in range(3)]
        pss = [self.ptile("pss%d" % i) for i in range(2)]
        prot = [self.ptile("prot%d" % i) for i in range(2)]
        sq = [self.tile("sq%d" % i, [128, TC], BF16) for i in range(3)]
        rs = [self.tile("rs%d" % i, [128, TC], F32) for i in range(3)]
        yy = [self.tile("yy%d" % i, [128, TC], F32) for i in range(3)]
        t2 = [self.tile("t2%d" % i, [128, TC], F32) for i in range(3)]
        qo = [self.tile("qo%d" % i, [128, TC], BF16) for i in range(3)]
        rowb = [self.tile("rowb%d" % i, [128, SEQ], BF16) for i in range(2)]
        go = [self.tile("go%d" % i, [128, TC], F32) for i in range(3)]
        vst = [self.tile("vst%d" % i, [128, 4, 520], BF16) for i in range(2)]
        vsta = [self.tile("vsta%d" % i, [128, 4, 512], BF16) for i in range(2)]
        for v in vst:
            self.A("pool", "memset", [], [v], v[:], 1.0)
        wv = self.w_in[l].rearrange("(k p) n -> p k n", p=128)
        cnt = dict(acc=0, q=0, row=0, go=0, vs=0, tmp=0, pp=0)
        G = P_QKG

        def wload(gidx, col0, w):
            wb = wbf[gidx % 2]
            self.load(wst, wst[:, :, 0:w], wv[:, :, col0:col0 + w])
            self.A("pool" if gidx % 2 == 0 else "dve", "tensor_copy", [wst], [wb], out=wb[:, :, 0:w], in_=wst[:, :, 0:w])

        def qk_items(items, wb, off, row, gcol, rope, r):
            mat = 0 if rope == "1d" else 1
            rb = None
            if r > 1:
                rb = rowb[cnt["row"] % 2]
                cnt["row"] += 1
            for t in range(NT):
                pa = pacc[cnt["acc"] % 3]
                cnt["acc"] += 1
                j = cnt["tmp"] % 3
                cnt["tmp"] += 1
                ps_, pr_ = pss[cnt["pp"] % 2], prot[cnt["pp"] % 2]
                cnt["pp"] += 1
                s, rr, y, a2 = sq[j], rs[j], yy[j], t2[j]
                tsl = slice(t * TC, (t + 1) * TC)
                if r > 1:
                    n_ = TC // r
                    dst_tl = rb
                    dst = rb[:].rearrange("p (c i) -> p c i", c=r)[:, :, t * n_:(t + 1) * n_]
                else:
                    dst_tl = qo[cnt["q"] % 3]
                    cnt["q"] += 1
                    dst = dst_tl[:]

                def s0(pa=pa, s=s, tsl=tsl):
                    for k in range(8):
                        self.mm(pa, pa[:], wb, wb[:, k, off:off + 128], hT, hT[:, k, tsl], k == 0, k == 7)
                    self.A("act", "activation", [pa], [s], out=s[:], in_=pa[:], func=AF.Square)

                def s1(pa=pa, s=s, rr=rr, y=y, ps_=ps_, dst_tl=dst_tl, dst=dst):
                    self.mm(ps_, ps_[:], self.blkb, self.blkb[:], s, s[:], True, True)
                    self.A("act", "activation", [ps_], [rr], out=rr[:], in_=ps_[:], func=AF.Ln, scale=1.0 / 64, bias=self.epsc[:, 0:1])
                    self.A("act", "activation", [rr], [rr], out=rr[:], in_=rr[:], func=AF.Exp, scale=-0.5)
                    if rope is None:
                        self.A("dve", "scalar_tensor_tensor", [pa, rr, self.par], [dst_tl], out=dst, in0=pa[:],
                               scalar=self.par[:, l, gcol:gcol + 1], in1=rr[:], op0=ALU.mult, op1=ALU.mult)
                    else:
                        self.A("dve", "scalar_tensor_tensor", [pa, rr, self.par], [y], out=y[:], in0=pa[:],
                               scalar=self.par[:, l, gcol:gcol + 1], in1=rr[:], op0=ALU.mult, op1=ALU.mult)

                def s2(y=y, a2=a2, pr_=pr_, dst_tl=dst_tl, dst=dst, tsl=tsl, t=t):
                    if rope is not None:
                        self.mm(pr_, pr_[:], self.mats, self.mats[:, mat, :], y, y[:], True, True)
                        self.A("dve", "tensor_tensor", [pr_, tab], [a2], out=a2[:], in0=pr_[:], in1=tab[:, 1, tsl], op=ALU.mult)
                        self.A("pool", "tensor_tensor", [y, tab], [y], out=y[:], in0=y[:], in1=tab[:, 0, tsl], op=ALU.mult)
                        if r > 1:
                            src1 = y[:].rearrange("p (i c) -> p c i", c=r)
                            src2 = a2[:].rearrange("p (i c) -> p c i", c=r)
                        else:
                            src1, src2 = y[:], a2[:]
                        self.A("pool", "tensor_tensor", [y, a2], [dst_tl], out=dst, in0=src1, in1=src2, op=ALU.add)
                    if r == 1:
                        self.store(dst_tl, self.QK[row * 128:(row + 1) * 128, tsl], dst_tl[:])
                    elif t == NT - 1:
                        self.store(rb, self.QK[row * 128:(row + 1) * 128, :], rb[:])

                items.append([s0, s1, s2])

        def v_items(items, wb, off, w, dst, r, nh, hd):
            hv = hT[:].rearrange("p k (i c) -> p k c i", c=r)
            L = SEQ // r
            stride = hd + 1 if hd == 64 else hd
            vs = None
            for tt in range(32):
                c = (tt * 128) // L
                i0 = (tt * 128) % L
                pa = pacc[cnt["acc"] % 3]
                cnt["acc"] += 1
                if tt % 4 == 0:
                    vs = (vst if hd == 64 else vsta)[cnt["vs"] % 2]
                    cnt["vs"] += 1

                def s0(pa=pa, c=c, i0=i0):
                    for k in range(8):
                        self.mm(pa, pa[:, 0:w], hT, hv[:, k, c, i0:i0 + 128], wb, wb[:, k, off:off + w], k == 0, k == 7)

                def s1(pa=pa, vs=vs, tt=tt):
                    o = vs[:, tt % 4, 0:nh * stride].rearrange("p (h d) -> p h d", h=nh)[:, :, 0:hd]
                    i_ = pa[:, 0:w].rearrange("p (h d) -> p h d", h=nh)
                    if tt % 2 == 0:
                        self.A("act", "activation", [pa], [vs], out=o, in_=i_, func=AF.Copy)
                    else:
                        self.A("dve", "tensor_copy", [pa], [vs], out=o, in_=i_)
                    if tt % 4 == 3:
                        t0 = (tt - 3) * 128
                        self.store(vs, dst[t0:t0 + 512, :].rearrange("(a p) n -> p a n", p=128), vs[:, :, 0:nh * stride])

                items.append([s0, s1])

        def gate_items(items, wb, off, grow):
            for t in range(NT):
                pa = pacc[cnt["acc"] % 3]
                cnt["acc"] += 1
                g = go[cnt["go"] % 3]
                cnt["go"] += 1
                tsl = slice(t * TC, (t + 1) * TC)

                def s0(pa=pa, tsl=tsl):
                    for k in range(8):
                        self.mm(pa, pa[:], wb, wb[:, k, off:off + 128], hT, hT[:, k, tsl], k == 0, k == 7)

                def s1(pa=pa, g=g, tsl=tsl):
                    self.A("act", "activation", [pa], [g], out=g[:], in_=pa[:], func=AF.Sigmoid)
                    self.store(g, self.G[grow * 128:(grow + 1) * 128, tsl], g[:])

                items.append([s0, s1])

        jobsB = [
            (COL["bq"], 512, [("qk", i * 128, 8 + i, G + 2, "ax", 1) for i in range(4)]),
            (COL["bk"], 256, [("qk", 0, 12, G + 3, "ax", 1), ("v", 128, 128, self.VB, 1, 2, 64)]),
        ]
        jobs = [
            (COL["aq"], 512, [("qk", i * 128, 0 + i, G + 0, "1d", 1) for i in range(4)]),
            (COL["ak"], 512, [("qk", i * 128, 4 + i, G + 1, "1d", 1) for i in range(4)]),
            (COL["av"], 512, [("v", 0, 512, self.VA, 1, 4, 128)]),
        ]
        for g in range(3):
            jobs.append((COL["cq"] + g * 512, 512, [("qk", i * 128, 13 + g * 4 + i, G + 4, "1d", DIL[g]) for i in range(4)]))
            jobs.append((COL["ck"] + g * 512, 512, [("qk", i * 128, 25 + g * 4 + i, G + 5, "1d", DIL[g]) for i in range(4)]))
            jobs.append((COL["cv"] + g * 512, 512, [("v", 0, 512, self.VC[g], DIL[g], 8, 64)]))
        jobs.append((COL["dq"], 512, [("qk", i * 128, 37 + i, G + 6, None, 1) for i in range(4)]))
        jobs.append((COL["dk"], 512, [("qk", i * 128, 41 + i, G + 7, None, 1) for i in range(4)]))
        jobs.append((COL["dv"], 512, [("v", 0, 512, self.VD, 1, 8, 64)]))
        for gi in range(8):
            jobs.append((COL["g"] + gi * 512, 512, [("gate", i * 128, gi * 4 + i) for i in range(4)]))

        gctr = [0]

        def run_jobs(jl):
            items = []
            gid0 = gctr[0]
            for ji, (col0, w, subs) in enumerate(jl):
                gid = gid0 + ji
                wb = wbf[gid % 2]
                first = len(items)
                for sj in subs:
                    if sj[0] == "qk":
                        qk_items(items, wb, *sj[1:])
                    elif sj[0] == "v":
                        v_items(items, wb, *sj[1:])
                    else:
                        gate_items(items, wb, *sj[1:])
                orig = items[first][0]
                nxt = jl[ji + 1] if ji + 1 < len(jl) else None

                def s0w(orig=orig, nxt=nxt, gid=gid):
                    if nxt is not None:
                        wload(gid + 1, nxt[0], nxt[1])
                    orig()
                items[first][0] = s0w
            wload(gid0, jl[0][0], jl[0][1])
            gctr[0] += len(jl)
            self.pipe(items, [0, 1, 2])

        self.load(tab, tab[:].rearrange("p a n -> p (a n)"), self.c_tab2)
        run_jobs(jobsB)
        self.load(tab, tab[:].rearrange("p a n -> p (a n)"), self.c_tab1)
        run_jobs(jobs)

    def normalize_rows65(self, po, res_tl, res_ap, n, tmp_row, tmp_bc, pbc):
        self.A("dve", "reciprocal", [po], [tmp_row], out=tmp_row[64:65, 0:n], in_=po[64:65, 0:n])
        self.mm(pbc, pbc[0:64, 0:n], self.mats, self.mats[64:65, 3, 0:64], tmp_row, tmp_row[64:65, 0:n], True, True)
        self.A("act", "activation", [pbc], [tmp_bc], out=tmp_bc[0:64, 0:n], in_=pbc[0:64, 0:n], func=AF.Copy)
        self.A("dve", "tensor_tensor", [po, tmp_bc], [res_tl], out=res_ap, in0=po[0:64, 0:n], in1=tmp_bc[0:64, 0:n], op=ALU.mult)

    def attn_a(self, l):
        qz = [self.tile("qz%d" % c, [128, 4, SEQ], BF16) for c in range(2)]
        ka = self.tile("ka", [128, 4, SEQ], BF16)
        va = self.tile("va", [128, 32, 512], BF16)
        for c in range(2):
            oc = 1 - c
            self.A("pool", "memset", [], [qz[c]], qz[c][oc * 64:(oc + 1) * 64, :, :], 0.0)
        for h in range(4):
            for c in range(2):
                self.load(qz[c], qz[c][c * 64:(c + 1) * 64, h, :], self.QK[h * 128 + c * 64:h * 128 + (c + 1) * 64, :])
            self.load(ka, ka[:, h, :], self.QK[(4 + h) * 128:(5 + h) * 128, :])
        for j in range(4):
            self.load(va, va[:, j * 8:(j + 1) * 8, :], self.VA[j * 1024:(j + 1) * 1024, :].rearrange("(a p) n -> p a n", p=128))
        psc = [self.ptile("psc%d" % i) for i in range(3)]
        po = [self.ptile("po%d" % c) for c in range(2)]
        psm = [self.ptile("psm%d" % c) for c in range(2)]
        pfin = self.ptile("pfin")
        pt = [self.tile("pt%d" % i, [128, TC], BF16) for i in range(6)]
        poS = [self.tile("poS%d" % c, [128, TC], F32) for c in range(2)]
        smS = [self.tile("smS%d" % c, [128, TC], F32) for c in range(2)]
        rc = self.tile("rc", [128, TC], F32)
        res = [self.tile("res%d" % i, [128, TC], F32) for i in range(2)]
        dd = self.tile("dd", [128, TC], F32)
        sq = self.tile("sqa", [128, TC], F32)
        rs = self.tile("rsa", [128, TC], F32)
        ob = [self.tile("oba%d" % i, [128, TC], BF16) for i in range(2)]
        items = []
        n = 0
        for h in range(4):
            for qc in range(NT):
                qs = slice(qc * TC, (qc + 1) * TC)
                for kt in range(32):
                    for c in range(2):
                        sc = psc[n % 3]
                        p = pt[n % 6]
                        n += 1
                        po_ = po[c]
                        psm_ = psm[c]

                        def s0(sc=sc, p=p, h=h, qs=qs, c=c, kt=kt):
                            self.mm(sc, sc[:], ka, ka[:, h, kt * 128:(kt + 1) * 128], qz[c], qz[c][:, h, qs], True, True)
                            self.A("act", "activation", [sc], [p], out=p[:], in_=sc[:], func=AF.Exp, scale=0.125)

                        def s1(p=p, h=h, c=c, kt=kt, po_=po_, psm_=psm_):
                            self.mm(po_, po_[:], va, va[:, kt, h * 128:(h + 1) * 128], p, p[:], kt == 0, kt == 31)
                            self.mm(psm_, psm_[:], self.onesb, self.onesb[:], p, p[:], kt == 0, kt == 31)
                            if kt == 31:
                                self.A("act", "activation", [po_], [poS[c]], out=poS[c][:], in_=po_[:], func=AF.Copy)
                                self.A("act", "activation", [psm_], [smS[c]], out=smS[c][:], in_=psm_[:], func=AF.Copy)

                        def s2(h=h, qc=qc, qs=qs, c=c, kt=kt):
                            if kt != 31:
                                return
                            self.A("dve", "reciprocal", [smS[c]], [rc], out=rc[:], in_=smS[c][:])
                            self.A("dve", "tensor_tensor", [poS[c], rc], [res[c]], out=res[c][:], in0=poS[c][:], in1=rc[:], op=ALU.mult)
                            if c != 1:
                                return
                            self.A("dve", "scalar_tensor_tensor", [res[0], res[1], self.nlam], [dd], out=dd[:], in0=res[1][:],
                                   scalar=self.nlam[:, l:l + 1], in1=res[0][:], op0=ALU.mult, op1=ALU.add)
                            self.A("pool", "tensor_tensor", [dd], [sq], out=sq[:], in0=dd[:], in1=dd[:], op=ALU.mult)
                            self.mm(pfin, pfin[:], self.mats, self.mats[:, 3, :], sq, sq[:], True, True)
                            self.A("act", "activation", [pfin], [rs], out=rs[:], in_=pfin[:], func=AF.Ln, scale=1.0 / 128, bias=self.epsc[:, 0:1])
                            self.A("act", "activation", [rs], [rs], out=rs[:], in_=rs[:], func=AF.Exp, scale=-0.5)
                            self.A("dve", "scalar_tensor_tensor", [dd, rs, self.par], [dd], out=dd[:], in0=dd[:],
                                   scalar=self.par[:, l, P_SUB:P_SUB + 1], in1=rs[:], op0=ALU.mult, op1=ALU.mult)
                            o = ob[(h * NT + qc) % 2]
                            self.A("act", "activation", [dd], [o], out=o[:], in_=dd[:], func=AF.Copy, scale=float(1.0 - self.lambda_init))
                            self.store(o, self.OT[0, h * 128:(h + 1) * 128, qs], o[:])

                        items.append([s0, s1, s2])
        self.pipe(items, [0, 3, 12])

    def attn_b(self, l):
        qb = self.tile("qb", [128, 8, SEQ], BF16)
        kb = self.tile("kb", [128, SEQ], BF16)
        vb = self.tile("vb", [128, 32, 130], BF16)
        for hq in range(8):
            og = 1 - hq // 4
            self.A("pool", "memset", [], [qb], qb[og * 64:(og + 1) * 64, hq, :], 0.0)
        for hq in range(8):
            g, s = hq // 4, hq % 4
            self.load(qb, qb[g * 64:(g + 1) * 64, hq, :], self.QK[8 * 128 + hq * 64:8 * 128 + (hq + 1) * 64, :])
        self.load(kb, kb[:], self.QK[12 * 128:13 * 128, :])
        self.load(vb, vb[:], self.VB.rearrange("(a p) n -> p a n", p=128))
        psc = [self.ptile("psc%d" % i) for i in range(3)]
        po = [[self.ptile("pob%d%d" % (g, i)) for i in range(2)] for g in range(2)]
        pbc = self.ptile("pbc")
        pt = [self.tile("pt%d" % i, [128, TC], BF16) for i in range(6)]
        trow = self.tile("trow", [128, TC], F32)
        tbc = self.tile("tbc", [128, TC], F32)
        ob = [self.tile("obb%d" % i, [128, TC], BF16) for i in range(2)]
        items = []
        n = 0
        m = 0
        gi = 0
        for s in range(4):
            for qc in range(NT):
                qs = slice(qc * TC, (qc + 1) * TC)
                buf = gi % 2
                gi += 1
                for kt in range(32):
                    for g in range(2):
                        hq = g * 4 + s
                        ps_ = slice(g * 64, (g + 1) * 64)
                        o_ps = po[g][buf]
                        sc = psc[n % 3]
                        p = pt[n % 6]
                        n += 1

                        def s0(sc=sc, p=p, hq=hq, qs=qs, kt=kt):
                            self.mm(sc, sc[:], kb, kb[:, kt * 128:(kt + 1) * 128], qb, qb[:, hq, qs], True, True)
                            self.A("act", "activation", [sc], [p], out=p[:], in_=sc[:], func=AF.Exp, scale=0.125)

                        def s1(p=p, g=g, kt=kt, o_ps=o_ps):
                            self.mm(o_ps, o_ps[0:65, :], vb, vb[:, kt, g * 65:(g + 1) * 65], p, p[:], kt == 0, kt == 31)

                        def s2(g=g, hq=hq, qs=qs, kt=kt, o_ps=o_ps):
                            if kt != 31:
                                return
                            o = ob[hq % 2]
                            self.normalize_rows65(o_ps, o, o[0:64, :], TC, trow, tbc, pbc)
                            self.store(o, self.OT[1, hq * 64:(hq + 1) * 64, qs], o[0:64, :])

                        items.append([s0, s1, s2])
        self.pipe(items, [0, 3, 12])

    def attn_c(self, l):
        mask = self.tile("mask3", [128, 3, 128], F32)
        self.load(mask, mask[:].rearrange("p a b -> p (a b)"), self.c_mask3)
        qc_ = self.tile("qc", [128, 3, SEQ], BF16)
        kc_ = self.tile("kc", [128, 3, SEQ], BF16)
        vc_ = self.tile("vc", [128, 3, 32, 130], BF16)
        acc = [self.tile("acc%d" % i, [65, SEQ], F32) for i in range(2)]
        psc = [self.ptile("pscc%d" % i, (128, 384)) for i in range(3)]
        po = [self.ptile("poc%d" % i) for i in range(2)]
        pbc = self.ptile("pbcc")
        ex = [self.tile("ex%d" % i, [128, 3, 128], F32) for i in range(3)]
        pt = [self.tile("ptc%d" % i, [128, 3, 128], BF16) for i in range(4)]
        trow = self.tile("trowc", [128, TC], F32)
        ob = [self.tile("obc%d" % i, [128, TC], BF16) for i in range(2)]
        n = 0
        m = 0
        for jp in range(4):
            for g in range(3):
                self.load(qc_, qc_[:, g, :], self.QK[(13 + g * 4 + jp) * 128:(14 + g * 4 + jp) * 128, :])
                self.load(kc_, kc_[:, g, :], self.QK[(25 + g * 4 + jp) * 128:(26 + g * 4 + jp) * 128, :])
                self.load(vc_, vc_[:, g, :, :], self.VC[g].rearrange("(a p) n -> p a n", p=128)[:, :, jp * 130:(jp + 1) * 130])
            items = []
            for hh in range(2):
                ps_ = slice(hh * 64, (hh + 1) * 64)
                ac = acc[hh]
                hd = jp * 2 + hh
                for g in range(3):
                    r = DIL[g]
                    L = SEQ // r
                    tps = L // 128
                    for qb4 in range(8):
                        o_ps = po[m % 2]
                        m += 1
                        for u in range(4):
                            qb = qb4 * 4 + u
                            seg = qb // tps
                            kts = [k for k in (qb - 1, qb, qb + 1) if k // tps == seg and 0 <= k < 32]
                            j0 = kts[0] - (qb - 1)
                            nk = len(kts)
                            sc = psc[n % 3]
                            e_ = ex[n % 3]
                            p = pt[n % 4]
                            n += 1

                            def s0(sc=sc, e_=e_, p=p, kts=kts, j0=j0, nk=nk, qb=qb, g=g, ps_=ps_):
                                for k in kts:
                                    j = k - (qb - 1)
                                    self.mm(sc, sc[:, j * 128:(j + 1) * 128], kc_, kc_[ps_, g, k * 128:(k + 1) * 128],
                                            qc_, qc_[ps_, g, qb * 128:(qb + 1) * 128], True, True)
                                scv = sc[:, 0:384].rearrange("p (a b) -> p a b", a=3)
                                self.A("act", "activation", [sc], [e_], out=e_[:, j0:j0 + nk, :], in_=scv[:, j0:j0 + nk, :], func=AF.Exp, scale=0.125)
                                self.A("pool", "tensor_tensor", [e_, mask], [p], out=p[:, j0:j0 + nk, :], in0=e_[:, j0:j0 + nk, :],
                                       in1=mask[:, j0:j0 + nk, :], op=ALU.mult)

                            def s1(p=p, kts=kts, nk=nk, qb=qb, g=g, hh=hh, u=u, o_ps=o_ps, qb4=qb4, r=r, L=L, ac=ac, hd=hd):
                                for ki, k in enumerate(kts):
                                    j = k - (qb - 1)
                                    self.mm(o_ps, o_ps[0:65, u * 128:(u + 1) * 128], vc_, vc_[:, g, k, hh * 65:(hh + 1) * 65],
                                            p, p[:, j, :], ki == 0, ki == nk - 1)
                                if u != 3:
                                    return
                                pos0 = qb4 * 512
                                av = ac[:].rearrange("p (i c) -> p c i", c=r)
                                if L >= 512:
                                    c0, i0 = pos0 // L, pos0 % L
                                    dst = av[0:65, c0:c0 + 1, i0:i0 + 512]
                                    src = o_ps[0:65, :].rearrange("p (c i) -> p c i", c=1)
                                else:
                                    ncl = 512 // L
                                    c0 = pos0 // L
                                    dst = av[0:65, c0:c0 + ncl, :]
                                    src = o_ps[0:65, :].rearrange("p (c i) -> p c i", c=ncl)
                                if g == 0:
                                    self.A("act", "activation", [o_ps], [ac], out=dst, in_=src, func=AF.Copy)
                                else:
                                    self.A("dve", "tensor_tensor", [o_ps, ac], [ac], out=dst, in0=dst, in1=src, op=ALU.add)
                                if g == 2 and qb4 == 7:
                                    for t in range(NT):
                                        o = ob[(hd * NT + t) % 2]
                                        ts_ = slice(t * TC, (t + 1) * TC)
                                        self.A("dve", "reciprocal", [ac], [trow], out=trow[64:65, :], in_=ac[64:65, ts_])
                                        self.mm(pbc, pbc[0:64, :], self.mats, self.mats[64:65, 3, 0:64], trow, trow[64:65, :], True, True)
                                        self.A("dve", "tensor_tensor", [pbc, ac], [o], out=o[0:64, :], in0=pbc[0:64, :], in1=ac[0:64, ts_], op=ALU.mult)
                                        self.store(o, self.OT[2, hd * 64:(hd + 1) * 64, ts_], o[0:64, :])

                            items.append([s0, s1])
            self.pipe(items, [0, 2])

    def attn_d(self, l):
        qd = self.tile("qd", [128, 4, SEQ], BF16)
        kd = self.tile("kd", [128, 4, SEQ], BF16)
        ve = self.tile("ve", [128, 32, 520], BF16)
        vo = self.tile("vo", [128, 31, 520], BF16)
        E = self.tile("E", [128, 8, 15, 64], F32)
        cm = self.tile("cm", [128, 64], F32)
        self.load(cm, cm[:], self.c_colmask)
        for i in range(4):
            self.load(qd, qd[:, i, :], self.QK[(37 + i) * 128:(38 + i) * 128, :])
            self.load(kd, kd[:, i, :], self.QK[(41 + i) * 128:(42 + i) * 128, :])
        for j in range(4):
            self.load(ve, ve[:, j * 8:(j + 1) * 8, :], self.VD[j * 1024:(j + 1) * 1024, :].rearrange("(a p) n -> p a n", p=128))
        for j in range(4):
            na = 8 if j < 3 else 7
            self.load(vo, vo[:, j * 8:j * 8 + na, :], self.VD[64 + j * 1024:64 + j * 1024 + na * 128, :].rearrange("(a p) n -> p a n", p=128))
        for h in range(8):
            self.load(E, E[:, h, :, :].rearrange("p a b -> p (a b)"), self.rpbE[l][:, h * 960:(h + 1) * 960])
        for h in range(8):
            self.A("act", "activation", [E], [E], out=E[:, h, :, :], in_=E[:, h, :, :], func=AF.Exp)
            self.A("pool", "tensor_tensor", [E, cm], [E], out=E[:, h, :, :], in0=E[:, h, :, :],
                   in1=cm[:].rearrange("p (a b) -> p a b", a=1).to_broadcast([128, 15, 64]), op=ALU.mult)
        psc = [self.ptile("pscd%d" % i, (128, 256)) for i in range(3)]
        po = [self.ptile("pod%d" % i) for i in range(2)]
        pbc = self.ptile("pbcd")
        ex = [self.tile("exd%d" % i, [128, 4, 64], F32) for i in range(3)]
        pt = [self.tile("ptd%d" % i, [128, 4, 64], BF16) for i in range(4)]
        trow = self.tile("trowd", [128, TC], F32)
        tbc = self.tile("tbcd", [128, TC], F32)
        ob = [self.tile("obd%d" % i, [128, TC], BF16) for i in range(2)]
        items = []
        n = 0
        m = 0
        for h in range(8):
            ch, hh = h // 2, h % 2
            ps_ = slice(hh * 64, (hh + 1) * 64)
            for r8 in range(8):
                o_ps = po[m % 2]
                o = ob[m % 2]
                m += 1
                for u in range(8):
                    r = r8 * 8 + u
                    rs_ = min(max(r - 4, 0), 56)
                    base = rs_ - r + 7
                    sc = psc[n % 3]
                    e_ = ex[n % 3]
                    p = pt[n % 4]
                    n += 1

                    def s0(sc=sc, e_=e_, p=p, r=r, rs_=rs_, base=base, h=h, ch=ch, ps_=ps_, n=n):
                        for i in range(4):
                            k0 = (rs_ + 2 * i) * 64
                            self.mm(sc, sc[:, i * 64:(i + 1) * 64], kd, kd[ps_, ch, k0:k0 + 128], qd, qd[ps_, ch, r * 64:(r + 1) * 64], True, True)
                        self.A("act", "activation", [sc], [e_], out=e_[:], in_=sc[:, 0:256].rearrange("p (a b) -> p a b", a=4), func=AF.Exp, scale=0.125)
                        ev = E[:, h, base:base + 7:2, :]
                        self.A("pool" if n % 2 == 0 else "dve", "tensor_tensor", [e_, E], [p], out=p[:], in0=e_[:], in1=ev, op=ALU.mult)

                    def s1(p=p, rs_=rs_, h=h, u=u, o_ps=o_ps, o=o, r8=r8):
                        for i in range(4):
                            row0 = rs_ + 2 * i
                            if row0 % 2 == 0:
                                vt, vi = ve, row0 // 2
                            else:
                                vt, vi = vo, (row0 - 1) // 2
                            self.mm(o_ps, o_ps[0:65, u * 64:(u + 1) * 64], vt, vt[:, vi, h * 65:(h + 1) * 65], p, p[:, i, :], i == 0, i == 3)
                        if u != 7:
                            return
                        self.normalize_rows65(o_ps, o, o[0:64, :], TC, trow, tbc, pbc)
                        self.store(o, self.OT[3, h * 64:(h + 1) * 64, r8 * TC:(r8 + 1) * TC], o[0:64, :])

                    items.append([s0, s1])
        self.pipe(items, [0, 2])

    def merge(self, l, xin):
        wb = self.tile("wbr", [128, 16, DM], BF16)
        wo = self.tile("wo", [128, 8, DM], BF16)
        stg = [self.tile("mstg%d" % i, [128, 4, DM], F32) for i in range(2)]
        n = 0
        for i in range(4):
            s = stg[n % 2]
            n += 1
            self.load(s, s[:], self.w_branch[l, i].rearrange("(k p) n -> p k n", p=128))
            self.A("pool" if n % 2 == 0 else "dve", "tensor_copy", [s], [wb], out=wb[:, i * 4:(i + 1) * 4, :], in_=s[:])
        for j in range(2):
            s = stg[n % 2]
            n += 1
            self.load(s, s[:], self.w_out[l][j * 512:(j + 1) * 512, :].rearrange("(k p) n -> p k n", p=128))
            self.A("pool" if n % 2 == 0 else "dve", "tensor_copy", [s], [wo], out=wo[:, j * 4:(j + 1) * 4, :], in_=s[:])
        ot = [self.tile("mot%d" % i, [128, 16, TC], BF16) for i in range(2)]
        xs = [self.tile("mx%d" % i, [128, 8, TC], F32) for i in range(2)]
        gt = [self.tile("mg%d" % i, [128, TC], F32) for i in range(4)]
        mt = [self.tile("mm%d" % i, [128, 8, TC], BF16) for i in range(2)]
        acc = [self.tile("macc%d" % i, [128, TC], F32) for i in range(2)]
        tmp = [self.tile("mtmp%d" % i, [128, TC], F32) for i in range(2)]
        py = [self.ptile("py%d" % i) for i in range(3)]
        px = [self.ptile("px%d" % i) for i in range(2)]
        xo = [self.tile("mxo%d" % i, [128, TC], F32) for i in range(3)]
        xv = xin.rearrange("(c p) n -> p c n", p=128)
        otv = self.OT.rearrange("b (k p) n -> p (b k) n", p=128)
        ng = 0
        ny = 0
        nx = 0
        for t in range(NT):
            ts_ = slice(t * TC, (t + 1) * TC)
            o = ot[t % 2]
            x = xs[t % 2]
            mtt = mt[t % 2]
            for i in range(4):
                self.load(o, o[:, i * 4:(i + 1) * 4, :], otv[:, i * 4:(i + 1) * 4, ts_])
            self.load(x, x[:], xv[:, :, ts_])
            for f in range(8):
                a = acc[f % 2]
                for i in range(4):
                    g = gt[ng % 4]
                    ng += 1
                    self.load(g, g[:], self.G[(i * 8 + f) * 128:(i * 8 + f + 1) * 128, ts_])
                    p = py[ny % 3]
                    ny += 1
                    for k in range(4):
                        self.mm(p, p[:], wb, wb[:, i * 4 + k, f * 128:(f + 1) * 128], o, o[:, i * 4 + k, :], k == 0, k == 3)
                    if i == 0:
                        self.A("dve", "tensor_tensor", [p, g], [a], out=a[:], in0=p[:], in1=g[:], op=ALU.mult)
                    else:
                        tm = tmp[i % 2]
                        self.A("dve", "tensor_tensor", [p, g], [tm], out=tm[:], in0=p[:], in1=g[:], op=ALU.mult)
                        if i < 3:
                            self.A("pool", "tensor_tensor", [a, tm], [a], out=a[:], in0=a[:], in1=tm[:], op=ALU.add)
                        else:
                            self.A("pool", "tensor_tensor", [a, tm], [mtt], out=mtt[:, f, :], in0=a[:], in1=tm[:], op=ALU.add)
            for fo in range(8):
                p = px[nx % 2]
                xo_ = xo[nx % 3]
                nx += 1
                for k in range(8):
                    self.mm(p, p[:], wo, wo[:, k, fo * 128:(fo + 1) * 128], mtt, mtt[:, k, :], k == 0, k == 7)
                self.A("dve", "tensor_tensor", [p, x], [xo_], out=xo_[:], in0=p[:], in1=x[:, fo, :], op=ALU.add)
                self.store(xo_, self.XM[fo * 128:(fo + 1) * 128, ts_], xo_[:])

    def ffn_up(self, l, hT):
        wst = [self.tile("fst%d" % i, [128, 8, 256], F32) for i in range(2)]
        wbf = [self.tile("fbf%d" % i, [128, 8, 256], BF16) for i in range(4)]
        ua = self.tile("ua", [128, SEQ + 2], F32)
        ug = self.tile("ug", [128, SEQ + 2], F32)
        ca = self.tile("ca", [128, SEQ], F32)
        cg = self.tile("cg", [128, SEQ], F32)
        mrow = [self.tile("mrow%d" % i, [128, SEQ], BF16) for i in range(2)]
        pacc = [self.ptile("fpa%d" % i) for i in range(4)]
        for u in (ua, ug):
            self.A("pool", "memset", [], [u], u[:, 0:1], 0.0)
            self.A("pool", "memset", [], [u], u[:, SEQ + 1:SEQ + 2], 0.0)
        wv = self.w_up[l].rearrange("(k p) n -> p k n", p=128)
        ngrp = 0
        nacc = 0
        for j in range(11):
            w = 256
            wbs = []
            for part in range(2):
                st = wst[ngrp % 2]
                wb = wbf[ngrp % 4]
                ngrp += 1
                c0 = part * D_FF + j * 256
                self.load(st, st[:, :, 0:w], wv[:, :, c0:c0 + w])
                self.A("pool" if part == 0 else "dve", "tensor_copy", [st], [wb], out=wb[:, :, 0:w], in_=st[:, :, 0:w])
                wbs.append(wb)
            for ii in range(w // 128):
                i = j * 2 + ii
                for part, (u, cdst) in enumerate(((ua, ca), (ug, cg))):
                    wb = wbs[part]
                    for t in range(NT):
                        p = pacc[nacc % 4]
                        nacc += 1
                        for k in range(8):
                            self.mm(p, p[:], wb, wb[:, k, ii * 128:(ii + 1) * 128], hT, hT[:, k, t * TC:(t + 1) * TC], k == 0, k == 7)
                        if t % 2 == 0:
                            self.A("act", "activation", [p], [u], out=u[:, 1 + t * TC:1 + (t + 1) * TC], in_=p[:], func=AF.Copy)
                        else:
                            self.A("dve", "tensor_copy", [p], [u], out=u[:, 1 + t * TC:1 + (t + 1) * TC], in_=p[:])
                    ch = part * 22 + i
                    w0 = self.par[:, l, P_CW + 0 * 44 + ch:P_CW + 0 * 44 + ch + 1]
                    w1 = self.par[:, l, P_CW + 1 * 44 + ch:P_CW + 1 * 44 + ch + 1]
                    w2 = self.par[:, l, P_CW + 2 * 44 + ch:P_CW + 2 * 44 + ch + 1]
                    bb = self.par[:, l, P_CB + ch:P_CB + ch + 1]
                    eng = "dve"
                    for hf in range(2):
                        hs = slice(hf * 2048, (hf + 1) * 2048)
                        self.A(eng, "tensor_scalar", [u, self.par], [cdst], out=cdst[:, hs], in0=u[:, hf * 2048:hf * 2048 + 2048],
                               scalar1=w0, scalar2=bb, op0=ALU.mult, op1=ALU.add)
                        self.A(eng, "scalar_tensor_tensor", [u, cdst, self.par], [cdst], out=cdst[:, hs], in0=u[:, 1 + hf * 2048:1 + hf * 2048 + 2048],
                               scalar=w1, in1=cdst[:, hs], op0=ALU.mult, op1=ALU.add)
                        self.A(eng, "scalar_tensor_tensor", [u, cdst, self.par], [cdst], out=cdst[:, hs], in0=u[:, 2 + hf * 2048:2 + hf * 2048 + 2048],
                               scalar=w2, in1=cdst[:, hs], op0=ALU.mult, op1=ALU.add)
                mr = mrow[i % 2]
                for hf in range(2):
                    hs = slice(hf * 2048, (hf + 1) * 2048)
                    self.A("act", "activation", [ca], [ca], out=ca[:, hs], in_=ca[:, hs], func=AF.Silu)
                    self.A("pool" if hf == 0 else "dve", "tensor_tensor", [ca, cg], [mr], out=mr[:, hs], in0=ca[:, hs], in1=cg[:, hs], op=ALU.mult)
                self.store(mr, self.M[i * 128:(i + 1) * 128, :], mr[:])

    def ffn_down(self, l, xout):
        wd = self.tile("wd", [128, 22, DM], BF16)
        stg = [self.tile("dstg%d" % i, [128, 2, DM], F32) for i in range(2)]
        wv = self.w_down[l].rearrange("(k p) n -> p k n", p=128)
        for j in range(11):
            s = stg[j % 2]
            self.load(s, s[:], wv[:, j * 2:(j + 1) * 2, :])
            self.A("pool" if j % 2 == 0 else "dve", "tensor_copy", [s], [wd], out=wd[:, j * 2:(j + 1) * 2, :], in_=s[:])
        mt = [self.tile("dm%d" % i, [128, 22, TC], BF16) for i in range(2)]
        xs = [self.tile("dx%d" % i, [128, 8, TC], F32) for i in range(2)]
        xo = [self.tile("dxo%d" % i, [128, TC], F32) for i in range(3)]
        px = [self.ptile("dpx%d" % i) for i in range(3)]
        mv = self.M.rearrange("(k p) n -> p k n", p=128)
        xv = self.XM.rearrange("(c p) n -> p c n", p=128)
        nx = 0
        for t in range(NT):
            ts_ = slice(t * TC, (t + 1) * TC)
            m = mt[t % 2]
            x = xs[t % 2]
            self.load(m, m[:, 0:11, :], mv[:, 0:11, ts_])
            self.load(m, m[:, 11:22, :], mv[:, 11:22, ts_])
            self.load(x, x[:], xv[:, :, ts_])
            for fo in range(8):
                p = px[nx % 3]
                xo_ = xo[nx % 3]
                nx += 1
                for k in range(22):
                    self.mm(p, p[:], wd, wd[:, k, fo * 128:(fo + 1) * 128], m, m[:, k, :], k == 0, k == 21)
                self.A("dve", "tensor_tensor", [p, x], [xo_], out=xo_[:], in0=p[:], in1=x[:, fo, :], op=ALU.add)
                self.store(xo_, xout[fo * 128:(fo + 1) * 128, ts_], xo_[:])


_CACHE = {}


def _get_nc(n_layers=NL, debug=False):
    key = (n_layers, debug)
    if key not in _CACHE:
        b = Builder(n_layers, debug)
        _CACHE[key] = (b.build(), b)
    return _CACHE[key]


def make_in_maps(inp):
    consts = _host_consts()
    params = _pack_params(inp)
    lam = np.ascontiguousarray(np.asarray(inp["lam"], np.float32).reshape(1, NL * 256))
    rpbE = _pack_rpb(np.asarray(inp["rpb"], np.float32))
    shared = dict(
        w_in=np.ascontiguousarray(inp["w_in"], np.float32),
        w_branch=np.ascontiguousarray(inp["w_branch"], np.float32),
        w_out=np.ascontiguousarray(inp["w_out"], np.float32),
        w_up=np.ascontiguousarray(inp["w_up"], np.float32),
        w_down=np.ascontiguousarray(inp["w_down"], np.float32),
        params=params, lam=lam, rpbE=rpbE,
        tab1=consts["tab1"], tab2=consts["tab2"], mats=consts["mats"], mask3=consts["mask3"],
        colmask=consts["colmask"],
    )
    x = np.asarray(inp["x"], np.float32)
    maps = []
    for b in range(8):
        d = dict(shared)
        d["xT"] = np.ascontiguousarray(x[b].T)
        maps.append(d)
    return maps


def kernel(**inputs):
    inp = {k: np.asarray(v) for k, v in inputs.items()}
    nc, _ = _get_nc()
    maps = make_in_maps(inp)
    res = run_bass_kernel_spmd(nc, maps, core_ids=list(range(8)))
    out = np.stack([np.ascontiguousarray(res.results[b]["yT"].T) for b in range(8)], axis=0)
    return out.astype(np.float32)
```

```python
import numpy as np
from contextlib import ExitStack
import concourse.bass as bass
import concourse.mybir as mybir
from concourse.bass_utils import run_bass_kernel_spmd

F32 = mybir.dt.float32
BF16 = mybir.dt.bfloat16
AF = mybir.ActivationFunctionType
ALU = mybir.AluOpType
AX = mybir.AxisListType

EPOCH = 24000
SAME_SYNC = True

SEQ = 4096
DM = 1024
NL = 4
N_IN = 12544
D_FF = 2816
EPS = 1e-6
NT = 8
TC = 512
COL = dict(aq=0, ak=512, av=1024, bq=1536, bk=2048, bv=2176, cq=2304, ck=3840, cv=5376,
           dq=6912, dk=7424, dv=7936, g=8448)
DIL = (1, 4, 16)


class Buf:
    __slots__ = ("name", "w", "r")

    def __init__(self, name=""):
        self.name = name
        self.w = None
        self.r = []


class _Op:
    __slots__ = ("eng", "fn", "deps", "ldeps", "ddeps", "dma", "pub", "seq")

    def __init__(self, eng, fn, deps, ddeps, dma, ldeps=()):
        self.eng = eng
        self.fn = fn
        self.deps = deps
        self.ldeps = ldeps
        self.ddeps = ddeps
        self.dma = dma
        self.pub = False
        self.seq = None


class Sched:
    ENGS = ("pe", "act", "dve", "pool", "sp")

    def __init__(self, nc, stack):
        self.nc = nc
        self.stack = stack
        self.ops = []
        self.base = 0
        self.cnt = {e: 0 for e in self.ENGS}
        self.sems = {e: [] for e in self.ENGS}
        self.dsem = {}
        self.dcnt = {}
        self.nsem = 0
        self.ninstr = 0

    def _new_sem(self, name):
        self.nsem += 1
        return self.stack.enter_context(self.nc.semaphore(name))

    def _esem(self, e, ep):
        while len(self.sems[e]) <= ep:
            self.sems[e].append(self._new_sem("s_%s_%d" % (e, len(self.sems[e]))))
        return self.sems[e][ep]

    def op(self, eng, fn, reads=(), writes=(), dma=None):
        deps = set()
        ldeps = set()
        ddeps = {}
        base = self.base
        ops = self.ops

        def add(i, dst):
            if i is None or i < base:
                return
            o = ops[i - base]
            if o.dma is not None:
                ddeps[o.dma] = self.dcnt[o.dma]
            else:
                dst.add(i)

        for b in reads:
            add(b.w, deps)
        for b in writes:
            add(b.w, ldeps)
            for r in b.r:
                add(r, ldeps)
        ldeps -= deps
        idx = base + len(ops)
        if dma is not None:
            if dma not in self.dsem:
                self.dsem[dma] = self._new_sem("d_%d" % len(self.dsem))
                self.dcnt[dma] = 0
            self.dcnt[dma] += 16
        ops.append(_Op(eng, fn, deps, ddeps, dma, ldeps))
        for b in reads:
            b.r.append(idx)
        for b in writes:
            b.w = idx
            b.r = []
        return idx

    def flush(self):
        ops = self.ops
        base = self.base
        last = {}
        for i, o in enumerate(ops):
            if o.dma is None and o.fn is not None:
                last[o.eng] = i
        for e in self.ENGS:
            deps = set(base + i for ee, i in last.items() if ee != e)
            ops.append(_Op(e, None, deps, dict(self.dcnt), None))
        import bisect

        def cross(od, o):
            return od.eng != o.eng or o.dma is not None or (SAME_SYNC and o.eng != "pe")

        for o in ops:
            for d in o.deps:
                od = ops[d - base]
                if cross(od, o):
                    od.pub = True
        publ = {e: [] for e in self.ENGS}
        for i, o in enumerate(ops):
            if o.pub:
                publ[o.eng].append(i)
        for i, o in enumerate(ops):
            for d in o.ldeps:
                od = ops[d - base]
                if not cross(od, o):
                    continue
                if od.pub:
                    o.deps.add(d)
                    continue
                lst = publ[od.eng]
                k = bisect.bisect_left(lst, d - base)
                if k < len(lst) and lst[k] < i:
                    o.deps.add(base + lst[k])
                else:
                    od.pub = True
                    bisect.insort(lst, d - base)
                    o.deps.add(d)
        for o in ops:
            if o.pub:
                c = self.cnt[o.eng]
                self.cnt[o.eng] = c + 1
                o.seq = (c // EPOCH, c % EPOCH + 1)
        per = {e: [] for e in self.ENGS}
        seen = {e: {} for e in self.ENGS}
        seend = {e: {} for e in self.ENGS}
        for o in ops:
            F = o.eng
            need = {}
            for d in o.deps:
                od = ops[d - base]
                if od.eng == F and o.dma is None and (F == "pe" or not SAME_SYNC):
                    continue
                if od.seq > need.get(od.eng, (-1, -1)):
                    need[od.eng] = od.seq
            waits = []
            for E, sq in need.items():
                if seen[F].get(E, (-1, -1)) >= sq:
                    continue
                seen[F][E] = sq
                waits.append((self._esem(E, sq[0]), sq[1]))
            for k, v in o.ddeps.items():
                if v == 0 or seend[F].get(k, 0) >= v:
                    continue
                seend[F][k] = v
                waits.append((self.dsem[k], v))
            inc = self._esem(F, o.seq[0]) if o.pub else None
            dinc = self.dsem[o.dma] if o.dma is not None else None
            per[F].append((waits, o.fn, inc, dinc))
            self.ninstr += len(waits) + 1

        def run(eng, lst):
            for waits, fn, inc, dinc in lst:
                emb = None
                if fn is not None and dinc is None and waits:
                    emb = waits[-1]
                    waits = waits[:-1]
                for s, v in waits:
                    eng.wait_ge(s, v)
                if fn is None:
                    continue
                ins = fn(eng)
                if emb is not None:
                    ins._wait_ge(emb[0], emb[1])
                if inc is not None:
                    ins.then_inc(inc, 1)
                if dinc is not None:
                    ins.then_inc(dinc, 16)

        with self.nc.Block() as block:
            @block.tensor
            def _(e):
                run(e, per["pe"])

            @block.scalar
            def _(e):
                run(e, per["act"])

            @block.vector
            def _(e):
                run(e, per["dve"])

            @block.gpsimd
            def _(e):
                run(e, per["pool"])

            @block.sync
            def _(e):
                run(e, per["sp"])
        self.base = base + len(ops)
        self.ops = []


class Tl:
    __slots__ = ("t", "b", "name", "psum")

    def __init__(self, t, name, psum=False):
        self.t = t
        self.b = Buf(name)
        self.name = name
        self.psum = psum

    def __getitem__(self, k):
        return self.t[k]


def _host_consts():
    c = {}
    pos = np.arange(SEQ, dtype=np.float32)
    p = np.arange(128)
    inv32 = (np.float32(10000.0) ** (-(np.arange(0, 64, 2, dtype=np.float32) / np.float32(64)))).astype(np.float32)
    f = p % 32
    half = (p % 64) // 32
    ang = (pos[None, :] * inv32[f][:, None]).astype(np.float32)
    sgn = np.where(half == 0, -1.0, 1.0).astype(np.float32)[:, None]
    tab1 = np.stack([np.cos(ang), np.sin(ang) * sgn], axis=1).astype(np.float32)
    inv16 = (np.float32(10000.0) ** (-(np.arange(0, 32, 2, dtype=np.float32) / np.float32(32)))).astype(np.float32)
    pp = p % 64
    blk = pp // 32
    q = pp % 32
    half2 = q // 16
    f2 = q % 16
    prow = np.floor(pos / 64).astype(np.float32)
    pcol = (pos - prow * 64).astype(np.float32)
    pf = np.where(blk[:, None] == 0, prow[None, :], pcol[None, :]).astype(np.float32)
    ang2 = (pf * inv16[f2][:, None]).astype(np.float32)
    sgn2 = np.where(half2 == 0, -1.0, 1.0).astype(np.float32)[:, None]
    tab2 = np.stack([np.cos(ang2), np.sin(ang2) * sgn2], axis=1).astype(np.float32)
    c["tab1"] = tab1.reshape(128, 2 * SEQ)
    c["tab2"] = tab2.reshape(128, 2 * SEQ)
    mats = np.zeros((128, 5, 128), np.float32)
    for m in range(128):
        part1 = m + 32 if (m % 64) // 32 == 0 else m - 32
        mats[part1, 0, m] = 1.0
        part2 = m + 16 if (m % 32) // 16 == 0 else m - 16
        mats[part2, 1, m] = 1.0
    mats[:, 2, :] = (p[:, None] // 64 == p[None, :] // 64)
    mats[:, 3, :] = 1.0
    mats[:, 4, :] = np.eye(128)
    c["mats"] = mats.reshape(128, 5 * 128)
    i = np.arange(128)[:, None]
    m = np.arange(128)[None, :]
    mask3 = np.stack([(i - m >= 64), (np.abs(i - m) <= 64), (m - i >= 64)], axis=1).astype(np.float32)
    c["mask3"] = mask3.reshape(128, 3 * 128)
    kc = np.arange(128)[:, None] % 64
    qc = np.arange(64)[None, :]
    ws = np.clip(qc - 8, 0, 48)
    c["colmask"] = ((kc >= ws) & (kc < ws + 16)).astype(np.float32)
    return c


P_G1 = 0
P_G2 = 8
P_QKG = 16
P_SUB = 24
P_CW = 25
P_CB = 25 + 132
P_N = 25 + 132 + 44


def _pack_params(inp):
    out = np.zeros((128, NL, P_N), np.float32)
    for l in range(NL):
        out[:, l, P_G1:P_G1 + 8] = inp["norm1_g"][l].reshape(8, 128).T
        out[:, l, P_G2:P_G2 + 8] = inp["norm2_g"][l].reshape(8, 128).T
        qg = inp["qk_g"][l].reshape(8, 64)
        out[:, l, P_QKG:P_QKG + 8] = np.concatenate([qg, qg], axis=1).T
        out[:, l, P_SUB] = inp["subln_g"][l]
        cw = inp["conv_w"][l].reshape(3, 44, 128)
        out[:, l, P_CW:P_CW + 132] = cw.transpose(2, 0, 1).reshape(128, 132)
        out[:, l, P_CB:P_CB + 44] = inp["conv_b"][l].reshape(44, 128).T
    return out.reshape(128, NL * P_N)


def _pack_rpb(rpb):
    kc = np.arange(64)[:, None]
    qc = np.arange(64)[None, :]
    idx = np.clip(kc - qc + 15, 0, 30)
    g = rpb[:, :, :, idx]
    g = np.transpose(g, (0, 3, 1, 2, 4))
    lo = g
    hi = np.concatenate([g[:, :, :, 1:, :], g[:, :, :, 14:15, :]], axis=3)
    return np.ascontiguousarray(np.concatenate([lo, hi], axis=1)).reshape(NL, 128, 8 * 15 * 64)


class Builder:
    def __init__(self, n_layers=NL, debug=False):
        self.n_layers = n_layers
        self.debug = debug
        self.nc = bass.Bass("TRN2", target_bir_lowering=False)
        nc = self.nc
        di = lambda n, s, dt=F32: nc.dram_tensor(n, s, dt, kind="ExternalInput").ap()
        self.xT = di("xT", [DM, SEQ])
        self.w_in = di("w_in", [NL, DM, N_IN])
        self.w_branch = di("w_branch", [NL, 4, 512, DM])
        self.w_out = di("w_out", [NL, DM, DM])
        self.w_up = di("w_up", [NL, DM, 2 * D_FF])
        self.w_down = di("w_down", [NL, D_FF, DM])
        self.params = di("params", [128, NL * P_N])
        self.lam = di("lam", [1, NL * 256])
        self.rpbE = di("rpbE", [NL, 128, 8 * 15 * 64])
        self.c_tab1 = di("tab1", [128, 2 * SEQ])
        self.c_tab2 = di("tab2", [128, 2 * SEQ])
        self.c_mats = di("mats", [128, 5 * 128])
        self.c_mask3 = di("mask3", [128, 3 * 128])
        self.c_colmask = di("colmask", [128, 64])
        self.yT = nc.dram_tensor("yT", [DM, SEQ], F32, kind="ExternalOutput").ap()
        kind = "ExternalOutput" if debug else "Internal"
        ds = lambda n, s, dt: nc.dram_tensor(n, s, dt, kind=kind).ap()
        self.QK = ds("s_qk", [45 * 128, SEQ], BF16)
        self.VA = ds("s_va", [SEQ, 512], BF16)
        self.VB = ds("s_vb", [SEQ, 130], BF16)
        self.VC = ds("s_vc", [3, SEQ, 520], BF16)
        self.VD = ds("s_vd", [SEQ, 520], BF16)
        self.G = ds("s_g", [4096, SEQ], F32)
        self.OT = ds("s_ot", [4, 512, SEQ], BF16)
        self.M = ds("s_m", [D_FF, SEQ], BF16)
        self.XM = ds("s_xm", [DM, SEQ], F32)
        self.XA = ds("s_xa", [DM, SEQ], F32)
        self.XB = ds("s_xb", [DM, SEQ], F32)

    def tile(self, name, shape, dt):
        t = self.ph.enter_context(self.nc.sbuf_tensor(name + "_%d" % self.uid, shape, dt))
        self.uid += 1
        return Tl(t, name)

    def ptile(self, name, shape=(128, 512), dt=F32):
        t = self.ph.enter_context(self.nc.psum_tensor(name + "_%d" % self.uid, [128, 512], F32))
        self.uid += 1
        return Tl(t, name, True)

    def A(self, eng, meth, reads, writes, *a, **k):
        wr = [x.b for x in writes] + [x.b for x in reads if x.psum]
        self.S.op(eng, lambda e: getattr(e, meth)(*a, **k), [x.b for x in reads], wr)

    def load(self, dst_tl, out_ap, in_ap, q="sp"):
        self.S.op(q, lambda e: e.dma_start(out=out_ap, in_=in_ap), [], [dst_tl.b], dma=("L", dst_tl.name))

    def store(self, src_tl, out_ap, in_ap, q="pool"):
        self.S.op(q, lambda e: e.dma_start(out=out_ap, in_=in_ap), [src_tl.b], [], dma=("S", src_tl.name))

    def mm(self, out_tl, out_ap, l_tl, l_ap, r_tl, r_ap, start, stop):
        self.S.op("pe", lambda e: e.matmul(out_ap, l_ap, r_ap, start=start, stop=stop),
                  [l_tl.b, r_tl.b], [out_tl.b])

    def begin(self):
        self.ph = ExitStack()
        self.ph.__enter__()

    def end(self):
        self.S.flush()
        self.ph.__exit__(None, None, None)

    def build(self):
        nc = self.nc
        self.uid = 0
        with ExitStack() as top:
            self.S = Sched(nc, top)
            self.top = top
            self.ph = top
            self.mats = self.tile("mats", [128, 5, 128], F32)
            self.onesb = self.tile("onesb", [128, 128], BF16)
            self.blkb = self.tile("blkb", [128, 128], BF16)
            self.par = self.tile("par", [128, NL, P_N], F32)
            self.nlam = self.tile("nlam", [128, NL], F32)
            self.begin()
            self.lamt = self.tile("lamt", [1, NL * 256], F32)
            self.load(self.mats, self.mats[:].rearrange("p a b -> p (a b)"), self.c_mats)
            self.load(self.par, self.par[:].rearrange("p a b -> p (a b)"), self.params)
            self.load(self.lamt, self.lamt[:], self.lam)
            self.A("dve", "tensor_copy", [self.mats], [self.onesb], out=self.onesb[:], in_=self.mats[:, 3, :])
            self.A("dve", "tensor_copy", [self.mats], [self.blkb], out=self.blkb[:], in_=self.mats[:, 2, :])
            self.lambda_setup()
            self.end()
            xin = self.xT
            for l in range(self.n_layers):
                last = (l == self.n_layers - 1)
                xout = self.yT if last else (self.XA if l % 2 == 0 else self.XB)
                self.layer(l, xin, xout)
                xin = xout
        return nc

    def lambda_setup(self):
        import math
        pr = self.tile("lampr", [1, NL * 2, 64], F32)
        sm = self.tile("lamsm", [1, NL * 2], F32)
        lv = self.tile("lamv", [1, NL], F32)
        ps = self.ptile("lamps", (128, NL))
        lt = self.lamt[:].rearrange("p (l a d) -> p l a d", l=NL, a=4)
        for l in range(NL):
            for j in range(2):
                self.A("dve", "tensor_tensor", [self.lamt], [pr], out=pr[:, l * 2 + j, :], in0=lt[:, l, 2 * j, :],
                       in1=lt[:, l, 2 * j + 1, :], op=ALU.mult)
        self.A("dve", "tensor_reduce", [pr], [sm], out=sm[:], in_=pr[:], axis=AX.X, op=ALU.add)
        self.A("act", "activation", [sm], [sm], out=sm[:], in_=sm[:], func=AF.Exp)
        smv = sm[:].rearrange("p (l j) -> p l j", j=2)
        for l in range(NL):
            li = 0.8 - 0.6 * math.exp(-0.3 * l)
            self.A("dve", "scalar_tensor_tensor", [sm], [lv], out=lv[:, l:l + 1], in0=smv[:, l, 1:2], scalar=-li,
                   in1=smv[:, l, 0:1], op0=ALU.add, op1=ALU.subtract)
        self.mm(ps, ps[:, 0:NL], self.mats, self.mats[0:1, 3, :], lv, lv[:], True, True)
        self.A("dve", "tensor_copy", [ps], [self.nlam], out=self.nlam[:], in_=ps[:, 0:NL])

    def rmsnorm_to_hT(self, l, xsrc, hT, goff):
        xs = [self.tile("nx%d" % i, [128, 8, TC], F32) for i in range(2)]
        sq = [self.tile("nsq%d" % i, [128, 8, TC], F32) for i in range(1)]
        rs = [self.tile("nrs%d" % i, [128, TC], F32) for i in range(2)]
        ps = [self.ptile("nps%d" % i) for i in range(2)]
        xv = xsrc.rearrange("(c p) n -> p c n", p=128)
        for t in range(NT):
            x = xs[t % 2]
            s = sq[0]
            r = rs[t % 2]
            p = ps[t % 2]
            self.load(x, x[:], xv[:, :, t * TC:(t + 1) * TC])
            self.A("pool", "tensor_tensor", [x], [s], out=s[:], in0=x[:], in1=x[:], op=ALU.mult)
            for c in range(8):
                self.mm(p, p[:], self.mats, self.mats[:, 3, :], s, s[:, c, :], c == 0, c == 7)
            self.A("act", "activation", [p], [r], out=r[:], in_=p[:], func=AF.Ln, scale=1.0 / DM, bias=self.epsc[:, 0:1])
            self.A("act", "activation", [r], [r], out=r[:], in_=r[:], func=AF.Exp, scale=-0.5)
            for c in range(8):
                self.A("dve", "scalar_tensor_tensor", [x, r, self.par], [hT], out=hT[:, c, t * TC:(t + 1) * TC],
                       in0=x[:, c, :], scalar=self.par[:, l, goff + c:goff + c + 1], in1=r[:], op0=ALU.mult, op1=ALU.mult)

    def load_weight_bf16(self, dst, dst_ap, src_ap, stg, shape_ap, eng):
        self.load(stg, shape_ap, src_ap)
        self.A(eng, "tensor_copy", [stg], [dst], out=dst_ap, in_=shape_ap)

    def layer(self, l, xin, xout):
        import math
        self.lambda_init = 0.8 - 0.6 * math.exp(-0.3 * l)
        self.begin()
        self.epsc = self.tile("epsc", [128, 1], F32)
        self.A("pool", "memset", [], [self.epsc], self.epsc[:], EPS)
        hT = self.tile("hT", [128, 8, SEQ], BF16)
        sub = ExitStack()
        outer = self.ph
        self.ph = sub
        sub.__enter__()
        self.rmsnorm_to_hT(l, xin, hT, P_G1)
        self.S.flush()
        sub.__exit__(None, None, None)
        self.ph = outer
        self.proj(l, hT)
        self.end()
        self.begin()
        self.epsc = self.tile("epsc", [128, 1], F32)
        self.A("pool", "memset", [], [self.epsc], self.epsc[:], EPS)
        self.attn_a(l)
        self.end()
        self.begin()
        self.attn_b(l)
        self.end()
        self.begin()
        self.attn_c(l)
        self.end()
        self.begin()
        self.attn_d(l)
        self.end()
        self.begin()
        self.merge(l, xin)
        self.end()
        self.begin()
        self.epsc = self.tile("epsc", [128, 1], F32)
        self.A("pool", "memset", [], [self.epsc], self.epsc[:], EPS)
        hT = self.tile("hT2", [128, 8, SEQ], BF16)
        sub = ExitStack()
        outer = self.ph
        self.ph = sub
        sub.__enter__()
        self.rmsnorm_to_hT(l, self.XM, hT, P_G2)
        self.S.flush()
        sub.__exit__(None, None, None)
        self.ph = outer
        self.ffn_up(l, hT)
        self.end()
        self.begin()
        self.ffn_down(l, xout)
        self.end()

    def pipe(self, items, lags):
        n = len(items)
        for j in range(n + max(lags)):
            for si, lg in enumerate(lags):
                i = j - lg
                if 0 <= i < n and len(items[i]) > si and items[i][si] is not None:
                    items[i][si]()

    def proj(self, l, hT):
        wst = self.tile("wst0", [128, 8, 512], F32)
        wbf = [self.tile("wbf%d" % i, [128, 8, 512], BF16) for i in range(2)]
        tab = self.tile("tab", [128, 2, SEQ], F32)
        pacc = [self.ptile("pacc%d" % i) for i in range(3)]
        pss = [self.ptile("pss%d" % i) for i in range(2)]
        prot = [self.ptile("prot%d" % i) for i in range(2)]
        sq = [self.tile("sq%d" % i, [128, TC], BF16) for i in range(3)]
        rs = [self.tile("rs%d" % i, [128, TC], F32) for i in range(3)]
        yy = [self.tile("yy%d" % i, [128, TC], F32) for i in range(3)]
        t2 = [self.tile("t2%d" % i, [128, TC], F32) for i in range(3)]
        qo = [self.tile("qo%d" % i, [128, TC], BF16) for i in range(3)]
        rowb = [self.tile("rowb%d" % i, [128, SEQ], BF16) for i in range(2)]
        go = [self.tile("go%d" % i, [128, TC], F32) for i in range(3)]
        vst = [self.tile("vst%d" % i, [128, 4, 520], BF16) for i in range(2)]
        vsta = [self.tile("vsta%d" % i, [128, 4, 512], BF16) for i in range(2)]
        for v in vst:
            self.A("pool", "memset", [], [v], v[:], 1.0)
        wv = self.w_in[l].rearrange("(k p) n -> p k n", p=128)
        cnt = dict(acc=0, q=0, row=0, go=0, vs=0, tmp=0, pp=0)
        G = P_QKG

        def wload(gidx, col0, w):
            wb = wbf[gidx % 2]
            self.load(wst, wst[:, :, 0:w], wv[:, :, col0:col0 + w])
            self.A("pool" if gidx % 2 == 0 else "dve", "tensor_copy", [wst], [wb], out=wb[:, :, 0:w], in_=wst[:, :, 0:w])

        def qk_items(items, wb, off, row, gcol, rope, r):
            mat = 0 if rope == "1d" else 1
            rb = None
            if r > 1:
                rb = rowb[cnt["row"] % 2]
                cnt["row"] += 1
            for t in range(NT):
                pa = pacc[cnt["acc"] % 3]
                cnt["acc"] += 1
                j = cnt["tmp"] % 3
                cnt["tmp"] += 1
                ps_, pr_ = pss[cnt["pp"] % 2], prot[cnt["pp"] % 2]
                cnt["pp"] += 1
                s, rr, y, a2 = sq[j], rs[j], yy[j], t2[j]
                tsl = slice(t * TC, (t + 1) * TC)
                if r > 1:
                    n_ = TC // r
                    dst_tl = rb
                    dst = rb[:].rearrange("p (c i) -> p c i", c=r)[:, :, t * n_:(t + 1) * n_]
                else:
                    dst_tl = qo[cnt["q"] % 3]
                    cnt["q"] += 1
                    dst = dst_tl[:]

                def s0(pa=pa, s=s, tsl=tsl):
                    for k in range(8):
                        self.mm(pa, pa[:], wb, wb[:, k, off:off + 128], hT, hT[:, k, tsl], k == 0, k == 7)
                    self.A("act", "activation", [pa], [s], out=s[:], in_=pa[:], func=AF.Square)

                def s1(pa=pa, s=s, rr=rr, y=y, ps_=ps_, dst_tl=dst_tl, dst=dst):
                    self.mm(ps_, ps_[:], self.blkb, self.blkb[:], s, s[:], True, True)
                    self.A("act", "activation", [ps_], [rr], out=rr[:], in_=ps_[:], func=AF.Ln, scale=1.0 / 64, bias=self.epsc[:, 0:1])
                    self.A("act", "activation", [rr], [rr], out=rr[:], in_=rr[:], func=AF.Exp, scale=-0.5)
                    if rope is None:
                        self.A("dve", "scalar_tensor_tensor", [pa, rr, self.par], [dst_tl], out=dst, in0=pa[:],
                               scalar=self.par[:, l, gcol:gcol + 1], in1=rr[:], op0=ALU.mult, op1=ALU.mult)
                    else:
                        self.A("dve", "scalar_tensor_tensor", [pa, rr, self.par], [y], out=y[:], in0=pa[:],
                               scalar=self.par[:, l, gcol:gcol + 1], in1=rr[:], op0=ALU.mult, op1=ALU.mult)

                def s2(y=y, a2=a2, pr_=pr_, dst_tl=dst_tl, dst=dst, tsl=tsl, t=t):
                    if rope is not None:
                        self.mm(pr_, pr_[:], self.mats, self.mats[:, mat, :], y, y[:], True, True)
                        self.A("dve", "tensor_tensor", [pr_, tab], [a2], out=a2[:], in0=pr_[:], in1=tab[:, 1, tsl], op=ALU.mult)
                        self.A("pool", "tensor_tensor", [y, tab], [y], out=y[:], in0=y[:], in1=tab[:, 0, tsl], op=ALU.mult)
                        if r > 1:
                            src1 = y[:].rearrange("p (i c) -> p c i", c=r)
                            src2 = a2[:].rearrange("p (i c) -> p c i", c=r)
                        else:
                            src1, src2 = y[:], a2[:]
                        self.A("pool", "tensor_tensor", [y, a2], [dst_tl], out=dst, in0=src1, in1=src2, op=ALU.add)
                    if r == 1:
                        self.store(dst_tl, self.QK[row * 128:(row + 1) * 128, tsl], dst_tl[:])
                    elif t == NT - 1:
                        self.store(rb, self.QK[row * 128:(row + 1) * 128, :], rb[:])

                items.append([s0, s1, s2])

        def v_items(items, wb, off, w, dst, r, nh, hd):
            hv = hT[:].rearrange("p k (i c) -> p k c i", c=r)
            L = SEQ // r
            stride = hd + 1 if hd == 64 else hd
            vs = None
            for tt in range(32):
                c = (tt * 128) // L
                i0 = (tt * 128) % L
                pa = pacc[cnt["acc"] % 3]
                cnt["acc"] += 1
                if tt % 4 == 0:
                    vs = (vst if hd == 64 else vsta)[cnt["vs"] % 2]
                    cnt["vs"] += 1

                def s0(pa=pa, c=c, i0=i0):
                    for k in range(8):
                        self.mm(pa, pa[:, 0:w], hT, hv[:, k, c, i0:i0 + 128], wb, wb[:, k, off:off + w], k == 0, k == 7)

                def s1(pa=pa, vs=vs, tt=tt):
                    o = vs[:, tt % 4, 0:nh * stride].rearrange("p (h d) -> p h d", h=nh)[:, :, 0:hd]
                    i_ = pa[:, 0:w].rearrange("p (h d) -> p h d", h=nh)
                    if tt % 2 == 0:
                        self.A("act", "activation", [pa], [vs], out=o, in_=i_, func=AF.Copy)
                    else:
                        self.A("dve", "tensor_copy", [pa], [vs], out=o, in_=i_)
                    if tt % 4 == 3:
                        t0 = (tt - 3) * 128
                        self.store(vs, dst[t0:t0 + 512, :].rearrange("(a p) n -> p a n", p=128), vs[:, :, 0:nh * stride])

                items.append([s0, s1])

        def gate_items(items, wb, off, grow):
            for t in range(NT):
                pa = pacc[cnt["acc"] % 3]
                cnt["acc"] += 1
                g = go[cnt["go"] % 3]
                cnt["go"] += 1
                tsl = slice(t * TC, (t + 1) * TC)

                def s0(pa=pa, tsl=tsl):
                    for k in range(8):
                        self.mm(pa, pa[:], wb, wb[:, k, off:off + 128], hT, hT[:, k, tsl], k == 0, k == 7)

                def s1(pa=pa, g=g, tsl=tsl):
                    self.A("act", "activation", [pa], [g], out=g[:], in_=pa[:], func=AF.Sigmoid)
                    self.store(g, self.G[grow * 128:(grow + 1) * 128, tsl], g[:])

                items.append([s0, s1])

        jobsB = [
            (COL["bq"], 512, [("qk", i * 128, 8 + i, G + 2, "ax", 1) for i in range(4)]),
            (COL["bk"], 256, [("qk", 0, 12, G + 3, "ax", 1), ("v", 128, 128, self.VB, 1, 2, 64)]),
        ]
        jobs = [
            (COL["aq"], 512, [("qk", i * 128, 0 + i, G + 0, "1d", 1) for i in range(4)]),
            (COL["ak"], 512, [("qk", i * 128, 4 + i, G + 1, "1d", 1) for i in range(4)]),
            (COL["av"], 512, [("v", 0, 512, self.VA, 1, 4, 128)]),
        ]
        for g in range(3):
            jobs.append((COL["cq"] + g * 512, 512, [("qk", i * 128, 13 + g * 4 + i, G + 4, "1d", DIL[g]) for i in range(4)]))
            jobs.append((COL["ck"] + g * 512, 512, [("qk", i * 128, 25 + g * 4 + i, G + 5, "1d", DIL[g]) for i in range(4)]))
            jobs.append((COL["cv"] + g * 512, 512, [("v", 0, 512, self.VC[g], DIL[g], 8, 64)]))
        jobs.append((COL["dq"], 512, [("qk", i * 128, 37 + i, G + 6, None, 1) for i in range(4)]))
        jobs.append((COL["dk"], 512, [("qk", i * 128, 41 + i, G + 7, None, 1) for i in range(4)]))
        jobs.append((COL["dv"], 512, [("v", 0, 512, self.VD, 1, 8, 64)]))
        for gi in range(8):
            jobs.append((COL["g"] + gi * 512, 512, [("gate", i * 128, gi * 4 + i) for i in range(4)]))

        gctr = [0]

        def run_jobs(jl):
            items = []
            gid0 = gctr[0]
            for ji, (col0, w, subs) in enumerate(jl):
                gid = gid0 + ji
                wb = wbf[gid % 2]
                first = len(items)
                for sj in subs:
                    if sj[0] == "qk":
                        qk_items(items, wb, *sj[1:])
                    elif sj[0] == "v":
                        v_items(items, wb, *sj[1:])
                    else:
                        gate_items(items, wb, *sj[1:])
                orig = items[first][0]
                nxt = jl[ji + 1] if ji + 1 < len(jl) else None

                def s0w(orig=orig, nxt=nxt, gid=gid):
                    if nxt is not None:
                        wload(gid + 1, nxt[0], nxt[1])
                    orig()
                items[first][0] = s0w
            wload(gid0, jl[0][0], jl[0][1])
            gctr[0] += len(jl)
            self.pipe(items, [0, 1, 2])

        self.load(tab, tab[:].rearrange("p a n -> p (a n)"), self.c_tab2)
        run_jobs(jobsB)
        self.load(tab, tab[:].rearrange("p a n -> p (a n)"), self.c_tab1)
        run_jobs(jobs)

    def normalize_rows65(self, po, res_tl, res_ap, n, tmp_row, tmp_bc, pbc):
        self.A("dve", "reciprocal", [po], [tmp_row], out=tmp_row[64:65, 0:n], in_=po[64:65, 0:n])
        self.mm(pbc, pbc[0:64, 0:n], self.mats, self.mats[64:65, 3, 0:64], tmp_row, tmp_row[64:65, 0:n], True, True)
        self.A("act", "activation", [pbc], [tmp_bc], out=tmp_bc[0:64, 0:n], in_=pbc[0:64, 0:n], func=AF.Copy)
        self.A("dve", "tensor_tensor", [po, tmp_bc], [res_tl], out=res_ap, in0=po[0:64, 0:n], in1=tmp_bc[0:64, 0:n], op=ALU.mult)

    def attn_a(self, l):
        qz = [self.tile("qz%d" % c, [128, 4, SEQ], BF16) for c in range(2)]
        ka = self.tile("ka", [128, 4, SEQ], BF16)
        va = self.tile("va", [128, 32, 512], BF16)
        for c in range(2):
            oc = 1 - c
            self.A("pool", "memset", [], [qz[c]], qz[c][oc * 64:(oc + 1) * 64, :, :], 0.0)
        for h in range(4):
            for c in range(2):
                self.load(qz[c], qz[c][c * 64:(c + 1) * 64, h, :], self.QK[h * 128 + c * 64:h * 128 + (c + 1) * 64, :])
            self.load(ka, ka[:, h, :], self.QK[(4 + h) * 128:(5 + h) * 128, :])
        for j in range(4):
            self.load(va, va[:, j * 8:(j + 1) * 8, :], self.VA[j * 1024:(j + 1) * 1024, :].rearrange("(a p) n -> p a n", p=128))
        psc = [self.ptile("psc%d" % i) for i in range(3)]
        po = [self.ptile("po%d" % c) for c in range(2)]
        psm = [self.ptile("psm%d" % c) for c in range(2)]
        pfin = self.ptile("pfin")
        pt = [self.tile("pt%d" % i, [128, TC], BF16) for i in range(6)]
        poS = [self.tile("poS%d" % c, [128, TC], F32) for c in range(2)]
        smS = [self.tile("smS%d" % c, [128, TC], F32) for c in range(2)]
        rc = self.tile("rc", [128, TC], F32)
        res = [self.tile("res%d" % i, [128, TC], F32) for i in range(2)]
        dd = self.tile("dd", [128, TC], F32)
        sq = self.tile("sqa", [128, TC], F32)
        rs = self.tile("rsa", [128, TC], F32)
        ob = [self.tile("oba%d" % i, [128, TC], BF16) for i in range(2)]
        items = []
        n = 0
        for h in range(4):
            for qc in range(NT):
                qs = slice(qc * TC, (qc + 1) * TC)
                for kt in range(32):
                    for c in range(2):
                        sc = psc[n % 3]
                        p = pt[n % 6]
                        n += 1
                        po_ = po[c]
                        psm_ = psm[c]

                        def s0(sc=sc, p=p, h=h, qs=qs, c=c, kt=kt):
                            self.mm(sc, sc[:], ka, ka[:, h, kt * 128:(kt + 1) * 128], qz[c], qz[c][:, h, qs], True, True)
                            self.A("act", "activation", [sc], [p], out=p[:], in_=sc[:], func=AF.Exp, scale=0.125)

                        def s1(p=p, h=h, c=c, kt=kt, po_=po_, psm_=psm_):
                            self.mm(po_, po_[:], va, va[:, kt, h * 128:(h + 1) * 128], p, p[:], kt == 0, kt == 31)
                            self.mm(psm_, psm_[:], self.onesb, self.onesb[:], p, p[:], kt == 0, kt == 31)
                            if kt == 31:
                                self.A("act", "activation", [po_], [poS[c]], out=poS[c][:], in_=po_[:], func=AF.Copy)
                                self.A("act", "activation", [psm_], [smS[c]], out=smS[c][:], in_=psm_[:], func=AF.Copy)

                        def s2(h=h, qc=qc, qs=qs, c=c, kt=kt):
                            if kt != 31:
                                return
                            self.A("dve", "reciprocal", [smS[c]], [rc], out=rc[:], in_=smS[c][:])
                            self.A("dve", "tensor_tensor", [poS[c], rc], [res[c]], out=res[c][:], in0=poS[c][:], in1=rc[:], op=ALU.mult)
                            if c != 1:
                                return
                            self.A("dve", "scalar_tensor_tensor", [res[0], res[1], self.nlam], [dd], out=dd[:], in0=res[1][:],
                                   scalar=self.nlam[:, l:l + 1], in1=res[0][:], op0=ALU.mult, op1=ALU.add)
                            self.A("pool", "tensor_tensor", [dd], [sq], out=sq[:], in0=dd[:], in1=dd[:], op=ALU.mult)
                            self.mm(pfin, pfin[:], self.mats, self.mats[:, 3, :], sq, sq[:], True, True)
                            self.A("act", "activation", [pfin], [rs], out=rs[:], in_=pfin[:], func=AF.Ln, scale=1.0 / 128, bias=self.epsc[:, 0:1])
                            self.A("act", "activation", [rs], [rs], out=rs[:], in_=rs[:], func=AF.Exp, scale=-0.5)
                            self.A("dve", "scalar_tensor_tensor", [dd, rs, self.par], [dd], out=dd[:], in0=dd[:],
                                   scalar=self.par[:, l, P_SUB:P_SUB + 1], in1=rs[:], op0=ALU.mult, op1=ALU.mult)
                            o = ob[(h * NT + qc) % 2]
                            self.A("act", "activation", [dd], [o], out=o[:], in_=dd[:], func=AF.Copy, scale=float(1.0 - self.lambda_init))
                            self.store(o, self.OT[0, h * 128:(h + 1) * 128, qs], o[:])

                        items.append([s0, s1, s2])
        self.pipe(items, [0, 3, 12])

    def attn_b(self, l):
        qb = self.tile("qb", [128, 8, SEQ], BF16)
        kb = self.tile("kb", [128, SEQ], BF16)
        vb = self.tile("vb", [128, 32, 130], BF16)
        for hq in range(8):
            og = 1 - hq // 4
            self.A("pool", "memset", [], [qb], qb[og * 64:(og + 1) * 64, hq, :], 0.0)
        for hq in range(8):
            g, s = hq // 4, hq % 4
            self.load(qb, qb[g * 64:(g + 1) * 64, hq, :], self.QK[8 * 128 + hq * 64:8 * 128 + (hq + 1) * 64, :])
        self.load(kb, kb[:], self.QK[12 * 128:13 * 128, :])
        self.load(vb, vb[:], self.VB.rearrange("(a p) n -> p a n", p=128))
        psc = [self.ptile("psc%d" % i) for i in range(3)]
        po = [[self.ptile("pob%d%d" % (g, i)) for i in range(2)] for g in range(2)]
        pbc = self.ptile("pbc")
        pt = [self.tile("pt%d" % i, [128, TC], BF16) for i in range(6)]
        trow = self.tile("trow", [128, TC], F32)
        tbc = self.tile("tbc", [128, TC], F32)
        ob = [self.tile("obb%d" % i, [128, TC], BF16) for i in range(2)]
        items = []
        n = 0
        m = 0
        gi = 0
        for s in range(4):
            for qc in range(NT):
                qs = slice(qc * TC, (qc + 1) * TC)
                buf = gi % 2
                gi += 1
                for kt in range(32):
                    for g in range(2):
                        hq = g * 4 + s
                        ps_ = slice(g * 64, (g + 1) * 64)
                        o_ps = po[g][buf]
                        sc = psc[n % 3]
                        p = pt[n % 6]
                        n += 1

                        def s0(sc=sc, p=p, hq=hq, qs=qs, kt=kt):
                            self.mm(sc, sc[:], kb, kb[:, kt * 128:(kt + 1) * 128], qb, qb[:, hq, qs], True, True)
                            self.A("act", "activation", [sc], [p], out=p[:], in_=sc[:], func=AF.Exp, scale=0.125)

                        def s1(p=p, g=g, kt=kt, o_ps=o_ps):
                            self.mm(o_ps, o_ps[0:65, :], vb, vb[:, kt, g * 65:(g + 1) * 65], p, p[:], kt == 0, kt == 31)

                        def s2(g=g, hq=hq, qs=qs, kt=kt, o_ps=o_ps):
                            if kt != 31:
                                return
                            o = ob[hq % 2]
                            self.normalize_rows65(o_ps, o, o[0:64, :], TC, trow, tbc, pbc)
                            self.store(o, self.OT[1, hq * 64:(hq + 1) * 64, qs], o[0:64, :])

                        items.append([s0, s1, s2])
        self.pipe(items, [0, 3, 12])

    def attn_c(self, l):
        mask = self.tile("mask3", [128, 3, 128], F32)
        self.load(mask, mask[:].rearrange("p a b -> p (a b)"), self.c_mask3)
        qc_ = self.tile("qc", [128, 3, SEQ], BF16)
        kc_ = self.tile("kc", [128, 3, SEQ], BF16)
        vc_ = self.tile("vc", [128, 3, 32, 130], BF16)
        acc = [self.tile("acc%d" % i, [65, SEQ], F32) for i in range(2)]
        psc = [self.ptile("pscc%d" % i, (128, 384)) for i in range(3)]
        po = [self.ptile("poc%d" % i) for i in range(2)]
        pbc = self.ptile("pbcc")
        ex = [self.tile("ex%d" % i, [128, 3, 128], F32) for i in range(3)]
        pt = [self.tile("ptc%d" % i, [128, 3, 128], BF16) for i in range(4)]
        trow = self.tile("trowc", [128, TC], F32)
        ob = [self.tile("obc%d" % i, [128, TC], BF16) for i in range(2)]
        n = 0
        m = 0
        for jp in range(4):
            for g in range(3):
                self.load(qc_, qc_[:, g, :], self.QK[(13 + g * 4 + jp) * 128:(14 + g * 4 + jp) * 128, :])
                self.load(kc_, kc_[:, g, :], self.QK[(25 + g * 4 + jp) * 128:(26 + g * 4 + jp) * 128, :])
                self.load(vc_, vc_[:, g, :, :], self.VC[g].rearrange("(a p) n -> p a n", p=128)[:, :, jp * 130:(jp + 1) * 130])
            items = []
            for hh in range(2):
                ps_ = slice(hh * 64, (hh + 1) * 64)
                ac = acc[hh]
                hd = jp * 2 + hh
                for g in range(3):
                    r = DIL[g]
                    L = SEQ // r
                    tps = L // 128
                    for qb4 in range(8):
                        o_ps = po[m % 2]
                        m += 1
                        for u in range(4):
                            qb = qb4 * 4 + u
                            seg = qb // tps
                            kts = [k for k in (qb - 1, qb, qb + 1) if k // tps == seg and 0 <= k < 32]
                            j0 = kts[0] - (qb - 1)
                            nk = len(kts)
                            sc = psc[n % 3]
                            e_ = ex[n % 3]
                            p = pt[n % 4]
                            n += 1

                            def s0(sc=sc, e_=e_, p=p, kts=kts, j0=j0, nk=nk, qb=qb, g=g, ps_=ps_):
                                for k in kts:
                                    j = k - (qb - 1)
                                    self.mm(sc, sc[:, j * 128:(j + 1) * 128], kc_, kc_[ps_, g, k * 128:(k + 1) * 128],
                                            qc_, qc_[ps_, g, qb * 128:(qb + 1) * 128], True, True)
                                scv = sc[:, 0:384].rearrange("p (a b) -> p a b", a=3)
                                self.A("act", "activation", [sc], [e_], out=e_[:, j0:j0 + nk, :], in_=scv[:, j0:j0 + nk, :], func=AF.Exp, scale=0.125)
                                self.A("pool", "tensor_tensor", [e_, mask], [p], out=p[:, j0:j0 + nk, :], in0=e_[:, j0:j0 + nk, :],
                                       in1=mask[:, j0:j0 + nk, :], op=ALU.mult)

                            def s1(p=p, kts=kts, nk=nk, qb=qb, g=g, hh=hh, u=u, o_ps=o_ps, qb4=qb4, r=r, L=L, ac=ac, hd=hd):
                                for ki, k in enumerate(kts):
                                    j = k - (qb - 1)
                                    self.mm(o_ps, o_ps[0:65, u * 128:(u + 1) * 128], vc_, vc_[:, g, k, hh * 65:(hh + 1) * 65],
                                            p, p[:, j, :], ki == 0, ki == nk - 1)
                                if u != 3:
                                    return
                                pos0 = qb4 * 512
                                av = ac[:].rearrange("p (i c) -> p c i", c=r)
                                if L >= 512:
                                    c0, i0 = pos0 // L, pos0 % L
                                    dst = av[0:65, c0:c0 + 1, i0:i0 + 512]
                                    src = o_ps[0:65, :].rearrange("p (c i) -> p c i", c=1)
                                else:
                                    ncl = 512 // L
                                    c0 = pos0 // L
                                    dst = av[0:65, c0:c0 + ncl, :]
                                    src = o_ps[0:65, :].rearrange("p (c i) -> p c i", c=ncl)
                                if g == 0:
                                    self.A("act", "activation", [o_ps], [ac], out=dst, in_=src, func=AF.Copy)
                                else:
                                    self.A("dve", "tensor_tensor", [o_ps, ac], [ac], out=dst, in0=dst, in1=src, op=ALU.add)
                                if g == 2 and qb4 == 7:
                                    for t in range(NT):
                                        o = ob[(hd * NT + t) % 2]
                                        ts_ = slice(t * TC, (t + 1) * TC)
                                        self.A("dve", "reciprocal", [ac], [trow], out=trow[64:65, :], in_=ac[64:65, ts_])
                                        self.mm(pbc, pbc[0:64, :], self.mats, self.mats[64:65, 3, 0:64], trow, trow[64:65, :], True, True)
                                        self.A("dve", "tensor_tensor", [pbc, ac], [o], out=o[0:64, :], in0=pbc[0:64, :], in1=ac[0:64, ts_], op=ALU.mult)
                                        self.store(o, self.OT[2, hd * 64:(hd + 1) * 64, ts_], o[0:64, :])

                            items.append([s0, s1])
            self.pipe(items, [0, 2])

    def attn_d(self, l):
        qd = self.tile("qd", [128, 4, SEQ], BF16)
        kd = self.tile("kd", [128, 4, SEQ], BF16)
        ve = self.tile("ve", [128, 32, 520], BF16)
        vo = self.tile("vo", [128, 31, 520], BF16)
        E = self.tile("E", [128, 8, 15, 64], F32)
        cm = self.tile("cm", [128, 64], F32)
        self.load(cm, cm[:], self.c_colmask)
        for i in range(4):
            self.load(qd, qd[:, i, :], self.QK[(37 + i) * 128:(38 + i) * 128, :])
            self.load(kd, kd[:, i, :], self.QK[(41 + i) * 128:(42 + i) * 128, :])
        for j in range(4):
            self.load(ve, ve[:, j * 8:(j + 1) * 8, :], self.VD[j * 1024:(j + 1) * 1024, :].rearrange("(a p) n -> p a n", p=128))
        for j in range(4):
            na = 8 if j < 3 else 7
            self.load(vo, vo[:, j * 8:j * 8 + na, :], self.VD[64 + j * 1024:64 + j * 1024 + na * 128, :].rearrange("(a p) n -> p a n", p=128))
        for h in range(8):
            self.load(E, E[:, h, :, :].rearrange("p a b -> p (a b)"), self.rpbE[l][:, h * 960:(h + 1) * 960])
        for h in range(8):
            self.A("act", "activation", [E], [E], out=E[:, h, :, :], in_=E[:, h, :, :], func=AF.Exp)
            self.A("pool", "tensor_tensor", [E, cm], [E], out=E[:, h, :, :], in0=E[:, h, :, :],
                   in1=cm[:].rearrange("p (a b) -> p a b", a=1).to_broadcast([128, 15, 64]), op=ALU.mult)
        psc = [self.ptile("pscd%d" % i, (128, 256)) for i in range(3)]
        po = [self.ptile("pod%d" % i) for i in range(2)]
        pbc = self.ptile("pbcd")
        ex = [self.tile("exd%d" % i, [128, 4, 64], F32) for i in range(3)]
        pt = [self.tile("ptd%d" % i, [128, 4, 64], BF16) for i in range(4)]
        trow = self.tile("trowd", [128, TC], F32)
        tbc = self.tile("tbcd", [128, TC], F32)
        ob = [self.tile("obd%d" % i, [128, TC], BF16) for i in range(2)]
        items = []
        n = 0
        m = 0
        for h in range(8):
            ch, hh = h // 2, h % 2
            ps_ = slice(hh * 64, (hh + 1) * 64)
            for r8 in range(8):
                o_ps = po[m % 2]
                o = ob[m % 2]
                m += 1
                for u in range(8):
                    r = r8 * 8 + u
                    rs_ = min(max(r - 4, 0), 56)
                    base = rs_ - r + 7
                    sc = psc[n % 3]
                    e_ = ex[n % 3]
                    p = pt[n % 4]
                    n += 1

                    def s0(sc=sc, e_=e_, p=p, r=r, rs_=rs_, base=base, h=h, ch=ch, ps_=ps_, n=n):
                        for i in range(4):
                            k0 = (rs_ + 2 * i) * 64
                            self.mm(sc, sc[:, i * 64:(i + 1) * 64], kd, kd[ps_, ch, k0:k0 + 128], qd, qd[ps_, ch, r * 64:(r + 1) * 64], True, True)
                        self.A("act", "activation", [sc], [e_], out=e_[:], in_=sc[:, 0:256].rearrange("p (a b) -> p a b", a=4), func=AF.Exp, scale=0.125)
                        ev = E[:, h, base:base + 7:2, :]
                        self.A("pool" if n % 2 == 0 else "dve", "tensor_tensor", [e_, E], [p], out=p[:], in0=e_[:], in1=ev, op=ALU.mult)

                    def s1(p=p, rs_=rs_, h=h, u=u, o_ps=o_ps, o=o, r8=r8):
                        for i in range(4):
                            row0 = rs_ + 2 * i
                            if row0 % 2 == 0:
                                vt, vi = ve, row0 // 2
                            else:
                                vt, vi = vo, (row0 - 1) // 2
                            self.mm(o_ps, o_ps[0:65, u * 64:(u + 1) * 64], vt, vt[:, vi, h * 65:(h + 1) * 65], p, p[:, i, :], i == 0, i == 3)
                        if u != 7:
                            return
                        self.normalize_rows65(o_ps, o, o[0:64, :], TC, trow, tbc, pbc)
                        self.store(o, self.OT[3, h * 64:(h + 1) * 64, r8 * TC:(r8 + 1) * TC], o[0:64, :])

                    items.append([s0, s1])
        self.pipe(items, [0, 2])

    def merge(self, l, xin):
        wb = self.tile("wbr", [128, 16, DM], BF16)
        wo = self.tile("wo", [128, 8, DM], BF16)
        stg = [self.tile("mstg%d" % i, [128, 4, DM], F32) for i in range(2)]
        n = 0
        for i in range(4):
            s = stg[n % 2]
            n += 1
            self.load(s, s[:], self.w_branch[l, i].rearrange("(k p) n -> p k n", p=128))
            self.A("pool" if n % 2 == 0 else "dve", "tensor_copy", [s], [wb], out=wb[:, i * 4:(i + 1) * 4, :], in_=s[:])
        for j in range(2):
            s = stg[n % 2]
            n += 1
            self.load(s, s[:], self.w_out[l][j * 512:(j + 1) * 512, :].rearrange("(k p) n -> p k n", p=128))
            self.A("pool" if n % 2 == 0 else "dve", "tensor_copy", [s], [wo], out=wo[:, j * 4:(j + 1) * 4, :], in_=s[:])
        ot = [self.tile("mot%d" % i, [128, 16, TC], BF16) for i in range(2)]
        xs = [self.tile("mx%d" % i, [128, 8, TC], F32) for i in range(2)]
        gt = [self.tile("mg%d" % i, [128, TC], F32) for i in range(4)]
        mt = [self.tile("mm%d" % i, [128, 8, TC], BF16) for i in range(2)]
        acc = [self.tile("macc%d" % i, [128, TC], F32) for i in range(2)]
        tmp = [self.tile("mtmp%d" % i, [128, TC], F32) for i in range(2)]
        py = [self.ptile("py%d" % i) for i in range(3)]
        px = [self.ptile("px%d" % i) for i in range(2)]
        xo = [self.tile("mxo%d" % i, [128, TC], F32) for i in range(3)]
        xv = xin.rearrange("(c p) n -> p c n", p=128)
        otv = self.OT.rearrange("b (k p) n -> p (b k) n", p=128)
        ng = 0
        ny = 0
        nx = 0
        for t in range(NT):
            ts_ = slice(t * TC, (t + 1) * TC)
            o = ot[t % 2]
            x = xs[t % 2]
            mtt = mt[t % 2]
            for i in range(4):
                self.load(o, o[:, i * 4:(i + 1) * 4, :], otv[:, i * 4:(i + 1) * 4, ts_])
            self.load(x, x[:], xv[:, :, ts_])
            for f in range(8):
                a = acc[f % 2]
                for i in range(4):
                    g = gt[ng % 4]
                    ng += 1
                    self.load(g, g[:], self.G[(i * 8 + f) * 128:(i * 8 + f + 1) * 128, ts_])
                    p = py[ny % 3]
                    ny += 1
                    for k in range(4):
                        self.mm(p, p[:], wb, wb[:, i * 4 + k, f * 128:(f + 1) * 128], o, o[:, i * 4 + k, :], k == 0, k == 3)
                    if i == 0:
                        self.A("dve", "tensor_tensor", [p, g], [a], out=a[:], in0=p[:], in1=g[:], op=ALU.mult)
                    else:
                        tm = tmp[i % 2]
                        self.A("dve", "tensor_tensor", [p, g], [tm], out=tm[:], in0=p[:], in1=g[:], op=ALU.mult)
                        if i < 3:
                            self.A("pool", "tensor_tensor", [a, tm], [a], out=a[:], in0=a[:], in1=tm[:], op=ALU.add)
                        else:
                            self.A("pool", "tensor_tensor", [a, tm], [mtt], out=mtt[:, f, :], in0=a[:], in1=tm[:], op=ALU.add)
            for fo in range(8):
                p = px[nx % 2]
                xo_ = xo[nx % 3]
                nx += 1
                for k in range(8):
                    self.mm(p, p[:], wo, wo[:, k, fo * 128:(fo + 1) * 128], mtt, mtt[:, k, :], k == 0, k == 7)
                self.A("dve", "tensor_tensor", [p, x], [xo_], out=xo_[:], in0=p[:], in1=x[:, fo, :], op=ALU.add)
                self.store(xo_, self.XM[fo * 128:(fo + 1) * 128, ts_], xo_[:])

    def ffn_up(self, l, hT):
        wst = [self.tile("fst%d" % i, [128, 8, 256], F32) for i in range(1)]
        wbf = [self.tile("fbf%d" % i, [128, 8, 256], BF16) for i in range(4)]
        uas = [self.tile("ua%d" % i, [128, SEQ + 2], F32) for i in range(2)]
        ugs = [self.tile("ug%d" % i, [128, SEQ + 2], F32) for i in range(2)]
        ca = self.tile("ca", [128, SEQ], F32)
        cg = self.tile("cg", [128, SEQ], F32)
        mrow = [self.tile("mrow%d" % i, [128, SEQ], BF16) for i in range(1)]
        pacc = [self.ptile("fpa%d" % i) for i in range(4)]
        for u in uas + ugs:
            self.A("pool", "memset", [], [u], u[:, 0:1], 0.0)
            self.A("pool", "memset", [], [u], u[:, SEQ + 1:SEQ + 2], 0.0)
        wv = self.w_up[l].rearrange("(k p) n -> p k n", p=128)
        ngrp = 0
        nacc = 0
        for j in range(11):
            w = 256
            wbs = []
            for part in range(2):
                st = wst[0]
                wb = wbf[ngrp % 4]
                ngrp += 1
                c0 = part * D_FF + j * 256
                self.load(st, st[:, :, 0:w], wv[:, :, c0:c0 + w])
                self.A("pool" if part == 0 else "dve", "tensor_copy", [st], [wb], out=wb[:, :, 0:w], in_=st[:, :, 0:w])
                wbs.append(wb)
            for ii in range(w // 128):
                i = j * 2 + ii
                ua, ug = uas[i % 2], ugs[i % 2]
                for part, (u, cdst) in enumerate(((ua, ca), (ug, cg))):
                    wb = wbs[part]
                    for t in range(NT):
                        p = pacc[nacc % 4]
                        nacc += 1
                        for k in range(8):
                            self.mm(p, p[:], wb, wb[:, k, ii * 128:(ii + 1) * 128], hT, hT[:, k, t * TC:(t + 1) * TC], k == 0, k == 7)
                        self.A("act", "activation", [p], [u], out=u[:, 1 + t * TC:1 + (t + 1) * TC], in_=p[:], func=AF.Copy)
                    ch = part * 22 + i
                    w0 = self.par[:, l, P_CW + 0 * 44 + ch:P_CW + 0 * 44 + ch + 1]
                    w1 = self.par[:, l, P_CW + 1 * 44 + ch:P_CW + 1 * 44 + ch + 1]
                    w2 = self.par[:, l, P_CW + 2 * 44 + ch:P_CW + 2 * 44 + ch + 1]
                    bb = self.par[:, l, P_CB + ch:P_CB + ch + 1]
                    eng = "dve"
                    for hf in range(2):
                        hs = slice(hf * 2048, (hf + 1) * 2048)
                        self.A(eng, "tensor_scalar", [u, self.par], [cdst], out=cdst[:, hs], in0=u[:, hf * 2048:hf * 2048 + 2048],
                               scalar1=w0, scalar2=bb, op0=ALU.mult, op1=ALU.add)
                        self.A(eng, "scalar_tensor_tensor", [u, cdst, self.par], [cdst], out=cdst[:, hs], in0=u[:, 1 + hf * 2048:1 + hf * 2048 + 2048],
                               scalar=w1, in1=cdst[:, hs], op0=ALU.mult, op1=ALU.add)
                        self.A(eng, "scalar_tensor_tensor", [u, cdst, self.par], [cdst], out=cdst[:, hs], in0=u[:, 2 + hf * 2048:2 + hf * 2048 + 2048],
                               scalar=w2, in1=cdst[:, hs], op0=ALU.mult, op1=ALU.add)
                mr = mrow[0]
                for hf in range(2):
                    hs = slice(hf * 2048, (hf + 1) * 2048)
                    self.A("act", "activation", [ca], [ca], out=ca[:, hs], in_=ca[:, hs], func=AF.Silu)
                    self.A("pool" if hf == 0 else "dve", "tensor_tensor", [ca, cg], [mr], out=mr[:, hs], in0=ca[:, hs], in1=cg[:, hs], op=ALU.mult)
                self.store(mr, self.M[i * 128:(i + 1) * 128, :], mr[:])

    def ffn_down(self, l, xout):
        wd = self.tile("wd", [128, 22, DM], BF16)
        stg = [self.tile("dstg%d" % i, [128, 2, DM], F32) for i in range(2)]
        wv = self.w_down[l].rearrange("(k p) n -> p k n", p=128)
        for j in range(11):
            s = stg[j % 2]
            self.load(s, s[:], wv[:, j * 2:(j + 1) * 2, :])
            self.A("pool" if j % 2 == 0 else "dve", "tensor_copy", [s], [wd], out=wd[:, j * 2:(j + 1) * 2, :], in_=s[:])
        mt = [self.tile("dm%d" % i, [128, 22, TC], BF16) for i in range(2)]
        xs = [self.tile("dx%d" % i, [128, 8, TC], F32) for i in range(2)]
        xo = [self.tile("dxo%d" % i, [128, TC], F32) for i in range(3)]
        px = [self.ptile("dpx%d" % i) for i in range(3)]
        mv = self.M.rearrange("(k p) n -> p k n", p=128)
        xv = self.XM.rearrange("(c p) n -> p c n", p=128)
        nx = 0
        for t in range(NT):
            ts_ = slice(t * TC, (t + 1) * TC)
            m = mt[t % 2]
            x = xs[t % 2]
            self.load(m, m[:, 0:11, :], mv[:, 0:11, ts_])
            self.load(m, m[:, 11:22, :], mv[:, 11:22, ts_])
            self.load(x, x[:], xv[:, :, ts_])
            for fo in range(8):
                p = px[nx % 3]
                xo_ = xo[nx % 3]
                nx += 1
                for k in range(22):
                    self.mm(p, p[:], wd, wd[:, k, fo * 128:(fo + 1) * 128], m, m[:, k, :], k == 0, k == 21)
                self.A("dve", "tensor_tensor", [p, x], [xo_], out=xo_[:], in0=p[:], in1=x[:, fo, :], op=ALU.add)
                self.store(xo_, xout[fo * 128:(fo + 1) * 128, ts_], xo_[:])


_CACHE = {}


def _get_nc(n_layers=NL, debug=False):
    key = (n_layers, debug)
    if key not in _CACHE:
        b = Builder(n_layers, debug)
        _CACHE[key] = (b.build(), b)
    return _CACHE[key]


def make_in_maps(inp):
    consts = _host_consts()
    params = _pack_params(inp)
    lam = np.ascontiguousarray(np.asarray(inp["lam"], np.float32).reshape(1, NL * 256))
    rpbE = _pack_rpb(np.asarray(inp["rpb"], np.float32))
    shared = dict(
        w_in=np.ascontiguousarray(inp["w_in"], np.float32),
        w_branch=np.ascontiguousarray(inp["w_branch"], np.float32),
        w_out=np.ascontiguousarray(inp["w_out"], np.float32),
        w_up=np.ascontiguousarray(inp["w_up"], np.float32),
        w_down=np.ascontiguousarray(inp["w_down"], np.float32),
        params=params, lam=lam, rpbE=rpbE,
        tab1=consts["tab1"], tab2=consts["tab2"], mats=consts["mats"], mask3=consts["mask3"],
        colmask=consts["colmask"],
    )
    x = np.asarray(inp["x"], np.float32)
    maps = []
    for b in range(8):
        d = dict(shared)
        d["xT"] = np.ascontiguousarray(x[b].T)
        maps.append(d)
    return maps


def kernel(**inputs):
    inp = {k: np.asarray(v) for k, v in inputs.items()}
    nc, _ = _get_nc()
    maps = make_in_maps(inp)
    res = run_bass_kernel_spmd(nc, maps, core_ids=list(range(8)))
    out = np.stack([np.ascontiguousarray(res.results[b]["yT"].T) for b in range(8)], axis=0)
    return out.astype(np.float32)
```

```python
import numpy as np
from contextlib import ExitStack
import concourse.bass as bass
import concourse.mybir as mybir
from concourse.bass_utils import run_bass_kernel_spmd

F32 = mybir.dt.float32
BF16 = mybir.dt.bfloat16
AF = mybir.ActivationFunctionType
ALU = mybir.AluOpType
AX = mybir.AxisListType

EPOCH = 24000
SAME_SYNC = True

SEQ = 4096
DM = 1024
NL = 4
N_IN = 12544
D_FF = 2816
EPS = 1e-6
NT = 8
TC = 512
COL = dict(aq=0, ak=512, av=1024, bq=1536, bk=2048, bv=2176, cq=2304, ck=3840, cv=5376,
           dq=6912, dk=7424, dv=7936, g=8448)
DIL = (1, 4, 16)


class Buf:
    __slots__ = ("name", "w", "r")

    def __init__(self, name=""):
        self.name = name
        self.w = None
        self.r = []


class _Op:
    __slots__ = ("eng", "fn", "deps", "ldeps", "ddeps", "dma", "pub", "seq")

    def __init__(self, eng, fn, deps, ddeps, dma, ldeps=()):
        self.eng = eng
        self.fn = fn
        self.deps = deps
        self.ldeps = ldeps
        self.ddeps = ddeps
        self.dma = dma
        self.pub = False
        self.seq = None


class Sched:
    ENGS = ("pe", "act", "dve", "pool", "sp")

    def __init__(self, nc, stack):
        self.nc = nc
        self.stack = stack
        self.ops = []
        self.base = 0
        self.cnt = {e: 0 for e in self.ENGS}
        self.sems = {e: [] for e in self.ENGS}
        self.dsem = {}
        self.dcnt = {}
        self.nsem = 0
        self.ninstr = 0

    def _new_sem(self, name):
        self.nsem += 1
        return self.stack.enter_context(self.nc.semaphore(name))

    def _esem(self, e, ep):
        while len(self.sems[e]) <= ep:
            self.sems[e].append(self._new_sem("s_%s_%d" % (e, len(self.sems[e]))))
        return self.sems[e][ep]

    def op(self, eng, fn, reads=(), writes=(), dma=None):
        deps = set()
        ldeps = set()
        ddeps = {}
        base = self.base
        ops = self.ops

        def add(i, dst):
            if i is None or i < base:
                return
            o = ops[i - base]
            if o.dma is not None:
                ddeps[o.dma] = self.dcnt[o.dma]
            else:
                dst.add(i)

        for b in reads:
            add(b.w, deps)
        for b in writes:
            add(b.w, ldeps)
            for r in b.r:
                add(r, ldeps)
        ldeps -= deps
        idx = base + len(ops)
        if dma is not None:
            if dma not in self.dsem:
                self.dsem[dma] = self._new_sem("d_%d" % len(self.dsem))
                self.dcnt[dma] = 0
            self.dcnt[dma] += 16
        ops.append(_Op(eng, fn, deps, ddeps, dma, ldeps))
        for b in reads:
            b.r.append(idx)
        for b in writes:
            b.w = idx
            b.r = []
        return idx

    def flush(self):
        ops = self.ops
        base = self.base
        last = {}
        for i, o in enumerate(ops):
            if o.dma is None and o.fn is not None:
                last[o.eng] = i
        for e in self.ENGS:
            deps = set(base + i for ee, i in last.items() if ee != e)
            ops.append(_Op(e, None, deps, dict(self.dcnt), None))
        import bisect

        def cross(od, o):
            return od.eng != o.eng or o.dma is not None or (SAME_SYNC and o.eng != "pe")

        for o in ops:
            for d in o.deps:
                od = ops[d - base]
                if cross(od, o):
                    od.pub = True
        publ = {e: [] for e in self.ENGS}
        for i, o in enumerate(ops):
            if o.pub:
                publ[o.eng].append(i)
        for i, o in enumerate(ops):
            for d in o.ldeps:
                od = ops[d - base]
                if not cross(od, o):
                    continue
                if od.pub:
                    o.deps.add(d)
                    continue
                lst = publ[od.eng]
                k = bisect.bisect_left(lst, d - base)
                if k < len(lst) and lst[k] < i:
                    o.deps.add(base + lst[k])
                else:
                    od.pub = True
                    bisect.insort(lst, d - base)
                    o.deps.add(d)
        for o in ops:
            if o.pub:
                c = self.cnt[o.eng]
                self.cnt[o.eng] = c + 1
                o.seq = (c // EPOCH, c % EPOCH + 1)
        per = {e: [] for e in self.ENGS}
        seen = {e: {} for e in self.ENGS}
        seend = {e: {} for e in self.ENGS}
        for o in ops:
            F = o.eng
            need = {}
            for d in o.deps:
                od = ops[d - base]
                if od.eng == F and o.dma is None and (F == "pe" or not SAME_SYNC):
                    continue
                if od.seq > need.get(od.eng, (-1, -1)):
                    need[od.eng] = od.seq
            waits = []
            for E, sq in need.items():
                if seen[F].get(E, (-1, -1)) >= sq:
                    continue
                seen[F][E] = sq
                waits.append((self._esem(E, sq[0]), sq[1]))
            for k, v in o.ddeps.items():
                if v == 0 or seend[F].get(k, 0) >= v:
                    continue
                seend[F][k] = v
                waits.append((self.dsem[k], v))
            inc = self._esem(F, o.seq[0]) if o.pub else None
            dinc = self.dsem[o.dma] if o.dma is not None else None
            per[F].append((waits, o.fn, inc, dinc))
            self.ninstr += len(waits) + 1

        def run(eng, lst):
            for waits, fn, inc, dinc in lst:
                emb = None
                if fn is not None and dinc is None and waits:
                    emb = waits[-1]
                    waits = waits[:-1]
                for s, v in waits:
                    eng.wait_ge(s, v)
                if fn is None:
                    continue
                ins = fn(eng)
                if emb is not None:
                    ins._wait_ge(emb[0], emb[1])
                if inc is not None:
                    ins.then_inc(inc, 1)
                if dinc is not None:
                    ins.then_inc(dinc, 16)

        with self.nc.Block() as block:
            @block.tensor
            def _(e):
                run(e, per["pe"])

            @block.scalar
            def _(e):
                run(e, per["act"])

            @block.vector
            def _(e):
                run(e, per["dve"])

            @block.gpsimd
            def _(e):
                run(e, per["pool"])

            @block.sync
            def _(e):
                run(e, per["sp"])
        self.base = base + len(ops)
        self.ops = []


class Tl:
    __slots__ = ("t", "b", "name", "psum")

    def __init__(self, t, name, psum=False):
        self.t = t
        self.b = Buf(name)
        self.name = name
        self.psum = psum

    def __getitem__(self, k):
        return self.t[k]


def _host_consts():
    c = {}
    pos = np.arange(SEQ, dtype=np.float32)
    p = np.arange(128)
    inv32 = (np.float32(10000.0) ** (-(np.arange(0, 64, 2, dtype=np.float32) / np.float32(64)))).astype(np.float32)
    f = p % 32
    half = (p % 64) // 32
    ang = (pos[None, :] * inv32[f][:, None]).astype(np.float32)
    sgn = np.where(half == 0, -1.0, 1.0).astype(np.float32)[:, None]
    tab1 = np.stack([np.cos(ang), np.sin(ang) * sgn], axis=1).astype(np.float32)
    inv16 = (np.float32(10000.0) ** (-(np.arange(0, 32, 2, dtype=np.float32) / np.float32(32)))).astype(np.float32)
    pp = p % 64
    blk = pp // 32
    q = pp % 32
    half2 = q // 16
    f2 = q % 16
    prow = np.floor(pos / 64).astype(np.float32)
    pcol = (pos - prow * 64).astype(np.float32)
    pf = np.where(blk[:, None] == 0, prow[None, :], pcol[None, :]).astype(np.float32)
    ang2 = (pf * inv16[f2][:, None]).astype(np.float32)
    sgn2 = np.where(half2 == 0, -1.0, 1.0).astype(np.float32)[:, None]
    tab2 = np.stack([np.cos(ang2), np.sin(ang2) * sgn2], axis=1).astype(np.float32)
    c["tab1"] = tab1.reshape(128, 2 * SEQ)
    c["tab2"] = tab2.reshape(128, 2 * SEQ)
    mats = np.zeros((128, 5, 128), np.float32)
    for m in range(128):
        part1 = m + 32 if (m % 64) // 32 == 0 else m - 32
        mats[part1, 0, m] = 1.0
        part2 = m + 16 if (m % 32) // 16 == 0 else m - 16
        mats[part2, 1, m] = 1.0
    mats[:, 2, :] = (p[:, None] // 64 == p[None, :] // 64)
    mats[:, 3, :] = 1.0
    mats[:, 4, :] = np.eye(128)
    c["mats"] = mats.reshape(128, 5 * 128)
    i = np.arange(128)[:, None]
    m = np.arange(128)[None, :]
    mask3 = np.stack([(i - m >= 64), (np.abs(i - m) <= 64), (m - i >= 64)], axis=1).astype(np.float32)
    c["mask3"] = mask3.reshape(128, 3 * 128)
    kc = np.arange(128)[:, None] % 64
    qc = np.arange(64)[None, :]
    ws = np.clip(qc - 8, 0, 48)
    c["colmask"] = ((kc >= ws) & (kc < ws + 16)).astype(np.float32)
    return c


P_G1 = 0
P_G2 = 8
P_QKG = 16
P_SUB = 24
P_CW = 25
P_CB = 25 + 132
P_N = 25 + 132 + 44


def _pack_params(inp):
    out = np.zeros((128, NL, P_N), np.float32)
    for l in range(NL):
        out[:, l, P_G1:P_G1 + 8] = inp["norm1_g"][l].reshape(8, 128).T
        out[:, l, P_G2:P_G2 + 8] = inp["norm2_g"][l].reshape(8, 128).T
        qg = inp["qk_g"][l].reshape(8, 64)
        out[:, l, P_QKG:P_QKG + 8] = np.concatenate([qg, qg], axis=1).T
        out[:, l, P_SUB] = inp["subln_g"][l]
        cw = inp["conv_w"][l].reshape(3, 44, 128)
        out[:, l, P_CW:P_CW + 132] = cw.transpose(2, 0, 1).reshape(128, 132)
        out[:, l, P_CB:P_CB + 44] = inp["conv_b"][l].reshape(44, 128).T
    return out.reshape(128, NL * P_N)


def _pack_rpb(rpb):
    kc = np.arange(64)[:, None]
    qc = np.arange(64)[None, :]
    idx = np.clip(kc - qc + 15, 0, 30)
    g = rpb[:, :, :, idx]
    g = np.transpose(g, (0, 3, 1, 2, 4))
    lo = g
    hi = np.concatenate([g[:, :, :, 1:, :], g[:, :, :, 14:15, :]], axis=3)
    return np.ascontiguousarray(np.concatenate([lo, hi], axis=1)).reshape(NL, 128, 8 * 15 * 64)


class Builder:
    def __init__(self, n_layers=NL, debug=False):
        self.n_layers = n_layers
        self.debug = debug
        self.nc = bass.Bass("TRN2", target_bir_lowering=False)
        nc = self.nc
        di = lambda n, s, dt=F32: nc.dram_tensor(n, s, dt, kind="ExternalInput").ap()
        self.xT = di("xT", [DM, SEQ])
        self.w_in = di("w_in", [NL, DM, N_IN])
        self.w_branch = di("w_branch", [NL, 4, 512, DM])
        self.w_out = di("w_out", [NL, DM, DM])
        self.w_up = di("w_up", [NL, DM, 2 * D_FF])
        self.w_down = di("w_down", [NL, D_FF, DM])
        self.params = di("params", [128, NL * P_N])
        self.lam = di("lam", [1, NL * 256])
        self.rpbE = di("rpbE", [NL, 128, 8 * 15 * 64])
        self.c_tab1 = di("tab1", [128, 2 * SEQ])
        self.c_tab2 = di("tab2", [128, 2 * SEQ])
        self.c_mats = di("mats", [128, 5 * 128])
        self.c_mask3 = di("mask3", [128, 3 * 128])
        self.c_colmask = di("colmask", [128, 64])
        self.yT = nc.dram_tensor("yT", [DM, SEQ], F32, kind="ExternalOutput").ap()
        kind = "ExternalOutput" if debug else "Internal"
        ds = lambda n, s, dt: nc.dram_tensor(n, s, dt, kind=kind).ap()
        self.QK = ds("s_qk", [45 * 128, SEQ], BF16)
        self.VA = ds("s_va", [SEQ, 512], BF16)
        self.VB = ds("s_vb", [SEQ, 130], BF16)
        self.VC = ds("s_vc", [3, SEQ, 520], BF16)
        self.VD = ds("s_vd", [SEQ, 520], BF16)
        self.G = ds("s_g", [4096, SEQ], F32)
        self.OT = ds("s_ot", [4, 512, SEQ], BF16)
        self.M = ds("s_m", [D_FF, SEQ], BF16)
        self.XM = ds("s_xm", [DM, SEQ], F32)
        self.XA = ds("s_xa", [DM, SEQ], F32)
        self.XB = ds("s_xb", [DM, SEQ], F32)

    def tile(self, name, shape, dt):
        t = self.ph.enter_context(self.nc.sbuf_tensor(name + "_%d" % self.uid, shape, dt))
        self.uid += 1
        return Tl(t, name)

    def ptile(self, name, shape=(128, 512), dt=F32):
        t = self.ph.enter_context(self.nc.psum_tensor(name + "_%d" % self.uid, [128, 512], F32))
        self.uid += 1
        return Tl(t, name, True)

    def A(self, eng, meth, reads, writes, *a, **k):
        wr = [x.b for x in writes] + [x.b for x in reads if x.psum]
        self.S.op(eng, lambda e: getattr(e, meth)(*a, **k), [x.b for x in reads], wr)

    def load(self, dst_tl, out_ap, in_ap, q="sp"):
        self.S.op(q, lambda e: e.dma_start(out=out_ap, in_=in_ap), [], [dst_tl.b], dma=("L", dst_tl.name))

    def store(self, src_tl, out_ap, in_ap, q="pool"):
        self.S.op(q, lambda e: e.dma_start(out=out_ap, in_=in_ap), [src_tl.b], [], dma=("S", src_tl.name))

    def mm(self, out_tl, out_ap, l_tl, l_ap, r_tl, r_ap, start, stop):
        self.S.op("pe", lambda e: e.matmul(out_ap, l_ap, r_ap, start=start, stop=stop),
                  [l_tl.b, r_tl.b], [out_tl.b])

    def begin(self):
        self.ph = ExitStack()
        self.ph.__enter__()

    def end(self):
        self.S.flush()
        self.ph.__exit__(None, None, None)

    def build(self):
        nc = self.nc
        self.uid = 0
        with ExitStack() as top:
            self.S = Sched(nc, top)
            self.top = top
            self.ph = top
            self.mats = self.tile("mats", [128, 5, 128], F32)
            self.onesb = self.tile("onesb", [128, 128], BF16)
            self.blkb = self.tile("blkb", [128, 128], BF16)
            self.par = self.tile("par", [128, NL, P_N], F32)
            self.nlam = self.tile("nlam", [128, NL], F32)
            self.begin()
            self.lamt = self.tile("lamt", [1, NL * 256], F32)
            self.load(self.mats, self.mats[:].rearrange("p a b -> p (a b)"), self.c_mats)
            self.load(self.par, self.par[:].rearrange("p a b -> p (a b)"), self.params)
            self.load(self.lamt, self.lamt[:], self.lam)
            self.A("dve", "tensor_copy", [self.mats], [self.onesb], out=self.onesb[:], in_=self.mats[:, 3, :])
            self.A("dve", "tensor_copy", [self.mats], [self.blkb], out=self.blkb[:], in_=self.mats[:, 2, :])
            self.lambda_setup()
            self.end()
            xin = self.xT
            for l in range(self.n_layers):
                last = (l == self.n_layers - 1)
                xout = self.yT if last else (self.XA if l % 2 == 0 else self.XB)
                self.layer(l, xin, xout)
                xin = xout
        return nc

    def lambda_setup(self):
        import math
        pr = self.tile("lampr", [1, NL * 2, 64], F32)
        sm = self.tile("lamsm", [1, NL * 2], F32)
        lv = self.tile("lamv", [1, NL], F32)
        ps = self.ptile("lamps", (128, NL))
        lt = self.lamt[:].rearrange("p (l a d) -> p l a d", l=NL, a=4)
        for l in range(NL):
            for j in range(2):
                self.A("dve", "tensor_tensor", [self.lamt], [pr], out=pr[:, l * 2 + j, :], in0=lt[:, l, 2 * j, :],
                       in1=lt[:, l, 2 * j + 1, :], op=ALU.mult)
        self.A("dve", "tensor_reduce", [pr], [sm], out=sm[:], in_=pr[:], axis=AX.X, op=ALU.add)
        self.A("act", "activation", [sm], [sm], out=sm[:], in_=sm[:], func=AF.Exp)
        smv = sm[:].rearrange("p (l j) -> p l j", j=2)
        for l in range(NL):
            li = 0.8 - 0.6 * math.exp(-0.3 * l)
            self.A("dve", "scalar_tensor_tensor", [sm], [lv], out=lv[:, l:l + 1], in0=smv[:, l, 1:2], scalar=-li,
                   in1=smv[:, l, 0:1], op0=ALU.add, op1=ALU.subtract)
        self.mm(ps, ps[:, 0:NL], self.mats, self.mats[0:1, 3, :], lv, lv[:], True, True)
        self.A("dve", "tensor_copy", [ps], [self.nlam], out=self.nlam[:], in_=ps[:, 0:NL])

    def rmsnorm_to_hT(self, l, xsrc, hT, goff):
        xs = [self.tile("nx%d" % i, [128, 8, TC], F32) for i in range(2)]
        sq = [self.tile("nsq%d" % i, [128, 8, TC], F32) for i in range(1)]
        rs = [self.tile("nrs%d" % i, [128, TC], F32) for i in range(2)]
        ps = [self.ptile("nps%d" % i) for i in range(2)]
        xv = xsrc.rearrange("(c p) n -> p c n", p=128)
        for t in range(NT):
            x = xs[t % 2]
            s = sq[0]
            r = rs[t % 2]
            p = ps[t % 2]
            self.load(x, x[:], xv[:, :, t * TC:(t + 1) * TC])
            self.A("pool", "tensor_tensor", [x], [s], out=s[:], in0=x[:], in1=x[:], op=ALU.mult)
            for c in range(8):
                self.mm(p, p[:], self.mats, self.mats[:, 3, :], s, s[:, c, :], c == 0, c == 7)
            self.A("act", "activation", [p], [r], out=r[:], in_=p[:], func=AF.Ln, scale=1.0 / DM, bias=self.epsc[:, 0:1])
            self.A("act", "activation", [r], [r], out=r[:], in_=r[:], func=AF.Exp, scale=-0.5)
            for c in range(8):
                self.A("dve", "scalar_tensor_tensor", [x, r, self.par], [hT], out=hT[:, c, t * TC:(t + 1) * TC],
                       in0=x[:, c, :], scalar=self.par[:, l, goff + c:goff + c + 1], in1=r[:], op0=ALU.mult, op1=ALU.mult)

    def load_weight_bf16(self, dst, dst_ap, src_ap, stg, shape_ap, eng):
        self.load(stg, shape_ap, src_ap)
        self.A(eng, "tensor_copy", [stg], [dst], out=dst_ap, in_=shape_ap)

    def layer(self, l, xin, xout):
        import math
        self.lambda_init = 0.8 - 0.6 * math.exp(-0.3 * l)
        self.begin()
        self.epsc = self.tile("epsc", [128, 1], F32)
        self.A("pool", "memset", [], [self.epsc], self.epsc[:], EPS)
        hT = self.tile("hT", [128, 8, SEQ], BF16)
        sub = ExitStack()
        outer = self.ph
        self.ph = sub
        sub.__enter__()
        self.rmsnorm_to_hT(l, xin, hT, P_G1)
        self.S.flush()
        sub.__exit__(None, None, None)
        self.ph = outer
        self.proj(l, hT)
        self.end()
        self.begin()
        self.epsc = self.tile("epsc", [128, 1], F32)
        self.A("pool", "memset", [], [self.epsc], self.epsc[:], EPS)
        self.attn_a(l)
        self.end()
        self.begin()
        self.attn_b(l)
        self.end()
        self.begin()
        self.attn_c(l)
        self.end()
        self.begin()
        self.attn_d(l)
        self.end()
        self.begin()
        self.merge(l, xin)
        self.end()
        self.begin()
        self.epsc = self.tile("epsc", [128, 1], F32)
        self.A("pool", "memset", [], [self.epsc], self.epsc[:], EPS)
        hT = self.tile("hT2", [128, 8, SEQ], BF16)
        sub = ExitStack()
        outer = self.ph
        self.ph = sub
        sub.__enter__()
        self.rmsnorm_to_hT(l, self.XM, hT, P_G2)
        self.S.flush()
        sub.__exit__(None, None, None)
        self.ph = outer
        self.ffn_up(l, hT)
        self.end()
        self.begin()
        self.ffn_down(l, xout)
        self.end()

    def pipe(self, items, lags):
        n = len(items)
        for j in range(n + max(lags)):
            for si, lg in enumerate(lags):
                i = j - lg
                if 0 <= i < n and len(items[i]) > si and items[i][si] is not None:
                    items[i][si]()

    def proj(self, l, hT):
        wst = self.tile("wst0", [128, 8, 512], F32)
        wbf = [self.tile("wbf%d" % i, [128, 8, 512], BF16) for i in range(2)]
        tab = self.tile("tab", [128, 2, SEQ], F32)
        pacc = [self.ptile("pacc%d" % i) for i in range(3)]
        pss = [self.ptile("pss%d" % i) for i in range(2)]
        prot = [self.ptile("prot%d" % i) for i in range(2)]
        sq = [self.tile("sq%d" % i, [128, TC], BF16) for i in range(3)]
        rs = [self.tile("rs%d" % i, [128, TC], F32) for i in range(3)]
        yy = [self.tile("yy%d" % i, [128, TC], F32) for i in range(3)]
        t2 = [self.tile("t2%d" % i, [128, TC], F32) for i in range(3)]
        qo = [self.tile("qo%d" % i, [128, TC], BF16) for i in range(3)]
        rowb = [self.tile("rowb%d" % i, [128, SEQ], BF16) for i in range(2)]
        go = [self.tile("go%d" % i, [128, TC], F32) for i in range(3)]
        vst = [self.tile("vst%d" % i, [128, 4, 520], BF16) for i in range(2)]
        vsta = [self.tile("vsta%d" % i, [128, 4, 512], BF16) for i in range(2)]
        for v in vst:
            self.A("pool", "memset", [], [v], v[:], 1.0)
        wv = self.w_in[l].rearrange("(k p) n -> p k n", p=128)
        cnt = dict(acc=0, q=0, row=0, go=0, vs=0, tmp=0, pp=0)
        G = P_QKG

        def wload(gidx, col0, w):
            wb = wbf[gidx % 2]
            self.load(wst, wst[:, :, 0:w], wv[:, :, col0:col0 + w])
            self.A("pool" if gidx % 2 == 0 else "dve", "tensor_copy", [wst], [wb], out=wb[:, :, 0:w], in_=wst[:, :, 0:w])

        def qk_items(items, wb, off, row, gcol, rope, r):
            mat = 0 if rope == "1d" else 1
            rb = None
            if r > 1:
                rb = rowb[cnt["row"] % 2]
                cnt["row"] += 1
            for t in range(NT):
                pa = pacc[cnt["acc"] % 3]
                cnt["acc"] += 1
                j = cnt["tmp"] % 3
                cnt["tmp"] += 1
                ps_, pr_ = pss[cnt["pp"] % 2], prot[cnt["pp"] % 2]
                cnt["pp"] += 1
                s, rr, y, a2 = sq[j], rs[j], yy[j], t2[j]
                tsl = slice(t * TC, (t + 1) * TC)
                if r > 1:
                    n_ = TC // r
                    dst_tl = rb
                    dst = rb[:].rearrange("p (c i) -> p c i", c=r)[:, :, t * n_:(t + 1) * n_]
                else:
                    dst_tl = qo[cnt["q"] % 3]
                    cnt["q"] += 1
                    dst = dst_tl[:]

                def s0(pa=pa, s=s, tsl=tsl):
                    for k in range(8):
                        self.mm(pa, pa[:], wb, wb[:, k, off:off + 128], hT, hT[:, k, tsl], k == 0, k == 7)
                    self.A("act", "activation", [pa], [s], out=s[:], in_=pa[:], func=AF.Square)

                def s1(pa=pa, s=s, rr=rr, y=y, ps_=ps_, dst_tl=dst_tl, dst=dst):
                    self.mm(ps_, ps_[:], self.blkb, self.blkb[:], s, s[:], True, True)
                    self.A("act", "activation", [ps_], [rr], out=rr[:], in_=ps_[:], func=AF.Ln, scale=1.0 / 64, bias=self.epsc[:, 0:1])
                    self.A("act", "activation", [rr], [rr], out=rr[:], in_=rr[:], func=AF.Exp, scale=-0.5)
                    if rope is None:
                        self.A("dve", "scalar_tensor_tensor", [pa, rr, self.par], [dst_tl], out=dst, in0=pa[:],
                               scalar=self.par[:, l, gcol:gcol + 1], in1=rr[:], op0=ALU.mult, op1=ALU.mult)
                    else:
                        self.A("dve", "scalar_tensor_tensor", [pa, rr, self.par], [y], out=y[:], in0=pa[:],
                               scalar=self.par[:, l, gcol:gcol + 1], in1=rr[:], op0=ALU.mult, op1=ALU.mult)

                def s2(y=y, a2=a2, pr_=pr_, dst_tl=dst_tl, dst=dst, tsl=tsl, t=t):
                    if rope is not None:
                        self.mm(pr_, pr_[:], self.mats, self.mats[:, mat, :], y, y[:], True, True)
                        self.A("dve", "tensor_tensor", [pr_, tab], [a2], out=a2[:], in0=pr_[:], in1=tab[:, 1, tsl], op=ALU.mult)
                        self.A("pool", "tensor_tensor", [y, tab], [y], out=y[:], in0=y[:], in1=tab[:, 0, tsl], op=ALU.mult)
                        if r > 1:
                            src1 = y[:].rearrange("p (i c) -> p c i", c=r)
                            src2 = a2[:].rearrange("p (i c) -> p c i", c=r)
                        else:
                            src1, src2 = y[:], a2[:]
                        self.A("pool", "tensor_tensor", [y, a2], [dst_tl], out=dst, in0=src1, in1=src2, op=ALU.add)
                    if r == 1:
                        self.store(dst_tl, self.QK[row * 128:(row + 1) * 128, tsl], dst_tl[:])
                    elif t == NT - 1:
                        self.store(rb, self.QK[row * 128:(row + 1) * 128, :], rb[:])

                items.append([s0, s1, s2])

        def v_items(items, wb, off, w, dst, r, nh, hd):
            hv = hT[:].rearrange("p k (i c) -> p k c i", c=r)
            L = SEQ // r
            stride = hd + 1 if hd == 64 else hd
            vs = None
            for tt in range(32):
                c = (tt * 128) // L
                i0 = (tt * 128) % L
                pa = pacc[cnt["acc"] % 3]
                cnt["acc"] += 1
                if tt % 4 == 0:
                    vs = (vst if hd == 64 else vsta)[cnt["vs"] % 2]
                    cnt["vs"] += 1

                def s0(pa=pa, c=c, i0=i0):
                    for k in range(8):
                        self.mm(pa, pa[:, 0:w], hT, hv[:, k, c, i0:i0 + 128], wb, wb[:, k, off:off + w], k == 0, k == 7)

                def s1(pa=pa, vs=vs, tt=tt):
                    o = vs[:, tt % 4, 0:nh * stride].rearrange("p (h d) -> p h d", h=nh)[:, :, 0:hd]
                    i_ = pa[:, 0:w].rearrange("p (h d) -> p h d", h=nh)
                    if tt % 2 == 0:
                        self.A("act", "activation", [pa], [vs], out=o, in_=i_, func=AF.Copy)
                    else:
                        self.A("dve", "tensor_copy", [pa], [vs], out=o, in_=i_)
                    if tt % 4 == 3:
                        t0 = (tt - 3) * 128
                        self.store(vs, dst[t0:t0 + 512, :].rearrange("(a p) n -> p a n", p=128), vs[:, :, 0:nh * stride])

                items.append([s0, s1])

        def gate_items(items, wb, off, grow):
            for t in range(NT):
                pa = pacc[cnt["acc"] % 3]
                cnt["acc"] += 1
                g = go[cnt["go"] % 3]
                cnt["go"] += 1
                tsl = slice(t * TC, (t + 1) * TC)

                def s0(pa=pa, tsl=tsl):
                    for k in range(8):
                        self.mm(pa, pa[:], wb, wb[:, k, off:off + 128], hT, hT[:, k, tsl], k == 0, k == 7)

                def s1(pa=pa, g=g, tsl=tsl):
                    self.A("act", "activation", [pa], [g], out=g[:], in_=pa[:], func=AF.Sigmoid)
                    self.store(g, self.G[grow * 128:(grow + 1) * 128, tsl], g[:])

                items.append([s0, s1])

        jobsB = [
            (COL["bq"], 512, [("qk", i * 128, 8 + i, G + 2, "ax", 1) for i in range(4)]),
            (COL["bk"], 256, [("qk", 0, 12, G + 3, "ax", 1), ("v", 128, 128, self.VB, 1, 2, 64)]),
        ]
        jobs = [
            (COL["aq"], 512, [("qk", i * 128, 0 + i, G + 0, "1d", 1) for i in range(4)]),
            (COL["ak"], 512, [("qk", i * 128, 4 + i, G + 1, "1d", 1) for i in range(4)]),
            (COL["av"], 512, [("v", 0, 512, self.VA, 1, 4, 128)]),
        ]
        for g in range(3):
            jobs.append((COL["cq"] + g * 512, 512, [("qk", i * 128, 13 + g * 4 + i, G + 4, "1d", DIL[g]) for i in range(4)]))
            jobs.append((COL["ck"] + g * 512, 512, [("qk", i * 128, 25 + g * 4 + i, G + 5, "1d", DIL[g]) for i in range(4)]))
            jobs.append((COL["cv"] + g * 512, 512, [("v", 0, 512, self.VC[g], DIL[g], 8, 64)]))
        jobs.append((COL["dq"], 512, [("qk", i * 128, 37 + i, G + 6, None, 1) for i in range(4)]))
        jobs.append((COL["dk"], 512, [("qk", i * 128, 41 + i, G + 7, None, 1) for i in range(4)]))
        jobs.append((COL["dv"], 512, [("v", 0, 512, self.VD, 1, 8, 64)]))
        for gi in range(8):
            jobs.append((COL["g"] + gi * 512, 512, [("gate", i * 128, gi * 4 + i) for i in range(4)]))

        gctr = [0]

        def run_jobs(jl):
            items = []
            gid0 = gctr[0]
            for ji, (col0, w, subs) in enumerate(jl):
                gid = gid0 + ji
                wb = wbf[gid % 2]
                first = len(items)
                for sj in subs:
                    if sj[0] == "qk":
                        qk_items(items, wb, *sj[1:])
                    elif sj[0] == "v":
                        v_items(items, wb, *sj[1:])
                    else:
                        gate_items(items, wb, *sj[1:])
                orig = items[first][0]
                nxt = jl[ji + 1] if ji + 1 < len(jl) else None

                def s0w(orig=orig, nxt=nxt, gid=gid):
                    if nxt is not None:
                        wload(gid + 1, nxt[0], nxt[1])
                    orig()
                items[first][0] = s0w
            wload(gid0, jl[0][0], jl[0][1])
            gctr[0] += len(jl)
            self.pipe(items, [0, 1, 2])

        self.load(tab, tab[:].rearrange("p a n -> p (a n)"), self.c_tab2)
        run_jobs(jobsB)
        self.load(tab, tab[:].rearrange("p a n -> p (a n)"), self.c_tab1)
        run_jobs(jobs)

    def normalize_rows65(self, po, res_tl, res_ap, n, tmp_row, tmp_bc, pbc):
        self.A("dve", "reciprocal", [po], [tmp_row], out=tmp_row[64:65, 0:n], in_=po[64:65, 0:n])
        self.mm(pbc, pbc[0:64, 0:n], self.mats, self.mats[64:65, 3, 0:64], tmp_row, tmp_row[64:65, 0:n], True, True)
        self.A("act", "activation", [pbc], [tmp_bc], out=tmp_bc[0:64, 0:n], in_=pbc[0:64, 0:n], func=AF.Copy)
        self.A("dve", "tensor_tensor", [po, tmp_bc], [res_tl], out=res_ap, in0=po[0:64, 0:n], in1=tmp_bc[0:64, 0:n], op=ALU.mult)

    def attn_a(self, l):
        qz = [self.tile("qz%d" % c, [128, 4, SEQ], BF16) for c in range(2)]
        ka = self.tile("ka", [128, 4, SEQ], BF16)
        va = self.tile("va", [128, 32, 512], BF16)
        for c in range(2):
            oc = 1 - c
            self.A("pool", "memset", [], [qz[c]], qz[c][oc * 64:(oc + 1) * 64, :, :], 0.0)
        for h in range(4):
            for c in range(2):
                self.load(qz[c], qz[c][c * 64:(c + 1) * 64, h, :], self.QK[h * 128 + c * 64:h * 128 + (c + 1) * 64, :])
            self.load(ka, ka[:, h, :], self.QK[(4 + h) * 128:(5 + h) * 128, :])
        for j in range(4):
            self.load(va, va[:, j * 8:(j + 1) * 8, :], self.VA[j * 1024:(j + 1) * 1024, :].rearrange("(a p) n -> p a n", p=128))
        psc = [self.ptile("psc%d" % i) for i in range(3)]
        po = [self.ptile("po%d" % c) for c in range(2)]
        psm = [self.ptile("psm%d" % c) for c in range(2)]
        pfin = self.ptile("pfin")
        pt = [self.tile("pt%d" % i, [128, TC], BF16) for i in range(6)]
        poS = [self.tile("poS%d" % c, [128, TC], F32) for c in range(2)]
        smS = [self.tile("smS%d" % c, [128, TC], F32) for c in range(2)]
        rc = self.tile("rc", [128, TC], F32)
        res = [self.tile("res%d" % i, [128, TC], F32) for i in range(2)]
        dd = self.tile("dd", [128, TC], F32)
        sq = self.tile("sqa", [128, TC], F32)
        rs = self.tile("rsa", [128, TC], F32)
        ob = [self.tile("oba%d" % i, [128, TC], BF16) for i in range(2)]
        items = []
        n = 0
        for h in range(4):
            for qc in range(NT):
                qs = slice(qc * TC, (qc + 1) * TC)
                for kt in range(32):
                    for c in range(2):
                        sc = psc[n % 3]
                        p = pt[n % 6]
                        n += 1
                        po_ = po[c]
                        psm_ = psm[c]

                        def s0(sc=sc, p=p, h=h, qs=qs, c=c, kt=kt):
                            self.mm(sc, sc[:], ka, ka[:, h, kt * 128:(kt + 1) * 128], qz[c], qz[c][:, h, qs], True, True)
                            self.A("act", "activation", [sc], [p], out=p[:], in_=sc[:], func=AF.Exp, scale=0.125)

                        def s1(p=p, h=h, c=c, kt=kt, po_=po_, psm_=psm_):
                            self.mm(po_, po_[:], va, va[:, kt, h * 128:(h + 1) * 128], p, p[:], kt == 0, kt == 31)
                            self.mm(psm_, psm_[:], self.onesb, self.onesb[:], p, p[:], kt == 0, kt == 31)
                            if kt == 31:
                                self.A("act", "activation", [po_], [poS[c]], out=poS[c][:], in_=po_[:], func=AF.Copy)
                                self.A("act", "activation", [psm_], [smS[c]], out=smS[c][:], in_=psm_[:], func=AF.Copy)

                        def s2(h=h, qc=qc, qs=qs, c=c, kt=kt):
                            if kt != 31:
                                return
                            self.A("dve", "reciprocal", [smS[c]], [rc], out=rc[:], in_=smS[c][:])
                            self.A("dve", "tensor_tensor", [poS[c], rc], [res[c]], out=res[c][:], in0=poS[c][:], in1=rc[:], op=ALU.mult)
                            if c != 1:
                                return
                            self.A("dve", "scalar_tensor_tensor", [res[0], res[1], self.nlam], [dd], out=dd[:], in0=res[1][:],
                                   scalar=self.nlam[:, l:l + 1], in1=res[0][:], op0=ALU.mult, op1=ALU.add)
                            self.A("pool", "tensor_tensor", [dd], [sq], out=sq[:], in0=dd[:], in1=dd[:], op=ALU.mult)
                            self.mm(pfin, pfin[:], self.mats, self.mats[:, 3, :], sq, sq[:], True, True)
                            self.A("act", "activation", [pfin], [rs], out=rs[:], in_=pfin[:], func=AF.Ln, scale=1.0 / 128, bias=self.epsc[:, 0:1])
                            self.A("act", "activation", [rs], [rs], out=rs[:], in_=rs[:], func=AF.Exp, scale=-0.5)
                            self.A("dve", "scalar_tensor_tensor", [dd, rs, self.par], [dd], out=dd[:], in0=dd[:],
                                   scalar=self.par[:, l, P_SUB:P_SUB + 1], in1=rs[:], op0=ALU.mult, op1=ALU.mult)
                            o = ob[(h * NT + qc) % 2]
                            self.A("act", "activation", [dd], [o], out=o[:], in_=dd[:], func=AF.Copy, scale=float(1.0 - self.lambda_init))
                            self.store(o, self.OT[0, h * 128:(h + 1) * 128, qs], o[:])

                        items.append([s0, s1, s2])
        self.pipe(items, [0, 3, 12])

    def attn_b(self, l):
        qb = self.tile("qb", [128, 8, SEQ], BF16)
        kb = self.tile("kb", [128, SEQ], BF16)
        vb = self.tile("vb", [128, 32, 130], BF16)
        for hq in range(8):
            og = 1 - hq // 4
            self.A("pool", "memset", [], [qb], qb[og * 64:(og + 1) * 64, hq, :], 0.0)
        for hq in range(8):
            g, s = hq // 4, hq % 4
            self.load(qb, qb[g * 64:(g + 1) * 64, hq, :], self.QK[8 * 128 + hq * 64:8 * 128 + (hq + 1) * 64, :])
        self.load(kb, kb[:], self.QK[12 * 128:13 * 128, :])
        self.load(vb, vb[:], self.VB.rearrange("(a p) n -> p a n", p=128))
        psc = [self.ptile("psc%d" % i) for i in range(3)]
        po = [[self.ptile("pob%d%d" % (g, i)) for i in range(2)] for g in range(2)]
        pbc = self.ptile("pbc")
        pt = [self.tile("pt%d" % i, [128, TC], BF16) for i in range(6)]
        trow = self.tile("trow", [128, TC], F32)
        tbc = self.tile("tbc", [128, TC], F32)
        ob = [self.tile("obb%d" % i, [128, TC], BF16) for i in range(2)]
        items = []
        n = 0
        m = 0
        gi = 0
        for s in range(4):
            for qc in range(NT):
                qs = slice(qc * TC, (qc + 1) * TC)
                buf = gi % 2
                gi += 1
                for kt in range(32):
                    for g in range(2):
                        hq = g * 4 + s
                        ps_ = slice(g * 64, (g + 1) * 64)
                        o_ps = po[g][buf]
                        sc = psc[n % 3]
                        p = pt[n % 6]
                        n += 1

                        def s0(sc=sc, p=p, hq=hq, qs=qs, kt=kt):
                            self.mm(sc, sc[:], kb, kb[:, kt * 128:(kt + 1) * 128], qb, qb[:, hq, qs], True, True)
                            self.A("act", "activation", [sc], [p], out=p[:], in_=sc[:], func=AF.Exp, scale=0.125)

                        def s1(p=p, g=g, kt=kt, o_ps=o_ps):
                            self.mm(o_ps, o_ps[0:65, :], vb, vb[:, kt, g * 65:(g + 1) * 65], p, p[:], kt == 0, kt == 31)

                        def s2(g=g, hq=hq, qs=qs, kt=kt, o_ps=o_ps):
                            if kt != 31:
                                return
                            o = ob[hq % 2]
                            self.normalize_rows65(o_ps, o, o[0:64, :], TC, trow, tbc, pbc)
                            self.store(o, self.OT[1, hq * 64:(hq + 1) * 64, qs], o[0:64, :])

                        items.append([s0, s1, s2])
        self.pipe(items, [0, 3, 12])

    def attn_c(self, l):
        mask = self.tile("mask3", [128, 3, 128], F32)
        self.load(mask, mask[:].rearrange("p a b -> p (a b)"), self.c_mask3)
        qc_ = self.tile("qc", [128, 3, SEQ], BF16)
        kc_ = self.tile("kc", [128, 3, SEQ], BF16)
        vc_ = self.tile("vc", [128, 3, 32, 130], BF16)
        acc = [self.tile("acc%d" % i, [65, SEQ], F32) for i in range(2)]
        psc = [self.ptile("pscc%d" % i, (128, 384)) for i in range(3)]
        po = [self.ptile("poc%d" % i) for i in range(2)]
        pbc = self.ptile("pbcc")
        ex = [self.tile("ex%d" % i, [128, 3, 128], F32) for i in range(3)]
        pt = [self.tile("ptc%d" % i, [128, 3, 128], BF16) for i in range(4)]
        trow = self.tile("trowc", [128, TC], F32)
        ob = [self.tile("obc%d" % i, [128, TC], BF16) for i in range(2)]
        n = 0
        m = 0
        for jp in range(4):
            for g in range(3):
                self.load(qc_, qc_[:, g, :], self.QK[(13 + g * 4 + jp) * 128:(14 + g * 4 + jp) * 128, :])
                self.load(kc_, kc_[:, g, :], self.QK[(25 + g * 4 + jp) * 128:(26 + g * 4 + jp) * 128, :])
                self.load(vc_, vc_[:, g, :, :], self.VC[g].rearrange("(a p) n -> p a n", p=128)[:, :, jp * 130:(jp + 1) * 130])
            items = []
            for hh in range(2):
                ps_ = slice(hh * 64, (hh + 1) * 64)
                ac = acc[hh]
                hd = jp * 2 + hh
                for g in range(3):
                    r = DIL[g]
                    L = SEQ // r
                    tps = L // 128
                    for qb4 in range(8):
                        o_ps = po[m % 2]
                        m += 1
                        for u in range(4):
                            qb = qb4 * 4 + u
                            seg = qb // tps
                            kts = [k for k in (qb - 1, qb, qb + 1) if k // tps == seg and 0 <= k < 32]
                            j0 = kts[0] - (qb - 1)
                            nk = len(kts)
                            sc = psc[n % 3]
                            e_ = ex[n % 3]
                            p = pt[n % 4]
                            n += 1

                            def s0(sc=sc, e_=e_, p=p, kts=kts, j0=j0, nk=nk, qb=qb, g=g, ps_=ps_, n=n):
                                for k in kts:
                                    j = k - (qb - 1)
                                    self.mm(sc, sc[:, j * 128:(j + 1) * 128], kc_, kc_[ps_, g, k * 128:(k + 1) * 128],
                                            qc_, qc_[ps_, g, qb * 128:(qb + 1) * 128], True, True)
                                scv = sc[:, 0:384].rearrange("p (a b) -> p a b", a=3)
                                self.A("act", "activation", [sc], [e_], out=e_[:, j0:j0 + nk, :], in_=scv[:, j0:j0 + nk, :], func=AF.Exp, scale=0.125)
                                self.A("pool" if n % 2 == 0 else "dve", "tensor_tensor", [e_, mask], [p], out=p[:, j0:j0 + nk, :], in0=e_[:, j0:j0 + nk, :],
                                       in1=mask[:, j0:j0 + nk, :], op=ALU.mult)

                            def s1(p=p, kts=kts, nk=nk, qb=qb, g=g, hh=hh, u=u, o_ps=o_ps, qb4=qb4, r=r, L=L, ac=ac, hd=hd):
                                for ki, k in enumerate(kts):
                                    j = k - (qb - 1)
                                    self.mm(o_ps, o_ps[0:65, u * 128:(u + 1) * 128], vc_, vc_[:, g, k, hh * 65:(hh + 1) * 65],
                                            p, p[:, j, :], ki == 0, ki == nk - 1)
                                if u != 3:
                                    return
                                pos0 = qb4 * 512
                                av = ac[:].rearrange("p (i c) -> p c i", c=r)
                                if L >= 512:
                                    c0, i0 = pos0 // L, pos0 % L
                                    dst = av[0:65, c0:c0 + 1, i0:i0 + 512]
                                    src = o_ps[0:65, :].rearrange("p (c i) -> p c i", c=1)
                                else:
                                    ncl = 512 // L
                                    c0 = pos0 // L
                                    dst = av[0:65, c0:c0 + ncl, :]
                                    src = o_ps[0:65, :].rearrange("p (c i) -> p c i", c=ncl)
                                if g == 0:
                                    self.A("act", "activation", [o_ps], [ac], out=dst, in_=src, func=AF.Copy)
                                else:
                                    self.A("dve", "tensor_tensor", [o_ps, ac], [ac], out=dst, in0=dst, in1=src, op=ALU.add)
                                if g == 2 and qb4 == 7:
                                    for t in range(NT):
                                        o = ob[(hd * NT + t) % 2]
                                        ts_ = slice(t * TC, (t + 1) * TC)
                                        self.A("dve", "reciprocal", [ac], [trow], out=trow[64:65, :], in_=ac[64:65, ts_])
                                        self.mm(pbc, pbc[0:64, :], self.mats, self.mats[64:65, 3, 0:64], trow, trow[64:65, :], True, True)
                                        self.A("dve", "tensor_tensor", [pbc, ac], [o], out=o[0:64, :], in0=pbc[0:64, :], in1=ac[0:64, ts_], op=ALU.mult)
                                        self.store(o, self.OT[2, hd * 64:(hd + 1) * 64, ts_], o[0:64, :])

                            items.append([s0, s1])
            self.pipe(items, [0, 2])

    def attn_d(self, l):
        qd = self.tile("qd", [128, 4, SEQ], BF16)
        kd = self.tile("kd", [128, 4, SEQ], BF16)
        ve = self.tile("ve", [128, 32, 520], BF16)
        vo = self.tile("vo", [128, 31, 520], BF16)
        E = self.tile("E", [128, 8, 15, 64], F32)
        cm = self.tile("cm", [128, 64], F32)
        self.load(cm, cm[:], self.c_colmask)
        for i in range(4):
            self.load(qd, qd[:, i, :], self.QK[(37 + i) * 128:(38 + i) * 128, :])
            self.load(kd, kd[:, i, :], self.QK[(41 + i) * 128:(42 + i) * 128, :])
        for j in range(4):
            self.load(ve, ve[:, j * 8:(j + 1) * 8, :], self.VD[j * 1024:(j + 1) * 1024, :].rearrange("(a p) n -> p a n", p=128))
        for j in range(4):
            na = 8 if j < 3 else 7
            self.load(vo, vo[:, j * 8:j * 8 + na, :], self.VD[64 + j * 1024:64 + j * 1024 + na * 128, :].rearrange("(a p) n -> p a n", p=128))
        for h in range(8):
            self.load(E, E[:, h, :, :].rearrange("p a b -> p (a b)"), self.rpbE[l][:, h * 960:(h + 1) * 960])
        for h in range(8):
            self.A("act", "activation", [E], [E], out=E[:, h, :, :], in_=E[:, h, :, :], func=AF.Exp)
            self.A("pool", "tensor_tensor", [E, cm], [E], out=E[:, h, :, :], in0=E[:, h, :, :],
                   in1=cm[:].rearrange("p (a b) -> p a b", a=1).to_broadcast([128, 15, 64]), op=ALU.mult)
        psc = [self.ptile("pscd%d" % i, (128, 256)) for i in range(3)]
        po = [self.ptile("pod%d" % i) for i in range(2)]
        pbc = self.ptile("pbcd")
        ex = [self.tile("exd%d" % i, [128, 4, 64], F32) for i in range(3)]
        pt = [self.tile("ptd%d" % i, [128, 4, 64], BF16) for i in range(4)]
        trow = self.tile("trowd", [128, TC], F32)
        tbc = self.tile("tbcd", [128, TC], F32)
        ob = [self.tile("obd%d" % i, [128, TC], BF16) for i in range(2)]
        items = []
        n = 0
        m = 0
        for h in range(8):
            ch, hh = h // 2, h % 2
            ps_ = slice(hh * 64, (hh + 1) * 64)
            for r8 in range(8):
                o_ps = po[m % 2]
                o = ob[m % 2]
                m += 1
                for u in range(8):
                    r = r8 * 8 + u
                    rs_ = min(max(r - 4, 0), 56)
                    base = rs_ - r + 7
                    sc = psc[n % 3]
                    e_ = ex[n % 3]
                    p = pt[n % 4]
                    n += 1

                    def s0(sc=sc, e_=e_, p=p, r=r, rs_=rs_, base=base, h=h, ch=ch, ps_=ps_, n=n):
                        for i in range(4):
                            k0 = (rs_ + 2 * i) * 64
                            self.mm(sc, sc[:, i * 64:(i + 1) * 64], kd, kd[ps_, ch, k0:k0 + 128], qd, qd[ps_, ch, r * 64:(r + 1) * 64], True, True)
                        self.A("act", "activation", [sc], [e_], out=e_[:], in_=sc[:, 0:256].rearrange("p (a b) -> p a b", a=4), func=AF.Exp, scale=0.125)
                        ev = E[:, h, base:base + 7:2, :]
                        self.A("pool" if n % 2 == 0 else "dve", "tensor_tensor", [e_, E], [p], out=p[:], in0=e_[:], in1=ev, op=ALU.mult)

                    def s1(p=p, rs_=rs_, h=h, u=u, o_ps=o_ps, o=o, r8=r8):
                        for i in range(4):
                            row0 = rs_ + 2 * i
                            if row0 % 2 == 0:
                                vt, vi = ve, row0 // 2
                            else:
                                vt, vi = vo, (row0 - 1) // 2
                            self.mm(o_ps, o_ps[0:65, u * 64:(u + 1) * 64], vt, vt[:, vi, h * 65:(h + 1) * 65], p, p[:, i, :], i == 0, i == 3)
                        if u != 7:
                            return
                        self.normalize_rows65(o_ps, o, o[0:64, :], TC, trow, tbc, pbc)
                        self.store(o, self.OT[3, h * 64:(h + 1) * 64, r8 * TC:(r8 + 1) * TC], o[0:64, :])

                    items.append([s0, s1])
        self.pipe(items, [0, 2])

    def merge(self, l, xin):
        wb = self.tile("wbr", [128, 16, DM], BF16)
        wo = self.tile("wo", [128, 8, DM], BF16)
        stg = [self.tile("mstg%d" % i, [128, 4, DM], F32) for i in range(2)]
        n = 0
        for i in range(4):
            s = stg[n % 2]
            n += 1
            self.load(s, s[:], self.w_branch[l, i].rearrange("(k p) n -> p k n", p=128))
            self.A("pool" if n % 2 == 0 else "dve", "tensor_copy", [s], [wb], out=wb[:, i * 4:(i + 1) * 4, :], in_=s[:])
        for j in range(2):
            s = stg[n % 2]
            n += 1
            self.load(s, s[:], self.w_out[l][j * 512:(j + 1) * 512, :].rearrange("(k p) n -> p k n", p=128))
            self.A("pool" if n % 2 == 0 else "dve", "tensor_copy", [s], [wo], out=wo[:, j * 4:(j + 1) * 4, :], in_=s[:])
        ot = [self.tile("mot%d" % i, [128, 16, TC], BF16) for i in range(2)]
        xs = [self.tile("mx%d" % i, [128, 8, TC], F32) for i in range(2)]
        gt = [self.tile("mg%d" % i, [128, TC], F32) for i in range(8)]
        mt = [self.tile("mm%d" % i, [128, 8, TC], BF16) for i in range(2)]
        acc = [self.tile("macc%d" % i, [128, TC], F32) for i in range(2)]
        tmp = [self.tile("mtmp%d" % i, [128, TC], F32) for i in range(2)]
        py = [self.ptile("py%d" % i) for i in range(3)]
        px = [self.ptile("px%d" % i) for i in range(2)]
        xo = [self.tile("mxo%d" % i, [128, TC], F32) for i in range(3)]
        xv = xin.rearrange("(c p) n -> p c n", p=128)
        otv = self.OT.rearrange("b (k p) n -> p (b k) n", p=128)
        ng = 0
        ny = 0
        nx = 0
        for t in range(NT):
            ts_ = slice(t * TC, (t + 1) * TC)
            o = ot[t % 2]
            x = xs[t % 2]
            mtt = mt[t % 2]
            for i in range(4):
                self.load(o, o[:, i * 4:(i + 1) * 4, :], otv[:, i * 4:(i + 1) * 4, ts_])
            self.load(x, x[:], xv[:, :, ts_])
            for f in range(8):
                a = acc[f % 2]
                for i in range(4):
                    g = gt[ng % 8]
                    ng += 1
                    self.load(g, g[:], self.G[(i * 8 + f) * 128:(i * 8 + f + 1) * 128, ts_])
                    p = py[ny % 3]
                    ny += 1
                    for k in range(4):
                        self.mm(p, p[:], wb, wb[:, i * 4 + k, f * 128:(f + 1) * 128], o, o[:, i * 4 + k, :], k == 0, k == 3)
                    if i == 0:
                        self.A("dve", "tensor_tensor", [p, g], [a], out=a[:], in0=p[:], in1=g[:], op=ALU.mult)
                    else:
                        tm = tmp[i % 2]
                        self.A("dve", "tensor_tensor", [p, g], [tm], out=tm[:], in0=p[:], in1=g[:], op=ALU.mult)
                        if i < 3:
                            self.A("pool", "tensor_tensor", [a, tm], [a], out=a[:], in0=a[:], in1=tm[:], op=ALU.add)
                        else:
                            self.A("pool", "tensor_tensor", [a, tm], [mtt], out=mtt[:, f, :], in0=a[:], in1=tm[:], op=ALU.add)
            for fo in range(8):
                p = px[nx % 2]
                xo_ = xo[nx % 3]
                nx += 1
                for k in range(8):
                    self.mm(p, p[:], wo, wo[:, k, fo * 128:(fo + 1) * 128], mtt, mtt[:, k, :], k == 0, k == 7)
                self.A("dve", "tensor_tensor", [p, x], [xo_], out=xo_[:], in0=p[:], in1=x[:, fo, :], op=ALU.add)
                self.store(xo_, self.XM[fo * 128:(fo + 1) * 128, ts_], xo_[:])

    def ffn_up(self, l, hT):
        wst = [self.tile("fst%d" % i, [128, 8, 256], F32) for i in range(1)]
        wbf = [self.tile("fbf%d" % i, [128, 8, 256], BF16) for i in range(4)]
        uas = [self.tile("ua%d" % i, [128, SEQ + 2], F32) for i in range(2)]
        ugs = [self.tile("ug%d" % i, [128, SEQ + 2], F32) for i in range(2)]
        ca = self.tile("ca", [128, SEQ], F32)
        cg = self.tile("cg", [128, SEQ], F32)
        mrow = [self.tile("mrow%d" % i, [128, SEQ], BF16) for i in range(1)]
        pacc = [self.ptile("fpa%d" % i) for i in range(4)]
        for u in uas + ugs:
            self.A("pool", "memset", [], [u], u[:, 0:1], 0.0)
            self.A("pool", "memset", [], [u], u[:, SEQ + 1:SEQ + 2], 0.0)
        wv = self.w_up[l].rearrange("(k p) n -> p k n", p=128)
        ngrp = 0
        nacc = 0
        for j in range(11):
            w = 256
            wbs = []
            for part in range(2):
                st = wst[0]
                wb = wbf[ngrp % 4]
                ngrp += 1
                c0 = part * D_FF + j * 256
                self.load(st, st[:, :, 0:w], wv[:, :, c0:c0 + w])
                self.A("pool" if part == 0 else "dve", "tensor_copy", [st], [wb], out=wb[:, :, 0:w], in_=st[:, :, 0:w])
                wbs.append(wb)
            for ii in range(w // 128):
                i = j * 2 + ii
                ua, ug = uas[i % 2], ugs[i % 2]
                for part, (u, cdst) in enumerate(((ua, ca), (ug, cg))):
                    wb = wbs[part]
                    for t in range(NT):
                        p = pacc[nacc % 4]
                        nacc += 1
                        for k in range(8):
                            self.mm(p, p[:], wb, wb[:, k, ii * 128:(ii + 1) * 128], hT, hT[:, k, t * TC:(t + 1) * TC], k == 0, k == 7)
                        self.A("act", "activation", [p], [u], out=u[:, 1 + t * TC:1 + (t + 1) * TC], in_=p[:], func=AF.Copy)
                    ch = part * 22 + i
                    w0 = self.par[:, l, P_CW + 0 * 44 + ch:P_CW + 0 * 44 + ch + 1]
                    w1 = self.par[:, l, P_CW + 1 * 44 + ch:P_CW + 1 * 44 + ch + 1]
                    w2 = self.par[:, l, P_CW + 2 * 44 + ch:P_CW + 2 * 44 + ch + 1]
                    bb = self.par[:, l, P_CB + ch:P_CB + ch + 1]
                    eng = "dve"
                    for hf in range(2):
                        hs = slice(hf * 2048, (hf + 1) * 2048)
                        self.A(eng, "tensor_scalar", [u, self.par], [cdst], out=cdst[:, hs], in0=u[:, hf * 2048:hf * 2048 + 2048],
                               scalar1=w0, scalar2=bb, op0=ALU.mult, op1=ALU.add)
                        self.A(eng, "scalar_tensor_tensor", [u, cdst, self.par], [cdst], out=cdst[:, hs], in0=u[:, 1 + hf * 2048:1 + hf * 2048 + 2048],
                               scalar=w1, in1=cdst[:, hs], op0=ALU.mult, op1=ALU.add)
                        self.A(eng, "scalar_tensor_tensor", [u, cdst, self.par], [cdst], out=cdst[:, hs], in0=u[:, 2 + hf * 2048:2 + hf * 2048 + 2048],
                               scalar=w2, in1=cdst[:, hs], op0=ALU.mult, op1=ALU.add)
                mr = mrow[0]
                for hf in range(2):
                    hs = slice(hf * 2048, (hf + 1) * 2048)
                    self.A("act", "activation", [ca], [ca], out=ca[:, hs], in_=ca[:, hs], func=AF.Silu)
                    self.A("pool" if hf == 0 else "dve", "tensor_tensor", [ca, cg], [mr], out=mr[:, hs], in0=ca[:, hs], in1=cg[:, hs], op=ALU.mult)
                self.store(mr, self.M[i * 128:(i + 1) * 128, :], mr[:])

    def ffn_down(self, l, xout):
        wd = self.tile("wd", [128, 22, DM], BF16)
        stg = [self.tile("dstg%d" % i, [128, 2, DM], F32) for i in range(2)]
        wv = self.w_down[l].rearrange("(k p) n -> p k n", p=128)
        for j in range(11):
            s = stg[j % 2]
            self.load(s, s[:], wv[:, j * 2:(j + 1) * 2, :])
            self.A("pool" if j % 2 == 0 else "dve", "tensor_copy", [s], [wd], out=wd[:, j * 2:(j + 1) * 2, :], in_=s[:])
        mt = [self.tile("dm%d" % i, [128, 22, TC], BF16) for i in range(2)]
        xs = [self.tile("dx%d" % i, [128, 8, TC], F32) for i in range(2)]
        xo = [self.tile("dxo%d" % i, [128, TC], F32) for i in range(3)]
        px = [self.ptile("dpx%d" % i) for i in range(3)]
        mv = self.M.rearrange("(k p) n -> p k n", p=128)
        xv = self.XM.rearrange("(c p) n -> p c n", p=128)
        nx = 0
        for t in range(NT):
            ts_ = slice(t * TC, (t + 1) * TC)
            m = mt[t % 2]
            x = xs[t % 2]
            self.load(m, m[:, 0:11, :], mv[:, 0:11, ts_])
            self.load(m, m[:, 11:22, :], mv[:, 11:22, ts_])
            self.load(x, x[:], xv[:, :, ts_])
            for fo in range(8):
                p = px[nx % 3]
                xo_ = xo[nx % 3]
                nx += 1
                for k in range(22):
                    self.mm(p, p[:], wd, wd[:, k, fo * 128:(fo + 1) * 128], m, m[:, k, :], k == 0, k == 21)
                self.A("dve", "tensor_tensor", [p, x], [xo_], out=xo_[:], in0=p[:], in1=x[:, fo, :], op=ALU.add)
                self.store(xo_, xout[fo * 128:(fo + 1) * 128, ts_], xo_[:])


_CACHE = {}


def _get_nc(n_layers=NL, debug=False):
    key = (n_layers, debug)
    if key not in _CACHE:
        b = Builder(n_layers, debug)
        _CACHE[key] = (b.build(), b)
    return _CACHE[key]


def make_in_maps(inp):
    consts = _host_consts()
    params = _pack_params(inp)
    lam = np.ascontiguousarray(np.asarray(inp["lam"], np.float32).reshape(1, NL * 256))
    rpbE = _pack_rpb(np.asarray(inp["rpb"], np.float32))
    shared = dict(
        w_in=np.ascontiguousarray(inp["w_in"], np.float32),
        w_branch=np.ascontiguousarray(inp["w_branch"], np.float32),
        w_out=np.ascontiguousarray(inp["w_out"], np.float32),
        w_up=np.ascontiguousarray(inp["w_up"], np.float32),
        w_down=np.ascontiguousarray(inp["w_down"], np.float32),
        params=params, lam=lam, rpbE=rpbE,
        tab1=consts["tab1"], tab2=consts["tab2"], mats=consts["mats"], mask3=consts["mask3"],
        colmask=consts["colmask"],
    )
    x = np.asarray(inp["x"], np.float32)
    maps = []
    for b in range(8):
        d = dict(shared)
        d["xT"] = np.ascontiguousarray(x[b].T)
        maps.append(d)
    return maps


def kernel(**inputs):
    inp = {k: np.asarray(v) for k, v in inputs.items()}
    nc, _ = _get_nc()
    maps = make_in_maps(inp)
    res = run_bass_kernel_spmd(nc, maps, core_ids=list(range(8)))
    out = np.stack([np.ascontiguousarray(res.results[b]["yT"].T) for b in range(8)], axis=0)
    return out.astype(np.float32)
```

```python
import numpy as np
from contextlib import ExitStack
import concourse.bass as bass
import concourse.mybir as mybir
from concourse.bass_utils import run_bass_kernel_spmd

F32 = mybir.dt.float32
BF16 = mybir.dt.bfloat16
AF = mybir.ActivationFunctionType
ALU = mybir.AluOpType
AX = mybir.AxisListType

EPOCH = 24000
SAME_SYNC = True

SEQ = 4096
DM = 1024
NL = 4
N_IN = 12544
D_FF = 2816
EPS = 1e-6
NT = 8
TC = 512
COL = dict(aq=0, ak=512, av=1024, bq=1536, bk=2048, bv=2176, cq=2304, ck=3840, cv=5376,
           dq=6912, dk=7424, dv=7936, g=8448)
DIL = (1, 4, 16)


class Buf:
    __slots__ = ("name", "w", "r")

    def __init__(self, name=""):
        self.name = name
        self.w = None
        self.r = []


class _Op:
    __slots__ = ("eng", "fn", "deps", "ldeps", "ddeps", "dma", "pub", "seq")

    def __init__(self, eng, fn, deps, ddeps, dma, ldeps=()):
        self.eng = eng
        self.fn = fn
        self.deps = deps
        self.ldeps = ldeps
        self.ddeps = ddeps
        self.dma = dma
        self.pub = False
        self.seq = None


class Sched:
    ENGS = ("pe", "act", "dve", "pool", "sp")

    def __init__(self, nc, stack):
        self.nc = nc
        self.stack = stack
        self.ops = []
        self.base = 0
        self.cnt = {e: 0 for e in self.ENGS}
        self.sems = {e: [] for e in self.ENGS}
        self.dsem = {}
        self.dcnt = {}
        self.nsem = 0
        self.ninstr = 0

    def _new_sem(self, name):
        self.nsem += 1
        return self.stack.enter_context(self.nc.semaphore(name))

    def _esem(self, e, ep):
        while len(self.sems[e]) <= ep:
            self.sems[e].append(self._new_sem("s_%s_%d" % (e, len(self.sems[e]))))
        return self.sems[e][ep]

    def op(self, eng, fn, reads=(), writes=(), dma=None):
        deps = set()
        ldeps = set()
        ddeps = {}
        base = self.base
        ops = self.ops

        def add(i, dst):
            if i is None or i < base:
                return
            o = ops[i - base]
            if o.dma is not None:
                ddeps[o.dma] = self.dcnt[o.dma]
            else:
                dst.add(i)

        for b in reads:
            add(b.w, deps)
        for b in writes:
            add(b.w, ldeps)
            for r in b.r:
                add(r, ldeps)
        ldeps -= deps
        idx = base + len(ops)
        if dma is not None:
            if dma not in self.dsem:
                self.dsem[dma] = self._new_sem("d_%d" % len(self.dsem))
                self.dcnt[dma] = 0
            self.dcnt[dma] += 16
        ops.append(_Op(eng, fn, deps, ddeps, dma, ldeps))
        for b in reads:
            b.r.append(idx)
        for b in writes:
            b.w = idx
            b.r = []
        return idx

    def flush(self):
        ops = self.ops
        base = self.base
        last = {}
        for i, o in enumerate(ops):
            if o.dma is None and o.fn is not None:
                last[o.eng] = i
        for e in self.ENGS:
            deps = set(base + i for ee, i in last.items() if ee != e)
            ops.append(_Op(e, None, deps, dict(self.dcnt), None))
        import bisect

        def cross(od, o):
            return od.eng != o.eng or o.dma is not None or (SAME_SYNC and o.eng != "pe")

        for o in ops:
            for d in o.deps:
                od = ops[d - base]
                if cross(od, o):
                    od.pub = True
        publ = {e: [] for e in self.ENGS}
        for i, o in enumerate(ops):
            if o.pub:
                publ[o.eng].append(i)
        for i, o in enumerate(ops):
            for d in o.ldeps:
                od = ops[d - base]
                if not cross(od, o):
                    continue
                if od.pub:
                    o.deps.add(d)
                    continue
                lst = publ[od.eng]
                k = bisect.bisect_left(lst, d - base)
                if k < len(lst) and lst[k] < i:
                    o.deps.add(base + lst[k])
                else:
                    od.pub = True
                    bisect.insort(lst, d - base)
                    o.deps.add(d)
        for o in ops:
            if o.pub:
                c = self.cnt[o.eng]
                self.cnt[o.eng] = c + 1
                o.seq = (c // EPOCH, c % EPOCH + 1)
        per = {e: [] for e in self.ENGS}
        seen = {e: {} for e in self.ENGS}
        seend = {e: {} for e in self.ENGS}
        for o in ops:
            F = o.eng
            need = {}
            for d in o.deps:
                od = ops[d - base]
                if od.eng == F and o.dma is None and (F == "pe" or not SAME_SYNC):
                    continue
                if od.seq > need.get(od.eng, (-1, -1)):
                    need[od.eng] = od.seq
            waits = []
            for E, sq in need.items():
                if seen[F].get(E, (-1, -1)) >= sq:
                    continue
                seen[F][E] = sq
                waits.append((self._esem(E, sq[0]), sq[1]))
            for k, v in o.ddeps.items():
                if v == 0 or seend[F].get(k, 0) >= v:
                    continue
                seend[F][k] = v
                waits.append((self.dsem[k], v))
            inc = self._esem(F, o.seq[0]) if o.pub else None
            dinc = self.dsem[o.dma] if o.dma is not None else None
            per[F].append((waits, o.fn, inc, dinc))
            self.ninstr += len(waits) + 1

        def run(eng, lst):
            for waits, fn, inc, dinc in lst:
                emb = None
                if fn is not None and dinc is None and waits:
                    emb = waits[-1]
                    waits = waits[:-1]
                for s, v in waits:
                    eng.wait_ge(s, v)
                if fn is None:
                    continue
                ins = fn(eng)
                if emb is not None:
                    ins._wait_ge(emb[0], emb[1])
                if inc is not None:
                    ins.then_inc(inc, 1)
                if dinc is not None:
                    ins.then_inc(dinc, 16)

        with self.nc.Block() as block:
            @block.tensor
            def _(e):
                run(e, per["pe"])

            @block.scalar
            def _(e):
                run(e, per["act"])

            @block.vector
            def _(e):
                run(e, per["dve"])

            @block.gpsimd
            def _(e):
                run(e, per["pool"])

            @block.sync
            def _(e):
                run(e, per["sp"])
        self.base = base + len(ops)
        self.ops = []


class Tl:
    __slots__ = ("t", "b", "name", "psum")

    def __init__(self, t, name, psum=False):
        self.t = t
        self.b = Buf(name)
        self.name = name
        self.psum = psum

    def __getitem__(self, k):
        return self.t[k]


def _host_consts():
    c = {}
    pos = np.arange(SEQ, dtype=np.float32)
    p = np.arange(128)
    inv32 = (np.float32(10000.0) ** (-(np.arange(0, 64, 2, dtype=np.float32) / np.float32(64)))).astype(np.float32)
    f = p % 32
    half = (p % 64) // 32
    ang = (pos[None, :] * inv32[f][:, None]).astype(np.float32)
    sgn = np.where(half == 0, -1.0, 1.0).astype(np.float32)[:, None]
    tab1 = np.stack([np.cos(ang), np.sin(ang) * sgn], axis=1).astype(np.float32)
    inv16 = (np.float32(10000.0) ** (-(np.arange(0, 32, 2, dtype=np.float32) / np.float32(32)))).astype(np.float32)
    pp = p % 64
    blk = pp // 32
    q = pp % 32
    half2 = q // 16
    f2 = q % 16
    prow = np.floor(pos / 64).astype(np.float32)
    pcol = (pos - prow * 64).astype(np.float32)
    pf = np.where(blk[:, None] == 0, prow[None, :], pcol[None, :]).astype(np.float32)
    ang2 = (pf * inv16[f2][:, None]).astype(np.float32)
    sgn2 = np.where(half2 == 0, -1.0, 1.0).astype(np.float32)[:, None]
    tab2 = np.stack([np.cos(ang2), np.sin(ang2) * sgn2], axis=1).astype(np.float32)
    c["tab1"] = tab1.reshape(128, 2 * SEQ)
    c["tab2"] = tab2.reshape(128, 2 * SEQ)
    mats = np.zeros((128, 5, 128), np.float32)
    for m in range(128):
        part1 = m + 32 if (m % 64) // 32 == 0 else m - 32
        mats[part1, 0, m] = 1.0
        part2 = m + 16 if (m % 32) // 16 == 0 else m - 16
        mats[part2, 1, m] = 1.0
    mats[:, 2, :] = (p[:, None] // 64 == p[None, :] // 64)
    mats[:, 3, :] = 1.0
    mats[:, 4, :] = np.eye(128)
    c["mats"] = mats.reshape(128, 5 * 128)
    i = np.arange(128)[:, None]
    m = np.arange(128)[None, :]
    mask3 = np.stack([(i - m >= 64), (np.abs(i - m) <= 64), (m - i >= 64)], axis=1).astype(np.float32)
    c["mask3"] = mask3.reshape(128, 3 * 128)
    kc = np.arange(128)[:, None] % 64
    qc = np.arange(64)[None, :]
    ws = np.clip(qc - 8, 0, 48)
    c["colmask"] = ((kc >= ws) & (kc < ws + 16)).astype(np.float32)
    return c


P_G1 = 0
P_G2 = 8
P_QKG = 16
P_SUB = 24
P_CW = 25
P_CB = 25 + 132
P_N = 25 + 132 + 44


def _pack_params(inp):
    out = np.zeros((128, NL, P_N), np.float32)
    for l in range(NL):
        out[:, l, P_G1:P_G1 + 8] = inp["norm1_g"][l].reshape(8, 128).T
        out[:, l, P_G2:P_G2 + 8] = inp["norm2_g"][l].reshape(8, 128).T
        qg = inp["qk_g"][l].reshape(8, 64)
        out[:, l, P_QKG:P_QKG + 8] = np.concatenate([qg, qg], axis=1).T
        out[:, l, P_SUB] = inp["subln_g"][l]
        cw = inp["conv_w"][l].reshape(3, 44, 128)
        out[:, l, P_CW:P_CW + 132] = cw.transpose(2, 0, 1).reshape(128, 132)
        out[:, l, P_CB:P_CB + 44] = inp["conv_b"][l].reshape(44, 128).T
    return out.reshape(128, NL * P_N)


def _pack_rpb(rpb):
    kc = np.arange(64)[:, None]
    qc = np.arange(64)[None, :]
    idx = np.clip(kc - qc + 15, 0, 30)
    g = rpb[:, :, :, idx]
    g = np.transpose(g, (0, 3, 1, 2, 4))
    lo = g
    hi = np.concatenate([g[:, :, :, 1:, :], g[:, :, :, 14:15, :]], axis=3)
    return np.ascontiguousarray(np.concatenate([lo, hi], axis=1)).reshape(NL, 128, 8 * 15 * 64)


class Builder:
    def __init__(self, n_layers=NL, debug=False):
        self.n_layers = n_layers
        self.debug = debug
        self.nc = bass.Bass("TRN2", target_bir_lowering=False)
        nc = self.nc
        di = lambda n, s, dt=F32: nc.dram_tensor(n, s, dt, kind="ExternalInput").ap()
        self.xT = di("xT", [DM, SEQ])
        self.w_in = di("w_in", [NL, DM, N_IN])
        self.w_branch = di("w_branch", [NL, 4, 512, DM])
        self.w_out = di("w_out", [NL, DM, DM])
        self.w_up = di("w_up", [NL, DM, 2 * D_FF])
        self.w_down = di("w_down", [NL, D_FF, DM])
        self.params = di("params", [128, NL * P_N])
        self.lam = di("lam", [1, NL * 256])
        self.rpbE = di("rpbE", [NL, 128, 8 * 15 * 64])
        self.c_tab1 = di("tab1", [128, 2 * SEQ])
        self.c_tab2 = di("tab2", [128, 2 * SEQ])
        self.c_mats = di("mats", [128, 5 * 128])
        self.c_mask3 = di("mask3", [128, 3 * 128])
        self.c_colmask = di("colmask", [128, 64])
        self.yT = nc.dram_tensor("yT", [DM, SEQ], F32, kind="ExternalOutput").ap()
        kind = "ExternalOutput" if debug else "Internal"
        ds = lambda n, s, dt: nc.dram_tensor(n, s, dt, kind=kind).ap()
        self.QK = ds("s_qk", [45 * 128, SEQ], BF16)
        self.VA = ds("s_va", [SEQ, 512], BF16)
        self.VB = ds("s_vb", [SEQ, 130], BF16)
        self.VC = ds("s_vc", [3, SEQ, 520], BF16)
        self.VD = ds("s_vd", [SEQ, 520], BF16)
        self.G = ds("s_g", [4096, SEQ], F32)
        self.OT = ds("s_ot", [4, 512, SEQ], BF16)
        self.M = ds("s_m", [D_FF, SEQ], BF16)
        self.XM = ds("s_xm", [DM, SEQ], F32)
        self.XA = ds("s_xa", [DM, SEQ], F32)
        self.XB = ds("s_xb", [DM, SEQ], F32)

    def tile(self, name, shape, dt):
        t = self.ph.enter_context(self.nc.sbuf_tensor(name + "_%d" % self.uid, shape, dt))
        self.uid += 1
        return Tl(t, name)

    def ptile(self, name, shape=(128, 512), dt=F32):
        t = self.ph.enter_context(self.nc.psum_tensor(name + "_%d" % self.uid, [128, 512], F32))
        self.uid += 1
        return Tl(t, name, True)

    def A(self, eng, meth, reads, writes, *a, **k):
        wr = [x.b for x in writes] + [x.b for x in reads if x.psum]
        self.S.op(eng, lambda e: getattr(e, meth)(*a, **k), [x.b for x in reads], wr)

    def load(self, dst_tl, out_ap, in_ap, q="sp"):
        self.S.op(q, lambda e: e.dma_start(out=out_ap, in_=in_ap), [], [dst_tl.b], dma=("L", dst_tl.name))

    def store(self, src_tl, out_ap, in_ap, q="pool"):
        self.S.op(q, lambda e: e.dma_start(out=out_ap, in_=in_ap), [src_tl.b], [], dma=("S", src_tl.name))

    def mm(self, out_tl, out_ap, l_tl, l_ap, r_tl, r_ap, start, stop):
        self.S.op("pe", lambda e: e.matmul(out_ap, l_ap, r_ap, start=start, stop=stop),
                  [l_tl.b, r_tl.b], [out_tl.b])

    def begin(self):
        self.ph = ExitStack()
        self.ph.__enter__()

    def end(self):
        self.S.flush()
        self.ph.__exit__(None, None, None)

    def build(self):
        nc = self.nc
        self.uid = 0
        with ExitStack() as top:
            self.S = Sched(nc, top)
            self.top = top
            self.ph = top
            self.mats = self.tile("mats", [128, 5, 128], F32)
            self.onesb = self.tile("onesb", [128, 128], BF16)
            self.blkb = self.tile("blkb", [128, 128], BF16)
            self.par = self.tile("par", [128, NL, P_N], F32)
            self.nlam = self.tile("nlam", [128, NL], F32)
            self.begin()
            self.lamt = self.tile("lamt", [1, NL * 256], F32)
            self.load(self.mats, self.mats[:].rearrange("p a b -> p (a b)"), self.c_mats)
            self.load(self.par, self.par[:].rearrange("p a b -> p (a b)"), self.params)
            self.load(self.lamt, self.lamt[:], self.lam)
            self.A("dve", "tensor_copy", [self.mats], [self.onesb], out=self.onesb[:], in_=self.mats[:, 3, :])
            self.A("dve", "tensor_copy", [self.mats], [self.blkb], out=self.blkb[:], in_=self.mats[:, 2, :])
            self.lambda_setup()
            self.end()
            xin = self.xT
            for l in range(self.n_layers):
                last = (l == self.n_layers - 1)
                xout = self.yT if last else (self.XA if l % 2 == 0 else self.XB)
                self.layer(l, xin, xout)
                xin = xout
        return nc

    def lambda_setup(self):
        import math
        pr = self.tile("lampr", [1, NL * 2, 64], F32)
        sm = self.tile("lamsm", [1, NL * 2], F32)
        lv = self.tile("lamv", [1, NL], F32)
        ps = self.ptile("lamps", (128, NL))
        lt = self.lamt[:].rearrange("p (l a d) -> p l a d", l=NL, a=4)
        for l in range(NL):
            for j in range(2):
                self.A("dve", "tensor_tensor", [self.lamt], [pr], out=pr[:, l * 2 + j, :], in0=lt[:, l, 2 * j, :],
                       in1=lt[:, l, 2 * j + 1, :], op=ALU.mult)
        self.A("dve", "tensor_reduce", [pr], [sm], out=sm[:], in_=pr[:], axis=AX.X, op=ALU.add)
        self.A("act", "activation", [sm], [sm], out=sm[:], in_=sm[:], func=AF.Exp)
        smv = sm[:].rearrange("p (l j) -> p l j", j=2)
        for l in range(NL):
            li = 0.8 - 0.6 * math.exp(-0.3 * l)
            self.A("dve", "scalar_tensor_tensor", [sm], [lv], out=lv[:, l:l + 1], in0=smv[:, l, 1:2], scalar=-li,
                   in1=smv[:, l, 0:1], op0=ALU.add, op1=ALU.subtract)
        self.mm(ps, ps[:, 0:NL], self.mats, self.mats[0:1, 3, :], lv, lv[:], True, True)
        self.A("dve", "tensor_copy", [ps], [self.nlam], out=self.nlam[:], in_=ps[:, 0:NL])

    def rmsnorm_to_hT(self, l, xsrc, hT, goff):
        xs = [self.tile("nx%d" % i, [128, 8, TC], F32) for i in range(2)]
        sq = [self.tile("nsq%d" % i, [128, 8, TC], F32) for i in range(1)]
        rs = [self.tile("nrs%d" % i, [128, TC], F32) for i in range(2)]
        ps = [self.ptile("nps%d" % i) for i in range(2)]
        xv = xsrc.rearrange("(c p) n -> p c n", p=128)
        for t in range(NT):
            x = xs[t % 2]
            s = sq[0]
            r = rs[t % 2]
            p = ps[t % 2]
            self.load(x, x[:], xv[:, :, t * TC:(t + 1) * TC])
            self.A("act", "activation", [x], [s], out=s[:], in_=x[:], func=AF.Square)
            for c in range(8):
                self.mm(p, p[:], self.mats, self.mats[:, 3, :], s, s[:, c, :], c == 0, c == 7)
            self.A("act", "activation", [p], [r], out=r[:], in_=p[:], func=AF.Ln, scale=1.0 / DM, bias=self.epsc[:, 0:1])
            self.A("act", "activation", [r], [r], out=r[:], in_=r[:], func=AF.Exp, scale=-0.5)
            for c in range(8):
                self.A("dve", "scalar_tensor_tensor", [x, r, self.par], [hT], out=hT[:, c, t * TC:(t + 1) * TC],
                       in0=x[:, c, :], scalar=self.par[:, l, goff + c:goff + c + 1], in1=r[:], op0=ALU.mult, op1=ALU.mult)

    def load_weight_bf16(self, dst, dst_ap, src_ap, stg, shape_ap, eng):
        self.load(stg, shape_ap, src_ap)
        self.A(eng, "tensor_copy", [stg], [dst], out=dst_ap, in_=shape_ap)

    def layer(self, l, xin, xout):
        import math
        self.lambda_init = 0.8 - 0.6 * math.exp(-0.3 * l)
        self.begin()
        self.epsc = self.tile("epsc", [128, 1], F32)
        self.A("pool", "memset", [], [self.epsc], self.epsc[:], EPS)
        hT = self.tile("hT", [128, 8, SEQ], BF16)
        sub = ExitStack()
        outer = self.ph
        self.ph = sub
        sub.__enter__()
        self.rmsnorm_to_hT(l, xin, hT, P_G1)
        self.S.flush()
        sub.__exit__(None, None, None)
        self.ph = outer
        self.proj(l, hT)
        self.end()
        self.begin()
        self.epsc = self.tile("epsc", [128, 1], F32)
        self.A("pool", "memset", [], [self.epsc], self.epsc[:], EPS)
        self.attn_a(l)
        self.end()
        self.begin()
        self.attn_b(l)
        self.end()
        self.begin()
        self.attn_c(l)
        self.end()
        self.begin()
        self.attn_d(l)
        self.end()
        self.begin()
        self.merge(l, xin)
        self.end()
        self.begin()
        self.epsc = self.tile("epsc", [128, 1], F32)
        self.A("pool", "memset", [], [self.epsc], self.epsc[:], EPS)
        hT = self.tile("hT2", [128, 8, SEQ], BF16)
        sub = ExitStack()
        outer = self.ph
        self.ph = sub
        sub.__enter__()
        self.rmsnorm_to_hT(l, self.XM, hT, P_G2)
        self.S.flush()
        sub.__exit__(None, None, None)
        self.ph = outer
        self.ffn_up(l, hT)
        self.end()
        self.begin()
        self.ffn_down(l, xout)
        self.end()

    def pipe(self, items, lags):
        n = len(items)
        for j in range(n + max(lags)):
            for si, lg in enumerate(lags):
                i = j - lg
                if 0 <= i < n and len(items[i]) > si and items[i][si] is not None:
                    items[i][si]()

    def proj(self, l, hT):
        wst = self.tile("wst0", [128, 8, 512], F32)
        wbf = [self.tile("wbf%d" % i, [128, 8, 512], BF16) for i in range(2)]
        tab = self.tile("tab", [128, 2, SEQ], F32)
        pacc = [self.ptile("pacc%d" % i) for i in range(3)]
        pss = [self.ptile("pss%d" % i) for i in range(2)]
        prot = [self.ptile("prot%d" % i) for i in range(2)]
        sq = [self.tile("sq%d" % i, [128, TC], BF16) for i in range(3)]
        rs = [self.tile("rs%d" % i, [128, TC], F32) for i in range(3)]
        yy = [self.tile("yy%d" % i, [128, TC], F32) for i in range(3)]
        t2 = [self.tile("t2%d" % i, [128, TC], F32) for i in range(3)]
        qo = [self.tile("qo%d" % i, [128, TC], BF16) for i in range(3)]
        rowb = [self.tile("rowb%d" % i, [128, SEQ], BF16) for i in range(2)]
        go = [self.tile("go%d" % i, [128, TC], F32) for i in range(3)]
        vst = [self.tile("vst%d" % i, [128, 4, 520], BF16) for i in range(2)]
        vsta = [self.tile("vsta%d" % i, [128, 4, 512], BF16) for i in range(2)]
        for v in vst:
            self.A("pool", "memset", [], [v], v[:], 1.0)
        wv = self.w_in[l].rearrange("(k p) n -> p k n", p=128)
        cnt = dict(acc=0, q=0, row=0, go=0, vs=0, tmp=0, pp=0)
        G = P_QKG

        def wload(gidx, col0, w):
            wb = wbf[gidx % 2]
            self.load(wst, wst[:, :, 0:w], wv[:, :, col0:col0 + w])
            self.A("pool" if gidx % 2 == 0 else "dve", "tensor_copy", [wst], [wb], out=wb[:, :, 0:w], in_=wst[:, :, 0:w])

        def qk_items(items, wb, off, row, gcol, rope, r):
            mat = 0 if rope == "1d" else 1
            rb = None
            if r > 1:
                rb = rowb[cnt["row"] % 2]
                cnt["row"] += 1
            for t in range(NT):
                pa = pacc[cnt["acc"] % 3]
                cnt["acc"] += 1
                j = cnt["tmp"] % 3
                cnt["tmp"] += 1
                ps_, pr_ = pss[cnt["pp"] % 2], prot[cnt["pp"] % 2]
                cnt["pp"] += 1
                s, rr, y, a2 = sq[j], rs[j], yy[j], t2[j]
                tsl = slice(t * TC, (t + 1) * TC)
                if r > 1:
                    n_ = TC // r
                    dst_tl = rb
                    dst = rb[:].rearrange("p (c i) -> p c i", c=r)[:, :, t * n_:(t + 1) * n_]
                else:
                    dst_tl = qo[cnt["q"] % 3]
                    cnt["q"] += 1
                    dst = dst_tl[:]

                def s0(pa=pa, s=s, tsl=tsl):
                    for k in range(8):
                        self.mm(pa, pa[:], wb, wb[:, k, off:off + 128], hT, hT[:, k, tsl], k == 0, k == 7)
                    self.A("act", "activation", [pa], [s], out=s[:], in_=pa[:], func=AF.Square)

                def s1(pa=pa, s=s, rr=rr, y=y, ps_=ps_, dst_tl=dst_tl, dst=dst):
                    self.mm(ps_, ps_[:], self.blkb, self.blkb[:], s, s[:], True, True)
                    self.A("act", "activation", [ps_], [rr], out=rr[:], in_=ps_[:], func=AF.Ln, scale=1.0 / 64, bias=self.epsc[:, 0:1])
                    self.A("act", "activation", [rr], [rr], out=rr[:], in_=rr[:], func=AF.Exp, scale=-0.5)
                    if rope is None:
                        self.A("dve", "scalar_tensor_tensor", [pa, rr, self.par], [dst_tl], out=dst, in0=pa[:],
                               scalar=self.par[:, l, gcol:gcol + 1], in1=rr[:], op0=ALU.mult, op1=ALU.mult)
                    else:
                        self.A("dve", "scalar_tensor_tensor", [pa, rr, self.par], [y], out=y[:], in0=pa[:],
                               scalar=self.par[:, l, gcol:gcol + 1], in1=rr[:], op0=ALU.mult, op1=ALU.mult)

                def s2(y=y, a2=a2, pr_=pr_, dst_tl=dst_tl, dst=dst, tsl=tsl, t=t):
                    if rope is not None:
                        self.mm(pr_, pr_[:], self.mats, self.mats[:, mat, :], y, y[:], True, True)
                        self.A("dve", "tensor_tensor", [pr_, tab], [a2], out=a2[:], in0=pr_[:], in1=tab[:, 1, tsl], op=ALU.mult)
                        self.A("dve", "tensor_tensor", [y, tab], [y], out=y[:], in0=y[:], in1=tab[:, 0, tsl], op=ALU.mult)
                        if r > 1:
                            src1 = y[:].rearrange("p (i c) -> p c i", c=r)
                            src2 = a2[:].rearrange("p (i c) -> p c i", c=r)
                        else:
                            src1, src2 = y[:], a2[:]
                        self.A("pool", "tensor_tensor", [y, a2], [dst_tl], out=dst, in0=src1, in1=src2, op=ALU.add)
                    if r == 1:
                        self.store(dst_tl, self.QK[row * 128:(row + 1) * 128, tsl], dst_tl[:])
                    elif t == NT - 1:
                        self.store(rb, self.QK[row * 128:(row + 1) * 128, :], rb[:])

                items.append([s0, s1, s2])

        def v_items(items, wb, off, w, dst, r, nh, hd):
            hv = hT[:].rearrange("p k (i c) -> p k c i", c=r)
            L = SEQ // r
            stride = hd + 1 if hd == 64 else hd
            vs = None
            for tt in range(32):
                c = (tt * 128) // L
                i0 = (tt * 128) % L
                pa = pacc[cnt["acc"] % 3]
                cnt["acc"] += 1
                if tt % 4 == 0:
                    vs = (vst if hd == 64 else vsta)[cnt["vs"] % 2]
                    cnt["vs"] += 1

                def s0(pa=pa, c=c, i0=i0):
                    for k in range(8):
                        self.mm(pa, pa[:, 0:w], hT, hv[:, k, c, i0:i0 + 128], wb, wb[:, k, off:off + w], k == 0, k == 7)

                def s1(pa=pa, vs=vs, tt=tt):
                    o = vs[:, tt % 4, 0:nh * stride].rearrange("p (h d) -> p h d", h=nh)[:, :, 0:hd]
                    i_ = pa[:, 0:w].rearrange("p (h d) -> p h d", h=nh)
                    if tt % 2 == 0:
                        self.A("act", "activation", [pa], [vs], out=o, in_=i_, func=AF.Copy)
                    else:
                        self.A("dve", "tensor_copy", [pa], [vs], out=o, in_=i_)
                    if tt % 4 == 3:
                        t0 = (tt - 3) * 128
                        self.store(vs, dst[t0:t0 + 512, :].rearrange("(a p) n -> p a n", p=128), vs[:, :, 0:nh * stride])

                items.append([s0, s1])

        def gate_items(items, wb, off, grow):
            for t in range(NT):
                pa = pacc[cnt["acc"] % 3]
                cnt["acc"] += 1
                g = go[cnt["go"] % 3]
                cnt["go"] += 1
                tsl = slice(t * TC, (t + 1) * TC)

                def s0(pa=pa, tsl=tsl):
                    for k in range(8):
                        self.mm(pa, pa[:], wb, wb[:, k, off:off + 128], hT, hT[:, k, tsl], k == 0, k == 7)

                def s1(pa=pa, g=g, tsl=tsl):
                    self.A("act", "activation", [pa], [g], out=g[:], in_=pa[:], func=AF.Sigmoid)
                    self.store(g, self.G[grow * 128:(grow + 1) * 128, tsl], g[:])

                items.append([s0, s1])

        jobsB = [
            (COL["bq"], 512, [("qk", i * 128, 8 + i, G + 2, "ax", 1) for i in range(4)]),
            (COL["bk"], 256, [("qk", 0, 12, G + 3, "ax", 1), ("v", 128, 128, self.VB, 1, 2, 64)]),
        ]
        jobs = [
            (COL["aq"], 512, [("qk", i * 128, 0 + i, G + 0, "1d", 1) for i in range(4)]),
            (COL["ak"], 512, [("qk", i * 128, 4 + i, G + 1, "1d", 1) for i in range(4)]),
            (COL["av"], 512, [("v", 0, 512, self.VA, 1, 4, 128)]),
        ]
        for g in range(3):
            jobs.append((COL["cq"] + g * 512, 512, [("qk", i * 128, 13 + g * 4 + i, G + 4, "1d", DIL[g]) for i in range(4)]))
            jobs.append((COL["ck"] + g * 512, 512, [("qk", i * 128, 25 + g * 4 + i, G + 5, "1d", DIL[g]) for i in range(4)]))
            jobs.append((COL["cv"] + g * 512, 512, [("v", 0, 512, self.VC[g], DIL[g], 8, 64)]))
        jobs.append((COL["dq"], 512, [("qk", i * 128, 37 + i, G + 6, None, 1) for i in range(4)]))
        jobs.append((COL["dk"], 512, [("qk", i * 128, 41 + i, G + 7, None, 1) for i in range(4)]))
        jobs.append((COL["dv"], 512, [("v", 0, 512, self.VD, 1, 8, 64)]))
        for gi in range(8):
            jobs.append((COL["g"] + gi * 512, 512, [("gate", i * 128, gi * 4 + i) for i in range(4)]))

        gctr = [0]

        def run_jobs(jl):
            items = []
            gid0 = gctr[0]
            for ji, (col0, w, subs) in enumerate(jl):
                gid = gid0 + ji
                wb = wbf[gid % 2]
                first = len(items)
                for sj in subs:
                    if sj[0] == "qk":
                        qk_items(items, wb, *sj[1:])
                    elif sj[0] == "v":
                        v_items(items, wb, *sj[1:])
                    else:
                        gate_items(items, wb, *sj[1:])
                orig = items[first][0]
                nxt = jl[ji + 1] if ji + 1 < len(jl) else None

                def s0w(orig=orig, nxt=nxt, gid=gid):
                    if nxt is not None:
                        wload(gid + 1, nxt[0], nxt[1])
                    orig()
                items[first][0] = s0w
            wload(gid0, jl[0][0], jl[0][1])
            gctr[0] += len(jl)
            self.pipe(items, [0, 1, 2])

        self.load(tab, tab[:].rearrange("p a n -> p (a n)"), self.c_tab2)
        run_jobs(jobsB)
        self.load(tab, tab[:].rearrange("p a n -> p (a n)"), self.c_tab1)
        run_jobs(jobs)

    def normalize_rows65(self, po, res_tl, res_ap, n, tmp_row, tmp_bc, pbc):
        self.A("dve", "reciprocal", [po], [tmp_row], out=tmp_row[64:65, 0:n], in_=po[64:65, 0:n])
        self.mm(pbc, pbc[0:64, 0:n], self.mats, self.mats[64:65, 3, 0:64], tmp_row, tmp_row[64:65, 0:n], True, True)
        self.A("act", "activation", [pbc], [tmp_bc], out=tmp_bc[0:64, 0:n], in_=pbc[0:64, 0:n], func=AF.Copy)
        self.A("dve", "tensor_tensor", [po, tmp_bc], [res_tl], out=res_ap, in0=po[0:64, 0:n], in1=tmp_bc[0:64, 0:n], op=ALU.mult)

    def attn_a(self, l):
        qz = [self.tile("qz%d" % c, [128, 4, SEQ], BF16) for c in range(2)]
        ka = self.tile("ka", [128, 4, SEQ], BF16)
        va = self.tile("va", [128, 32, 512], BF16)
        for c in range(2):
            oc = 1 - c
            self.A("pool", "memset", [], [qz[c]], qz[c][oc * 64:(oc + 1) * 64, :, :], 0.0)
        for h in range(4):
            for c in range(2):
                self.load(qz[c], qz[c][c * 64:(c + 1) * 64, h, :], self.QK[h * 128 + c * 64:h * 128 + (c + 1) * 64, :])
            self.load(ka, ka[:, h, :], self.QK[(4 + h) * 128:(5 + h) * 128, :])
        for j in range(4):
            self.load(va, va[:, j * 8:(j + 1) * 8, :], self.VA[j * 1024:(j + 1) * 1024, :].rearrange("(a p) n -> p a n", p=128))
        psc = [self.ptile("psc%d" % i) for i in range(3)]
        po = [self.ptile("po%d" % c) for c in range(2)]
        psm = [self.ptile("psm%d" % c) for c in range(2)]
        pfin = self.ptile("pfin")
        pt = [self.tile("pt%d" % i, [128, TC], BF16) for i in range(6)]
        poS = [self.tile("poS%d" % c, [128, TC], F32) for c in range(2)]
        smS = [self.tile("smS%d" % c, [128, TC], F32) for c in range(2)]
        rc = self.tile("rc", [128, TC], F32)
        res = [self.tile("res%d" % i, [128, TC], F32) for i in range(2)]
        dd = self.tile("dd", [128, TC], F32)
        sq = self.tile("sqa", [128, TC], F32)
        rs = self.tile("rsa", [128, TC], F32)
        ob = [self.tile("oba%d" % i, [128, TC], BF16) for i in range(2)]
        items = []
        n = 0
        for h in range(4):
            for qc in range(NT):
                qs = slice(qc * TC, (qc + 1) * TC)
                for kt in range(32):
                    for c in range(2):
                        sc = psc[n % 3]
                        p = pt[n % 6]
                        n += 1
                        po_ = po[c]
                        psm_ = psm[c]

                        def s0(sc=sc, p=p, h=h, qs=qs, c=c, kt=kt):
                            self.mm(sc, sc[:], ka, ka[:, h, kt * 128:(kt + 1) * 128], qz[c], qz[c][:, h, qs], True, True)
                            self.A("act", "activation", [sc], [p], out=p[:], in_=sc[:], func=AF.Exp, scale=0.125)

                        def s1(p=p, h=h, c=c, kt=kt, po_=po_, psm_=psm_):
                            self.mm(po_, po_[:], va, va[:, kt, h * 128:(h + 1) * 128], p, p[:], kt == 0, kt == 31)
                            self.mm(psm_, psm_[:], self.onesb, self.onesb[:], p, p[:], kt == 0, kt == 31)
                            if kt == 31:
                                self.A("act", "activation", [po_], [poS[c]], out=poS[c][:], in_=po_[:], func=AF.Copy)
                                self.A("act", "activation", [psm_], [smS[c]], out=smS[c][:], in_=psm_[:], func=AF.Copy)

                        def s2(h=h, qc=qc, qs=qs, c=c, kt=kt):
                            if kt != 31:
                                return
                            self.A("dve", "reciprocal", [smS[c]], [rc], out=rc[:], in_=smS[c][:])
                            self.A("dve", "tensor_tensor", [poS[c], rc], [res[c]], out=res[c][:], in0=poS[c][:], in1=rc[:], op=ALU.mult)
                            if c != 1:
                                return
                            self.A("dve", "scalar_tensor_tensor", [res[0], res[1], self.nlam], [dd], out=dd[:], in0=res[1][:],
                                   scalar=self.nlam[:, l:l + 1], in1=res[0][:], op0=ALU.mult, op1=ALU.add)
                            self.A("pool", "tensor_tensor", [dd], [sq], out=sq[:], in0=dd[:], in1=dd[:], op=ALU.mult)
                            self.mm(pfin, pfin[:], self.mats, self.mats[:, 3, :], sq, sq[:], True, True)
                            self.A("act", "activation", [pfin], [rs], out=rs[:], in_=pfin[:], func=AF.Ln, scale=1.0 / 128, bias=self.epsc[:, 0:1])
                            self.A("act", "activation", [rs], [rs], out=rs[:], in_=rs[:], func=AF.Exp, scale=-0.5)
                            self.A("dve", "scalar_tensor_tensor", [dd, rs, self.par], [dd], out=dd[:], in0=dd[:],
                                   scalar=self.par[:, l, P_SUB:P_SUB + 1], in1=rs[:], op0=ALU.mult, op1=ALU.mult)
                            o = ob[(h * NT + qc) % 2]
                            self.A("act", "activation", [dd], [o], out=o[:], in_=dd[:], func=AF.Copy, scale=float(1.0 - self.lambda_init))
                            self.store(o, self.OT[0, h * 128:(h + 1) * 128, qs], o[:])

                        items.append([s0, s1, s2])
        self.pipe(items, [0, 3, 12])

    def attn_b(self, l):
        qb = self.tile("qb", [128, 8, SEQ], BF16)
        kb = self.tile("kb", [128, SEQ], BF16)
        vb = self.tile("vb", [128, 32, 130], BF16)
        for hq in range(8):
            og = 1 - hq // 4
            self.A("pool", "memset", [], [qb], qb[og * 64:(og + 1) * 64, hq, :], 0.0)
        for hq in range(8):
            g, s = hq // 4, hq % 4
            self.load(qb, qb[g * 64:(g + 1) * 64, hq, :], self.QK[8 * 128 + hq * 64:8 * 128 + (hq + 1) * 64, :])
        self.load(kb, kb[:], self.QK[12 * 128:13 * 128, :])
        self.load(vb, vb[:], self.VB.rearrange("(a p) n -> p a n", p=128))
        psc = [self.ptile("psc%d" % i) for i in range(3)]
        po = [[self.ptile("pob%d%d" % (g, i)) for i in range(2)] for g in range(2)]
        pbc = self.ptile("pbc")
        pt = [self.tile("pt%d" % i, [128, TC], BF16) for i in range(6)]
        trow = self.tile("trow", [128, TC], F32)
        tbc = self.tile("tbc", [128, TC], F32)
        ob = [self.tile("obb%d" % i, [128, TC], BF16) for i in range(2)]
        items = []
        n = 0
        m = 0
        gi = 0
        for s in range(4):
            for qc in range(NT):
                qs = slice(qc * TC, (qc + 1) * TC)
                buf = gi % 2
                gi += 1
                for kt in range(32):
                    for g in range(2):
                        hq = g * 4 + s
                        ps_ = slice(g * 64, (g + 1) * 64)
                        o_ps = po[g][buf]
                        sc = psc[n % 3]
                        p = pt[n % 6]
                        n += 1

                        def s0(sc=sc, p=p, hq=hq, qs=qs, kt=kt):
                            self.mm(sc, sc[:], kb, kb[:, kt * 128:(kt + 1) * 128], qb, qb[:, hq, qs], True, True)
                            self.A("act", "activation", [sc], [p], out=p[:], in_=sc[:], func=AF.Exp, scale=0.125)

                        def s1(p=p, g=g, kt=kt, o_ps=o_ps):
                            self.mm(o_ps, o_ps[0:65, :], vb, vb[:, kt, g * 65:(g + 1) * 65], p, p[:], kt == 0, kt == 31)

                        def s2(g=g, hq=hq, qs=qs, kt=kt, o_ps=o_ps):
                            if kt != 31:
                                return
                            o = ob[hq % 2]
                            self.normalize_rows65(o_ps, o, o[0:64, :], TC, trow, tbc, pbc)
                            self.store(o, self.OT[1, hq * 64:(hq + 1) * 64, qs], o[0:64, :])

                        items.append([s0, s1, s2])
        self.pipe(items, [0, 3, 12])

    def attn_c(self, l):
        mask = self.tile("mask3", [128, 3, 128], F32)
        self.load(mask, mask[:].rearrange("p a b -> p (a b)"), self.c_mask3)
        qc_ = self.tile("qc", [128, 3, SEQ], BF16)
        kc_ = self.tile("kc", [128, 3, SEQ], BF16)
        vc_ = self.tile("vc", [128, 3, 32, 130], BF16)
        acc = [self.tile("acc%d" % i, [65, SEQ], F32) for i in range(2)]
        psc = [self.ptile("pscc%d" % i, (128, 384)) for i in range(3)]
        po = [self.ptile("poc%d" % i) for i in range(2)]
        pbc = self.ptile("pbcc")
        ex = [self.tile("ex%d" % i, [128, 3, 128], F32) for i in range(3)]
        pt = [self.tile("ptc%d" % i, [128, 3, 128], BF16) for i in range(4)]
        trow = self.tile("trowc", [128, TC], F32)
        ob = [self.tile("obc%d" % i, [128, TC], BF16) for i in range(2)]
        n = 0
        m = 0
        for jp in range(4):
            for g in range(3):
                self.load(qc_, qc_[:, g, :], self.QK[(13 + g * 4 + jp) * 128:(14 + g * 4 + jp) * 128, :])
                self.load(kc_, kc_[:, g, :], self.QK[(25 + g * 4 + jp) * 128:(26 + g * 4 + jp) * 128, :])
                self.load(vc_, vc_[:, g, :, :], self.VC[g].rearrange("(a p) n -> p a n", p=128)[:, :, jp * 130:(jp + 1) * 130])
            items = []
            for hh in range(2):
                ps_ = slice(hh * 64, (hh + 1) * 64)
                ac = acc[hh]
                hd = jp * 2 + hh
                for g in range(3):
                    r = DIL[g]
                    L = SEQ // r
                    tps = L // 128
                    for qb4 in range(8):
                        o_ps = po[m % 2]
                        m += 1
                        for u in range(4):
                            qb = qb4 * 4 + u
                            seg = qb // tps
                            kts = [k for k in (qb - 1, qb, qb + 1) if k // tps == seg and 0 <= k < 32]
                            j0 = kts[0] - (qb - 1)
                            nk = len(kts)
                            sc = psc[n % 3]
                            e_ = ex[n % 3]
                            p = pt[n % 4]
                            n += 1

                            def s0(sc=sc, e_=e_, p=p, kts=kts, j0=j0, nk=nk, qb=qb, g=g, ps_=ps_):
                                for k in kts:
                                    j = k - (qb - 1)
                                    self.mm(sc, sc[:, j * 128:(j + 1) * 128], kc_, kc_[ps_, g, k * 128:(k + 1) * 128],
                                            qc_, qc_[ps_, g, qb * 128:(qb + 1) * 128], True, True)
                                scv = sc[:, 0:384].rearrange("p (a b) -> p a b", a=3)
                                self.A("act", "activation", [sc], [e_], out=e_[:, j0:j0 + nk, :], in_=scv[:, j0:j0 + nk, :], func=AF.Exp, scale=0.125)
                                self.A("pool", "tensor_tensor", [e_, mask], [p], out=p[:, j0:j0 + nk, :], in0=e_[:, j0:j0 + nk, :],
                                       in1=mask[:, j0:j0 + nk, :], op=ALU.mult)

                            def s1(p=p, kts=kts, nk=nk, qb=qb, g=g, hh=hh, u=u, o_ps=o_ps, qb4=qb4, r=r, L=L, ac=ac, hd=hd):
                                for ki, k in enumerate(kts):
                                    j = k - (qb - 1)
                                    self.mm(o_ps, o_ps[0:65, u * 128:(u + 1) * 128], vc_, vc_[:, g, k, hh * 65:(hh + 1) * 65],
                                            p, p[:, j, :], ki == 0, ki == nk - 1)
                                if u != 3:
                                    return
                                pos0 = qb4 * 512
                                av = ac[:].rearrange("p (i c) -> p c i", c=r)
                                if L >= 512:
                                    c0, i0 = pos0 // L, pos0 % L
                                    dst = av[0:65, c0:c0 + 1, i0:i0 + 512]
                                    src = o_ps[0:65, :].rearrange("p (c i) -> p c i", c=1)
                                else:
                                    ncl = 512 // L
                                    c0 = pos0 // L
                                    dst = av[0:65, c0:c0 + ncl, :]
                                    src = o_ps[0:65, :].rearrange("p (c i) -> p c i", c=ncl)
                                if g == 0:
                                    self.A("act", "activation", [o_ps], [ac], out=dst, in_=src, func=AF.Copy)
                                else:
                                    self.A("dve", "tensor_tensor", [o_ps, ac], [ac], out=dst, in0=dst, in1=src, op=ALU.add)
                                if g == 2 and qb4 == 7:
                                    for t in range(NT):
                                        o = ob[(hd * NT + t) % 2]
                                        ts_ = slice(t * TC, (t + 1) * TC)
                                        self.A("dve", "reciprocal", [ac], [trow], out=trow[64:65, :], in_=ac[64:65, ts_])
                                        self.mm(pbc, pbc[0:64, :], self.mats, self.mats[64:65, 3, 0:64], trow, trow[64:65, :], True, True)
                                        self.A("dve", "tensor_tensor", [pbc, ac], [o], out=o[0:64, :], in0=pbc[0:64, :], in1=ac[0:64, ts_], op=ALU.mult)
                                        self.store(o, self.OT[2, hd * 64:(hd + 1) * 64, ts_], o[0:64, :])

                            items.append([s0, s1])
            self.pipe(items, [0, 2])

    def attn_d(self, l):
        qd = self.tile("qd", [128, 4, SEQ], BF16)
        kd = self.tile("kd", [128, 4, SEQ], BF16)
        ve = self.tile("ve", [128, 32, 520], BF16)
        vo = self.tile("vo", [128, 31, 520], BF16)
        E = self.tile("E", [128, 8, 15, 64], F32)
        cm = self.tile("cm", [128, 64], F32)
        self.load(cm, cm[:], self.c_colmask)
        for i in range(4):
            self.load(qd, qd[:, i, :], self.QK[(37 + i) * 128:(38 + i) * 128, :])
            self.load(kd, kd[:, i, :], self.QK[(41 + i) * 128:(42 + i) * 128, :])
        for j in range(4):
            self.load(ve, ve[:, j * 8:(j + 1) * 8, :], self.VD[j * 1024:(j + 1) * 1024, :].rearrange("(a p) n -> p a n", p=128))
        for j in range(4):
            na = 8 if j < 3 else 7
            self.load(vo, vo[:, j * 8:j * 8 + na, :], self.VD[64 + j * 1024:64 + j * 1024 + na * 128, :].rearrange("(a p) n -> p a n", p=128))
        for h in range(8):
            self.load(E, E[:, h, :, :].rearrange("p a b -> p (a b)"), self.rpbE[l][:, h * 960:(h + 1) * 960])
        for h in range(8):
            self.A("act", "activation", [E], [E], out=E[:, h, :, :], in_=E[:, h, :, :], func=AF.Exp)
            self.A("pool", "tensor_tensor", [E, cm], [E], out=E[:, h, :, :], in0=E[:, h, :, :],
                   in1=cm[:].rearrange("p (a b) -> p a b", a=1).to_broadcast([128, 15, 64]), op=ALU.mult)
        psc = [self.ptile("pscd%d" % i, (128, 256)) for i in range(3)]
        po = [self.ptile("pod%d" % i) for i in range(2)]
        pbc = self.ptile("pbcd")
        ex = [self.tile("exd%d" % i, [128, 4, 64], F32) for i in range(3)]
        pt = [self.tile("ptd%d" % i, [128, 4, 64], BF16) for i in range(4)]
        trow = self.tile("trowd", [128, TC], F32)
        tbc = self.tile("tbcd", [128, TC], F32)
        ob = [self.tile("obd%d" % i, [128, TC], BF16) for i in range(2)]
        items = []
        n = 0
        m = 0
        for h in range(8):
            ch, hh = h // 2, h % 2
            ps_ = slice(hh * 64, (hh + 1) * 64)
            for r8 in range(8):
                o_ps = po[m % 2]
                o = ob[m % 2]
                m += 1
                for u in range(8):
                    r = r8 * 8 + u
                    rs_ = min(max(r - 4, 0), 56)
                    base = rs_ - r + 7
                    sc = psc[n % 3]
                    e_ = ex[n % 3]
                    p = pt[n % 4]
                    n += 1

                    def s0(sc=sc, e_=e_, p=p, r=r, rs_=rs_, base=base, h=h, ch=ch, ps_=ps_, n=n):
                        for i in range(4):
                            k0 = (rs_ + 2 * i) * 64
                            self.mm(sc, sc[:, i * 64:(i + 1) * 64], kd, kd[ps_, ch, k0:k0 + 128], qd, qd[ps_, ch, r * 64:(r + 1) * 64], True, True)
                        self.A("act", "activation", [sc], [e_], out=e_[:], in_=sc[:, 0:256].rearrange("p (a b) -> p a b", a=4), func=AF.Exp, scale=0.125)
                        ev = E[:, h, base:base + 7:2, :]
                        self.A("pool" if n % 2 == 0 else "dve", "tensor_tensor", [e_, E], [p], out=p[:], in0=e_[:], in1=ev, op=ALU.mult)

                    def s1(p=p, rs_=rs_, h=h, u=u, o_ps=o_ps, o=o, r8=r8):
                        for i in range(4):
                            row0 = rs_ + 2 * i
                            if row0 % 2 == 0:
                                vt, vi = ve, row0 // 2
                            else:
                                vt, vi = vo, (row0 - 1) // 2
                            self.mm(o_ps, o_ps[0:65, u * 64:(u + 1) * 64], vt, vt[:, vi, h * 65:(h + 1) * 65], p, p[:, i, :], i == 0, i == 3)
                        if u != 7:
                            return
                        self.normalize_rows65(o_ps, o, o[0:64, :], TC, trow, tbc, pbc)
                        self.store(o, self.OT[3, h * 64:(h + 1) * 64, r8 * TC:(r8 + 1) * TC], o[0:64, :])

                    items.append([s0, s1])
        self.pipe(items, [0, 2])

    def merge(self, l, xin):
        wb = self.tile("wbr", [128, 16, DM], BF16)
        wo = self.tile("wo", [128, 8, DM], BF16)
        stg = [self.tile("mstg%d" % i, [128, 4, DM], F32) for i in range(2)]
        n = 0
        for i in range(4):
            s = stg[n % 2]
            n += 1
            self.load(s, s[:], self.w_branch[l, i].rearrange("(k p) n -> p k n", p=128))
            self.A("pool" if n % 2 == 0 else "dve", "tensor_copy", [s], [wb], out=wb[:, i * 4:(i + 1) * 4, :], in_=s[:])
        for j in range(2):
            s = stg[n % 2]
            n += 1
            self.load(s, s[:], self.w_out[l][j * 512:(j + 1) * 512, :].rearrange("(k p) n -> p k n", p=128))
            self.A("pool" if n % 2 == 0 else "dve", "tensor_copy", [s], [wo], out=wo[:, j * 4:(j + 1) * 4, :], in_=s[:])
        ot = [self.tile("mot%d" % i, [128, 16, TC], BF16) for i in range(2)]
        xs = [self.tile("mx%d" % i, [128, 8, TC], F32) for i in range(2)]
        gt = [self.tile("mg%d" % i, [128, TC], F32) for i in range(4)]
        mt = [self.tile("mm%d" % i, [128, 8, TC], BF16) for i in range(2)]
        acc = [self.tile("macc%d" % i, [128, TC], F32) for i in range(2)]
        tmp = [self.tile("mtmp%d" % i, [128, TC], F32) for i in range(2)]
        py = [self.ptile("py%d" % i) for i in range(3)]
        px = [self.ptile("px%d" % i) for i in range(2)]
        xo = [self.tile("mxo%d" % i, [128, TC], F32) for i in range(3)]
        xv = xin.rearrange("(c p) n -> p c n", p=128)
        otv = self.OT.rearrange("b (k p) n -> p (b k) n", p=128)
        ng = 0
        ny = 0
        nx = 0
        for t in range(NT):
            ts_ = slice(t * TC, (t + 1) * TC)
            o = ot[t % 2]
            x = xs[t % 2]
            mtt = mt[t % 2]
            for i in range(4):
                self.load(o, o[:, i * 4:(i + 1) * 4, :], otv[:, i * 4:(i + 1) * 4, ts_])
            self.load(x, x[:], xv[:, :, ts_])
            for f in range(8):
                a = acc[f % 2]
                for i in range(4):
                    g = gt[ng % 4]
                    ng += 1
                    self.load(g, g[:], self.G[(i * 8 + f) * 128:(i * 8 + f + 1) * 128, ts_])
                    p = py[ny % 3]
                    ny += 1
                    for k in range(4):
                        self.mm(p, p[:], wb, wb[:, i * 4 + k, f * 128:(f + 1) * 128], o, o[:, i * 4 + k, :], k == 0, k == 3)
                    if i == 0:
                        self.A("dve", "tensor_tensor", [p, g], [a], out=a[:], in0=p[:], in1=g[:], op=ALU.mult)
                    else:
                        tm = tmp[i % 2]
                        self.A("dve", "tensor_tensor", [p, g], [tm], out=tm[:], in0=p[:], in1=g[:], op=ALU.mult)
                        if i < 3:
                            self.A("pool", "tensor_tensor", [a, tm], [a], out=a[:], in0=a[:], in1=tm[:], op=ALU.add)
                        else:
                            self.A("pool", "tensor_tensor", [a, tm], [mtt], out=mtt[:, f, :], in0=a[:], in1=tm[:], op=ALU.add)
            for fo in range(8):
                p = px[nx % 2]
                xo_ = xo[nx % 3]
                nx += 1
                for k in range(8):
                    self.mm(p, p[:], wo, wo[:, k, fo * 128:(fo + 1) * 128], mtt, mtt[:, k, :], k == 0, k == 7)
                self.A("dve", "tensor_tensor", [p, x], [xo_], out=xo_[:], in0=p[:], in1=x[:, fo, :], op=ALU.add)
                self.store(xo_, self.XM[fo * 128:(fo + 1) * 128, ts_], xo_[:])

    def ffn_up(self, l, hT):
        wst = [self.tile("fst%d" % i, [128, 8, 256], F32) for i in range(1)]
        wbf = [self.tile("fbf%d" % i, [128, 8, 256], BF16) for i in range(4)]
        uas = [self.tile("ua%d" % i, [128, SEQ + 2], F32) for i in range(2)]
        ugs = [self.tile("ug%d" % i, [128, SEQ + 2], F32) for i in range(2)]
        ca = self.tile("ca", [128, SEQ], F32)
        cg = self.tile("cg", [128, SEQ], F32)
        mrow = [self.tile("mrow%d" % i, [128, SEQ], BF16) for i in range(1)]
        pacc = [self.ptile("fpa%d" % i) for i in range(4)]
        for u in uas + ugs:
            self.A("pool", "memset", [], [u], u[:, 0:1], 0.0)
            self.A("pool", "memset", [], [u], u[:, SEQ + 1:SEQ + 2], 0.0)
        wv = self.w_up[l].rearrange("(k p) n -> p k n", p=128)
        ngrp = 0
        nacc = 0
        for j in range(11):
            w = 256
            wbs = []
            for part in range(2):
                st = wst[0]
                wb = wbf[ngrp % 4]
                ngrp += 1
                c0 = part * D_FF + j * 256
                self.load(st, st[:, :, 0:w], wv[:, :, c0:c0 + w])
                self.A("pool" if part == 0 else "dve", "tensor_copy", [st], [wb], out=wb[:, :, 0:w], in_=st[:, :, 0:w])
                wbs.append(wb)
            for ii in range(w // 128):
                i = j * 2 + ii
                ua, ug = uas[i % 2], ugs[i % 2]
                for part, (u, cdst) in enumerate(((ua, ca), (ug, cg))):
                    wb = wbs[part]
                    for t in range(NT):
                        p = pacc[nacc % 4]
                        nacc += 1
                        for k in range(8):
                            self.mm(p, p[:], wb, wb[:, k, ii * 128:(ii + 1) * 128], hT, hT[:, k, t * TC:(t + 1) * TC], k == 0, k == 7)
                        self.A("act", "activation", [p], [u], out=u[:, 1 + t * TC:1 + (t + 1) * TC], in_=p[:], func=AF.Copy)
                    ch = part * 22 + i
                    w0 = self.par[:, l, P_CW + 0 * 44 + ch:P_CW + 0 * 44 + ch + 1]
                    w1 = self.par[:, l, P_CW + 1 * 44 + ch:P_CW + 1 * 44 + ch + 1]
                    w2 = self.par[:, l, P_CW + 2 * 44 + ch:P_CW + 2 * 44 + ch + 1]
                    bb = self.par[:, l, P_CB + ch:P_CB + ch + 1]
                    eng = "dve"
                    for hf in range(2):
                        hs = slice(hf * 2048, (hf + 1) * 2048)
                        self.A(eng, "tensor_scalar", [u, self.par], [cdst], out=cdst[:, hs], in0=u[:, hf * 2048:hf * 2048 + 2048],
                               scalar1=w0, scalar2=bb, op0=ALU.mult, op1=ALU.add)
                        self.A(eng, "scalar_tensor_tensor", [u, cdst, self.par], [cdst], out=cdst[:, hs], in0=u[:, 1 + hf * 2048:1 + hf * 2048 + 2048],
                               scalar=w1, in1=cdst[:, hs], op0=ALU.mult, op1=ALU.add)
                        self.A(eng, "scalar_tensor_tensor", [u, cdst, self.par], [cdst], out=cdst[:, hs], in0=u[:, 2 + hf * 2048:2 + hf * 2048 + 2048],
                               scalar=w2, in1=cdst[:, hs], op0=ALU.mult, op1=ALU.add)
                mr = mrow[0]
                for hf in range(2):
                    hs = slice(hf * 2048, (hf + 1) * 2048)
                    self.A("act", "activation", [ca], [ca], out=ca[:, hs], in_=ca[:, hs], func=AF.Silu)
                    self.A("pool" if hf == 0 else "dve", "tensor_tensor", [ca, cg], [mr], out=mr[:, hs], in0=ca[:, hs], in1=cg[:, hs], op=ALU.mult)
                self.store(mr, self.M[i * 128:(i + 1) * 128, :], mr[:])

    def ffn_down(self, l, xout):
        wd = self.tile("wd", [128, 22, DM], BF16)
        stg = [self.tile("dstg%d" % i, [128, 2, DM], F32) for i in range(2)]
        wv = self.w_down[l].rearrange("(k p) n -> p k n", p=128)
        for j in range(11):
            s = stg[j % 2]
            self.load(s, s[:], wv[:, j * 2:(j + 1) * 2, :])
            self.A("pool" if j % 2 == 0 else "dve", "tensor_copy", [s], [wd], out=wd[:, j * 2:(j + 1) * 2, :], in_=s[:])
        mt = [self.tile("dm%d" % i, [128, 22, TC], BF16) for i in range(2)]
        xs = [self.tile("dx%d" % i, [128, 8, TC], F32) for i in range(2)]
        xo = [self.tile("dxo%d" % i, [128, TC], F32) for i in range(3)]
        px = [self.ptile("dpx%d" % i) for i in range(3)]
        mv = self.M.rearrange("(k p) n -> p k n", p=128)
        xv = self.XM.rearrange("(c p) n -> p c n", p=128)
        nx = 0
        for t in range(NT):
            ts_ = slice(t * TC, (t + 1) * TC)
            m = mt[t % 2]
            x = xs[t % 2]
            self.load(m, m[:, 0:11, :], mv[:, 0:11, ts_])
            self.load(m, m[:, 11:22, :], mv[:, 11:22, ts_])
            self.load(x, x[:], xv[:, :, ts_])
            for fo in range(8):
                p = px[nx % 3]
                xo_ = xo[nx % 3]
                nx += 1
                for k in range(22):
                    self.mm(p, p[:], wd, wd[:, k, fo * 128:(fo + 1) * 128], m, m[:, k, :], k == 0, k == 21)
                self.A("dve", "tensor_tensor", [p, x], [xo_], out=xo_[:], in0=p[:], in1=x[:, fo, :], op=ALU.add)
                self.store(xo_, xout[fo * 128:(fo + 1) * 128, ts_], xo_[:])


_CACHE = {}


def _get_nc(n_layers=NL, debug=False):
    key = (n_layers, debug)
    if key not in _CACHE:
        b = Builder(n_layers, debug)
        _CACHE[key] = (b.build(), b)
    return _CACHE[key]


def make_in_maps(inp):
    consts = _host_consts()
    params = _pack_params(inp)
    lam = np.ascontiguousarray(np.asarray(inp["lam"], np.float32).reshape(1, NL * 256))
    rpbE = _pack_rpb(np.asarray(inp["rpb"], np.float32))
    shared = dict(
        w_in=np.ascontiguousarray(inp["w_in"], np.float32),
        w_branch=np.ascontiguousarray(inp["w_branch"], np.float32),
        w_out=np.ascontiguousarray(inp["w_out"], np.float32),
        w_up=np.ascontiguousarray(inp["w_up"], np.float32),
        w_down=np.ascontiguousarray(inp["w_down"], np.float32),
        params=params, lam=lam, rpbE=rpbE,
        tab1=consts["tab1"], tab2=consts["tab2"], mats=consts["mats"], mask3=consts["mask3"],
        colmask=consts["colmask"],
    )
    x = np.asarray(inp["x"], np.float32)
    maps = []
    for b in range(8):
        d = dict(shared)
        d["xT"] = np.ascontiguousarray(x[b].T)
        maps.append(d)
    return maps


def kernel(**inputs):
    inp = {k: np.asarray(v) for k, v in inputs.items()}
    nc, _ = _get_nc()
    maps = make_in_maps(inp)
    res = run_bass_kernel_spmd(nc, maps, core_ids=list(range(8)))
    out = np.stack([np.ascontiguousarray(res.results[b]["yT"].T) for b in range(8)], axis=0)
    return out.astype(np.float32)
```
